# Optimizing a Trainium2 kernel written in Bass

```python
import math
import jax, jax.numpy as jnp
from jax import lax
import numpy as np

D_MODEL = 1024
BATCH = 32
SEQ = 2048
DEPTH = 2

CHUNK = 64

N_A = DEPTH // 2
N_B = DEPTH - N_A

SSM_EXPAND = 2
D_INNER = SSM_EXPAND * D_MODEL
SSM_HEAD_DIM = 64
SSM_HEADS = D_INNER // SSM_HEAD_DIM
SSM_GROUPS = 8
HEADS_PER_GROUP = SSM_HEADS // SSM_GROUPS
SSM_STATE = 128
CONV_K = 4
CONV_DIM = D_INNER + 2 * SSM_GROUPS * SSM_STATE
D_IN_PROJ = D_INNER + CONV_DIM + SSM_HEADS
DT_MIN = 0.001
DT_MAX = 0.1

ATT_HEADS = 16
ATT_HEAD_DIM = 64
ATT_DIM = ATT_HEADS * ATT_HEAD_DIM
Q_BLOCK = 128
FORGET_BIAS_MEAN = 3.0

D_FF = ((8 * D_MODEL + 3 * 256 - 1) // (3 * 256)) * 256

DEEPNORM_ALPHA = (2.0 * DEPTH) ** 0.25
DEEPNORM_BETA = (8.0 * DEPTH) ** -0.25

LN_EPS = 1e-5
RMS_EPS = 1e-5

kernel_name = "yoco_ssd_fox_deepnorm_trunk"


def _layer_norm(x, g, b):
    xf = x.astype(jnp.float32)
    mu = jnp.mean(xf, axis=-1, keepdims=True)
    var = jnp.mean(jnp.square(xf - mu), axis=-1, keepdims=True)
    y = (xf - mu) * lax.rsqrt(var + LN_EPS) * g.astype(jnp.float32) + b.astype(jnp.float32)
    return y.astype(x.dtype)


def _swiglu(x, w_gate, w_up, w_down):
    return (jax.nn.silu(x @ w_gate) * (x @ w_up)) @ w_down


def _causal_depthwise_conv(u, w, b):
    k = w.shape[0]
    out = lax.conv_general_dilated(
        u, w[:, None, :].astype(u.dtype), window_strides=(1,), padding=[(k - 1, 0)],
        dimension_numbers=('NWC', 'WIO', 'NWC'), feature_group_count=u.shape[-1])
    return out + b


def _ssd_scan(xh, dt, A, Bm, Cm):
    bsz, seq = xh.shape[0], xh.shape[1]
    nc = seq // CHUNK

    def to_chunks(t):
        return jnp.moveaxis(t.reshape((bsz, nc, CHUNK) + t.shape[2:]), 1, 0)

    xc, dtc, Bc, Cc = to_chunks(xh), to_chunks(dt), to_chunks(Bm), to_chunks(Cm)
    causal = jnp.tril(jnp.ones((CHUNK, CHUNK), dtype=bool))[None, :, :, None, None]

    def step(state, inp):
        x_c, dt_c, B_c, C_c = inp
        cum = jnp.cumsum(dt_c * A, axis=1)
        seg = cum[:, :, None] - cum[:, None, :]
        decay = jnp.exp(jnp.where(causal, seg, -jnp.inf))
        cb = jnp.einsum('blgn,bsgn->blsg', C_c, B_c)
        y_intra = jnp.einsum('blsg,blsgr,bsgr,bsgrp->blgrp', cb, decay, dt_c, x_c)
        y_state = jnp.einsum('blgn,bgrpn->blgrp', C_c, state) * jnp.exp(cum)[..., None]
        w_end = jnp.exp(cum[:, -1:] - cum) * dt_c
        new_state = state * jnp.exp(cum[:, -1])[..., None, None] + jnp.einsum(
            'bsgn,bsgr,bsgrp->bgrpn', B_c, w_end, x_c)
        return new_state, y_intra + y_state

    state0 = jnp.zeros((bsz, SSM_GROUPS, HEADS_PER_GROUP, SSM_HEAD_DIM, SSM_STATE), jnp.float32)
    _, y = lax.scan(step, state0, (xc, dtc, Bc, Cc))
    return jnp.moveaxis(y, 0, 1).reshape(xh.shape)


def _mamba2_mixer(x, in_w, conv_w, conv_b, dt_bias, a_log, d_skip, norm_w, out_w):
    f32 = jnp.float32
    bsz, seq, _ = x.shape
    proj = x @ in_w
    z, xbc, dt_raw = jnp.split(proj, [D_INNER, D_INNER + CONV_DIM], axis=-1)
    xbc = jax.nn.silu(_causal_depthwise_conv(xbc, conv_w, conv_b))
    xs, Bm, Cm = jnp.split(xbc, [D_INNER, D_INNER + SSM_GROUPS * SSM_STATE], axis=-1)
    xh = xs.reshape(bsz, seq, SSM_GROUPS, HEADS_PER_GROUP, SSM_HEAD_DIM).astype(f32)
    Bm = Bm.reshape(bsz, seq, SSM_GROUPS, SSM_STATE).astype(f32)
    Cm = Cm.reshape(bsz, seq, SSM_GROUPS, SSM_STATE).astype(f32)
    dt = jax.nn.softplus(dt_raw.astype(f32) + dt_bias.astype(f32))
    dt = dt.reshape(bsz, seq, SSM_GROUPS, HEADS_PER_GROUP)
    A = -jnp.exp(a_log.astype(f32)).reshape(SSM_GROUPS, HEADS_PER_GROUP)
    y = _ssd_scan(xh, dt, A, Bm, Cm)
    y = y + d_skip.astype(f32).reshape(SSM_GROUPS, HEADS_PER_GROUP)[..., None] * xh
    y = y.reshape(bsz, seq, D_INNER) * jax.nn.silu(z.astype(f32))
    yg = y.reshape(bsz, seq, SSM_GROUPS, D_INNER // SSM_GROUPS)
    yg = yg * lax.rsqrt(jnp.mean(jnp.square(yg), axis=-1, keepdims=True) + RMS_EPS)
    y = yg.reshape(bsz, seq, D_INNER) * norm_w.astype(f32)
    return y.astype(x.dtype) @ out_w


def _shared_kv(x, kv_w, kv_b_f):
    bsz, seq, _ = x.shape
    kvf = x @ kv_w
    k, v, f = jnp.split(kvf, [ATT_DIM, 2 * ATT_DIM], axis=-1)
    k = k.reshape(bsz, seq, ATT_HEADS, ATT_HEAD_DIM)
    v = v.reshape(bsz, seq, ATT_HEADS, ATT_HEAD_DIM)
    log_f = jax.nn.log_sigmoid(f.astype(jnp.float32) + kv_b_f.astype(jnp.float32))
    fcum = jnp.transpose(jnp.cumsum(log_f, axis=1), (0, 2, 1))
    return k, v, fcum


def _forgetting_attention(x, q_w, o_w, k, v, fcum):
    bsz, seq, _ = x.shape
    q = (x @ q_w).reshape(bsz, seq, ATT_HEADS, ATT_HEAD_DIM) * (ATT_HEAD_DIM ** -0.5)
    outs = []
    for blk in range(seq // Q_BLOCK):
        lo, hi = blk * Q_BLOCK, (blk + 1) * Q_BLOCK
        s = jnp.einsum('bqhd,bkhd->bhqk', q[:, lo:hi], k[:, :hi]).astype(jnp.float32)
        bias = fcum[:, :, lo:hi, None] - fcum[:, :, None, :hi]
        mask = jnp.arange(hi)[None, :] <= jnp.arange(lo, hi)[:, None]
        p = jax.nn.softmax(jnp.where(mask, s + bias, -jnp.inf), axis=-1).astype(v.dtype)
        outs.append(jnp.einsum('bhqk,bkhd->bqhd', p, v[:, :hi]))
    o = jnp.concatenate(outs, axis=1).reshape(bsz, seq, ATT_DIM)
    return o @ o_w


def setup_inputs(seed: int = 0) -> dict:
    key = jax.random.key(seed)
    ks = jax.random.split(key, 24)
    f32 = jnp.float32
    nrm = lambda k, shape, scale: jax.random.normal(k, shape, f32) * scale
    beta = DEEPNORM_BETA

    x = jax.random.normal(ks[0], (BATCH, SEQ, D_MODEL), f32)

    ssm_in_w = nrm(ks[1], (N_A, D_MODEL, D_IN_PROJ), D_MODEL ** -0.5)
    ssm_conv_w = nrm(ks[2], (N_A, CONV_K, CONV_DIM), CONV_K ** -0.5)
    ssm_conv_b = nrm(ks[3], (N_A, CONV_DIM), 0.02)
    dt0 = jnp.exp(jax.random.uniform(ks[4], (N_A, SSM_HEADS), f32,
                                     math.log(DT_MIN), math.log(DT_MAX)))
    ssm_dt_bias = dt0 + jnp.log(-jnp.expm1(-dt0))
    ssm_a_log = jnp.log(jax.random.uniform(ks[5], (N_A, SSM_HEADS), f32, 1.0, 16.0))
    ssm_d = 1.0 + nrm(ks[6], (N_A, SSM_HEADS), 0.1)
    ssm_norm_w = 1.0 + nrm(ks[7], (N_A, D_INNER), 0.1)
    ssm_out_w = nrm(ks[8], (N_A, D_INNER, D_MODEL), beta * D_INNER ** -0.5)

    kv_col_scale = jnp.concatenate([jnp.ones((ATT_DIM,), f32), jnp.full((ATT_DIM,), beta, f32),
                                    jnp.ones((ATT_HEADS,), f32)])
    kv_w = nrm(ks[9], (D_MODEL, 2 * ATT_DIM + ATT_HEADS), D_MODEL ** -0.5) * kv_col_scale
    kv_b_f = FORGET_BIAS_MEAN + nrm(ks[10], (ATT_HEADS,), 0.5)
    att_q_w = nrm(ks[11], (N_B, D_MODEL, ATT_DIM), D_MODEL ** -0.5)
    att_o_w = nrm(ks[12], (N_B, ATT_DIM, D_MODEL), beta * ATT_DIM ** -0.5)

    ffn_gate_w = nrm(ks[13], (DEPTH, D_MODEL, D_FF), D_MODEL ** -0.5)
    ffn_up_w = nrm(ks[14], (DEPTH, D_MODEL, D_FF), D_MODEL ** -0.5)
    ffn_down_w = nrm(ks[15], (DEPTH, D_FF, D_MODEL), beta * D_FF ** -0.5)

    ln_mix_g = 1.0 + nrm(ks[16], (DEPTH, D_MODEL), 0.1)
    ln_mix_b = nrm(ks[17], (DEPTH, D_MODEL), 0.02)
    ln_ffn_g = 1.0 + nrm(ks[18], (DEPTH, D_MODEL), 0.1)
    ln_ffn_b = nrm(ks[19], (DEPTH, D_MODEL), 0.02)

    return {"x": x, "ssm_in_w": ssm_in_w, "ssm_conv_w": ssm_conv_w, "ssm_conv_b": ssm_conv_b,
            "ssm_dt_bias": ssm_dt_bias, "ssm_a_log": ssm_a_log, "ssm_d": ssm_d,
            "ssm_norm_w": ssm_norm_w, "ssm_out_w": ssm_out_w, "kv_w": kv_w, "kv_b_f": kv_b_f,
            "att_q_w": att_q_w, "att_o_w": att_o_w, "ffn_gate_w": ffn_gate_w,
            "ffn_up_w": ffn_up_w, "ffn_down_w": ffn_down_w, "ln_mix_g": ln_mix_g,
            "ln_mix_b": ln_mix_b, "ln_ffn_g": ln_ffn_g, "ln_ffn_b": ln_ffn_b}


def reference(x, ssm_in_w, ssm_conv_w, ssm_conv_b, ssm_dt_bias, ssm_a_log, ssm_d, ssm_norm_w,
              ssm_out_w, kv_w, kv_b_f, att_q_w, att_o_w, ffn_gate_w, ffn_up_w, ffn_down_w,
              ln_mix_g, ln_mix_b, ln_ffn_g, ln_ffn_b):
    alpha = DEEPNORM_ALPHA
    shared = None
    for layer in range(DEPTH):
        if layer < N_A:
            i = layer
            h = _mamba2_mixer(x, ssm_in_w[i], ssm_conv_w[i], ssm_conv_b[i], ssm_dt_bias[i],
                              ssm_a_log[i], ssm_d[i], ssm_norm_w[i], ssm_out_w[i])
        else:
            j = layer - N_A
            k_sh, v_sh, fcum_sh = shared
            h = _forgetting_attention(x, att_q_w[j], att_o_w[j], k_sh, v_sh, fcum_sh)
        x = _layer_norm(alpha * x + h, ln_mix_g[layer], ln_mix_b[layer])
        x = _layer_norm(alpha * x + _swiglu(x, ffn_gate_w[layer], ffn_up_w[layer], ffn_down_w[layer]),
                        ln_ffn_g[layer], ln_ffn_b[layer])
        if layer == N_A - 1:
            shared = _shared_kv(x, kv_w, kv_b_f)
    return x
```

```python
import numpy as np
from contextlib import ExitStack
from collections import defaultdict

import concourse.bass as bass
import concourse.mybir as mybir
from concourse.bass_utils import run_bass_kernel_spmd

F32 = mybir.dt.float32
BF16 = mybir.dt.bfloat16
AF = mybir.ActivationFunctionType
ALU = mybir.AluOpType

D = 1024
KD = 8
DI = 2048
NG = 8
NHEAD = 32
DFF = 2816
NF = 22
AH = 16
TS = 512
DEPTH = 2
ALPHA = (2.0 * DEPTH) ** 0.25
LN_EPS = 1e-5
RMS_EPS = 1e-5
SLOT = 4096
NSLOT = 3
NEG = -30000.0


class Dom:
    def __init__(self, nc, es, name, step, epoch):
        self.nc, self.es, self.name, self.step, self.epoch = nc, es, name, step, epoch
        self.sems = []
        self.count = 0

    def sem_for(self, cnt):
        e = (cnt - 1) // self.epoch
        while len(self.sems) <= e:
            self.sems.append(self.es.enter_context(self.nc.semaphore(f"s_{self.name}_{len(self.sems)}")))
        return self.sems[e], ((cnt - 1) % self.epoch + 1) * self.step


class Sched:
    def __init__(self, nc, es):
        self.nc, self.es = nc, es
        self.eng = {"pe": nc.tensor, "act": nc.scalar, "dve": nc.vector, "pool": nc.gpsimd, "sp": nc.sync}
        self.dom = {e: Dom(nc, es, e, 1, 4096) for e in ("pe", "act", "dve", "pool")}
        self.seen = defaultdict(int)
        self.lastw = {}
        self.readers = defaultdict(dict)
        self.ndma = 0
        self.nins = defaultdict(int)

    def new_dma_dom(self, name):
        return Dom(self.nc, self.es, name, 16, 1024)

    def _deps(self, own, reads, writes):
        deps = {}

        def need(dc, same_ok):
            dom, cnt = dc
            if dom is own and same_ok:
                return
            if deps.get(dom, 0) < cnt:
                deps[dom] = cnt

        for k in reads:
            if k in self.lastw:
                need(self.lastw[k], False)
            if isinstance(k, tuple) and k[0] == "ps":
                for dom, cnt in self.readers[k].items():
                    need((dom, cnt), True)
        for k in writes:
            if k in self.lastw:
                need(self.lastw[k], True)
            for dom, cnt in self.readers[k].items():
                need((dom, cnt), True)
        return deps

    def _wait(self, e, deps, own=None):
        for dom, cnt in deps.items():
            if self.seen[(e, dom.name)] >= cnt:
                continue
            if dom is own:
                assert cnt <= own.count, "same-engine wait on a future completion"
            sem, val = dom.sem_for(cnt)
            self.eng[e].wait_ge(sem, val)
            self.nins[e] += 1
            self.seen[(e, dom.name)] = cnt

    def op(self, e, fn, reads=(), writes=(), inc=True):
        own = self.dom[e]
        self._wait(e, self._deps(own, reads, writes), own)
        ins = fn()
        self.nins[e] += 1
        tag = own.count + 1
        if inc:
            own.count += 1
            sem, _ = own.sem_for(own.count)
            ins.then_inc(sem, 1)
        for k in reads:
            if self.readers[k].get(own, 0) < tag:
                self.readers[k][own] = tag
        for k in writes:
            self.lastw[k] = (own, tag)
            self.readers[k] = {}
        return ins

    def dma(self, q, out, in_, reads, writes, dom):
        self._wait(q, self._deps(None, reads, writes))
        ins = self.eng[q].dma_start(out=out, in_=in_)
        self.nins[q] += 1
        self.ndma += 1
        dom.count += 1
        sem, _ = dom.sem_for(dom.count)
        ins.then_inc(sem, 16)
        for k in reads:
            self.readers[k][dom] = dom.count
        for k in writes:
            self.lastw[k] = (dom, dom.count)
            self.readers[k] = {}
        return ins

    def fence(self):
        es_ = ("pe", "act", "dve", "pool")
        for e in es_:
            self._wait(e, {self.dom[f]: self.dom[f].count for f in es_ if f != e and self.dom[f].count > 0})

    def wait_all(self, e, doms):
        for dom in doms:
            if dom.count > 0:
                self._wait(e, {dom: dom.count})


def bc(ap, shape):
    return ap.to_broadcast(list(shape))


def weight_tiles():
    tiles = []
    for g in range(NG):
        tiles.append((("inA", g), 8 * 512, [("ssm_in_w", 0, g * 256, 256, "kpc", 0, 512, 0),
                                             ("ssm_in_w", 0, 2048 + g * 256, 256, "kpc", 0, 512, 256)]))
        tiles.append((("inB", g), 8 * 256, [("ssm_in_w", 0, 4096 + g * 128, 128, "kpc", 0, 256, 0),
                                             ("ssm_in_w", 0, 5120 + g * 128, 128, "kpc", 0, 256, 128)]))
    for j in range(4):
        tiles.append((("out", j), 16 * 256, [("ssm_out_w", 0, j * 256, 256, "kpc", 0, 256, 0)]))

    def ffn(l):
        for j in range(11):
            parts = []
            for fc in range(2):
                f = 2 * j + fc
                parts.append(("ffn_gate_w", l, f * 128, 128, "kpc", fc * 1024, 128, 0))
                parts.append(("ffn_up_w", l, f * 128, 128, "kpc", 2048 + fc * 1024, 128, 0))
            tiles.append((("gu", l, j), 4096, parts))
        for j in range(8):
            tiles.append((("dn", l, j), NF * 128, [("ffn_down_w", l, j * 128, 128, "kpc", 0, 128, 0)]))

    ffn(0)
    for j in range(2):
        parts = [("kv_w", None, j * 512 + pr * 128, 128, "kpc", pr * 1024, 128, 0) for pr in range(4)]
        tiles.append((("kvk", j), 4096, parts))
    for j in range(2):
        tiles.append((("kvv", j), 4096, [("kv_w", None, 1024 + j * 512, 512, "kpc", 0, 512, 0)]))
    for j in range(2):
        parts = [("att_q_w", 0, j * 512 + pr * 128, 128, "kpc", pr * 1024, 128, 0) for pr in range(4)]
        tiles.append((("q", j), 4096, parts))
    for j in range(4):
        tiles.append((("o", j), 16 * 256, [("att_o_w", 0, j * 256, 256, "hpc", 0, 256, 0)]))
    ffn(1)
    return tiles


IN_SPECS = [
    ("x", None), ("ssm_in_w", [1, 1024, 6176]), ("ssm_conv_w", [1, 4, 4096]), ("ssm_conv_b", [1, 4096]),
    ("ssm_dt_bias", [1, 32]), ("ssm_a_log", [1, 32]), ("ssm_d", [1, 32]), ("ssm_norm_w", [1, 2048]),
    ("ssm_out_w", [1, 2048, 1024]), ("kv_w", [1024, 2064]), ("kv_b_f", [16]), ("att_q_w", [1, 1024, 1024]),
    ("att_o_w", [1, 1024, 1024]), ("ffn_gate_w", [2, 1024, 2816]), ("ffn_up_w", [2, 1024, 2816]),
    ("ffn_down_w", [2, 2816, 1024]), ("ln_mix_g", [2, 1024]), ("ln_mix_b", [2, 1024]), ("ln_ffn_g", [2, 1024]),
    ("ln_ffn_b", [2, 1024]),
]


def build(NB=4, SEQ=2048, debug=False, stop_after=None):
    nc = bass.Bass("TRN2", target_bir_lowering=False)
    NSC = SEQ // TS
    NKT = SEQ // 128
    dr = {}
    for name, shp in IN_SPECS:
        if name == "x":
            shp = [NB, SEQ, D]
        dr[name] = nc.dram_tensor(name, shp, F32, kind="ExternalInput").ap()
    out_d = nc.dram_tensor("out", [NB, SEQ, D], F32, kind="ExternalOutput").ap()
    tiles = weight_tiles()
    NT = len(tiles)
    wscr = nc.dram_tensor("wscr", [NT, 128, SLOT], BF16, kind="Internal").ap()
    dbg = {}

    with ExitStack() as es:
        ec = es.enter_context
        S = Sched(nc, es)

        def sb(name, shape, dt=F32):
            return ec(nc.sbuf_tensor(name, list(shape), dt))

        identf = sb("identf", [128, 128]); identb = sb("identb", [128, 128], BF16)
        onesf = sb("onesf", [128, 128]); trif = sb("trif", [128, 128])
        lnones = sb("lnones", [128, 128], BF16)
        negmask = sb("negmask", [128, 512], BF16)
        prmA = sb("prmA", [128, 128]); prmB = sb("prmB", [128, 128])
        dtb_bc = sb("dtb_bc", [128, 32]); A_bc = sb("A_bc", [128, 32]); D_bc = sb("D_bc", [128, 32])
        bf_col = sb("bf_col", [16, 1])
        wdt = sb("wdt", [128, 8, 32], BF16); wf = sb("wf", [128, 8, 16], BF16)
        wslot = [sb(f"wslot{i}", [128, SLOT], BF16) for i in range(NSLOT)]
        scr16 = sb("scr16", [128, 4096])
        xin = scr16[:].rearrange("p (t d) -> p t d", d=D)
        lnb = scr16[:, 0:2048].bitcast(BF16).rearrange("p (k t) -> p k t", t=TS)
        lnsq = scr16[:, 2048:4096].bitcast(BF16).rearrange("p (k t) -> p k t", t=TS)
        xT = sb("xT", [128, KD, TS]); xTb = sb("xTb", [128, KD, TS], BF16)
        mean_sb = sb("mean_sb", [128, TS]); rstd_sb = sb("rstd_sb", [128, TS])
        big = sb("big", [128, NF, TS], BF16)
        acc = [sb(f"acc{i}", [128, TS]) for i in range(2)]
        lnt = acc
        stateT = sb("stateT", [128, NHEAD * 64]); stbf = sb("stbf", [128, NHEAD * 64], BF16)
        halo = sb("halo", [128, 32, 3])
        KT = sb("KT", [128, 8, SEQ], BF16)
        VA = sb("VA", [128, NKT, AH, 65], BF16)
        Fcarry = sb("Fcarry", [16, 1]); Fcol = sb("Fcol", [128, NKT, AH])
        lneps = sb("lneps", [128, 2])
        ARENA = 7360
        arena = sb("arena", [128, ARENA])
        ar = {"o": 0}

        def carve(shape, dt=F32):
            n = int(np.prod(shape[1:]))
            w = n if dt == F32 else (n + 1) // 2
            o = ar["o"]
            assert o + w <= ARENA, ("arena overflow", o, w)
            ar["o"] = o + w
            v = arena[0:shape[0], o:o + w]
            if dt != F32:
                v = v.bitcast(dt)
            if len(shape) == 3:
                v = v.rearrange("p (a b) -> p a b", b=shape[2])
            return v

        stg1 = carve([128, 128]); stg2 = carve([128, 128])
        ar["o"] = 0
        dt_sb = carve([128, 4, 32]); a_sb = carve([128, 4, 32]); cumcol = carve([128, 4, 32]); expcum = carve([128, 4, 32])
        sp_t = [carve([128, 4, 32]) for _ in range(3)]
        zs = carve([128, 4, 256], BF16)
        ubuf = carve([128, 2, TS + 4])
        xbc = carve([128, 4, TS], BF16)
        xdt = carve([128, 256], BF16); xD = carve([128, 256], BF16); btok = carve([128, 128], BF16)
        atri = carve([128, 512]); seg = atri
        decayT = carve([128, 512], BF16); GT = carve([128, 512], BF16); xw = carve([128, 256], BF16)
        e4 = carve([128, 4]); ys = carve([128, 256]); ysum = carve([128, 256])
        yg = carve([128, 4, 256]); ss = carve([128, 4]); sd4 = carve([128, 4]); rstd4 = carve([128, 4])
        junk = carve([128, 256]); ygn = carve([128, 256], BF16); sttmp = carve([128, 256])
        ssd_top = ar["o"]
        ar["o"] = 0
        QT = carve([128, 8, TS], BF16)
        f_v = carve([16, TS]); f_a = carve([16, TS]); f_l = carve([16, TS]); Frow = carve([16, TS])
        fdiag = carve([16, 32]); Fq0 = carve([128, 2, AH]); bcol = carve([128, 2 * NKT, AH])
        PT = [carve([128, 256], BF16) for _ in range(3)]
        rr = carve([128, 512]); Rs = carve([64, 512])
        att_top = ar["o"]
        ps = [ec(nc.psum_tensor(f"ps{i}", [128, 512], F32)) for i in range(8)]

        def psb(i):
            return ps[i][:].bitcast(BF16)

        ring = {"i": 0}

        def nb():
            i = ring["i"]
            ring["i"] = (i + 1) % 6
            return i

        def pk(i):
            return ("ps", i)

        P_ = "pool"
        S.op(P_, lambda: nc.gpsimd.memset(identf[:], 0.0), writes=["identf"])
        S.op(P_, lambda: nc.gpsimd.affine_select(out=identf[:], in_=identf[:], pattern=[[-1, 128]], compare_op=ALU.not_equal,
                                                 fill=1.0, base=0, channel_multiplier=1), reads=["identf"], writes=["identf"])
        S.op(P_, lambda: nc.gpsimd.tensor_copy(out=identb[:], in_=identf[:]), reads=["identf"], writes=["identb"])
        S.op(P_, lambda: nc.gpsimd.memset(onesf[:], 1.0), writes=["onesf"])
        S.op(P_, lambda: nc.gpsimd.memset(lnones[:], 1.0 / D), writes=["lnones"])
        S.op(P_, lambda: nc.gpsimd.memset(trif[:], 1.0), writes=["trif"])
        S.op(P_, lambda: nc.gpsimd.affine_select(out=trif[:], in_=trif[:], pattern=[[1, 128]], compare_op=ALU.is_ge,
                                                 fill=0.0, base=0, channel_multiplier=-1), reads=["trif"], writes=["trif"])
        S.op(P_, lambda: nc.gpsimd.memset(negmask[:], 0.0), writes=["negmask"])
        nm3 = negmask[:].rearrange("p (h l) -> p h l", h=4)
        S.op(P_, lambda: nc.gpsimd.affine_select(out=nm3, in_=nm3, pattern=[[0, 4], [1, 128]], compare_op=ALU.is_ge,
                                                 fill=NEG, base=0, channel_multiplier=-1), reads=["negmask"], writes=["negmask"])
        S.op(P_, lambda: nc.gpsimd.memset(VA[:], 1.0), writes=[("VA", kt) for kt in range(NKT)])
        S.op(P_, lambda: nc.gpsimd.memset(stg2[:], 0.0), writes=["stg2"])

        cdom = S.new_dma_dom("cst")
        S.dma("sp", stg1[:], dr["ssm_conv_w"][0].rearrange("k (c p) -> (k c) p", p=128), [], ["stg1"], cdom)
        rows = [("ssm_conv_b", dr["ssm_conv_b"][0], 32, 0), ("ssm_norm_w", dr["ssm_norm_w"][0], 16, 32),
                ("ln_mix_g", dr["ln_mix_g"].rearrange("l d -> (l d)"), 16, 48), ("ln_mix_b", dr["ln_mix_b"].rearrange("l d -> (l d)"), 16, 64),
                ("ln_ffn_g", dr["ln_ffn_g"].rearrange("l d -> (l d)"), 16, 80), ("ln_ffn_b", dr["ln_ffn_b"].rearrange("l d -> (l d)"), 16, 96)]
        for (_, src, n, r0) in rows:
            S.dma("sp", stg2[r0:r0 + n, :], src.rearrange("(c p) -> c p", p=128), [], ["stg2"], cdom)
        S.dma("sp", dtb_bc[:], dr["ssm_dt_bias"].partition_broadcast(128), [], ["dtb_bc"], cdom)
        S.dma("sp", A_bc[:], dr["ssm_a_log"].partition_broadcast(128), [], ["A_bc"], cdom)
        S.dma("sp", D_bc[:], dr["ssm_d"].partition_broadcast(128), [], ["D_bc"], cdom)
        S.dma("sp", bf_col[:], dr["kv_b_f"].rearrange("(h o) -> h o", o=1), [], ["bf_col"], cdom)
        S.op("act", lambda: nc.scalar.activation(out=A_bc[:], in_=A_bc[:], func=AF.Exp), reads=["A_bc"], writes=["A_bc"])
        S.op("dve", lambda: nc.vector.tensor_scalar_mul(out=A_bc[:], in0=A_bc[:], scalar1=-1.0), reads=["A_bc"], writes=["A_bc"])
        b0 = nb()
        S.op("pe", lambda: nc.tensor.transpose(out=ps[b0][:, 0:128], in_=stg1[:], identity=identf[:]), reads=["stg1", "identf"], writes=[pk(b0)])
        S.op("dve", lambda: nc.vector.tensor_copy(out=prmA[:], in_=ps[b0][:, 0:128]), reads=[pk(b0)], writes=["prmA"])
        b1 = nb()
        S.op("pe", lambda: nc.tensor.transpose(out=ps[b1][:, 0:128], in_=stg2[:], identity=identf[:]), reads=["stg2", "identf"], writes=[pk(b1)])
        S.op("dve", lambda: nc.vector.tensor_copy(out=prmB[:], in_=ps[b1][:, 0:128]), reads=[pk(b1)], writes=["prmB"])
        cw = prmA[:].rearrange("p (k c) -> p k c", k=4)
        cb = prmB[:, 0:32]
        normw = prmB[:, 32:48]
        lng = {0: prmB[:, 48:56], 1: prmB[:, 80:88], 2: prmB[:, 56:64], 3: prmB[:, 88:96]}
        lnbias = {0: prmB[:, 64:72], 1: prmB[:, 96:104], 2: prmB[:, 72:80], 3: prmB[:, 104:112]}

        import os
        wcdom = S.new_dma_dom("wcv")
        cvl = [S.new_dma_dom("cvl0"), S.new_dma_dom("cvl1")]
        cvs = [S.new_dma_dom("cvs0"), S.new_dma_dom("cvs1")]
        stf = [scr16[:], xT[:].rearrange("p k t -> p (k t)")]
        stb = [big[:, 0:8, :].rearrange("p k t -> p (k t)"), big[:, 8:16, :].rearrange("p k t -> p (k t)")]
        cast_eng = ["dve", "act", "pool"]
        for ti, (name, nel, parts) in enumerate(tiles):
            if os.environ.get("SKIP_CONV"):
                break
            sl = ti % 2
            npart = 64 if name[0] == "o" else 128
            for (src, idx, c0, n, kind, base, cstride, coff) in parts:
                w = dr[src] if idx is None else dr[src][idx]
                if kind == "kpc":
                    nk = w.shape[0] // 128
                    s_ap = w[:, c0:c0 + n].rearrange("(k p) c -> p k c", p=128)
                    d_ap = stf[sl][:, base:base + nk * cstride].rearrange("p (k c) -> p k c", c=cstride)[:, :, coff:coff + n]
                else:
                    s_ap = w[:, c0:c0 + n].rearrange("(h p) c -> p h c", p=64)
                    d_ap = stf[sl][0:64, base:base + 16 * cstride].rearrange("p (h c) -> p h c", c=cstride)[:, :, coff:coff + n]
                S.dma("sp", d_ap, s_ap, [], [("stf", sl)], cvl[sl])
            ce = cast_eng[ti % 3]
            if ce == "dve":
                S.op("dve", lambda: nc.vector.tensor_copy(out=stb[sl][0:npart, 0:nel], in_=stf[sl][0:npart, 0:nel]), reads=[("stf", sl)], writes=[("stb", sl)])
            elif ce == "act":
                S.op("act", lambda: nc.scalar.copy(out=stb[sl][0:npart, 0:nel], in_=stf[sl][0:npart, 0:nel]), reads=[("stf", sl)], writes=[("stb", sl)])
            else:
                S.op("pool", lambda: nc.gpsimd.tensor_copy(out=stb[sl][0:npart, 0:nel], in_=stf[sl][0:npart, 0:nel]), reads=[("stf", sl)], writes=[("stb", sl)])
            S.dma("sp", wscr[ti][0:npart, 0:nel], stb[sl][0:npart, 0:nel], [("stb", sl)], [("wscr", ti)], cvs[sl])
        S.dma("pool", wdt[:], dr["ssm_in_w"][0][:, 6144:6176].rearrange("(k p) c -> p k c", p=128), [], ["wdt"], wcdom)
        S.dma("pool", wf[:], dr["kv_w"][:, 2048:2064].rearrange("(k p) c -> p k c", p=128), [], ["wf"], wcdom)
        S.wait_all("pe", cvs + cvl)
        S.wait_all("act", cvs + cvl)
        S.wait_all("dve", cvs + cvl)
        S.wait_all("pool", cvs + cvl)

        wdoms = [S.new_dma_dom(f"w{i}") for i in range(NSLOT)]
        wstate = {"next_load": 0, "next_use": 0}
        total_tiles = NB * NSC * NT

        def prefetch():
            i = wstate["next_load"]
            if i >= total_tiles:
                return
            wstate["next_load"] += 1
            ti = i % NT
            s = i % NSLOT
            nel = tiles[ti][1]
            npart = 64 if tiles[ti][0][0] == "o" else 128
            S.dma("sp", wslot[s][0:npart, 0:nel], wscr[ti][0:npart, 0:nel], [("wscr", ti)], [("wslot", s)], wdoms[s])

        def use_tile(expect):
            if stop_after is not None:
                while tiles[wstate["next_use"] % NT][0] != expect:
                    wstate["next_use"] += 1
                    prefetch()
            i = wstate["next_use"]
            wstate["next_use"] += 1
            ti = i % NT
            assert tiles[ti][0] == expect, (tiles[ti][0], expect)
            s = i % NSLOT
            return wslot[s], ("wslot", s)

        def done_tile():
            prefetch()

        for _ in range(NSLOT):
            prefetch()

        iodom_in = S.new_dma_dom("xin")
        iodom_out = S.new_dma_dom("xout")
        dbgdom = S.new_dma_dom("dbg")

        def tap(name, ap, key, shape):
            if not debug:
                return
            if name not in dbg:
                dbg[name] = nc.dram_tensor(name, [NB * NSC] + list(shape), ap.dtype, kind="ExternalOutput").ap()
            S.dma("sp", dbg[name][tap.idx], ap, [key], [], dbgdom)
        tap.idx = 0

        def ln_accum(k, bank, ln_idx):
            S.op("dve", lambda: nc.vector.scalar_tensor_tensor(out=xT[:, k, :], in0=xT[:, k, :], scalar=ALPHA, in1=ps[bank][:],
                                                               op0=ALU.mult, op1=ALU.add),
                 reads=[("xT", k), pk(bank)], writes=[("xT", k)])
            S.op("act", lambda: nc.scalar.copy(out=lnb[:, k, :], in_=xT[:, k, :]), reads=[("xT", k)], writes=[("lnb", k)])
            S.op("act", lambda: nc.scalar.activation(out=lnsq[:, k, :], in_=xT[:, k, :], func=AF.Square), reads=[("xT", k)], writes=[("lnsq", k)])

        def ln_finish(ln_idx):
            bm, be = nb(), nb()
            for k in range(KD):
                S.op("pe", lambda: nc.tensor.matmul(ps[bm][:], lhsT=lnones[:], rhs=lnb[:, k, :], start=(k == 0), stop=(k == KD - 1)),
                     reads=[("lnb", k), "lnones"], writes=[pk(bm)], inc=(k == KD - 1))
            for k in range(KD):
                S.op("pe", lambda: nc.tensor.matmul(ps[be][:], lhsT=lnones[:], rhs=lnsq[:, k, :], start=(k == 0), stop=(k == KD - 1)),
                     reads=[("lnsq", k), "lnones"], writes=[pk(be)], inc=(k == KD - 1))
            S.op("act", lambda: nc.scalar.copy(out=mean_sb[:], in_=ps[bm][:]), reads=[pk(bm)], writes=["mean_sb"])
            S.op("dve", lambda: nc.vector.tensor_tensor(out=rstd_sb[:], in0=ps[bm][:], in1=mean_sb[:], op=ALU.mult),
                 reads=[pk(bm), "mean_sb"], writes=["rstd_sb"])
            S.op("dve", lambda: nc.vector.tensor_tensor(out=rstd_sb[:], in0=ps[be][:], in1=rstd_sb[:], op=ALU.subtract),
                 reads=[pk(be), "rstd_sb"], writes=["rstd_sb"])
            S.op("act", lambda: nc.scalar.activation(out=rstd_sb[:], in_=rstd_sb[:], func=AF.Sqrt, bias=lneps[:, 0:1]),
                 reads=["rstd_sb", "lneps"], writes=["rstd_sb"])
            S.op("dve", lambda: nc.vector.reciprocal(out=rstd_sb[:], in_=rstd_sb[:]), reads=["rstd_sb"], writes=["rstd_sb"])
            for k in range(KD):
                t1, t2 = lnt[0], lnt[1]
                k1, k2 = ("acc", 0), ("acc", 1)
                S.op("dve", lambda: nc.vector.tensor_tensor(out=t1[:], in0=xT[:, k, :], in1=mean_sb[:], op=ALU.subtract),
                     reads=[("xT", k), "mean_sb"], writes=[k1])
                S.op("pool", lambda: nc.gpsimd.tensor_tensor(out=t2[:], in0=t1[:], in1=rstd_sb[:], op=ALU.mult),
                     reads=[k1, "rstd_sb"], writes=[k2])
                S.op("act", lambda: nc.scalar.activation(out=xT[:, k, :], in_=t2[:], func=AF.Identity, scale=lng[ln_idx][:, k:k + 1],
                                                         bias=lnbias[ln_idx][:, k:k + 1]),
                     reads=[k2, "prmB"], writes=[("xT", k)])
                S.op("act", lambda: nc.scalar.activation(out=xTb[:, k, :], in_=t2[:], func=AF.Identity, scale=lng[ln_idx][:, k:k + 1],
                                                         bias=lnbias[ln_idx][:, k:k + 1]),
                     reads=[k2, "prmB"], writes=[("xTb", k)])

        S.op("pool", lambda: nc.gpsimd.memset(lneps[:, 0:1], LN_EPS), writes=["lneps"])
        S.op("pool", lambda: nc.gpsimd.memset(lneps[:, 1:2], RMS_EPS), writes=["lneps"])

        def ffn_phase(l, ln_idx):
            for j in range(11):
                slot, skey = use_tile(("gu", l, j))
                for fc in range(2):
                    f = 2 * j + fc
                    bg, bu = nb(), nb()
                    gv = slot[:, fc * 1024:(fc + 1) * 1024].rearrange("p (k c) -> p k c", c=128)
                    uv = slot[:, 2048 + fc * 1024:2048 + (fc + 1) * 1024].rearrange("p (k c) -> p k c", c=128)
                    for k in range(KD):
                        S.op("pe", lambda: nc.tensor.matmul(ps[bg][:], lhsT=gv[:, k, :], rhs=xTb[:, k, :], start=(k == 0), stop=(k == KD - 1)),
                             reads=[skey, ("xTb", k)], writes=[pk(bg)], inc=(k == KD - 1))
                    for k in range(KD):
                        S.op("pe", lambda: nc.tensor.matmul(ps[bu][:], lhsT=uv[:, k, :], rhs=xTb[:, k, :], start=(k == 0), stop=(k == KD - 1)),
                             reads=[skey, ("xTb", k)], writes=[pk(bu)], inc=(k == KD - 1))
                    a_ = acc[f % 2]
                    ak = ("acc", f % 2)
                    S.op("act", lambda: nc.scalar.activation(out=a_[:], in_=ps[bg][:], func=AF.Silu), reads=[pk(bg)], writes=[ak])
                    S.op("dve", lambda: nc.vector.tensor_tensor(out=big[:, f, :], in0=a_[:], in1=ps[bu][:], op=ALU.mult),
                         reads=[ak, pk(bu)], writes=[("big", f)])
                done_tile()
            for k in range(KD):
                slot, skey = use_tile(("dn", l, k))
                dv = slot[:, 0:NF * 128].rearrange("p (f c) -> p f c", c=128)
                b = nb()
                for f in range(NF):
                    S.op("pe", lambda: nc.tensor.matmul(ps[b][:], lhsT=dv[:, f, :], rhs=big[:, f, :], start=(f == 0), stop=(f == NF - 1)),
                         reads=[skey, ("big", f)], writes=[pk(b)], inc=(f == NF - 1))
                ln_accum(k, b, ln_idx)
                done_tile()
            ln_finish(ln_idx)

        def ssd_phase(first_in_seq):
            bd = nb()
            for tt in range(4):
                for k in range(KD):
                    S.op("pe", lambda: nc.tensor.matmul(ps[bd][:, tt * 32:(tt + 1) * 32], lhsT=xTb[:, k, tt * 128:(tt + 1) * 128], rhs=wdt[:, k, :],
                                                        start=(k == 0), stop=(k == KD - 1)),
                         reads=[("xTb", k), "wdt"], writes=[pk(bd)], inc=(tt == 3 and k == KD - 1))
            pd = ps[bd][:, 0:128].rearrange("p (t h) -> p t h", h=32)
            v_, av_, l_ = sp_t
            S.op("dve", lambda: nc.vector.tensor_tensor(out=v_[:], in0=pd, in1=bc(dtb_bc[:].unsqueeze(1), [128, 4, 32]), op=ALU.add),
                 reads=[pk(bd), "dtb_bc"], writes=["sp_v"])
            S.op("act", lambda: nc.scalar.activation(out=av_[:], in_=v_[:], func=AF.Abs), reads=["sp_v"], writes=["sp_a"])
            S.op("act", lambda: nc.scalar.activation(out=av_[:], in_=av_[:], func=AF.Exp, scale=-1.0), reads=["sp_a"], writes=["sp_a"])
            S.op("act", lambda: nc.scalar.activation(out=l_[:], in_=av_[:], func=AF.Ln, bias=1.0), reads=["sp_a"], writes=["sp_l"])
            S.op("dve", lambda: nc.vector.scalar_tensor_tensor(out=dt_sb[:], in0=v_[:], scalar=0.0, in1=l_[:], op0=ALU.max, op1=ALU.add),
                 reads=["sp_v", "sp_l"], writes=["dt_sb"])
            S.op("dve", lambda: nc.vector.tensor_tensor(out=a_sb[:], in0=dt_sb[:], in1=bc(A_bc[:].unsqueeze(1), [128, 4, 32]), op=ALU.mult),
                 reads=["dt_sb", "A_bc"], writes=["a_sb"])
            bcu = nb()
            for c in range(4):
                S.op("pe", lambda: nc.tensor.matmul(ps[bcu][:, c * 32:(c + 1) * 32], lhsT=trif[:], rhs=a_sb[:, c, :], start=True, stop=True),
                     reads=["trif", "a_sb"], writes=[pk(bcu)], inc=(c == 3))
            pc = ps[bcu][:, 0:128].rearrange("p (t h) -> p t h", h=32)
            S.op("dve", lambda: nc.vector.tensor_copy(out=cumcol[:], in_=pc), reads=[pk(bcu)], writes=["cumcol"])
            S.op("act", lambda: nc.scalar.activation(out=expcum[:], in_=pc, func=AF.Exp), reads=[pk(bcu)], writes=["expcum"])
            if first_in_seq:
                S.op("pool", lambda: nc.gpsimd.memset(stateT[:], 0.0), writes=[("stateT", g) for g in range(NG)])
                S.op("pool", lambda: nc.gpsimd.memset(stbf[:], 0.0), writes=[("stbf", g) for g in range(NG)])
                S.op("pool", lambda: nc.gpsimd.memset(halo[:], 0.0), writes=[("halo", ci) for ci in range(32)])

            for g in range(NG):
                slot, skey = use_tile(("inA", g))
                Wv = slot[:, 0:8 * 512].rearrange("p (k c) -> p k c", c=512)
                for half in range(2):
                    bz = nb()
                    for t2 in range(2):
                        tt = 2 * half + t2
                        for k in range(KD):
                            S.op("pe", lambda: nc.tensor.matmul(ps[bz][:, t2 * 256:(t2 + 1) * 256], lhsT=xTb[:, k, tt * 128:(tt + 1) * 128],
                                                                rhs=Wv[:, k, 0:256], start=(k == 0), stop=(k == KD - 1)),
                                 reads=[skey, ("xTb", k)], writes=[pk(bz)], inc=(t2 == 1 and k == KD - 1))
                    S.op("act", lambda: nc.scalar.activation(out=zs[:, 2 * half:2 * half + 2, :], in_=ps[bz][:].rearrange("p (t c) -> p t c", c=256),
                                                             func=AF.Silu), reads=[pk(bz)], writes=["zs"])
                for r in range(4):
                    ci = (2 * g + r) if r < 2 else (16 + g if r == 2 else 24 + g)
                    if r == 2:
                        done_tile()
                        slot, skey = use_tile(("inB", g))
                        Wv = slot[:, 0:8 * 256].rearrange("p (k c) -> p k c", c=256)
                    wcol = (256 + r * 128) if r < 2 else (r - 2) * 128
                    bx = nb()
                    for k in range(KD):
                        S.op("pe", lambda: nc.tensor.matmul(ps[bx][:], lhsT=Wv[:, k, wcol:wcol + 128], rhs=xTb[:, k, :],
                                                            start=(k == 0), stop=(k == KD - 1)),
                             reads=[skey, ("xTb", k)], writes=[pk(bx)], inc=(k == KD - 1))
                    ur = r % 2
                    uk = ("ubuf", ur)
                    S.op("pool", lambda: nc.gpsimd.tensor_copy(out=ubuf[:, ur, 0:3], in_=halo[:, ci, :]), reads=[("halo", ci)], writes=[uk])
                    S.op("act", lambda: nc.scalar.copy(out=ubuf[:, ur, 3:TS + 3], in_=ps[bx][:]), reads=[pk(bx)], writes=[uk])
                    a_ = acc[r % 2]
                    ak = ("acc", r % 2)
                    S.op("act", lambda: nc.scalar.activation(out=a_[:], in_=ps[bx][:], func=AF.Identity, scale=cw[:, 3, ci:ci + 1], bias=cb[:, ci:ci + 1]),
                         reads=[pk(bx), "prmA", "prmB"], writes=[ak])
                    for kk in range(3):
                        S.op("dve", lambda: nc.vector.scalar_tensor_tensor(out=a_[:], in0=ubuf[:, ur, kk:kk + TS], scalar=cw[:, kk, ci:ci + 1], in1=a_[:],
                                                                           op0=ALU.mult, op1=ALU.add), reads=[uk, ak, "prmA"], writes=[ak])
                    S.op("pool", lambda: nc.gpsimd.tensor_copy(out=halo[:, ci, :], in_=ubuf[:, ur, TS:TS + 3]), reads=[uk], writes=[("halo", ci)])
                    S.op("act", lambda: nc.scalar.activation(out=xbc[:, r, :], in_=a_[:], func=AF.Silu), reads=[ak], writes=[("xbc", r)])
                done_tile()
                hs = slice(4 * g, 4 * g + 4)
                st_g = stateT[:, g * 256:(g + 1) * 256]
                stb_g = stbf[:, g * 256:(g + 1) * 256]
                for c in range(4):
                    cs = slice(c * 128, (c + 1) * 128)
                    bt = nb()
                    T1 = psb(bt)
                    for j, r in enumerate((0, 1, 2)):
                        S.op("pe", lambda: nc.tensor.transpose(out=T1[:, j * 128:(j + 1) * 128], in_=xbc[:, r, cs], identity=identb[:]),
                             reads=[("xbc", r), "identb"], writes=[pk(bt)], inc=(j == 2))
                    T1x = T1[:, 0:256].rearrange("p (h q) -> p h q", q=64)
                    S.op("dve", lambda: nc.vector.tensor_tensor(out=xdt[:].rearrange("p (h q) -> p h q", q=64), in0=T1x,
                                                                in1=bc(dt_sb[:, c, hs].unsqueeze(2), [128, 4, 64]), op=ALU.mult),
                         reads=[pk(bt), "dt_sb"], writes=["xdt"])
                    S.op("dve", lambda: nc.vector.tensor_tensor(out=xD[:].rearrange("p (h q) -> p h q", q=64), in0=T1x,
                                                                in1=bc(D_bc[:, hs].unsqueeze(2), [128, 4, 64]), op=ALU.mult),
                         reads=[pk(bt), "D_bc"], writes=["xD"])
                    S.op("act", lambda: nc.scalar.copy(out=btok[:], in_=T1[:, 256:384]), reads=[pk(bt)], writes=["btok"])
                    S.op("pool", lambda: nc.gpsimd.tensor_tensor(out=atri[:].rearrange("p (h l) -> p h l", l=128),
                                                                 in0=bc(trif[:].unsqueeze(1), [128, 4, 128]),
                                                                 in1=bc(a_sb[:, c, hs].unsqueeze(2), [128, 4, 128]), op=ALU.mult),
                         reads=["trif", "a_sb"], writes=["atri"])
                    b1_ = nb()
                    S.op("pe", lambda: nc.tensor.matmul(ps[b1_][:], lhsT=onesf[:], rhs=atri[:], start=True, stop=False),
                         reads=["onesf", "atri"], writes=[pk(b1_)], inc=False)
                    S.op("pe", lambda: nc.tensor.matmul(ps[b1_][:], lhsT=identb[:], rhs=negmask[:], start=False, stop=True),
                         reads=["identb", "negmask"], writes=[pk(b1_)])
                    X1 = ps[b1_][:].rearrange("p (h l) -> p h l", l=128)
                    S.op("dve", lambda: nc.vector.tensor_tensor(out=seg[:].rearrange("p (h l) -> p h l", l=128), in0=X1,
                                                                in1=bc(cumcol[:, c, hs].unsqueeze(2), [128, 4, 128]), op=ALU.subtract),
                         reads=[pk(b1_), "cumcol"], writes=["atri"])
                    S.op("act", lambda: nc.scalar.activation(out=decayT[:], in_=seg[:], func=AF.Exp), reads=["atri"], writes=["decayT"])
                    S.op("act", lambda: nc.scalar.activation(out=e4[:], in_=X1[:, :, 127], func=AF.Exp), reads=[pk(b1_)], writes=["e4"])
                    b2_ = nb()
                    S.op("pe", lambda: nc.tensor.matmul(ps[b2_][:, 0:128], lhsT=xbc[:, 2, cs], rhs=xbc[:, 3, cs], start=True, stop=True),
                         reads=[("xbc", 2), ("xbc", 3)], writes=[pk(b2_)])
                    S.op("dve", lambda: nc.vector.tensor_tensor(out=GT[:].rearrange("p (h l) -> p h l", l=128),
                                                                in0=decayT[:].rearrange("p (h l) -> p h l", l=128),
                                                                in1=bc(ps[b2_][:, 0:128].unsqueeze(1), [128, 4, 128]), op=ALU.mult),
                         reads=["decayT", pk(b2_)], writes=["GT"])
                    dlast = decayT[:].rearrange("p (h l) -> p h l", l=128)[:, :, 127:128]
                    S.op("dve", lambda: nc.vector.tensor_tensor(out=xw[:].rearrange("p (h q) -> p h q", q=64),
                                                                in0=xdt[:].rearrange("p (h q) -> p h q", q=64),
                                                                in1=bc(dlast, [128, 4, 64]), op=ALU.mult),
                         reads=["xdt", "decayT"], writes=["xw"])
                    b3_ = nb()
                    S.op("pe", lambda: nc.tensor.matmul(ps[b3_][:, 0:256], lhsT=identb[:], rhs=xD[:], start=True, stop=False),
                         reads=["identb", "xD"], writes=[pk(b3_)], inc=False)
                    for h in range(4):
                        S.op("pe", lambda: nc.tensor.matmul(ps[b3_][:, h * 64:(h + 1) * 64], lhsT=GT[:, h * 128:(h + 1) * 128],
                                                            rhs=xdt[:, h * 64:(h + 1) * 64], start=False, stop=True),
                             reads=["GT", "xdt"], writes=[pk(b3_)], inc=False)
                    S.op("pe", lambda: nc.tensor.matmul(ps[b3_][:, 256:512], lhsT=xbc[:, 3, cs], rhs=stb_g, start=True, stop=True),
                         reads=[("xbc", 3), ("stbf", g)], writes=[pk(b3_)])
                    S.op("dve", lambda: nc.vector.tensor_tensor(out=ys[:].rearrange("p (h q) -> p h q", q=64),
                                                                in0=ps[b3_][:, 256:512].rearrange("p (h q) -> p h q", q=64),
                                                                in1=bc(expcum[:, c, hs].unsqueeze(2), [128, 4, 64]), op=ALU.mult),
                         reads=[pk(b3_), "expcum"], writes=["ys"])
                    S.op("dve", lambda: nc.vector.tensor_tensor(out=ysum[:], in0=ps[b3_][:, 0:256], in1=ys[:], op=ALU.add),
                         reads=[pk(b3_), "ys"], writes=["ysum"])
                    S.op("pool", lambda: nc.gpsimd.tensor_tensor(out=yg[:, c, :], in0=ysum[:], in1=zs[:, c, :], op=ALU.mult),
                         reads=["ysum", "zs"], writes=[("yg", c)])
                    S.op("act", lambda: nc.scalar.activation(out=junk[:], in_=yg[:, c, :], func=AF.Square, accum_out=ss[:, c:c + 1]),
                         reads=[("yg", c)], writes=["junk", ("ss", c)])
                    b4_ = nb()
                    S.op("pe", lambda: nc.tensor.matmul(ps[b4_][:, 0:256], lhsT=btok[:], rhs=xw[:], start=True, stop=True),
                         reads=["btok", "xw"], writes=[pk(b4_)])
                    S.op("dve", lambda: nc.vector.tensor_tensor(out=sttmp[:].rearrange("p (h q) -> p h q", q=64),
                                                                in0=st_g.rearrange("p (h q) -> p h q", q=64),
                                                                in1=bc(e4[:].unsqueeze(2), [128, 4, 64]), op=ALU.mult),
                         reads=[("stateT", g), "e4"], writes=["sttmp"])
                    S.op("dve", lambda: nc.vector.tensor_tensor(out=st_g, in0=sttmp[:], in1=ps[b4_][:, 0:256], op=ALU.add),
                         reads=["sttmp", pk(b4_)], writes=[("stateT", g)])
                    S.op("act", lambda: nc.scalar.copy(out=stb_g, in_=st_g), reads=[("stateT", g)], writes=[("stbf", g)])
                S.op("act", lambda: nc.scalar.activation(out=sd4[:], in_=ss[:], func=AF.Sqrt, scale=1.0 / 256.0, bias=lneps[:, 1:2]),
                     reads=[("ss", c) for c in range(4)] + ["lneps"], writes=["sd4"])
                S.op("dve", lambda: nc.vector.reciprocal(out=rstd4[:], in_=sd4[:]), reads=["sd4"], writes=["rstd4"])
                bn_ = nb()
                Tn = psb(bn_)
                for c in range(4):
                    S.op("act", lambda: nc.scalar.activation(out=ygn[:], in_=yg[:, c, :], func=AF.Copy, scale=rstd4[:, c:c + 1]),
                         reads=[("yg", c), "rstd4"], writes=["ygn"])
                    for j in range(2):
                        S.op("pe", lambda: nc.tensor.transpose(out=Tn[:, (j * 4 + c) * 128:(j * 4 + c + 1) * 128], in_=ygn[:, j * 128:(j + 1) * 128],
                                                               identity=identb[:]),
                             reads=["ygn", "identb"], writes=[pk(bn_)], inc=(j == 1))
                for j in range(2):
                    kc = 2 * g + j
                    S.op("dve", lambda: nc.vector.tensor_scalar(out=big[:, kc, :], in0=Tn[:, j * 512:(j + 1) * 512], scalar1=normw[:, kc:kc + 1],
                                                                scalar2=None, op0=ALU.mult),
                         reads=[pk(bn_), "prmB"], writes=[("big", kc)])
            for j in range(4):
                slot, skey = use_tile(("out", j))
                ov = slot[:, 0:16 * 256].rearrange("p (k c) -> p k c", c=256)
                for c in range(2):
                    k = 2 * j + c
                    b = nb()
                    for kk in range(16):
                        S.op("pe", lambda: nc.tensor.matmul(ps[b][:], lhsT=ov[:, kk, c * 128:(c + 1) * 128], rhs=big[:, kk, :],
                                                            start=(kk == 0), stop=(kk == 15)),
                             reads=[skey, ("big", kk)], writes=[pk(b)], inc=(kk == 15))
                    ln_accum(k, b, 0)
                done_tile()
            ln_finish(0)

        def attn_phase(sc, first_in_seq):
            t0 = sc * TS
            for j in range(2):
                slot, skey = use_tile(("kvk", j))
                for pr in range(4):
                    kv_ = slot[:, pr * 1024:(pr + 1) * 1024].rearrange("p (k c) -> p k c", c=128)
                    b = nb()
                    for k in range(KD):
                        S.op("pe", lambda: nc.tensor.matmul(ps[b][:], lhsT=kv_[:, k, :], rhs=xTb[:, k, :], start=(k == 0), stop=(k == KD - 1)),
                             reads=[skey, ("xTb", k)], writes=[pk(b)], inc=(k == KD - 1))
                    S.op("act", lambda: nc.scalar.copy(out=KT[:, 4 * j + pr, t0:t0 + TS], in_=ps[b][:]), reads=[pk(b)], writes=[("KT", 4 * j + pr)])
                done_tile()
            for j in range(2):
                slot, skey = use_tile(("kvv", j))
                vv = slot[:, 0:4096].rearrange("p (k c) -> p k c", c=512)
                for tt in range(4):
                    b = nb()
                    for k in range(KD):
                        S.op("pe", lambda: nc.tensor.matmul(ps[b][:], lhsT=xTb[:, k, tt * 128:(tt + 1) * 128], rhs=vv[:, k, :],
                                                            start=(k == 0), stop=(k == KD - 1)),
                             reads=[skey, ("xTb", k)], writes=[pk(b)], inc=(k == KD - 1))
                    kt = 4 * sc + tt
                    S.op("dve", lambda: nc.vector.tensor_copy(out=VA[:, kt, 8 * j:8 * j + 8, 0:64], in_=ps[b][:].rearrange("p (h q) -> p h q", q=64)),
                         reads=[pk(b)], writes=[("VA", kt)])
                done_tile()
            bf_ = nb()
            for k in range(KD):
                S.op("pe", lambda: nc.tensor.matmul(ps[bf_][0:16, :], lhsT=wf[:, k, :], rhs=xTb[:, k, :], start=(k == 0), stop=(k == KD - 1)),
                     reads=["wf", ("xTb", k)], writes=[pk(bf_)], inc=(k == KD - 1))
            S.op("dve", lambda: nc.vector.tensor_scalar(out=f_v[:], in0=ps[bf_][0:16, :], scalar1=bf_col[:, 0:1], scalar2=None, op0=ALU.add),
                 reads=[pk(bf_), "bf_col"], writes=["f_v"])
            S.op("act", lambda: nc.scalar.activation(out=f_a[:], in_=f_v[:], func=AF.Abs), reads=["f_v"], writes=["f_a"])
            S.op("act", lambda: nc.scalar.activation(out=f_a[:], in_=f_a[:], func=AF.Exp, scale=-1.0), reads=["f_a"], writes=["f_a"])
            S.op("act", lambda: nc.scalar.activation(out=f_l[:], in_=f_a[:], func=AF.Ln, bias=1.0), reads=["f_a"], writes=["f_l"])
            S.op("dve", lambda: nc.vector.scalar_tensor_tensor(out=f_l[:], in0=f_v[:], scalar=0.0, in1=f_l[:], op0=ALU.min, op1=ALU.subtract),
                 reads=["f_v", "f_l"], writes=["f_l"])
            if first_in_seq:
                S.op("pool", lambda: nc.gpsimd.memset(Fcarry[:], 0.0), writes=["Fcarry"])
            S.op("dve", lambda: nc.vector.tensor_tensor_scan(out=Frow[:], data0=bc(onesf[0:16, 0:1], [16, TS]), data1=f_l[:], initial=Fcarry[:, 0:1],
                                                             op0=ALU.mult, op1=ALU.add),
                 reads=["onesf", "f_l", "Fcarry"], writes=["Frow"])
            S.op("dve", lambda: nc.vector.tensor_copy(out=Fcarry[:], in_=Frow[:, TS - 1:TS]), reads=["Frow"], writes=["Fcarry"])
            bt_ = nb()
            for tt in range(4):
                S.op("pe", lambda: nc.tensor.transpose(out=ps[bt_][:, tt * 16:(tt + 1) * 16], in_=Frow[:, tt * 128:(tt + 1) * 128], identity=identf[0:16, 0:16]),
                     reads=["Frow", "identf"], writes=[pk(bt_)], inc=False)
            for s in range(2):
                S.op("dve", lambda: nc.vector.tensor_scalar(out=fdiag[:, s * 16:(s + 1) * 16], in0=identf[0:16, 0:16], scalar1=Frow[:, s * 256:s * 256 + 1],
                                                            scalar2=None, op0=ALU.mult),
                     reads=["identf", "Frow"], writes=["fdiag"])
            S.op("pe", lambda: nc.tensor.matmul(ps[bt_][:, 64:96], lhsT=onesf[0:16, :], rhs=fdiag[:], start=True, stop=True),
                 reads=["onesf", "fdiag"], writes=[pk(bt_)])
            S.op("dve", lambda: nc.vector.tensor_copy(out=Fcol[:, 4 * sc:4 * sc + 4, :], in_=ps[bt_][:, 0:64].rearrange("p (t h) -> p t h", h=16)),
                 reads=[pk(bt_)], writes=["Fcol"])
            S.op("dve", lambda: nc.vector.tensor_copy(out=Fq0[:], in_=ps[bt_][:, 64:96].rearrange("p (s h) -> p s h", h=16)),
                 reads=[pk(bt_)], writes=["Fq0"])
            nkt_all = 4 * sc + 4
            for s in range(2):
                S.op("dve", lambda: nc.vector.tensor_tensor(out=bcol[:, s * NKT:s * NKT + nkt_all, :], in0=bc(Fq0[:, s, :].unsqueeze(1), [128, nkt_all, 16]),
                                                            in1=Fcol[:, 0:nkt_all, :], op=ALU.subtract),
                     reads=["Fq0", "Fcol"], writes=["bcol"])
            for j in range(2):
                slot, skey = use_tile(("q", j))
                for pr in range(4):
                    qv = slot[:, pr * 1024:(pr + 1) * 1024].rearrange("p (k c) -> p k c", c=128)
                    b = nb()
                    for k in range(KD):
                        S.op("pe", lambda: nc.tensor.matmul(ps[b][:], lhsT=qv[:, k, :], rhs=xTb[:, k, :], start=(k == 0), stop=(k == KD - 1)),
                             reads=[skey, ("xTb", k)], writes=[pk(b)], inc=(k == KD - 1))
                    S.op("act", lambda: nc.scalar.activation(out=QT[:, 4 * j + pr, :], in_=ps[b][:], func=AF.Copy, scale=0.125),
                         reads=[pk(b)], writes=[("QT", 4 * j + pr)])
                done_tile()
            OT = big
            it = 0
            for h in range(AH):
                pr, po = h // 2, (h % 2) * 64
                ob = 6 + (h % 2)
                okey = pk(ob)
                for s in range(2):
                    q0 = s * 256
                    nkt = 4 * sc + 2 * s + 2
                    oreg = ps[ob][0:65, s * 256:(s + 1) * 256]
                    for kt in range(nkt):
                        jd = kt - (4 * sc + 2 * s)
                        c0 = 128 if jd == 1 else 0
                        n = 256 - c0
                        b = nb()
                        S.op("pe", lambda: nc.tensor.matmul(ps[b][:, 0:n], lhsT=KT[po:po + 64, pr, kt * 128:(kt + 1) * 128],
                                                            rhs=QT[po:po + 64, pr, q0 + c0:q0 + 256], start=True, stop=True),
                             reads=[("KT", pr), ("QT", pr)], writes=[pk(b)])
                        pt = PT[it % 3]
                        ptk = ("PT", it % 3)
                        it += 1
                        S.op("act", lambda: nc.scalar.activation(out=pt[:, c0:256], in_=ps[b][:, 0:n], func=AF.Exp, bias=bcol[:, s * NKT + kt, h:h + 1]),
                             reads=[pk(b), "bcol"], writes=[ptk])
                        if jd >= 0:
                            S.op("pool", lambda: nc.gpsimd.affine_select(out=pt[:, c0:c0 + 128], in_=pt[:, c0:c0 + 128], pattern=[[1, 128]],
                                                                         compare_op=ALU.is_ge, fill=0.0, base=0, channel_multiplier=-1),
                                 reads=[ptk], writes=[ptk])
                        S.op("pe", lambda: nc.tensor.matmul(oreg[:, c0:256], lhsT=VA[:, kt, h, :], rhs=pt[:, c0:256], start=(kt == 0), stop=(kt == nkt - 1)),
                             reads=[("VA", kt), ptk], writes=[okey], inc=(s == 1 and kt == nkt - 1))
                S.op("dve", lambda: nc.vector.reciprocal(out=rr[64:65, :], in_=ps[ob][64:65, :]), reads=[okey], writes=["rr"])
                b = nb()
                S.op("pe", lambda: nc.tensor.matmul(ps[b][0:64, :], lhsT=onesf[64:65, 0:64], rhs=rr[64:65, :], start=True, stop=True),
                     reads=["onesf", "rr"], writes=[pk(b)])
                S.op("act", lambda: nc.scalar.copy(out=Rs[:], in_=ps[b][0:64, :]), reads=[pk(b)], writes=["Rs"])
                S.op("dve", lambda: nc.vector.tensor_tensor(out=OT[0:64, h, :], in0=ps[ob][0:64, :], in1=Rs[:], op=ALU.mult),
                     reads=[okey, "Rs"], writes=[("big", h)])
            for j in range(4):
                slot, skey = use_tile(("o", j))
                ov = slot[0:64, 0:16 * 256].rearrange("p (h c) -> p h c", c=256)
                for c in range(2):
                    k = 2 * j + c
                    b = nb()
                    for h in range(AH):
                        S.op("pe", lambda: nc.tensor.matmul(ps[b][:], lhsT=ov[:, h, c * 128:(c + 1) * 128], rhs=OT[0:64, h, :],
                                                            start=(h == 0), stop=(h == AH - 1)),
                             reads=[skey, ("big", h)], writes=[pk(b)], inc=(h == AH - 1))
                    ln_accum(k, b, 2)
                done_tile()
            ln_finish(2)

        def xk(tt):
            nm = "lnb" if tt < 2 else "lnsq"
            return [(nm, 4 * (tt % 2) + i) for i in range(4)]
        XK = xk(0) + xk(1) + xk(2) + xk(3)
        S.fence()
        gi = 0
        for bseq in range(NB if stop_after != "setup" else 0):
            for sc in range(NSC):
                tap.idx = gi
                t0 = sc * TS
                first = (sc == 0)
                S.dma("sp", xin, dr["x"][bseq, t0:t0 + TS, :].rearrange("(t p) d -> p t d", p=128), [], XK, iodom_in)
                for k in range(KD if stop_after != "xdma" else 0):
                    b = nb()
                    for tt in range(4):
                        S.op("pe", lambda: nc.tensor.transpose(out=ps[b][:, tt * 128:(tt + 1) * 128], in_=xin[:, tt, k * 128:(k + 1) * 128], identity=identf[:]),
                             reads=xk(tt) + ["identf"], writes=[pk(b)], inc=(tt == 3))
                    S.op("act", lambda: nc.scalar.copy(out=xT[:, k, :], in_=ps[b][:]), reads=[pk(b)], writes=[("xT", k)])
                    S.op("dve", lambda: nc.vector.tensor_copy(out=xTb[:, k, :], in_=ps[b][:]), reads=[pk(b)], writes=[("xTb", k)])
                S.fence()
                if stop_after not in ("xload", "xdma"):
                    ssd_phase(first)
                    tap("dbg_x1", xT[:, 0, :], ("xT", 0), [128, TS])
                if stop_after not in ("xload", "ssd", "xdma"):
                    ffn_phase(0, 1)
                    tap("dbg_x2", xT[:, 0, :], ("xT", 0), [128, TS])
                if stop_after not in ("xload", "ssd", "ffn0", "xdma"):
                    S.fence()
                    attn_phase(sc, first)
                    tap("dbg_x3", xT[:, 0, :], ("xT", 0), [128, TS])
                    ffn_phase(1, 3)
                for tt in range(4 if stop_after != "xdma" else 0):
                    for hf in range(2):
                        b = nb()
                        for kq in range(4):
                            k = hf * 4 + kq
                            S.op("pe", lambda: nc.tensor.transpose(out=ps[b][:, kq * 128:(kq + 1) * 128], in_=xT[:, k, tt * 128:(tt + 1) * 128], identity=identf[:]),
                                 reads=[("xT", k), "identf"], writes=[pk(b)], inc=(kq == 3))
                        if hf == 0:
                            S.op("act", lambda: nc.scalar.copy(out=xin[:, tt, hf * 512:(hf + 1) * 512], in_=ps[b][:]), reads=[pk(b)], writes=xk(tt))
                        else:
                            S.op("dve", lambda: nc.vector.tensor_copy(out=xin[:, tt, hf * 512:(hf + 1) * 512], in_=ps[b][:]), reads=[pk(b)], writes=xk(tt))
                S.dma("sp", out_d[bseq, t0:t0 + TS, :].rearrange("(t p) d -> p t d", p=128), xin, XK, [], iodom_out)
                gi += 1
        assert stop_after is not None or wstate["next_use"] == total_tiles, (wstate, total_tiles)
        S.wait_all("sp", [iodom_out, dbgdom])
        build.stats = dict(nins=dict(S.nins), ndma=S.ndma, counts={k: v.count for k, v in S.dom.items()})
    return nc, list(dbg.keys())


_CACHE = {}


def kernel(**inputs):
    n_cores = 8
    x = np.ascontiguousarray(inputs["x"], dtype=np.float32)
    B, SEQ, _ = x.shape
    NB = B // n_cores
    key = (NB, SEQ)
    if key not in _CACHE:
        _CACHE[key] = build(NB, SEQ)[0]
    nc = _CACHE[key]
    in_maps = []
    for c in range(n_cores):
        m = {k: np.ascontiguousarray(v, dtype=np.float32) for k, v in inputs.items() if k != "x"}
        m["x"] = x[c * NB:(c + 1) * NB]
        in_maps.append(m)
    res = run_bass_kernel_spmd(nc, in_maps, core_ids=list(range(n_cores)))
    return np.concatenate([r["out"] for r in res.results], axis=0)
```

```python
import numpy as np
from contextlib import ExitStack
from collections import defaultdict

import concourse.bass as bass
import concourse.mybir as mybir
from concourse.bass_utils import run_bass_kernel_spmd

F32 = mybir.dt.float32
BF16 = mybir.dt.bfloat16
AF = mybir.ActivationFunctionType
ALU = mybir.AluOpType

D = 1024
KD = 8
DI = 2048
NG = 8
NHEAD = 32
DFF = 2816
NF = 22
AH = 16
TS = 512
DEPTH = 2
ALPHA = (2.0 * DEPTH) ** 0.25
LN_EPS = 1e-5
RMS_EPS = 1e-5
SLOT = 4096
NSLOT = 3
NEG = -30000.0


class Dom:
    def __init__(self, nc, es, name, step, epoch):
        self.nc, self.es, self.name, self.step, self.epoch = nc, es, name, step, epoch
        self.sems = []
        self.count = 0

    def sem_for(self, cnt):
        e = (cnt - 1) // self.epoch
        while len(self.sems) <= e:
            self.sems.append(self.es.enter_context(self.nc.semaphore(f"s_{self.name}_{len(self.sems)}")))
        return self.sems[e], ((cnt - 1) % self.epoch + 1) * self.step


class Sched:
    def __init__(self, nc, es):
        self.nc, self.es = nc, es
        self.eng = {"pe": nc.tensor, "act": nc.scalar, "dve": nc.vector, "pool": nc.gpsimd, "sp": nc.sync}
        self.dom = {e: Dom(nc, es, e, 1, 4096) for e in ("pe", "act", "dve", "pool")}
        self.seen = defaultdict(int)
        self.lastw = {}
        self.readers = defaultdict(dict)
        self.ndma = 0
        self.nins = defaultdict(int)

    def new_dma_dom(self, name):
        return Dom(self.nc, self.es, name, 16, 1024)

    def _deps(self, own, reads, writes):
        deps = {}

        def need(dc, same_ok):
            dom, cnt = dc
            if dom is own and same_ok:
                return
            if deps.get(dom, 0) < cnt:
                deps[dom] = cnt

        for k in reads:
            if k in self.lastw:
                need(self.lastw[k], False)
            if isinstance(k, tuple) and k[0] == "ps":
                for dom, cnt in self.readers[k].items():
                    need((dom, cnt), True)
        for k in writes:
            if k in self.lastw:
                need(self.lastw[k], True)
            for dom, cnt in self.readers[k].items():
                need((dom, cnt), True)
        return deps

    def _wait(self, e, deps, own=None):
        for dom, cnt in deps.items():
            if self.seen[(e, dom.name)] >= cnt:
                continue
            if dom is own:
                assert cnt <= own.count, "same-engine wait on a future completion"
            sem, val = dom.sem_for(cnt)
            self.eng[e].wait_ge(sem, val)
            self.nins[e] += 1
            self.seen[(e, dom.name)] = cnt

    def op(self, e, fn, reads=(), writes=(), inc=True):
        own = self.dom[e]
        self._wait(e, self._deps(own, reads, writes), own)
        ins = fn()
        self.nins[e] += 1
        tag = own.count + 1
        if inc:
            own.count += 1
            sem, _ = own.sem_for(own.count)
            ins.then_inc(sem, 1)
        for k in reads:
            if self.readers[k].get(own, 0) < tag:
                self.readers[k][own] = tag
        for k in writes:
            self.lastw[k] = (own, tag)
            self.readers[k] = {}
        return ins

    def dma(self, q, out, in_, reads, writes, dom):
        self._wait(q, self._deps(None, reads, writes))
        ins = self.eng[q].dma_start(out=out, in_=in_)
        self.nins[q] += 1
        self.ndma += 1
        dom.count += 1
        sem, _ = dom.sem_for(dom.count)
        ins.then_inc(sem, 16)
        for k in reads:
            self.readers[k][dom] = dom.count
        for k in writes:
            self.lastw[k] = (dom, dom.count)
            self.readers[k] = {}
        return ins

    def fence(self):
        es_ = ("pe", "act", "dve", "pool")
        for e in es_:
            self._wait(e, {self.dom[f]: self.dom[f].count for f in es_ if f != e and self.dom[f].count > 0})

    def wait_all(self, e, doms):
        for dom in doms:
            if dom.count > 0:
                self._wait(e, {dom: dom.count})


def bc(ap, shape):
    return ap.to_broadcast(list(shape))


def weight_tiles():
    tiles = []
    for g in range(NG):
        tiles.append((("inA", g), 8 * 512, [("ssm_in_w", 0, g * 256, 256, "kpc", 0, 512, 0),
                                             ("ssm_in_w", 0, 2048 + g * 256, 256, "kpc", 0, 512, 256)]))
        tiles.append((("inB", g), 8 * 256, [("ssm_in_w", 0, 4096 + g * 128, 128, "kpc", 0, 256, 0),
                                             ("ssm_in_w", 0, 5120 + g * 128, 128, "kpc", 0, 256, 128)]))
    for j in range(4):
        tiles.append((("out", j), 16 * 256, [("ssm_out_w", 0, j * 256, 256, "kpc", 0, 256, 0)]))

    def ffn(l):
        for j in range(11):
            parts = []
            for fc in range(2):
                f = 2 * j + fc
                parts.append(("ffn_gate_w", l, f * 128, 128, "kpc", fc * 1024, 128, 0))
                parts.append(("ffn_up_w", l, f * 128, 128, "kpc", 2048 + fc * 1024, 128, 0))
            tiles.append((("gu", l, j), 4096, parts))
        for j in range(8):
            tiles.append((("dn", l, j), NF * 128, [("ffn_down_w", l, j * 128, 128, "kpc", 0, 128, 0)]))

    ffn(0)
    for j in range(2):
        parts = [("kv_w", None, j * 512 + pr * 128, 128, "kpc", pr * 1024, 128, 0) for pr in range(4)]
        tiles.append((("kvk", j), 4096, parts))
    for j in range(2):
        tiles.append((("kvv", j), 4096, [("kv_w", None, 1024 + j * 512, 512, "kpc", 0, 512, 0)]))
    for j in range(2):
        parts = [("att_q_w", 0, j * 512 + pr * 128, 128, "kpc", pr * 1024, 128, 0) for pr in range(4)]
        tiles.append((("q", j), 4096, parts))
    for j in range(4):
        tiles.append((("o", j), 16 * 256, [("att_o_w", 0, j * 256, 256, "hpc", 0, 256, 0)]))
    ffn(1)
    return tiles


IN_SPECS = [
    ("x", None), ("ssm_in_w", [1, 1024, 6176]), ("ssm_conv_w", [1, 4, 4096]), ("ssm_conv_b", [1, 4096]),
    ("ssm_dt_bias", [1, 32]), ("ssm_a_log", [1, 32]), ("ssm_d", [1, 32]), ("ssm_norm_w", [1, 2048]),
    ("ssm_out_w", [1, 2048, 1024]), ("kv_w", [1024, 2064]), ("kv_b_f", [16]), ("att_q_w", [1, 1024, 1024]),
    ("att_o_w", [1, 1024, 1024]), ("ffn_gate_w", [2, 1024, 2816]), ("ffn_up_w", [2, 1024, 2816]),
    ("ffn_down_w", [2, 2816, 1024]), ("ln_mix_g", [2, 1024]), ("ln_mix_b", [2, 1024]), ("ln_ffn_g", [2, 1024]),
    ("ln_ffn_b", [2, 1024]),
]


def build(NB=4, SEQ=2048, debug=False, stop_after=None):
    nc = bass.Bass("TRN2", target_bir_lowering=False)
    NSC = SEQ // TS
    NKT = SEQ // 128
    dr = {}
    for name, shp in IN_SPECS:
        if name == "x":
            shp = [NB, SEQ, D]
        dr[name] = nc.dram_tensor(name, shp, F32, kind="ExternalInput").ap()
    out_d = nc.dram_tensor("out", [NB, SEQ, D], F32, kind="ExternalOutput").ap()
    tiles = weight_tiles()
    NT = len(tiles)
    wscr = nc.dram_tensor("wscr", [NT, 128, SLOT], BF16, kind="Internal").ap()
    dbg = {}

    with ExitStack() as es:
        ec = es.enter_context
        S = Sched(nc, es)

        def sb(name, shape, dt=F32):
            return ec(nc.sbuf_tensor(name, list(shape), dt))

        identf = sb("identf", [128, 128]); identb = sb("identb", [128, 128], BF16)
        onesf = sb("onesf", [128, 128]); trif = sb("trif", [128, 128])
        lnones = sb("lnones", [128, 128], BF16)
        negmask = sb("negmask", [128, 512], BF16)
        prmA = sb("prmA", [128, 128]); prmB = sb("prmB", [128, 128])
        dtb_bc = sb("dtb_bc", [128, 32]); A_bc = sb("A_bc", [128, 32]); D_bc = sb("D_bc", [128, 32])
        bf_col = sb("bf_col", [16, 1])
        wdt = sb("wdt", [128, 8, 32], BF16); wf = sb("wf", [128, 8, 16], BF16)
        wslot = [sb(f"wslot{i}", [128, SLOT], BF16) for i in range(NSLOT)]
        scr16 = sb("scr16", [128, 4096])
        xin = scr16[:].rearrange("p (t d) -> p t d", d=D)
        lnb = scr16[:, 0:2048].bitcast(BF16).rearrange("p (k t) -> p k t", t=TS)
        lnsq = scr16[:, 2048:4096].bitcast(BF16).rearrange("p (k t) -> p k t", t=TS)
        xT = sb("xT", [128, KD, TS]); xTb = sb("xTb", [128, KD, TS], BF16)
        mean_sb = sb("mean_sb", [128, TS]); rstd_sb = sb("rstd_sb", [128, TS])
        big = sb("big", [128, NF, TS], BF16)
        acc = [sb(f"acc{i}", [128, TS]) for i in range(2)]
        lnt = acc
        stateT = sb("stateT", [128, NHEAD * 64]); stbf = sb("stbf", [128, NHEAD * 64], BF16)
        halo = sb("halo", [128, 32, 3])
        KT = sb("KT", [128, 8, SEQ], BF16)
        VA = sb("VA", [128, NKT, AH, 65], BF16)
        Fcarry = sb("Fcarry", [16, 1]); Fcol = sb("Fcol", [128, NKT, AH])
        lneps = sb("lneps", [128, 2])
        ARENA = 7360
        arena = sb("arena", [128, ARENA])
        ar = {"o": 0}

        def carve(shape, dt=F32):
            n = int(np.prod(shape[1:]))
            w = n if dt == F32 else (n + 1) // 2
            o = ar["o"]
            assert o + w <= ARENA, ("arena overflow", o, w)
            ar["o"] = o + w
            v = arena[0:shape[0], o:o + w]
            if dt != F32:
                v = v.bitcast(dt)
            if len(shape) == 3:
                v = v.rearrange("p (a b) -> p a b", b=shape[2])
            return v

        stg1 = carve([128, 128]); stg2 = carve([128, 128])
        ar["o"] = 0
        dt_sb = carve([128, 4, 32]); a_sb = carve([128, 4, 32]); cumcol = carve([128, 4, 32]); expcum = carve([128, 4, 32])
        sp_t = [carve([128, 4, 32]) for _ in range(3)]
        cumcolp = carve([128, 4, 32])
        ubuf = carve([128, 2, TS + 4])
        e4 = [carve([128, 4]) for _ in range(2)]
        ys = carve([128, 256]); ysum = carve([128, 256])
        yg = carve([128, 4, 256]); ss = carve([128, 4]); sd4 = carve([128, 4]); rstd4 = carve([128, 4])
        junk = carve([128, 256]); ygn = carve([128, 256], BF16); sttmp = carve([128, 256])
        zs = [carve([128, 4, 256], BF16), None]
        xbc = [carve([128, 4, TS], BF16), None]

        def chunk_set():
            return dict(xtok=carve([128, 256], BF16), xD=carve([128, 256], BF16), btok=carve([128, 128], BF16),
                        atri=carve([128, 512]), decayT=carve([128, 512], BF16), GT=carve([128, 512], BF16), xw=carve([128, 256], BF16))
        cset = [chunk_set(), None]
        ssd_top = ar["o"]
        sav = (arena, ar["o"])
        arena_main = arena

        def carve16(shape, dt=F32):
            n = int(np.prod(shape[1:]))
            w = n if dt == F32 else (n + 1) // 2
            o = c16["o"]
            assert o + w <= 4096, ("scr16 overflow", o, w)
            c16["o"] = o + w
            v = scr16[0:shape[0], o:o + w]
            if dt != F32:
                v = v.bitcast(dt)
            if len(shape) == 3:
                v = v.rearrange("p (a b) -> p a b", b=shape[2])
            return v
        c16 = {"o": 0}
        zs[1] = carve16([128, 4, 256], BF16)
        xbc[1] = carve16([128, 4, TS], BF16)
        cset[1] = dict(xtok=carve16([128, 256], BF16), xD=carve16([128, 256], BF16), btok=carve16([128, 128], BF16),
                       atri=carve16([128, 512]), decayT=carve16([128, 512], BF16), GT=carve16([128, 512], BF16), xw=carve16([128, 256], BF16))
        ar["o"] = 0
        QT = carve([128, 8, TS], BF16)
        f_v = carve([16, TS]); f_a = carve([16, TS]); f_l = carve([16, TS]); Frow = carve([16, TS])
        fdiag = carve([16, 32]); Fq0 = carve([128, 2, AH]); bcol = carve([128, 2 * NKT, AH])
        PT = [carve([128, 256], BF16) for _ in range(4)]
        rr = [carve([128, 512]) for _ in range(2)]; Rs = [carve([64, 512]) for _ in range(2)]
        att_top = ar["o"]
        ps = [ec(nc.psum_tensor(f"ps{i}", [128, 512], F32)) for i in range(8)]

        def psb(i):
            return ps[i][:].bitcast(BF16)

        ring = {"i": 0, "n": 8}

        def nb():
            i = ring["i"] % ring["n"]
            ring["i"] = (i + 1) % ring["n"]
            return i

        def pk(i):
            return ("ps", i)

        P_ = "pool"
        S.op(P_, lambda: nc.gpsimd.memset(identf[:], 0.0), writes=["identf"])
        S.op(P_, lambda: nc.gpsimd.affine_select(out=identf[:], in_=identf[:], pattern=[[-1, 128]], compare_op=ALU.not_equal,
                                                 fill=1.0, base=0, channel_multiplier=1), reads=["identf"], writes=["identf"])
        S.op(P_, lambda: nc.gpsimd.tensor_copy(out=identb[:], in_=identf[:]), reads=["identf"], writes=["identb"])
        S.op(P_, lambda: nc.gpsimd.memset(onesf[:], 1.0), writes=["onesf"])
        S.op(P_, lambda: nc.gpsimd.memset(lnones[:], 1.0 / D), writes=["lnones"])
        S.op(P_, lambda: nc.gpsimd.memset(trif[:], 1.0), writes=["trif"])
        S.op(P_, lambda: nc.gpsimd.affine_select(out=trif[:], in_=trif[:], pattern=[[1, 128]], compare_op=ALU.is_ge,
                                                 fill=0.0, base=0, channel_multiplier=-1), reads=["trif"], writes=["trif"])
        S.op(P_, lambda: nc.gpsimd.memset(negmask[:], 0.0), writes=["negmask"])
        nm3 = negmask[:].rearrange("p (h l) -> p h l", h=4)
        S.op(P_, lambda: nc.gpsimd.affine_select(out=nm3, in_=nm3, pattern=[[0, 4], [1, 128]], compare_op=ALU.is_ge,
                                                 fill=NEG, base=0, channel_multiplier=-1), reads=["negmask"], writes=["negmask"])
        S.op(P_, lambda: nc.gpsimd.memset(stg2[:], 0.0), writes=["stg2"])

        cdom = S.new_dma_dom("cst")
        S.dma("sp", stg1[:], dr["ssm_conv_w"][0].rearrange("k (c p) -> (k c) p", p=128), [], ["stg1"], cdom)
        rows = [("ssm_conv_b", dr["ssm_conv_b"][0], 32, 0), ("ssm_norm_w", dr["ssm_norm_w"][0], 16, 32),
                ("ln_mix_g", dr["ln_mix_g"].rearrange("l d -> (l d)"), 16, 48), ("ln_mix_b", dr["ln_mix_b"].rearrange("l d -> (l d)"), 16, 64),
                ("ln_ffn_g", dr["ln_ffn_g"].rearrange("l d -> (l d)"), 16, 80), ("ln_ffn_b", dr["ln_ffn_b"].rearrange("l d -> (l d)"), 16, 96)]
        for (_, src, n, r0) in rows:
            S.dma("sp", stg2[r0:r0 + n, :], src.rearrange("(c p) -> c p", p=128), [], ["stg2"], cdom)
        S.dma("sp", dtb_bc[:], dr["ssm_dt_bias"].partition_broadcast(128), [], ["dtb_bc"], cdom)
        S.dma("sp", A_bc[:], dr["ssm_a_log"].partition_broadcast(128), [], ["A_bc"], cdom)
        S.dma("sp", D_bc[:], dr["ssm_d"].partition_broadcast(128), [], ["D_bc"], cdom)
        S.dma("sp", bf_col[:], dr["kv_b_f"].rearrange("(h o) -> h o", o=1), [], ["bf_col"], cdom)
        S.op("act", lambda: nc.scalar.activation(out=A_bc[:], in_=A_bc[:], func=AF.Exp), reads=["A_bc"], writes=["A_bc"])
        S.op("dve", lambda: nc.vector.tensor_scalar_mul(out=A_bc[:], in0=A_bc[:], scalar1=-1.0), reads=["A_bc"], writes=["A_bc"])
        b0 = nb()
        S.op("pe", lambda: nc.tensor.transpose(out=ps[b0][:, 0:128], in_=stg1[:], identity=identf[:]), reads=["stg1", "identf"], writes=[pk(b0)])
        S.op("dve", lambda: nc.vector.tensor_copy(out=prmA[:], in_=ps[b0][:, 0:128]), reads=[pk(b0)], writes=["prmA"])
        b1 = nb()
        S.op("pe", lambda: nc.tensor.transpose(out=ps[b1][:, 0:128], in_=stg2[:], identity=identf[:]), reads=["stg2", "identf"], writes=[pk(b1)])
        S.op("dve", lambda: nc.vector.tensor_copy(out=prmB[:], in_=ps[b1][:, 0:128]), reads=[pk(b1)], writes=["prmB"])
        cw = prmA[:].rearrange("p (k c) -> p k c", k=4)
        cb = prmB[:, 0:32]
        normw = prmB[:, 32:48]
        lng = {0: prmB[:, 48:56], 1: prmB[:, 80:88], 2: prmB[:, 56:64], 3: prmB[:, 88:96]}
        lnbias = {0: prmB[:, 64:72], 1: prmB[:, 96:104], 2: prmB[:, 72:80], 3: prmB[:, 104:112]}

        import os
        wcdom = S.new_dma_dom("wcv")
        stf = [scr16[:], xT[:].rearrange("p k t -> p (k t)")]
        stb = [big[:, 0:8, :].rearrange("p k t -> p (k t)"), big[:, 8:16, :].rearrange("p k t -> p (k t)")]
        ktf = KT[:].rearrange("p k t -> p (k t)").bitcast(F32)
        for i in range(ktf.shape[1] // 4096):
            stf.append(ktf[:, i * 4096:(i + 1) * 4096])
        vaf = VA[:].rearrange("p a b c -> p (a b c)")
        for i in range(vaf.shape[1] // 4096):
            stb.append(vaf[:, i * 4096:(i + 1) * 4096])
        NST = min(len(stf), len(stb), 4)
        cvl = [S.new_dma_dom(f"cvl{i}") for i in range(NST)]
        cvs = [S.new_dma_dom(f"cvs{i}") for i in range(NST)]
        cast_eng = ["dve", "act", "pool"]
        for ti, (name, nel, parts) in enumerate(tiles):
            if os.environ.get("SKIP_CONV"):
                break
            sl = ti % NST
            npart = 64 if name[0] == "o" else 128
            for (src, idx, c0, n, kind, base, cstride, coff) in parts:
                w = dr[src] if idx is None else dr[src][idx]
                if kind == "kpc":
                    nk = w.shape[0] // 128
                    s_ap = w[:, c0:c0 + n].rearrange("(k p) c -> p k c", p=128)
                    d_ap = stf[sl][:, base:base + nk * cstride].rearrange("p (k c) -> p k c", c=cstride)[:, :, coff:coff + n]
                else:
                    s_ap = w[:, c0:c0 + n].rearrange("(h p) c -> p h c", p=64)
                    d_ap = stf[sl][0:64, base:base + 16 * cstride].rearrange("p (h c) -> p h c", c=cstride)[:, :, coff:coff + n]
                S.dma("sp", d_ap, s_ap, [], [("stf", sl)], cvl[sl])
            ce = cast_eng[ti % 3]
            if ce == "dve":
                S.op("dve", lambda: nc.vector.tensor_copy(out=stb[sl][0:npart, 0:nel], in_=stf[sl][0:npart, 0:nel]), reads=[("stf", sl)], writes=[("stb", sl)])
            elif ce == "act":
                S.op("act", lambda: nc.scalar.copy(out=stb[sl][0:npart, 0:nel], in_=stf[sl][0:npart, 0:nel]), reads=[("stf", sl)], writes=[("stb", sl)])
            else:
                S.op("pool", lambda: nc.gpsimd.tensor_copy(out=stb[sl][0:npart, 0:nel], in_=stf[sl][0:npart, 0:nel]), reads=[("stf", sl)], writes=[("stb", sl)])
            S.dma("sp", wscr[ti][0:npart, 0:nel], stb[sl][0:npart, 0:nel], [("stb", sl)], [("wscr", ti)], cvs[sl])
        S.dma("pool", wdt[:], dr["ssm_in_w"][0][:, 6144:6176].rearrange("(k p) c -> p k c", p=128), [], ["wdt"], wcdom)
        S.dma("pool", wf[:], dr["kv_w"][:, 2048:2064].rearrange("(k p) c -> p k c", p=128), [], ["wf"], wcdom)
        S.wait_all("pool", cvs + cvl)
        S.op("pool", lambda: nc.gpsimd.memset(VA[:], 1.0), reads=[("stb", i) for i in range(NST)], writes=[("VA", kt) for kt in range(NKT)])
        S.wait_all("pe", cvs + cvl)
        S.wait_all("act", cvs + cvl)
        S.wait_all("dve", cvs + cvl)
        S.wait_all("pool", cvs + cvl)

        wdoms = [S.new_dma_dom(f"w{i}") for i in range(NSLOT)]
        wstate = {"next_load": 0, "next_use": 0}
        total_tiles = NB * NSC * NT

        def prefetch():
            i = wstate["next_load"]
            if i >= total_tiles:
                return
            wstate["next_load"] += 1
            ti = i % NT
            s = i % NSLOT
            nel = tiles[ti][1]
            npart = 64 if tiles[ti][0][0] == "o" else 128
            S.dma("sp", wslot[s][0:npart, 0:nel], wscr[ti][0:npart, 0:nel], [("wscr", ti)], [("wslot", s)], wdoms[s])

        def use_tile(expect):
            if stop_after is not None:
                while tiles[wstate["next_use"] % NT][0] != expect:
                    wstate["next_use"] += 1
                    prefetch()
            i = wstate["next_use"]
            wstate["next_use"] += 1
            ti = i % NT
            assert tiles[ti][0] == expect, (tiles[ti][0], expect)
            s = i % NSLOT
            return wslot[s], ("wslot", s)

        def done_tile():
            prefetch()

        for _ in range(NSLOT):
            prefetch()

        iodom_in = S.new_dma_dom("xin")
        iodom_out = S.new_dma_dom("xout")
        dbgdom = S.new_dma_dom("dbg")

        def tap(name, ap, key, shape):
            if not debug:
                return
            if name not in dbg:
                dbg[name] = nc.dram_tensor(name, [NB * NSC] + list(shape), ap.dtype, kind="ExternalOutput").ap()
            S.dma("sp", dbg[name][tap.idx], ap, [key], [], dbgdom)
        tap.idx = 0

        def ln_accum(k, bank, ln_idx):
            S.op("dve", lambda: nc.vector.scalar_tensor_tensor(out=xT[:, k, :], in0=xT[:, k, :], scalar=ALPHA, in1=ps[bank][:],
                                                               op0=ALU.mult, op1=ALU.add),
                 reads=[("xT", k), pk(bank)], writes=[("xT", k)])
            S.op("act", lambda: nc.scalar.copy(out=lnb[:, k, :], in_=xT[:, k, :]), reads=[("xT", k)], writes=[("lnb", k)])
            S.op("act", lambda: nc.scalar.activation(out=lnsq[:, k, :], in_=xT[:, k, :], func=AF.Square), reads=[("xT", k)], writes=[("lnsq", k)])

        def ln_finish(ln_idx):
            bm, be = nb(), nb()
            for k in range(KD):
                S.op("pe", lambda: nc.tensor.matmul(ps[bm][:], lhsT=lnones[:], rhs=lnb[:, k, :], start=(k == 0), stop=(k == KD - 1)),
                     reads=[("lnb", k), "lnones"], writes=[pk(bm)], inc=(k == KD - 1))
            for k in range(KD):
                S.op("pe", lambda: nc.tensor.matmul(ps[be][:], lhsT=lnones[:], rhs=lnsq[:, k, :], start=(k == 0), stop=(k == KD - 1)),
                     reads=[("lnsq", k), "lnones"], writes=[pk(be)], inc=(k == KD - 1))
            S.op("act", lambda: nc.scalar.copy(out=mean_sb[:], in_=ps[bm][:]), reads=[pk(bm)], writes=["mean_sb"])
            S.op("dve", lambda: nc.vector.tensor_tensor(out=rstd_sb[:], in0=ps[bm][:], in1=mean_sb[:], op=ALU.mult),
                 reads=[pk(bm), "mean_sb"], writes=["rstd_sb"])
            S.op("dve", lambda: nc.vector.tensor_tensor(out=rstd_sb[:], in0=ps[be][:], in1=rstd_sb[:], op=ALU.subtract),
                 reads=[pk(be), "rstd_sb"], writes=["rstd_sb"])
            S.op("act", lambda: nc.scalar.activation(out=rstd_sb[:], in_=rstd_sb[:], func=AF.Sqrt, bias=lneps[:, 0:1]),
                 reads=["rstd_sb", "lneps"], writes=["rstd_sb"])
            S.op("dve", lambda: nc.vector.reciprocal(out=rstd_sb[:], in_=rstd_sb[:]), reads=["rstd_sb"], writes=["rstd_sb"])
            for k in range(KD):
                t1, t2 = lnt[0], lnt[1]
                k1, k2 = ("acc", 0), ("acc", 1)
                S.op("dve", lambda: nc.vector.tensor_tensor(out=t1[:], in0=xT[:, k, :], in1=mean_sb[:], op=ALU.subtract),
                     reads=[("xT", k), "mean_sb"], writes=[k1])
                S.op("pool", lambda: nc.gpsimd.tensor_tensor(out=t2[:], in0=t1[:], in1=rstd_sb[:], op=ALU.mult),
                     reads=[k1, "rstd_sb"], writes=[k2])
                S.op("act", lambda: nc.scalar.activation(out=xT[:, k, :], in_=t2[:], func=AF.Identity, scale=lng[ln_idx][:, k:k + 1],
                                                         bias=lnbias[ln_idx][:, k:k + 1]),
                     reads=[k2, "prmB"], writes=[("xT", k)])
                S.op("act", lambda: nc.scalar.activation(out=xTb[:, k, :], in_=t2[:], func=AF.Identity, scale=lng[ln_idx][:, k:k + 1],
                                                         bias=lnbias[ln_idx][:, k:k + 1]),
                     reads=[k2, "prmB"], writes=[("xTb", k)])

        S.op("pool", lambda: nc.gpsimd.memset(lneps[:, 0:1], LN_EPS), writes=["lneps"])
        S.op("pool", lambda: nc.gpsimd.memset(lneps[:, 1:2], RMS_EPS), writes=["lneps"])

        def ffn_phase(l, ln_idx):
            for j in range(11):
                slot, skey = use_tile(("gu", l, j))
                for fc in range(2):
                    f = 2 * j + fc
                    bg, bu = nb(), nb()
                    gv = slot[:, fc * 1024:(fc + 1) * 1024].rearrange("p (k c) -> p k c", c=128)
                    uv = slot[:, 2048 + fc * 1024:2048 + (fc + 1) * 1024].rearrange("p (k c) -> p k c", c=128)
                    for k in range(KD):
                        S.op("pe", lambda: nc.tensor.matmul(ps[bg][:], lhsT=gv[:, k, :], rhs=xTb[:, k, :], start=(k == 0), stop=(k == KD - 1)),
                             reads=[skey, ("xTb", k)], writes=[pk(bg)], inc=(k == KD - 1))
                    for k in range(KD):
                        S.op("pe", lambda: nc.tensor.matmul(ps[bu][:], lhsT=uv[:, k, :], rhs=xTb[:, k, :], start=(k == 0), stop=(k == KD - 1)),
                             reads=[skey, ("xTb", k)], writes=[pk(bu)], inc=(k == KD - 1))
                    a_ = acc[f % 2]
                    ak = ("acc", f % 2)
                    S.op("act", lambda: nc.scalar.activation(out=a_[:], in_=ps[bg][:], func=AF.Silu), reads=[pk(bg)], writes=[ak])
                    S.op("dve", lambda: nc.vector.tensor_tensor(out=big[:, f, :], in0=a_[:], in1=ps[bu][:], op=ALU.mult),
                         reads=[ak, pk(bu)], writes=[("big", f)])
                done_tile()
            for k in range(KD):
                slot, skey = use_tile(("dn", l, k))
                dv = slot[:, 0:NF * 128].rearrange("p (f c) -> p f c", c=128)
                b = nb()
                for f in range(NF):
                    S.op("pe", lambda: nc.tensor.matmul(ps[b][:], lhsT=dv[:, f, :], rhs=big[:, f, :], start=(f == 0), stop=(f == NF - 1)),
                         reads=[skey, ("big", f)], writes=[pk(b)], inc=(f == NF - 1))
                ln_accum(k, b, ln_idx)
                done_tile()
            ln_finish(ln_idx)

        def ssd_phase(first_in_seq):
            bd = nb()
            for tt in range(4):
                for k in range(KD):
                    S.op("pe", lambda: nc.tensor.matmul(ps[bd][:, tt * 32:(tt + 1) * 32], lhsT=xTb[:, k, tt * 128:(tt + 1) * 128], rhs=wdt[:, k, :],
                                                        start=(k == 0), stop=(k == KD - 1)),
                         reads=[("xTb", k), "wdt"], writes=[pk(bd)], inc=(tt == 3 and k == KD - 1))
            pd = ps[bd][:, 0:128].rearrange("p (t h) -> p t h", h=32)
            v_, av_, l_ = sp_t
            S.op("dve", lambda: nc.vector.tensor_tensor(out=v_[:], in0=pd, in1=bc(dtb_bc[:].unsqueeze(1), [128, 4, 32]), op=ALU.add),
                 reads=[pk(bd), "dtb_bc"], writes=["sp_v"])
            S.op("act", lambda: nc.scalar.activation(out=av_[:], in_=v_[:], func=AF.Abs), reads=["sp_v"], writes=["sp_a"])
            S.op("act", lambda: nc.scalar.activation(out=av_[:], in_=av_[:], func=AF.Exp, scale=-1.0), reads=["sp_a"], writes=["sp_a"])
            S.op("act", lambda: nc.scalar.activation(out=l_[:], in_=av_[:], func=AF.Ln, bias=1.0), reads=["sp_a"], writes=["sp_l"])
            S.op("dve", lambda: nc.vector.scalar_tensor_tensor(out=dt_sb[:], in0=v_[:], scalar=0.0, in1=l_[:], op0=ALU.max, op1=ALU.add),
                 reads=["sp_v", "sp_l"], writes=["dt_sb"])
            S.op("act", lambda: nc.scalar.activation(out=l_[:], in_=dt_sb[:], func=AF.Ln), reads=["dt_sb"], writes=["sp_l"])
            S.op("dve", lambda: nc.vector.tensor_tensor(out=a_sb[:], in0=dt_sb[:], in1=bc(A_bc[:].unsqueeze(1), [128, 4, 32]), op=ALU.mult),
                 reads=["dt_sb", "A_bc"], writes=["a_sb"])
            bcu = nb()
            for c in range(4):
                S.op("pe", lambda: nc.tensor.matmul(ps[bcu][:, c * 32:(c + 1) * 32], lhsT=trif[:], rhs=a_sb[:, c, :], start=True, stop=True),
                     reads=["trif", "a_sb"], writes=[pk(bcu)], inc=(c == 3))
            pc = ps[bcu][:, 0:128].rearrange("p (t h) -> p t h", h=32)
            S.op("dve", lambda: nc.vector.tensor_tensor(out=cumcolp[:], in0=pc, in1=l_[:], op=ALU.subtract), reads=[pk(bcu), "sp_l"], writes=["cumcolp"])
            S.op("act", lambda: nc.scalar.activation(out=expcum[:], in_=pc, func=AF.Exp), reads=[pk(bcu)], writes=["expcum"])
            if first_in_seq:
                S.op("pool", lambda: nc.gpsimd.memset(stateT[:], 0.0), writes=[("stateT", g) for g in range(NG)])
                S.op("pool", lambda: nc.gpsimd.memset(stbf[:], 0.0), writes=[("stbf", g) for g in range(NG)])
                S.op("pool", lambda: nc.gpsimd.memset(halo[:], 0.0), writes=[("halo", ci) for ci in range(32)])

            def inproj_pieces(g):
                gb = g % 2
                zs_, xbc_ = zs[gb], xbc[gb]
                st = {}

                def p_open_a():
                    st["slot"], st["skey"] = use_tile(("inA", g))
                    st["Wv"] = st["slot"][:, 0:8 * 512].rearrange("p (k c) -> p k c", c=512)

                def p_z(half):
                    def f():
                        if half == 0:
                            p_open_a()
                        Wv, skey = st["Wv"], st["skey"]
                        bz = nb()
                        for t2 in range(2):
                            tt = 2 * half + t2
                            for k in range(KD):
                                S.op("pe", lambda: nc.tensor.matmul(ps[bz][:, t2 * 256:(t2 + 1) * 256], lhsT=xTb[:, k, tt * 128:(tt + 1) * 128],
                                                                    rhs=Wv[:, k, 0:256], start=(k == 0), stop=(k == KD - 1)),
                                     reads=[skey, ("xTb", k)], writes=[pk(bz)], inc=(t2 == 1 and k == KD - 1))
                        S.op("act", lambda: nc.scalar.activation(out=zs_[:, 2 * half:2 * half + 2, :], in_=ps[bz][:].rearrange("p (t c) -> p t c", c=256),
                                                                 func=AF.Silu), reads=[pk(bz)], writes=[("zs", gb)])
                    return f

                def p_x(r):
                    def f():
                        ci = (2 * g + r) if r < 2 else (16 + g if r == 2 else 24 + g)
                        if r == 2:
                            done_tile()
                            st["slot"], st["skey"] = use_tile(("inB", g))
                            st["Wv"] = st["slot"][:, 0:8 * 256].rearrange("p (k c) -> p k c", c=256)
                        Wv, skey = st["Wv"], st["skey"]
                        wcol = (256 + r * 128) if r < 2 else (r - 2) * 128
                        bx = nb()
                        for k in range(KD):
                            S.op("pe", lambda: nc.tensor.matmul(ps[bx][:], lhsT=Wv[:, k, wcol:wcol + 128], rhs=xTb[:, k, :],
                                                                start=(k == 0), stop=(k == KD - 1)),
                                 reads=[skey, ("xTb", k)], writes=[pk(bx)], inc=(k == KD - 1))
                        ur = r % 2
                        uk = ("ubuf", ur)
                        S.op("pool", lambda: nc.gpsimd.tensor_copy(out=ubuf[:, ur, 0:3], in_=halo[:, ci, :]), reads=[("halo", ci)], writes=[uk])
                        S.op("act", lambda: nc.scalar.copy(out=ubuf[:, ur, 3:TS + 3], in_=ps[bx][:]), reads=[pk(bx)], writes=[uk])
                        a_ = acc[r % 2]
                        ak = ("acc", r % 2)
                        S.op("act", lambda: nc.scalar.activation(out=a_[:], in_=ps[bx][:], func=AF.Identity, scale=cw[:, 3, ci:ci + 1], bias=cb[:, ci:ci + 1]),
                             reads=[pk(bx), "prmA", "prmB"], writes=[ak])
                        for kk in range(3):
                            S.op("dve", lambda: nc.vector.scalar_tensor_tensor(out=a_[:], in0=ubuf[:, ur, kk:kk + TS], scalar=cw[:, kk, ci:ci + 1], in1=a_[:],
                                                                               op0=ALU.mult, op1=ALU.add), reads=[uk, ak, "prmA"], writes=[ak])
                        S.op("pool", lambda: nc.gpsimd.tensor_copy(out=halo[:, ci, :], in_=ubuf[:, ur, TS:TS + 3]), reads=[uk], writes=[("halo", ci)])
                        S.op("act", lambda: nc.scalar.activation(out=xbc_[:, r, :], in_=a_[:], func=AF.Silu), reads=[ak], writes=[("xbc", gb, r)])
                        if r == 3:
                            done_tile()
                    return f
                return [p_z(0), p_z(1), p_x(0), p_x(1), p_x(2), p_x(3)]

            cst = {}

            def stage_A(g, c):
                gb = g % 2
                xbc_ = xbc[gb]
                cs_ = cset[c % 2]
                ck = ("cs", c % 2)
                hs = slice(4 * g, 4 * g + 4)
                cs = slice(c * 128, (c + 1) * 128)
                bt = nb()
                T1 = psb(bt)
                for j, r in enumerate((0, 1, 2)):
                    S.op("pe", lambda: nc.tensor.transpose(out=T1[:, j * 128:(j + 1) * 128], in_=xbc_[:, r, cs], identity=identb[:]),
                         reads=[("xbc", gb, r), "identb"], writes=[pk(bt)], inc=(j == 2))
                T1x = T1[:, 0:256].rearrange("p (h q) -> p h q", q=64)
                S.op("act", lambda: nc.scalar.copy(out=cs_["xtok"][:], in_=T1[:, 0:256]), reads=[pk(bt)], writes=[(ck, "xtok")])
                S.op("dve", lambda: nc.vector.tensor_tensor(out=cs_["xD"][:].rearrange("p (h q) -> p h q", q=64), in0=T1x,
                                                            in1=bc(D_bc[:, hs].unsqueeze(2), [128, 4, 64]), op=ALU.mult),
                     reads=[pk(bt), "D_bc"], writes=[(ck, "xD")])
                S.op("act", lambda: nc.scalar.copy(out=cs_["btok"][:], in_=T1[:, 256:384]), reads=[pk(bt)], writes=[(ck, "btok")])
                S.op("pool", lambda: nc.gpsimd.tensor_tensor(out=cs_["atri"][:].rearrange("p (h l) -> p h l", l=128),
                                                             in0=bc(trif[:].unsqueeze(1), [128, 4, 128]),
                                                             in1=bc(a_sb[:, c, hs].unsqueeze(2), [128, 4, 128]), op=ALU.mult),
                     reads=["trif", "a_sb"], writes=[(ck, "atri")])
                b1_ = nb()
                S.op("pe", lambda: nc.tensor.matmul(ps[b1_][:], lhsT=onesf[:], rhs=cs_["atri"][:], start=True, stop=False),
                     reads=["onesf", (ck, "atri")], writes=[pk(b1_)], inc=False)
                S.op("pe", lambda: nc.tensor.matmul(ps[b1_][:], lhsT=identb[:], rhs=negmask[:], start=False, stop=True),
                     reads=["identb", "negmask"], writes=[pk(b1_)])
                b2_ = nb()
                S.op("pe", lambda: nc.tensor.matmul(ps[b2_][:, 0:128], lhsT=xbc_[:, 2, cs], rhs=xbc_[:, 3, cs], start=True, stop=True),
                     reads=[("xbc", gb, 2), ("xbc", gb, 3)], writes=[pk(b2_)])
                cst[(g, c)] = dict(b1=b1_, b2=b2_)

            def stage_B(g, c):
                cs_ = cset[c % 2]
                ck = ("cs", c % 2)
                hs = slice(4 * g, 4 * g + 4)
                b1_, b2_ = cst[(g, c)]["b1"], cst[(g, c)]["b2"]
                X1 = ps[b1_][:].rearrange("p (h l) -> p h l", l=128)
                seg = cs_["atri"]
                S.op("dve", lambda: nc.vector.tensor_tensor(out=seg[:].rearrange("p (h l) -> p h l", l=128), in0=X1,
                                                            in1=bc(cumcolp[:, c, hs].unsqueeze(2), [128, 4, 128]), op=ALU.subtract),
                     reads=[pk(b1_), "cumcolp"], writes=[(ck, "atri")])
                S.op("act", lambda: nc.scalar.activation(out=e4[c % 2][:], in_=X1[:, :, 127], func=AF.Exp), reads=[pk(b1_)], writes=[("e4", c % 2)])
                S.op("act", lambda: nc.scalar.activation(out=cs_["decayT"][:], in_=seg[:], func=AF.Exp), reads=[(ck, "atri")], writes=[(ck, "decayT")])
                S.op("dve", lambda: nc.vector.tensor_tensor(out=cs_["GT"][:].rearrange("p (h l) -> p h l", l=128),
                                                            in0=cs_["decayT"][:].rearrange("p (h l) -> p h l", l=128),
                                                            in1=bc(ps[b2_][:, 0:128].unsqueeze(1), [128, 4, 128]), op=ALU.mult),
                     reads=[(ck, "decayT"), pk(b2_)], writes=[(ck, "GT")])
                dlast = cs_["decayT"][:].rearrange("p (h l) -> p h l", l=128)[:, :, 127:128]
                S.op("dve", lambda: nc.vector.tensor_tensor(out=cs_["xw"][:].rearrange("p (h q) -> p h q", q=64),
                                                            in0=cs_["xtok"][:].rearrange("p (h q) -> p h q", q=64),
                                                            in1=bc(dlast, [128, 4, 64]), op=ALU.mult),
                     reads=[(ck, "xtok"), (ck, "decayT")], writes=[(ck, "xw")])
                b3_ = nb()
                S.op("pe", lambda: nc.tensor.matmul(ps[b3_][:, 0:256], lhsT=identb[:], rhs=cs_["xD"][:], start=True, stop=False),
                     reads=["identb", (ck, "xD")], writes=[pk(b3_)], inc=False)
                for h in range(4):
                    S.op("pe", lambda: nc.tensor.matmul(ps[b3_][:, h * 64:(h + 1) * 64], lhsT=cs_["GT"][:, h * 128:(h + 1) * 128],
                                                        rhs=cs_["xtok"][:, h * 64:(h + 1) * 64], start=False, stop=True),
                         reads=[(ck, "GT"), (ck, "xtok")], writes=[pk(b3_)], inc=False)
                S.op("pe", lambda: nc.tensor.matmul(ps[b3_][:, 256:512], lhsT=cs_["btok"][:], rhs=cs_["xw"][:], start=True, stop=True),
                     reads=[(ck, "btok"), (ck, "xw")], writes=[pk(b3_)])
                cst[(g, c)]["b3"] = b3_

            def stage_C(g, c):
                gb = g % 2
                xbc_ = xbc[gb]
                hs = slice(4 * g, 4 * g + 4)
                cs = slice(c * 128, (c + 1) * 128)
                b3_ = cst[(g, c)]["b3"]
                st_g = stateT[:, g * 256:(g + 1) * 256]
                stb_g = stbf[:, g * 256:(g + 1) * 256]
                b4_ = nb()
                S.op("pe", lambda: nc.tensor.matmul(ps[b4_][:, 0:256], lhsT=xbc_[:, 3, cs], rhs=stb_g, start=True, stop=True),
                     reads=[("xbc", gb, 3), ("stbf", g)], writes=[pk(b4_)])
                S.op("dve", lambda: nc.vector.tensor_tensor(out=sttmp[:].rearrange("p (h q) -> p h q", q=64),
                                                            in0=st_g.rearrange("p (h q) -> p h q", q=64),
                                                            in1=bc(e4[c % 2][:].unsqueeze(2), [128, 4, 64]), op=ALU.mult),
                     reads=[("stateT", g), ("e4", c % 2)], writes=["sttmp"])
                S.op("dve", lambda: nc.vector.tensor_tensor(out=st_g, in0=sttmp[:], in1=ps[b3_][:, 256:512], op=ALU.add),
                     reads=["sttmp", pk(b3_)], writes=[("stateT", g)])
                S.op("dve", lambda: nc.vector.tensor_tensor(out=ys[:].rearrange("p (h q) -> p h q", q=64),
                                                            in0=ps[b4_][:, 0:256].rearrange("p (h q) -> p h q", q=64),
                                                            in1=bc(expcum[:, c, hs].unsqueeze(2), [128, 4, 64]), op=ALU.mult),
                     reads=[pk(b4_), "expcum"], writes=["ys"])
                S.op("act", lambda: nc.scalar.copy(out=stb_g, in_=st_g), reads=[("stateT", g)], writes=[("stbf", g)])
                S.op("dve", lambda: nc.vector.tensor_tensor(out=ysum[:], in0=ps[b3_][:, 0:256], in1=ys[:], op=ALU.add),
                     reads=[pk(b3_), "ys"], writes=["ysum"])
                S.op("pool", lambda: nc.gpsimd.tensor_tensor(out=yg[:, c, :], in0=ysum[:], in1=zs[gb][:, c, :], op=ALU.mult),
                     reads=["ysum", ("zs", gb)], writes=[("yg", c)])
                S.op("act", lambda: nc.scalar.activation(out=junk[:], in_=yg[:, c, :], func=AF.Square, accum_out=ss[:, c:c + 1]),
                     reads=[("yg", c)], writes=["junk", ("ss", c)])

            def group_end(g):
                S.op("act", lambda: nc.scalar.activation(out=sd4[:], in_=ss[:], func=AF.Sqrt, scale=1.0 / 256.0, bias=lneps[:, 1:2]),
                     reads=[("ss", c) for c in range(4)] + ["lneps"], writes=["sd4"])
                S.op("dve", lambda: nc.vector.reciprocal(out=rstd4[:], in_=sd4[:]), reads=["sd4"], writes=["rstd4"])
                bn_ = nb()
                Tn = psb(bn_)
                for c in range(4):
                    S.op("act", lambda: nc.scalar.activation(out=ygn[:], in_=yg[:, c, :], func=AF.Copy, scale=rstd4[:, c:c + 1]),
                         reads=[("yg", c), "rstd4"], writes=["ygn"])
                    for j in range(2):
                        S.op("pe", lambda: nc.tensor.transpose(out=Tn[:, (j * 4 + c) * 128:(j * 4 + c + 1) * 128], in_=ygn[:, j * 128:(j + 1) * 128],
                                                               identity=identb[:]),
                             reads=["ygn", "identb"], writes=[pk(bn_)], inc=(j == 1))
                for j in range(2):
                    kc = 2 * g + j
                    S.op("dve", lambda: nc.vector.tensor_scalar(out=big[:, kc, :], in0=Tn[:, j * 512:(j + 1) * 512], scalar1=normw[:, kc:kc + 1],
                                                                scalar2=None, op0=ALU.mult),
                         reads=[pk(bn_), "prmB"], writes=[("big", kc)])

            for f in inproj_pieces(0):
                f()
            for g in range(NG):
                P = inproj_pieces(g + 1) if g + 1 < NG else []
                P = P + [lambda: None] * (6 - len(P))
                A_ = lambda c: (lambda: stage_A(g, c))
                B_ = lambda c: (lambda: stage_B(g, c))
                C_ = lambda c: (lambda: stage_C(g, c))
                order = [P[0], A_(0), P[1], A_(1), B_(0), P[2], A_(2), B_(1), C_(0), P[3], A_(3), B_(2), C_(1), P[4], B_(3), C_(2), P[5], C_(3),
                         lambda: group_end(g)]
                for f in order:
                    f()
            S.fence()
            for j in range(4):
                slot, skey = use_tile(("out", j))
                ov = slot[:, 0:16 * 256].rearrange("p (k c) -> p k c", c=256)
                for c in range(2):
                    k = 2 * j + c
                    b = nb()
                    for kk in range(16):
                        S.op("pe", lambda: nc.tensor.matmul(ps[b][:], lhsT=ov[:, kk, c * 128:(c + 1) * 128], rhs=big[:, kk, :],
                                                            start=(kk == 0), stop=(kk == 15)),
                             reads=[skey, ("big", kk)], writes=[pk(b)], inc=(kk == 15))
                    ln_accum(k, b, 0)
                done_tile()
            ln_finish(0)

        def attn_phase(sc, first_in_seq):
            t0 = sc * TS
            for j in range(2):
                slot, skey = use_tile(("kvk", j))
                for pr in range(4):
                    kv_ = slot[:, pr * 1024:(pr + 1) * 1024].rearrange("p (k c) -> p k c", c=128)
                    b = nb()
                    for k in range(KD):
                        S.op("pe", lambda: nc.tensor.matmul(ps[b][:], lhsT=kv_[:, k, :], rhs=xTb[:, k, :], start=(k == 0), stop=(k == KD - 1)),
                             reads=[skey, ("xTb", k)], writes=[pk(b)], inc=(k == KD - 1))
                    S.op("act", lambda: nc.scalar.copy(out=KT[:, 4 * j + pr, t0:t0 + TS], in_=ps[b][:]), reads=[pk(b)], writes=[("KT", 4 * j + pr)])
                done_tile()
            for j in range(2):
                slot, skey = use_tile(("kvv", j))
                vv = slot[:, 0:4096].rearrange("p (k c) -> p k c", c=512)
                for tt in range(4):
                    b = nb()
                    for k in range(KD):
                        S.op("pe", lambda: nc.tensor.matmul(ps[b][:], lhsT=xTb[:, k, tt * 128:(tt + 1) * 128], rhs=vv[:, k, :],
                                                            start=(k == 0), stop=(k == KD - 1)),
                             reads=[skey, ("xTb", k)], writes=[pk(b)], inc=(k == KD - 1))
                    kt = 4 * sc + tt
                    S.op("dve", lambda: nc.vector.tensor_copy(out=VA[:, kt, 8 * j:8 * j + 8, 0:64], in_=ps[b][:].rearrange("p (h q) -> p h q", q=64)),
                         reads=[pk(b)], writes=[("VA", kt)])
                done_tile()
            bf_ = nb()
            for k in range(KD):
                S.op("pe", lambda: nc.tensor.matmul(ps[bf_][0:16, :], lhsT=wf[:, k, :], rhs=xTb[:, k, :], start=(k == 0), stop=(k == KD - 1)),
                     reads=["wf", ("xTb", k)], writes=[pk(bf_)], inc=(k == KD - 1))
            S.op("dve", lambda: nc.vector.tensor_scalar(out=f_v[:], in0=ps[bf_][0:16, :], scalar1=bf_col[:, 0:1], scalar2=None, op0=ALU.add),
                 reads=[pk(bf_), "bf_col"], writes=["f_v"])
            S.op("act", lambda: nc.scalar.activation(out=f_a[:], in_=f_v[:], func=AF.Abs), reads=["f_v"], writes=["f_a"])
            S.op("act", lambda: nc.scalar.activation(out=f_a[:], in_=f_a[:], func=AF.Exp, scale=-1.0), reads=["f_a"], writes=["f_a"])
            S.op("act", lambda: nc.scalar.activation(out=f_l[:], in_=f_a[:], func=AF.Ln, bias=1.0), reads=["f_a"], writes=["f_l"])
            S.op("dve", lambda: nc.vector.scalar_tensor_tensor(out=f_l[:], in0=f_v[:], scalar=0.0, in1=f_l[:], op0=ALU.min, op1=ALU.subtract),
                 reads=["f_v", "f_l"], writes=["f_l"])
            if first_in_seq:
                S.op("pool", lambda: nc.gpsimd.memset(Fcarry[:], 0.0), writes=["Fcarry"])
            S.op("dve", lambda: nc.vector.tensor_tensor_scan(out=Frow[:], data0=bc(onesf[0:16, 0:1], [16, TS]), data1=f_l[:], initial=Fcarry[:, 0:1],
                                                             op0=ALU.mult, op1=ALU.add),
                 reads=["onesf", "f_l", "Fcarry"], writes=["Frow"])
            S.op("dve", lambda: nc.vector.tensor_copy(out=Fcarry[:], in_=Frow[:, TS - 1:TS]), reads=["Frow"], writes=["Fcarry"])
            bt_ = nb()
            for tt in range(4):
                S.op("pe", lambda: nc.tensor.transpose(out=ps[bt_][:, tt * 16:(tt + 1) * 16], in_=Frow[:, tt * 128:(tt + 1) * 128], identity=identf[0:16, 0:16]),
                     reads=["Frow", "identf"], writes=[pk(bt_)], inc=False)
            for s in range(2):
                S.op("dve", lambda: nc.vector.tensor_scalar(out=fdiag[:, s * 16:(s + 1) * 16], in0=identf[0:16, 0:16], scalar1=Frow[:, s * 256:s * 256 + 1],
                                                            scalar2=None, op0=ALU.mult),
                     reads=["identf", "Frow"], writes=["fdiag"])
            S.op("pe", lambda: nc.tensor.matmul(ps[bt_][:, 64:96], lhsT=onesf[0:16, :], rhs=fdiag[:], start=True, stop=True),
                 reads=["onesf", "fdiag"], writes=[pk(bt_)])
            S.op("dve", lambda: nc.vector.tensor_copy(out=Fcol[:, 4 * sc:4 * sc + 4, :], in_=ps[bt_][:, 0:64].rearrange("p (t h) -> p t h", h=16)),
                 reads=[pk(bt_)], writes=["Fcol"])
            S.op("dve", lambda: nc.vector.tensor_copy(out=Fq0[:], in_=ps[bt_][:, 64:96].rearrange("p (s h) -> p s h", h=16)),
                 reads=[pk(bt_)], writes=["Fq0"])
            nkt_all = 4 * sc + 4
            for s in range(2):
                S.op("dve", lambda: nc.vector.tensor_tensor(out=bcol[:, s * NKT:s * NKT + nkt_all, :], in0=bc(Fq0[:, s, :].unsqueeze(1), [128, nkt_all, 16]),
                                                            in1=Fcol[:, 0:nkt_all, :], op=ALU.subtract),
                     reads=["Fq0", "Fcol"], writes=["bcol"])
            for j in range(2):
                slot, skey = use_tile(("q", j))
                for pr in range(4):
                    qv = slot[:, pr * 1024:(pr + 1) * 1024].rearrange("p (k c) -> p k c", c=128)
                    b = nb()
                    for k in range(KD):
                        S.op("pe", lambda: nc.tensor.matmul(ps[b][:], lhsT=qv[:, k, :], rhs=xTb[:, k, :], start=(k == 0), stop=(k == KD - 1)),
                             reads=[skey, ("xTb", k)], writes=[pk(b)], inc=(k == KD - 1))
                    S.op("act", lambda: nc.scalar.activation(out=QT[:, 4 * j + pr, :], in_=ps[b][:], func=AF.Copy, scale=0.125),
                         reads=[pk(b)], writes=[("QT", 4 * j + pr)])
                done_tile()
            OT = big
            ring["n"] = 6
            ring["i"] = 0
            jobs = []
            for h in range(AH):
                for s_ in range(2):
                    nkt = 4 * sc + 2 * s_ + 2
                    for kt in range(nkt):
                        jobs.append((h, s_, kt, nkt))
            LA = 2
            NPT = 4
            pend = {}
            deferred = []

            def emit_st(i):
                h, s_, kt, nkt = jobs[i]
                pr, po = h // 2, (h % 2) * 64
                q0 = s_ * 256
                jd = kt - (4 * sc + 2 * s_)
                c0 = 128 if jd == 1 else 0
                n = 256 - c0
                b = nb()
                S.op("pe", lambda: nc.tensor.matmul(ps[b][:, 0:n], lhsT=KT[po:po + 64, pr, kt * 128:(kt + 1) * 128],
                                                    rhs=QT[po:po + 64, pr, q0 + c0:q0 + 256], start=True, stop=True),
                     reads=[("KT", pr), ("QT", pr)], writes=[pk(b)])
                pend[i] = b

            def emit_rest(i):
                h, s_, kt, nkt = jobs[i]
                ob = 6 + (h % 2)
                okey = pk(ob)
                jd = kt - (4 * sc + 2 * s_)
                c0 = 128 if jd == 1 else 0
                n = 256 - c0
                b = pend.pop(i)
                pt = PT[i % NPT]
                ptk = ("PT", i % NPT)
                oreg = ps[ob][0:65, s_ * 256:(s_ + 1) * 256]
                S.op("act", lambda: nc.scalar.activation(out=pt[:, c0:256], in_=ps[b][:, 0:n], func=AF.Exp, bias=bcol[:, s_ * NKT + kt, h:h + 1]),
                     reads=[pk(b), "bcol"], writes=[ptk])
                if jd >= 0:
                    S.op("pool", lambda: nc.gpsimd.affine_select(out=pt[:, c0:c0 + 128], in_=pt[:, c0:c0 + 128], pattern=[[1, 128]],
                                                                 compare_op=ALU.is_ge, fill=0.0, base=0, channel_multiplier=-1),
                         reads=[ptk], writes=[ptk])
                last = (s_ == 1 and kt == nkt - 1)
                S.op("pe", lambda: nc.tensor.matmul(oreg[:, c0:256], lhsT=VA[:, kt, h, :], rhs=pt[:, c0:256], start=(kt == 0), stop=(kt == nkt - 1)),
                     reads=[("VA", kt), ptk], writes=[okey], inc=last)
                if last:
                    rr_, Rs_ = rr[h % 2], Rs[h % 2]
                    S.op("dve", lambda: nc.vector.reciprocal(out=rr_[64:65, :], in_=ps[ob][64:65, :]), reads=[okey], writes=[("rr", h % 2)])

                    def fin(h=h, ob=ob, okey=okey, rr_=rr_, Rs_=Rs_):
                        b2 = nb()
                        S.op("pe", lambda: nc.tensor.matmul(ps[b2][0:64, :], lhsT=onesf[64:65, 0:64], rhs=rr_[64:65, :], start=True, stop=True),
                             reads=["onesf", ("rr", h % 2)], writes=[pk(b2)])
                        S.op("act", lambda: nc.scalar.copy(out=Rs_[:], in_=ps[b2][0:64, :]), reads=[pk(b2)], writes=[("Rs", h % 2)])
                        S.op("dve", lambda: nc.vector.tensor_tensor(out=OT[0:64, h, :], in0=ps[ob][0:64, :], in1=Rs_[:], op=ALU.mult),
                             reads=[okey, ("Rs", h % 2)], writes=[("big", h)])
                    deferred.append([2, fin])

            nj = len(jobs)
            for i in range(nj + LA):
                if i < nj:
                    emit_st(i)
                for dfr in list(deferred):
                    dfr[0] -= 1
                    if dfr[0] <= 0:
                        deferred.remove(dfr)
                        dfr[1]()
                if i >= LA:
                    emit_rest(i - LA)
            for dfr in deferred:
                dfr[1]()
            ring["n"] = 8
            for j in range(4):
                slot, skey = use_tile(("o", j))
                ov = slot[0:64, 0:16 * 256].rearrange("p (h c) -> p h c", c=256)
                for c in range(2):
                    k = 2 * j + c
                    b = nb()
                    for h in range(AH):
                        S.op("pe", lambda: nc.tensor.matmul(ps[b][:], lhsT=ov[:, h, c * 128:(c + 1) * 128], rhs=OT[0:64, h, :],
                                                            start=(h == 0), stop=(h == AH - 1)),
                             reads=[skey, ("big", h)], writes=[pk(b)], inc=(h == AH - 1))
                    ln_accum(k, b, 2)
                done_tile()
            ln_finish(2)

        def xk(tt):
            nm = "lnb" if tt < 2 else "lnsq"
            return [(nm, 4 * (tt % 2) + i) for i in range(4)]
        XK = xk(0) + xk(1) + xk(2) + xk(3)
        S.fence()
        gi = 0
        for bseq in range(NB if stop_after != "setup" else 0):
            for sc in range(NSC):
                tap.idx = gi
                t0 = sc * TS
                first = (sc == 0)
                S.dma("sp", xin, dr["x"][bseq, t0:t0 + TS, :].rearrange("(t p) d -> p t d", p=128), [], XK, iodom_in)
                for k in range(KD if stop_after != "xdma" else 0):
                    b = nb()
                    for tt in range(4):
                        S.op("pe", lambda: nc.tensor.transpose(out=ps[b][:, tt * 128:(tt + 1) * 128], in_=xin[:, tt, k * 128:(k + 1) * 128], identity=identf[:]),
                             reads=xk(tt) + ["identf"], writes=[pk(b)], inc=(tt == 3))
                    S.op("act", lambda: nc.scalar.copy(out=xT[:, k, :], in_=ps[b][:]), reads=[pk(b)], writes=[("xT", k)])
                    S.op("dve", lambda: nc.vector.tensor_copy(out=xTb[:, k, :], in_=ps[b][:]), reads=[pk(b)], writes=[("xTb", k)])
                S.fence()
                if stop_after not in ("xload", "xdma"):
                    ssd_phase(first)
                    tap("dbg_x1", xT[:, 0, :], ("xT", 0), [128, TS])
                if stop_after not in ("xload", "ssd", "xdma"):
                    ffn_phase(0, 1)
                    tap("dbg_x2", xT[:, 0, :], ("xT", 0), [128, TS])
                if stop_after not in ("xload", "ssd", "ffn0", "xdma"):
                    S.fence()
                    attn_phase(sc, first)
                    tap("dbg_x3", xT[:, 0, :], ("xT", 0), [128, TS])
                    ffn_phase(1, 3)
                for tt in range(4 if stop_after != "xdma" else 0):
                    for hf in range(2):
                        b = nb()
                        for kq in range(4):
                            k = hf * 4 + kq
                            S.op("pe", lambda: nc.tensor.transpose(out=ps[b][:, kq * 128:(kq + 1) * 128], in_=xT[:, k, tt * 128:(tt + 1) * 128], identity=identf[:]),
                                 reads=[("xT", k), "identf"], writes=[pk(b)], inc=(kq == 3))
                        if hf == 0:
                            S.op("act", lambda: nc.scalar.copy(out=xin[:, tt, hf * 512:(hf + 1) * 512], in_=ps[b][:]), reads=[pk(b)], writes=xk(tt))
                        else:
                            S.op("dve", lambda: nc.vector.tensor_copy(out=xin[:, tt, hf * 512:(hf + 1) * 512], in_=ps[b][:]), reads=[pk(b)], writes=xk(tt))
                S.dma("sp", out_d[bseq, t0:t0 + TS, :].rearrange("(t p) d -> p t d", p=128), xin, XK, [], iodom_out)
                gi += 1
        assert stop_after is not None or wstate["next_use"] == total_tiles, (wstate, total_tiles)
        S.wait_all("sp", [iodom_out, dbgdom])
        build.stats = dict(nins=dict(S.nins), ndma=S.ndma, counts={k: v.count for k, v in S.dom.items()})
    return nc, list(dbg.keys())


_CACHE = {}


def kernel(**inputs):
    n_cores = 8
    x = np.ascontiguousarray(inputs["x"], dtype=np.float32)
    B, SEQ, _ = x.shape
    NB = B // n_cores
    key = (NB, SEQ)
    if key not in _CACHE:
        _CACHE[key] = build(NB, SEQ)[0]
    nc = _CACHE[key]
    in_maps = []
    for c in range(n_cores):
        m = {k: np.ascontiguousarray(v, dtype=np.float32) for k, v in inputs.items() if k != "x"}
        m["x"] = x[c * NB:(c + 1) * NB]
        in_maps.append(m)
    res = run_bass_kernel_spmd(nc, in_maps, core_ids=list(range(n_cores)))
    return np.concatenate([r["out"] for r in res.results], axis=0)
```

```python
import numpy as np
from contextlib import ExitStack
from collections import defaultdict

import concourse.bass as bass
import concourse.mybir as mybir
from concourse.bass_utils import run_bass_kernel_spmd

F32 = mybir.dt.float32
BF16 = mybir.dt.bfloat16
AF = mybir.ActivationFunctionType
ALU = mybir.AluOpType

D = 1024
KD = 8
DI = 2048
NG = 8
NHEAD = 32
DFF = 2816
NF = 22
AH = 16
TS = 512
DEPTH = 2
ALPHA = (2.0 * DEPTH) ** 0.25
LN_EPS = 1e-5
RMS_EPS = 1e-5
SLOT = 4096
NSLOT = 3
NEG = -30000.0


class Dom:
    def __init__(self, nc, es, name, step, epoch):
        self.nc, self.es, self.name, self.step, self.epoch = nc, es, name, step, epoch
        self.sems = []
        self.count = 0

    def sem_for(self, cnt):
        e = (cnt - 1) // self.epoch
        while len(self.sems) <= e:
            self.sems.append(self.es.enter_context(self.nc.semaphore(f"s_{self.name}_{len(self.sems)}")))
        return self.sems[e], ((cnt - 1) % self.epoch + 1) * self.step


class Sched:
    def __init__(self, nc, es):
        self.nc, self.es = nc, es
        self.eng = {"pe": nc.tensor, "act": nc.scalar, "dve": nc.vector, "pool": nc.gpsimd, "sp": nc.sync}
        self.dom = {e: Dom(nc, es, e, 1, 4096) for e in ("pe", "act", "dve", "pool")}
        self.seen = defaultdict(int)
        self.lastw = {}
        self.readers = defaultdict(dict)
        self.ndma = 0
        self.nins = defaultdict(int)

    def new_dma_dom(self, name):
        return Dom(self.nc, self.es, name, 16, 1024)

    def _deps(self, own, reads, writes):
        deps = {}

        def need(dc, same_ok):
            dom, cnt = dc
            if dom is own and same_ok:
                return
            if deps.get(dom, 0) < cnt:
                deps[dom] = cnt

        for k in reads:
            if k in self.lastw:
                need(self.lastw[k], False)
            if isinstance(k, tuple) and k[0] == "ps":
                for dom, cnt in self.readers[k].items():
                    need((dom, cnt), True)
        for k in writes:
            if k in self.lastw:
                need(self.lastw[k], True)
            for dom, cnt in self.readers[k].items():
                need((dom, cnt), True)
        return deps

    def _wait(self, e, deps, own=None):
        for dom, cnt in deps.items():
            if self.seen[(e, dom.name)] >= cnt:
                continue
            if dom is own:
                assert cnt <= own.count, "same-engine wait on a future completion"
            sem, val = dom.sem_for(cnt)
            self.eng[e].wait_ge(sem, val)
            self.nins[e] += 1
            self.seen[(e, dom.name)] = cnt

    def op(self, e, fn, reads=(), writes=(), inc=True):
        own = self.dom[e]
        self._wait(e, self._deps(own, reads, writes), own)
        ins = fn()
        self.nins[e] += 1
        tag = own.count + 1
        if inc:
            own.count += 1
            sem, _ = own.sem_for(own.count)
            ins.then_inc(sem, 1)
        for k in reads:
            if self.readers[k].get(own, 0) < tag:
                self.readers[k][own] = tag
        for k in writes:
            self.lastw[k] = (own, tag)
            self.readers[k] = {}
        return ins

    def dma(self, q, out, in_, reads, writes, dom):
        self._wait(q, self._deps(None, reads, writes))
        ins = self.eng[q].dma_start(out=out, in_=in_)
        self.nins[q] += 1
        self.ndma += 1
        dom.count += 1
        sem, _ = dom.sem_for(dom.count)
        ins.then_inc(sem, 16)
        for k in reads:
            self.readers[k][dom] = dom.count
        for k in writes:
            self.lastw[k] = (dom, dom.count)
            self.readers[k] = {}
        return ins

    def fence(self):
        es_ = ("pe", "act", "dve", "pool")
        for e in es_:
            self._wait(e, {self.dom[f]: self.dom[f].count for f in es_ if f != e and self.dom[f].count > 0})

    def wait_all(self, e, doms):
        for dom in doms:
            if dom.count > 0:
                self._wait(e, {dom: dom.count})


def bc(ap, shape):
    return ap.to_broadcast(list(shape))


def weight_tiles():
    tiles = []
    for g in range(NG):
        tiles.append((("inA", g), 8 * 512, [("ssm_in_w", 0, g * 256, 256, "kpc", 0, 512, 0),
                                             ("ssm_in_w", 0, 2048 + g * 256, 256, "kpc", 0, 512, 256)]))
        tiles.append((("inB", g), 8 * 256, [("ssm_in_w", 0, 4096 + g * 128, 128, "kpc", 0, 256, 0),
                                             ("ssm_in_w", 0, 5120 + g * 128, 128, "kpc", 0, 256, 128)]))
    for j in range(4):
        tiles.append((("out", j), 16 * 256, [("ssm_out_w", 0, j * 256, 256, "kpc", 0, 256, 0)]))

    def ffn(l):
        for j in range(11):
            parts = []
            for fc in range(2):
                f = 2 * j + fc
                parts.append(("ffn_gate_w", l, f * 128, 128, "kpc", fc * 1024, 128, 0))
                parts.append(("ffn_up_w", l, f * 128, 128, "kpc", 2048 + fc * 1024, 128, 0))
            tiles.append((("gu", l, j), 4096, parts))
        for j in range(8):
            tiles.append((("dn", l, j), NF * 128, [("ffn_down_w", l, j * 128, 128, "kpc", 0, 128, 0)]))

    ffn(0)
    for j in range(2):
        parts = [("kv_w", None, j * 512 + pr * 128, 128, "kpc", pr * 1024, 128, 0) for pr in range(4)]
        tiles.append((("kvk", j), 4096, parts))
    for j in range(2):
        tiles.append((("kvv", j), 4096, [("kv_w", None, 1024 + j * 512, 512, "kpc", 0, 512, 0)]))
    for j in range(2):
        parts = [("att_q_w", 0, j * 512 + pr * 128, 128, "kpc", pr * 1024, 128, 0) for pr in range(4)]
        tiles.append((("q", j), 4096, parts))
    for j in range(4):
        tiles.append((("o", j), 16 * 256, [("att_o_w", 0, j * 256, 256, "hpc", 0, 256, 0)]))
    ffn(1)
    return tiles


IN_SPECS = [
    ("x", None), ("ssm_in_w", [1, 1024, 6176]), ("ssm_conv_w", [1, 4, 4096]), ("ssm_conv_b", [1, 4096]),
    ("ssm_dt_bias", [1, 32]), ("ssm_a_log", [1, 32]), ("ssm_d", [1, 32]), ("ssm_norm_w", [1, 2048]),
    ("ssm_out_w", [1, 2048, 1024]), ("kv_w", [1024, 2064]), ("kv_b_f", [16]), ("att_q_w", [1, 1024, 1024]),
    ("att_o_w", [1, 1024, 1024]), ("ffn_gate_w", [2, 1024, 2816]), ("ffn_up_w", [2, 1024, 2816]),
    ("ffn_down_w", [2, 2816, 1024]), ("ln_mix_g", [2, 1024]), ("ln_mix_b", [2, 1024]), ("ln_ffn_g", [2, 1024]),
    ("ln_ffn_b", [2, 1024]),
]


def build(NB=4, SEQ=2048, debug=False, stop_after=None):
    nc = bass.Bass("TRN2", target_bir_lowering=False)
    NSC = SEQ // TS
    NKT = SEQ // 128
    dr = {}
    for name, shp in IN_SPECS:
        if name == "x":
            shp = [NB, SEQ, D]
        dr[name] = nc.dram_tensor(name, shp, F32, kind="ExternalInput").ap()
    out_d = nc.dram_tensor("out", [NB, SEQ, D], F32, kind="ExternalOutput").ap()
    tiles = weight_tiles()
    NT = len(tiles)
    wscr = nc.dram_tensor("wscr", [NT, 128, SLOT], BF16, kind="Internal").ap()
    dbg = {}

    with ExitStack() as es:
        ec = es.enter_context
        S = Sched(nc, es)

        def sb(name, shape, dt=F32):
            return ec(nc.sbuf_tensor(name, list(shape), dt))

        identf = sb("identf", [128, 128]); identb = sb("identb", [128, 128], BF16)
        onesf = sb("onesf", [128, 128]); trif = sb("trif", [128, 128])
        lnones = sb("lnones", [128, 128], BF16)
        negmask = sb("negmask", [128, 512], BF16)
        prmA = sb("prmA", [128, 128]); prmB = sb("prmB", [128, 128])
        dtb_bc = sb("dtb_bc", [128, 32]); A_bc = sb("A_bc", [128, 32]); D_bc = sb("D_bc", [128, 32])
        bf_col = sb("bf_col", [16, 1])
        wdt = sb("wdt", [128, 8, 32], BF16); wf = sb("wf", [128, 8, 16], BF16)
        wslot = [sb(f"wslot{i}", [128, SLOT], BF16) for i in range(NSLOT)]
        scr16 = sb("scr16", [128, 4096])
        xin = scr16[:].rearrange("p (t d) -> p t d", d=D)
        lnb = scr16[:, 0:2048].bitcast(BF16).rearrange("p (k t) -> p k t", t=TS)
        lnsq = scr16[:, 2048:4096].bitcast(BF16).rearrange("p (k t) -> p k t", t=TS)
        xT = sb("xT", [128, KD, TS]); xTb = sb("xTb", [128, KD, TS], BF16)
        mean_sb = sb("mean_sb", [128, TS]); rstd_sb = sb("rstd_sb", [128, TS])
        big = sb("big", [128, NF, TS], BF16)
        acc = [sb(f"acc{i}", [128, TS]) for i in range(2)]
        lnt = acc
        stateT = sb("stateT", [128, NHEAD * 64]); stbf = sb("stbf", [128, NHEAD * 64], BF16)
        halo = sb("halo", [128, 32, 3])
        KT = sb("KT", [128, 8, SEQ], BF16)
        VA = sb("VA", [128, NKT, AH, 65], BF16)
        Fcarry = sb("Fcarry", [16, 1]); Fcol = sb("Fcol", [128, NKT, AH])
        lneps = sb("lneps", [128, 2])
        ARENA = 7360
        arena = sb("arena", [128, ARENA])
        ar = {"o": 0}

        def carve(shape, dt=F32):
            n = int(np.prod(shape[1:]))
            w = n if dt == F32 else (n + 1) // 2
            o = ar["o"]
            assert o + w <= ARENA, ("arena overflow", o, w)
            ar["o"] = o + w
            v = arena[0:shape[0], o:o + w]
            if dt != F32:
                v = v.bitcast(dt)
            if len(shape) == 3:
                v = v.rearrange("p (a b) -> p a b", b=shape[2])
            return v

        stg1 = carve([128, 128]); stg2 = carve([128, 128])
        ar["o"] = 0
        dt_sb = carve([128, 4, 32]); a_sb = carve([128, 4, 32]); cumcol = carve([128, 4, 32]); expcum = carve([128, 4, 32])
        sp_t = [carve([128, 4, 32]) for _ in range(3)]
        cumcolp = carve([128, 4, 32])
        ubuf = carve([128, 2, TS + 4])
        e4 = [carve([128, 4]) for _ in range(2)]
        ys = carve([128, 256]); ysum = carve([128, 256])
        yg = carve([128, 4, 256]); ss = carve([128, 4]); sd4 = carve([128, 4]); rstd4 = carve([128, 4])
        junk = carve([128, 256]); ygn = carve([128, 256], BF16); sttmp = carve([128, 256])
        zs = [carve([128, 4, 256], BF16), None]
        xbc = [carve([128, 4, TS], BF16), None]

        def chunk_set():
            return dict(xtok=carve([128, 256], BF16), xD=carve([128, 256], BF16), btok=carve([128, 128], BF16),
                        atri=carve([128, 512]), decayT=carve([128, 512], BF16), GT=carve([128, 512], BF16), xw=carve([128, 256], BF16))
        cset = [chunk_set(), None]
        ssd_top = ar["o"]
        sav = (arena, ar["o"])
        arena_main = arena

        def carve16(shape, dt=F32):
            n = int(np.prod(shape[1:]))
            w = n if dt == F32 else (n + 1) // 2
            o = c16["o"]
            assert o + w <= 4096, ("scr16 overflow", o, w)
            c16["o"] = o + w
            v = scr16[0:shape[0], o:o + w]
            if dt != F32:
                v = v.bitcast(dt)
            if len(shape) == 3:
                v = v.rearrange("p (a b) -> p a b", b=shape[2])
            return v
        c16 = {"o": 0}
        zs[1] = carve16([128, 4, 256], BF16)
        xbc[1] = carve16([128, 4, TS], BF16)
        cset[1] = dict(xtok=carve16([128, 256], BF16), xD=carve16([128, 256], BF16), btok=carve16([128, 128], BF16),
                       atri=carve16([128, 512]), decayT=carve16([128, 512], BF16), GT=carve16([128, 512], BF16), xw=carve16([128, 256], BF16))
        ar["o"] = 0
        QT = carve([128, 8, TS], BF16)
        f_v = carve([16, TS]); f_a = carve([16, TS]); f_l = carve([16, TS]); Frow = carve([16, TS])
        fdiag = carve([16, 32]); Fq0 = carve([128, 2, AH]); bcol = carve([128, 2 * NKT, AH])
        PT = [carve([128, 256], BF16) for _ in range(4)]
        rr = [carve([128, 512]) for _ in range(2)]; Rs = [carve([64, 512]) for _ in range(2)]
        att_top = ar["o"]
        ps = [ec(nc.psum_tensor(f"ps{i}", [128, 512], F32)) for i in range(8)]

        def psb(i):
            return ps[i][:].bitcast(BF16)

        ring = {"i": 0, "n": 8}

        def nb():
            i = ring["i"] % ring["n"]
            ring["i"] = (i + 1) % ring["n"]
            return i

        def pk(i):
            return ("ps", i)

        P_ = "pool"
        neg_reg = nc.gpsimd.to_reg(NEG)
        zero_reg = nc.gpsimd.to_reg(0.0)
        S.op(P_, lambda: nc.gpsimd.memset(identf[:], 0.0), writes=["identf"])
        S.op(P_, lambda: nc.gpsimd.affine_select(out=identf[:], in_=identf[:], pattern=[[-1, 128]], compare_op=ALU.not_equal,
                                                 fill=1.0, base=0, channel_multiplier=1), reads=["identf"], writes=["identf"])
        S.op(P_, lambda: nc.gpsimd.tensor_copy(out=identb[:], in_=identf[:]), reads=["identf"], writes=["identb"])
        S.op(P_, lambda: nc.gpsimd.memset(onesf[:], 1.0), writes=["onesf"])
        S.op(P_, lambda: nc.gpsimd.memset(lnones[:], 1.0 / D), writes=["lnones"])
        S.op(P_, lambda: nc.gpsimd.memset(trif[:], 1.0), writes=["trif"])
        S.op(P_, lambda: nc.gpsimd.affine_select(out=trif[:], in_=trif[:], pattern=[[1, 128]], compare_op=ALU.is_ge,
                                                 fill=0.0, base=0, channel_multiplier=-1), reads=["trif"], writes=["trif"])
        S.op(P_, lambda: nc.gpsimd.memset(negmask[:], 0.0), writes=["negmask"])
        nm3 = negmask[:].rearrange("p (h l) -> p h l", h=4)
        S.op(P_, lambda: nc.gpsimd.affine_select(out=nm3, in_=nm3, pattern=[[0, 4], [1, 128]], compare_op=ALU.is_ge,
                                                 fill=NEG, base=0, channel_multiplier=-1), reads=["negmask"], writes=["negmask"])
        S.op(P_, lambda: nc.gpsimd.memset(stg2[:], 0.0), writes=["stg2"])

        cdom = S.new_dma_dom("cst")
        S.dma("sp", stg1[:], dr["ssm_conv_w"][0].rearrange("k (c p) -> (k c) p", p=128), [], ["stg1"], cdom)
        rows = [("ssm_conv_b", dr["ssm_conv_b"][0], 32, 0), ("ssm_norm_w", dr["ssm_norm_w"][0], 16, 32),
                ("ln_mix_g", dr["ln_mix_g"].rearrange("l d -> (l d)"), 16, 48), ("ln_mix_b", dr["ln_mix_b"].rearrange("l d -> (l d)"), 16, 64),
                ("ln_ffn_g", dr["ln_ffn_g"].rearrange("l d -> (l d)"), 16, 80), ("ln_ffn_b", dr["ln_ffn_b"].rearrange("l d -> (l d)"), 16, 96)]
        for (_, src, n, r0) in rows:
            S.dma("sp", stg2[r0:r0 + n, :], src.rearrange("(c p) -> c p", p=128), [], ["stg2"], cdom)
        S.dma("sp", dtb_bc[:], dr["ssm_dt_bias"].partition_broadcast(128), [], ["dtb_bc"], cdom)
        S.dma("sp", A_bc[:], dr["ssm_a_log"].partition_broadcast(128), [], ["A_bc"], cdom)
        S.dma("sp", D_bc[:], dr["ssm_d"].partition_broadcast(128), [], ["D_bc"], cdom)
        S.dma("sp", bf_col[:], dr["kv_b_f"].rearrange("(h o) -> h o", o=1), [], ["bf_col"], cdom)
        S.op("act", lambda: nc.scalar.activation(out=A_bc[:], in_=A_bc[:], func=AF.Exp), reads=["A_bc"], writes=["A_bc"])
        S.op("dve", lambda: nc.vector.tensor_scalar_mul(out=A_bc[:], in0=A_bc[:], scalar1=-1.0), reads=["A_bc"], writes=["A_bc"])
        b0 = nb()
        S.op("pe", lambda: nc.tensor.transpose(out=ps[b0][:, 0:128], in_=stg1[:], identity=identf[:]), reads=["stg1", "identf"], writes=[pk(b0)])
        S.op("dve", lambda: nc.vector.tensor_copy(out=prmA[:], in_=ps[b0][:, 0:128]), reads=[pk(b0)], writes=["prmA"])
        b1 = nb()
        S.op("pe", lambda: nc.tensor.transpose(out=ps[b1][:, 0:128], in_=stg2[:], identity=identf[:]), reads=["stg2", "identf"], writes=[pk(b1)])
        S.op("dve", lambda: nc.vector.tensor_copy(out=prmB[:], in_=ps[b1][:, 0:128]), reads=[pk(b1)], writes=["prmB"])
        S.op("dve", lambda: nc.vector.tensor_scalar_mul(out=prmA[:], in0=prmA[:], scalar1=0.5), reads=["prmA"], writes=["prmA"])
        S.op("dve", lambda: nc.vector.tensor_scalar_mul(out=prmB[:, 0:32], in0=prmB[:, 0:32], scalar1=0.5), reads=["prmB"], writes=["prmB"])
        cw = prmA[:].rearrange("p (k c) -> p k c", k=4)
        cb = prmB[:, 0:32]
        normw = prmB[:, 32:48]
        lng = {0: prmB[:, 48:56], 1: prmB[:, 80:88], 2: prmB[:, 56:64], 3: prmB[:, 88:96]}
        lnbias = {0: prmB[:, 64:72], 1: prmB[:, 96:104], 2: prmB[:, 72:80], 3: prmB[:, 104:112]}

        import os
        wcdom = S.new_dma_dom("wcv")
        stf = [scr16[:], xT[:].rearrange("p k t -> p (k t)")]
        stb = [big[:, 0:8, :].rearrange("p k t -> p (k t)"), big[:, 8:16, :].rearrange("p k t -> p (k t)")]
        ktf = KT[:].rearrange("p k t -> p (k t)").bitcast(F32)
        for i in range(ktf.shape[1] // 4096):
            stf.append(ktf[:, i * 4096:(i + 1) * 4096])
        vaf = VA[:].rearrange("p a b c -> p (a b c)")
        for i in range(vaf.shape[1] // 4096):
            stb.append(vaf[:, i * 4096:(i + 1) * 4096])
        NST = min(len(stf), len(stb), 4)
        cvl = [S.new_dma_dom(f"cvl{i}") for i in range(NST)]
        cvs = [S.new_dma_dom(f"cvs{i}") for i in range(NST)]
        cast_eng = ["dve", "act", "pool"]
        for ti, (name, nel, parts) in enumerate(tiles):
            if os.environ.get("SKIP_CONV"):
                break
            sl = ti % NST
            npart = 64 if name[0] == "o" else 128
            for (src, idx, c0, n, kind, base, cstride, coff) in parts:
                w = dr[src] if idx is None else dr[src][idx]
                if kind == "kpc":
                    nk = w.shape[0] // 128
                    s_ap = w[:, c0:c0 + n].rearrange("(k p) c -> p k c", p=128)
                    d_ap = stf[sl][:, base:base + nk * cstride].rearrange("p (k c) -> p k c", c=cstride)[:, :, coff:coff + n]
                else:
                    s_ap = w[:, c0:c0 + n].rearrange("(h p) c -> p h c", p=64)
                    d_ap = stf[sl][0:64, base:base + 16 * cstride].rearrange("p (h c) -> p h c", c=cstride)[:, :, coff:coff + n]
                S.dma("sp", d_ap, s_ap, [], [("stf", sl)], cvl[sl])
            ce = cast_eng[ti % 3]
            if ce == "dve":
                S.op("dve", lambda: nc.vector.tensor_copy(out=stb[sl][0:npart, 0:nel], in_=stf[sl][0:npart, 0:nel]), reads=[("stf", sl)], writes=[("stb", sl)])
            elif ce == "act":
                S.op("act", lambda: nc.scalar.copy(out=stb[sl][0:npart, 0:nel], in_=stf[sl][0:npart, 0:nel]), reads=[("stf", sl)], writes=[("stb", sl)])
            else:
                S.op("pool", lambda: nc.gpsimd.tensor_copy(out=stb[sl][0:npart, 0:nel], in_=stf[sl][0:npart, 0:nel]), reads=[("stf", sl)], writes=[("stb", sl)])
            S.dma("sp", wscr[ti][0:npart, 0:nel], stb[sl][0:npart, 0:nel], [("stb", sl)], [("wscr", ti)], cvs[sl])
        S.dma("pool", wdt[:], dr["ssm_in_w"][0][:, 6144:6176].rearrange("(k p) c -> p k c", p=128), [], ["wdt"], wcdom)
        S.dma("pool", wf[:], dr["kv_w"][:, 2048:2064].rearrange("(k p) c -> p k c", p=128), [], ["wf"], wcdom)
        S.wait_all("pool", cvs + cvl)
        S.op("pool", lambda: nc.gpsimd.memset(VA[:], 1.0), reads=[("stb", i) for i in range(NST)], writes=[("VA", kt) for kt in range(NKT)])
        S.wait_all("pe", cvs + cvl)
        S.wait_all("act", cvs + cvl)
        S.wait_all("dve", cvs + cvl)
        S.wait_all("pool", cvs + cvl)

        wdoms = [S.new_dma_dom(f"w{i}") for i in range(NSLOT)]
        wstate = {"next_load": 0, "next_use": 0}
        total_tiles = NB * NSC * NT

        def prefetch():
            i = wstate["next_load"]
            if i >= total_tiles:
                return
            wstate["next_load"] += 1
            ti = i % NT
            s = i % NSLOT
            nel = tiles[ti][1]
            npart = 64 if tiles[ti][0][0] == "o" else 128
            S.dma("sp", wslot[s][0:npart, 0:nel], wscr[ti][0:npart, 0:nel], [("wscr", ti)], [("wslot", s)], wdoms[s])

        def use_tile(expect):
            if stop_after is not None:
                while tiles[wstate["next_use"] % NT][0] != expect:
                    wstate["next_use"] += 1
                    prefetch()
            i = wstate["next_use"]
            wstate["next_use"] += 1
            ti = i % NT
            assert tiles[ti][0] == expect, (tiles[ti][0], expect)
            s = i % NSLOT
            return wslot[s], ("wslot", s)

        def done_tile():
            prefetch()

        for _ in range(NSLOT):
            prefetch()

        iodom_in = S.new_dma_dom("xin")
        iodom_out = S.new_dma_dom("xout")
        dbgdom = S.new_dma_dom("dbg")

        def tap(name, ap, key, shape):
            if not debug:
                return
            if name not in dbg:
                dbg[name] = nc.dram_tensor(name, [NB * NSC] + list(shape), ap.dtype, kind="ExternalOutput").ap()
            S.dma("sp", dbg[name][tap.idx], ap, [key], [], dbgdom)
        tap.idx = 0

        def ln_accum(k, bank, ln_idx):
            S.op("dve", lambda: nc.vector.scalar_tensor_tensor(out=xT[:, k, :], in0=xT[:, k, :], scalar=ALPHA, in1=ps[bank][:],
                                                               op0=ALU.mult, op1=ALU.add),
                 reads=[("xT", k), pk(bank)], writes=[("xT", k)])
            S.op("act", lambda: nc.scalar.copy(out=lnb[:, k, :], in_=xT[:, k, :]), reads=[("xT", k)], writes=[("lnb", k)])
            S.op("act", lambda: nc.scalar.activation(out=lnsq[:, k, :], in_=xT[:, k, :], func=AF.Square), reads=[("xT", k)], writes=[("lnsq", k)])

        def ln_finish(ln_idx):
            bm, be = nb(), nb()
            for k in range(KD):
                S.op("pe", lambda: nc.tensor.matmul(ps[bm][:], lhsT=lnones[:], rhs=lnb[:, k, :], start=(k == 0), stop=(k == KD - 1)),
                     reads=[("lnb", k), "lnones"], writes=[pk(bm)], inc=(k == KD - 1))
            for k in range(KD):
                S.op("pe", lambda: nc.tensor.matmul(ps[be][:], lhsT=lnones[:], rhs=lnsq[:, k, :], start=(k == 0), stop=(k == KD - 1)),
                     reads=[("lnsq", k), "lnones"], writes=[pk(be)], inc=(k == KD - 1))
            S.op("act", lambda: nc.scalar.copy(out=mean_sb[:], in_=ps[bm][:]), reads=[pk(bm)], writes=["mean_sb"])
            S.op("dve", lambda: nc.vector.tensor_tensor(out=rstd_sb[:], in0=ps[bm][:], in1=mean_sb[:], op=ALU.mult),
                 reads=[pk(bm), "mean_sb"], writes=["rstd_sb"])
            S.op("dve", lambda: nc.vector.tensor_tensor(out=rstd_sb[:], in0=ps[be][:], in1=rstd_sb[:], op=ALU.subtract),
                 reads=[pk(be), "rstd_sb"], writes=["rstd_sb"])
            S.op("act", lambda: nc.scalar.activation(out=rstd_sb[:], in_=rstd_sb[:], func=AF.Sqrt, bias=lneps[:, 0:1]),
                 reads=["rstd_sb", "lneps"], writes=["rstd_sb"])
            S.op("dve", lambda: nc.vector.reciprocal(out=rstd_sb[:], in_=rstd_sb[:]), reads=["rstd_sb"], writes=["rstd_sb"])
            for k in range(KD):
                t1, t2 = lnt[0], lnt[1]
                k1, k2 = ("acc", 0), ("acc", 1)
                S.op("dve", lambda: nc.vector.tensor_tensor(out=t1[:], in0=xT[:, k, :], in1=mean_sb[:], op=ALU.subtract),
                     reads=[("xT", k), "mean_sb"], writes=[k1])
                S.op("pool", lambda: nc.gpsimd.tensor_tensor(out=t2[:], in0=t1[:], in1=rstd_sb[:], op=ALU.mult),
                     reads=[k1, "rstd_sb"], writes=[k2])
                S.op("act", lambda: nc.scalar.activation(out=xT[:, k, :], in_=t2[:], func=AF.Identity, scale=lng[ln_idx][:, k:k + 1],
                                                         bias=lnbias[ln_idx][:, k:k + 1]),
                     reads=[k2, "prmB"], writes=[("xT", k)])
                S.op("act", lambda: nc.scalar.activation(out=xTb[:, k, :], in_=t2[:], func=AF.Identity, scale=lng[ln_idx][:, k:k + 1],
                                                         bias=lnbias[ln_idx][:, k:k + 1]),
                     reads=[k2, "prmB"], writes=[("xTb", k)])

        S.op("pool", lambda: nc.gpsimd.memset(lneps[:, 0:1], LN_EPS), writes=["lneps"])
        S.op("pool", lambda: nc.gpsimd.memset(lneps[:, 1:2], 4.0 * RMS_EPS), writes=["lneps"])

        def ffn_phase(l, ln_idx):
            for j in range(11):
                slot, skey = use_tile(("gu", l, j))
                for fc in range(2):
                    f = 2 * j + fc
                    bg, bu = nb(), nb()
                    gv = slot[:, fc * 1024:(fc + 1) * 1024].rearrange("p (k c) -> p k c", c=128)
                    uv = slot[:, 2048 + fc * 1024:2048 + (fc + 1) * 1024].rearrange("p (k c) -> p k c", c=128)
                    for k in range(KD):
                        S.op("pe", lambda: nc.tensor.matmul(ps[bg][:], lhsT=gv[:, k, :], rhs=xTb[:, k, :], start=(k == 0), stop=(k == KD - 1)),
                             reads=[skey, ("xTb", k)], writes=[pk(bg)], inc=(k == KD - 1))
                    for k in range(KD):
                        S.op("pe", lambda: nc.tensor.matmul(ps[bu][:], lhsT=uv[:, k, :], rhs=xTb[:, k, :], start=(k == 0), stop=(k == KD - 1)),
                             reads=[skey, ("xTb", k)], writes=[pk(bu)], inc=(k == KD - 1))
                    a_ = acc[f % 2]
                    ak = ("acc", f % 2)
                    S.op("act", lambda: nc.scalar.activation(out=a_[:], in_=ps[bg][:], func=AF.Silu), reads=[pk(bg)], writes=[ak])
                    S.op("dve", lambda: nc.vector.tensor_tensor(out=big[:, f, :], in0=a_[:], in1=ps[bu][:], op=ALU.mult),
                         reads=[ak, pk(bu)], writes=[("big", f)])
                done_tile()
            for k in range(KD):
                slot, skey = use_tile(("dn", l, k))
                dv = slot[:, 0:NF * 128].rearrange("p (f c) -> p f c", c=128)
                b = nb()
                for f in range(NF):
                    S.op("pe", lambda: nc.tensor.matmul(ps[b][:], lhsT=dv[:, f, :], rhs=big[:, f, :], start=(f == 0), stop=(f == NF - 1)),
                         reads=[skey, ("big", f)], writes=[pk(b)], inc=(f == NF - 1))
                ln_accum(k, b, ln_idx)
                done_tile()
            ln_finish(ln_idx)

        def ssd_phase(first_in_seq):
            bd = nb()
            for tt in range(4):
                for k in range(KD):
                    S.op("pe", lambda: nc.tensor.matmul(ps[bd][:, tt * 32:(tt + 1) * 32], lhsT=xTb[:, k, tt * 128:(tt + 1) * 128], rhs=wdt[:, k, :],
                                                        start=(k == 0), stop=(k == KD - 1)),
                         reads=[("xTb", k), "wdt"], writes=[pk(bd)], inc=(tt == 3 and k == KD - 1))
            pd = ps[bd][:, 0:128].rearrange("p (t h) -> p t h", h=32)
            v_, av_, l_ = sp_t
            S.op("dve", lambda: nc.vector.tensor_tensor(out=v_[:], in0=pd, in1=bc(dtb_bc[:].unsqueeze(1), [128, 4, 32]), op=ALU.add),
                 reads=[pk(bd), "dtb_bc"], writes=["sp_v"])
            S.op("act", lambda: nc.scalar.activation(out=av_[:], in_=v_[:], func=AF.Abs), reads=["sp_v"], writes=["sp_a"])
            S.op("act", lambda: nc.scalar.activation(out=av_[:], in_=av_[:], func=AF.Exp, scale=-1.0), reads=["sp_a"], writes=["sp_a"])
            S.op("act", lambda: nc.scalar.activation(out=l_[:], in_=av_[:], func=AF.Ln, bias=1.0), reads=["sp_a"], writes=["sp_l"])
            S.op("dve", lambda: nc.vector.scalar_tensor_tensor(out=dt_sb[:], in0=v_[:], scalar=0.0, in1=l_[:], op0=ALU.max, op1=ALU.add),
                 reads=["sp_v", "sp_l"], writes=["dt_sb"])
            S.op("act", lambda: nc.scalar.activation(out=l_[:], in_=dt_sb[:], func=AF.Ln), reads=["dt_sb"], writes=["sp_l"])
            S.op("dve", lambda: nc.vector.tensor_tensor(out=a_sb[:], in0=dt_sb[:], in1=bc(A_bc[:].unsqueeze(1), [128, 4, 32]), op=ALU.mult),
                 reads=["dt_sb", "A_bc"], writes=["a_sb"])
            bcu = nb()
            for c in range(4):
                S.op("pe", lambda: nc.tensor.matmul(ps[bcu][:, c * 32:(c + 1) * 32], lhsT=trif[:], rhs=a_sb[:, c, :], start=True, stop=True),
                     reads=["trif", "a_sb"], writes=[pk(bcu)], inc=(c == 3))
            pc = ps[bcu][:, 0:128].rearrange("p (t h) -> p t h", h=32)
            S.op("dve", lambda: nc.vector.tensor_tensor(out=cumcolp[:], in0=pc, in1=l_[:], op=ALU.subtract), reads=[pk(bcu), "sp_l"], writes=["cumcolp"])
            S.op("act", lambda: nc.scalar.activation(out=expcum[:], in_=pc, func=AF.Exp), reads=[pk(bcu)], writes=["expcum"])
            if first_in_seq:
                S.op("pool", lambda: nc.gpsimd.memset(stateT[:], 0.0), writes=[("stateT", g) for g in range(NG)])
                S.op("pool", lambda: nc.gpsimd.memset(stbf[:], 0.0), writes=[("stbf", g) for g in range(NG)])
                S.op("pool", lambda: nc.gpsimd.memset(halo[:], 0.0), writes=[("halo", ci) for ci in range(32)])

            def inproj_pieces(g):
                gb = g % 2
                zs_, xbc_ = zs[gb], xbc[gb]
                st = {}

                def p_open_a():
                    st["slot"], st["skey"] = use_tile(("inA", g))
                    st["Wv"] = st["slot"][:, 0:8 * 512].rearrange("p (k c) -> p k c", c=512)

                def p_z(half):
                    def f():
                        if half == 0:
                            p_open_a()
                        Wv, skey = st["Wv"], st["skey"]
                        bz = nb()
                        for t2 in range(2):
                            tt = 2 * half + t2
                            for k in range(KD):
                                S.op("pe", lambda: nc.tensor.matmul(ps[bz][:, t2 * 256:(t2 + 1) * 256], lhsT=xTb[:, k, tt * 128:(tt + 1) * 128],
                                                                    rhs=Wv[:, k, 0:256], start=(k == 0), stop=(k == KD - 1)),
                                     reads=[skey, ("xTb", k)], writes=[pk(bz)], inc=(t2 == 1 and k == KD - 1))
                        S.op("act", lambda: nc.scalar.activation(out=acc[half][:], in_=ps[bz][:], func=AF.Tanh, scale=0.5), reads=[pk(bz)], writes=[("acc", half)])
                        S.op("dve", lambda: nc.vector.scalar_tensor_tensor(out=zs_[:, 2 * half:2 * half + 2, :].rearrange("p t c -> p (t c)"), in0=acc[half][:], scalar=1.0,
                                                                           in1=ps[bz][:], op0=ALU.add, op1=ALU.mult),
                             reads=[("acc", half), pk(bz)], writes=[("zs", gb)])
                    return f

                def p_x(r):
                    def f():
                        ci = (2 * g + r) if r < 2 else (16 + g if r == 2 else 24 + g)
                        if r == 2:
                            done_tile()
                            st["slot"], st["skey"] = use_tile(("inB", g))
                            st["Wv"] = st["slot"][:, 0:8 * 256].rearrange("p (k c) -> p k c", c=256)
                        Wv, skey = st["Wv"], st["skey"]
                        wcol = (256 + r * 128) if r < 2 else (r - 2) * 128
                        bx = nb()
                        for k in range(KD):
                            S.op("pe", lambda: nc.tensor.matmul(ps[bx][:], lhsT=Wv[:, k, wcol:wcol + 128], rhs=xTb[:, k, :],
                                                                start=(k == 0), stop=(k == KD - 1)),
                                 reads=[skey, ("xTb", k)], writes=[pk(bx)], inc=(k == KD - 1))
                        ur = r % 2
                        uk = ("ubuf", ur)
                        S.op("pool", lambda: nc.gpsimd.tensor_copy(out=ubuf[:, ur, 0:3], in_=halo[:, ci, :]), reads=[("halo", ci)], writes=[uk])
                        S.op("act", lambda: nc.scalar.copy(out=ubuf[:, ur, 3:TS + 3], in_=ps[bx][:]), reads=[pk(bx)], writes=[uk])
                        a_ = acc[r % 2]
                        ak = ("acc", r % 2)
                        S.op("act", lambda: nc.scalar.activation(out=a_[:], in_=ps[bx][:], func=AF.Identity, scale=cw[:, 3, ci:ci + 1], bias=cb[:, ci:ci + 1]),
                             reads=[pk(bx), "prmA", "prmB"], writes=[ak])
                        for kk in range(3):
                            S.op("dve", lambda: nc.vector.scalar_tensor_tensor(out=a_[:], in0=ubuf[:, ur, kk:kk + TS], scalar=cw[:, kk, ci:ci + 1], in1=a_[:],
                                                                               op0=ALU.mult, op1=ALU.add), reads=[uk, ak, "prmA"], writes=[ak])
                        S.op("pool", lambda: nc.gpsimd.tensor_copy(out=halo[:, ci, :], in_=ubuf[:, ur, TS:TS + 3]), reads=[uk], writes=[("halo", ci)])
                        S.op("act", lambda: nc.scalar.activation(out=ubuf[:, ur, 0:TS], in_=a_[:], func=AF.Tanh), reads=[ak], writes=[uk])
                        S.op("dve", lambda: nc.vector.scalar_tensor_tensor(out=xbc_[:, r, :], in0=ubuf[:, ur, 0:TS], scalar=1.0, in1=a_[:],
                                                                           op0=ALU.add, op1=ALU.mult), reads=[uk, ak], writes=[("xbc", gb, r)])
                        if r == 3:
                            done_tile()
                    return f
                return [p_z(0), p_z(1), p_x(0), p_x(1), p_x(2), p_x(3)]

            cst = {}

            def stage_A(g, c):
                gb = g % 2
                xbc_ = xbc[gb]
                cs_ = cset[c % 2]
                ck = ("cs", c % 2)
                hs = slice(4 * g, 4 * g + 4)
                cs = slice(c * 128, (c + 1) * 128)
                bt = nb()
                T1 = psb(bt)
                for j, r in enumerate((0, 1, 2)):
                    S.op("pe", lambda: nc.tensor.transpose(out=T1[:, j * 128:(j + 1) * 128], in_=xbc_[:, r, cs], identity=identb[:]),
                         reads=[("xbc", gb, r), "identb"], writes=[pk(bt)], inc=(j == 2))
                T1x = T1[:, 0:256].rearrange("p (h q) -> p h q", q=64)
                S.op("act", lambda: nc.scalar.copy(out=cs_["xtok"][:], in_=T1[:, 0:256]), reads=[pk(bt)], writes=[(ck, "xtok")])
                S.op("dve", lambda: nc.vector.tensor_tensor(out=cs_["xD"][:].rearrange("p (h q) -> p h q", q=64), in0=T1x,
                                                            in1=bc(D_bc[:, hs].unsqueeze(2), [128, 4, 64]), op=ALU.mult),
                     reads=[pk(bt), "D_bc"], writes=[(ck, "xD")])
                S.op("act", lambda: nc.scalar.copy(out=cs_["btok"][:], in_=T1[:, 256:384]), reads=[pk(bt)], writes=[(ck, "btok")])
                S.op("pool", lambda: nc.gpsimd.tensor_tensor(out=cs_["atri"][:].rearrange("p (h l) -> p h l", l=128),
                                                             in0=bc(trif[:].unsqueeze(1), [128, 4, 128]),
                                                             in1=bc(a_sb[:, c, hs].unsqueeze(2), [128, 4, 128]), op=ALU.mult),
                     reads=["trif", "a_sb"], writes=[(ck, "atri")])
                b1_ = nb()
                S.op("pe", lambda: nc.tensor.matmul(ps[b1_][:], lhsT=onesf[:], rhs=cs_["atri"][:], start=True, stop=True),
                     reads=["onesf", (ck, "atri")], writes=[pk(b1_)])
                b2_ = nb()
                S.op("pe", lambda: nc.tensor.matmul(ps[b2_][:, 0:128], lhsT=xbc_[:, 2, cs], rhs=xbc_[:, 3, cs], start=True, stop=True),
                     reads=[("xbc", gb, 2), ("xbc", gb, 3)], writes=[pk(b2_)])
                cst[(g, c)] = dict(b1=b1_, b2=b2_)

            def stage_B(g, c):
                cs_ = cset[c % 2]
                ck = ("cs", c % 2)
                hs = slice(4 * g, 4 * g + 4)
                b1_, b2_ = cst[(g, c)]["b1"], cst[(g, c)]["b2"]
                X1 = ps[b1_][:].rearrange("p (h l) -> p h l", l=128)
                seg = cs_["atri"]
                S.op("dve", lambda: nc.vector.tensor_tensor(out=seg[:].rearrange("p (h l) -> p h l", l=128), in0=X1,
                                                            in1=bc(cumcolp[:, c, hs].unsqueeze(2), [128, 4, 128]), op=ALU.subtract),
                     reads=[pk(b1_), "cumcolp"], writes=[(ck, "atri")])
                seg3 = seg[:].rearrange("p (h l) -> p h l", l=128)
                S.op("pool", lambda: nc.gpsimd.affine_select(out=seg3, in_=seg3, pattern=[[0, 4], [1, 128]], compare_op=ALU.is_ge,
                                                             fill=neg_reg, base=0, channel_multiplier=-1), reads=[(ck, "atri")], writes=[(ck, "atri")])
                S.op("act", lambda: nc.scalar.activation(out=e4[c % 2][:], in_=X1[:, :, 127], func=AF.Exp), reads=[pk(b1_)], writes=[("e4", c % 2)])
                S.op("act", lambda: nc.scalar.activation(out=cs_["decayT"][:], in_=seg[:], func=AF.Exp), reads=[(ck, "atri")], writes=[(ck, "decayT")])
                S.op("dve", lambda: nc.vector.tensor_tensor(out=cs_["GT"][:].rearrange("p (h l) -> p h l", l=128),
                                                            in0=cs_["decayT"][:].rearrange("p (h l) -> p h l", l=128),
                                                            in1=bc(ps[b2_][:, 0:128].unsqueeze(1), [128, 4, 128]), op=ALU.mult),
                     reads=[(ck, "decayT"), pk(b2_)], writes=[(ck, "GT")])
                dlast = cs_["decayT"][:].rearrange("p (h l) -> p h l", l=128)[:, :, 127:128]
                S.op("dve", lambda: nc.vector.tensor_tensor(out=cs_["xw"][:].rearrange("p (h q) -> p h q", q=64),
                                                            in0=cs_["xtok"][:].rearrange("p (h q) -> p h q", q=64),
                                                            in1=bc(dlast, [128, 4, 64]), op=ALU.mult),
                     reads=[(ck, "xtok"), (ck, "decayT")], writes=[(ck, "xw")])
                b3_ = nb()
                S.op("pe", lambda: nc.tensor.matmul(ps[b3_][:, 0:256], lhsT=identb[:], rhs=cs_["xD"][:], start=True, stop=False),
                     reads=["identb", (ck, "xD")], writes=[pk(b3_)], inc=False)
                for h in range(4):
                    S.op("pe", lambda: nc.tensor.matmul(ps[b3_][:, h * 64:(h + 1) * 64], lhsT=cs_["GT"][:, h * 128:(h + 1) * 128],
                                                        rhs=cs_["xtok"][:, h * 64:(h + 1) * 64], start=False, stop=True),
                         reads=[(ck, "GT"), (ck, "xtok")], writes=[pk(b3_)], inc=False)
                S.op("pe", lambda: nc.tensor.matmul(ps[b3_][:, 256:512], lhsT=cs_["btok"][:], rhs=cs_["xw"][:], start=True, stop=True),
                     reads=[(ck, "btok"), (ck, "xw")], writes=[pk(b3_)])
                cst[(g, c)]["b3"] = b3_

            def stage_C(g, c):
                gb = g % 2
                xbc_ = xbc[gb]
                hs = slice(4 * g, 4 * g + 4)
                cs = slice(c * 128, (c + 1) * 128)
                b3_ = cst[(g, c)]["b3"]
                st_g = stateT[:, g * 256:(g + 1) * 256]
                stb_g = stbf[:, g * 256:(g + 1) * 256]
                b4_ = nb()
                S.op("pe", lambda: nc.tensor.matmul(ps[b4_][:, 0:256], lhsT=xbc_[:, 3, cs], rhs=stb_g, start=True, stop=True),
                     reads=[("xbc", gb, 3), ("stbf", g)], writes=[pk(b4_)])
                S.op("dve", lambda: nc.vector.tensor_tensor(out=sttmp[:].rearrange("p (h q) -> p h q", q=64),
                                                            in0=st_g.rearrange("p (h q) -> p h q", q=64),
                                                            in1=bc(e4[c % 2][:].unsqueeze(2), [128, 4, 64]), op=ALU.mult),
                     reads=[("stateT", g), ("e4", c % 2)], writes=["sttmp"])
                S.op("dve", lambda: nc.vector.tensor_tensor(out=st_g, in0=sttmp[:], in1=ps[b3_][:, 256:512], op=ALU.add),
                     reads=["sttmp", pk(b3_)], writes=[("stateT", g)])
                S.op("dve", lambda: nc.vector.tensor_tensor(out=ys[:].rearrange("p (h q) -> p h q", q=64),
                                                            in0=ps[b4_][:, 0:256].rearrange("p (h q) -> p h q", q=64),
                                                            in1=bc(expcum[:, c, hs].unsqueeze(2), [128, 4, 64]), op=ALU.mult),
                     reads=[pk(b4_), "expcum"], writes=["ys"])
                S.op("act", lambda: nc.scalar.copy(out=stb_g, in_=st_g), reads=[("stateT", g)], writes=[("stbf", g)])
                S.op("dve", lambda: nc.vector.tensor_tensor(out=ysum[:], in0=ps[b3_][:, 0:256], in1=ys[:], op=ALU.add),
                     reads=[pk(b3_), "ys"], writes=["ysum"])
                S.op("pool", lambda: nc.gpsimd.tensor_tensor(out=yg[:, c, :], in0=ysum[:], in1=zs[gb][:, c, :], op=ALU.mult),
                     reads=["ysum", ("zs", gb)], writes=[("yg", c)])
                S.op("act", lambda: nc.scalar.activation(out=junk[:], in_=yg[:, c, :], func=AF.Square, accum_out=ss[:, c:c + 1]),
                     reads=[("yg", c)], writes=["junk", ("ss", c)])

            def group_end(g):
                S.op("act", lambda: nc.scalar.activation(out=sd4[:], in_=ss[:], func=AF.Sqrt, scale=1.0 / 256.0, bias=lneps[:, 1:2]),
                     reads=[("ss", c) for c in range(4)] + ["lneps"], writes=["sd4"])
                S.op("dve", lambda: nc.vector.reciprocal(out=rstd4[:], in_=sd4[:]), reads=["sd4"], writes=["rstd4"])
                bn_ = nb()
                Tn = psb(bn_)
                for c in range(4):
                    S.op("act", lambda: nc.scalar.activation(out=ygn[:], in_=yg[:, c, :], func=AF.Copy, scale=rstd4[:, c:c + 1]),
                         reads=[("yg", c), "rstd4"], writes=["ygn"])
                    for j in range(2):
                        S.op("pe", lambda: nc.tensor.transpose(out=Tn[:, (j * 4 + c) * 128:(j * 4 + c + 1) * 128], in_=ygn[:, j * 128:(j + 1) * 128],
                                                               identity=identb[:]),
                             reads=["ygn", "identb"], writes=[pk(bn_)], inc=(j == 1))
                for j in range(2):
                    kc = 2 * g + j
                    S.op("dve", lambda: nc.vector.tensor_scalar(out=big[:, kc, :], in0=Tn[:, j * 512:(j + 1) * 512], scalar1=normw[:, kc:kc + 1],
                                                                scalar2=None, op0=ALU.mult),
                         reads=[pk(bn_), "prmB"], writes=[("big", kc)])

            for f in inproj_pieces(0):
                f()
            for g in range(NG):
                P = inproj_pieces(g + 1) if g + 1 < NG else []
                P = P + [lambda: None] * (6 - len(P))
                A_ = lambda c: (lambda: stage_A(g, c))
                B_ = lambda c: (lambda: stage_B(g, c))
                C_ = lambda c: (lambda: stage_C(g, c))
                order = [P[0], A_(0), P[1], A_(1), B_(0), P[2], A_(2), B_(1), C_(0), P[3], A_(3), B_(2), C_(1), P[4], B_(3), C_(2), P[5], C_(3),
                         lambda: group_end(g)]
                for f in order:
                    f()
            S.fence()
            for j in range(4):
                slot, skey = use_tile(("out", j))
                ov = slot[:, 0:16 * 256].rearrange("p (k c) -> p k c", c=256)
                for c in range(2):
                    k = 2 * j + c
                    b = nb()
                    for kk in range(16):
                        S.op("pe", lambda: nc.tensor.matmul(ps[b][:], lhsT=ov[:, kk, c * 128:(c + 1) * 128], rhs=big[:, kk, :],
                                                            start=(kk == 0), stop=(kk == 15)),
                             reads=[skey, ("big", kk)], writes=[pk(b)], inc=(kk == 15))
                    ln_accum(k, b, 0)
                done_tile()
            ln_finish(0)

        def attn_phase(sc, first_in_seq):
            t0 = sc * TS
            for j in range(2):
                slot, skey = use_tile(("kvk", j))
                for pr in range(4):
                    kv_ = slot[:, pr * 1024:(pr + 1) * 1024].rearrange("p (k c) -> p k c", c=128)
                    b = nb()
                    for k in range(KD):
                        S.op("pe", lambda: nc.tensor.matmul(ps[b][:], lhsT=kv_[:, k, :], rhs=xTb[:, k, :], start=(k == 0), stop=(k == KD - 1)),
                             reads=[skey, ("xTb", k)], writes=[pk(b)], inc=(k == KD - 1))
                    S.op("act", lambda: nc.scalar.copy(out=KT[:, 4 * j + pr, t0:t0 + TS], in_=ps[b][:]), reads=[pk(b)], writes=[("KT", 4 * j + pr)])
                done_tile()
            for j in range(2):
                slot, skey = use_tile(("kvv", j))
                vv = slot[:, 0:4096].rearrange("p (k c) -> p k c", c=512)
                for tt in range(4):
                    b = nb()
                    for k in range(KD):
                        S.op("pe", lambda: nc.tensor.matmul(ps[b][:], lhsT=xTb[:, k, tt * 128:(tt + 1) * 128], rhs=vv[:, k, :],
                                                            start=(k == 0), stop=(k == KD - 1)),
                             reads=[skey, ("xTb", k)], writes=[pk(b)], inc=(k == KD - 1))
                    kt = 4 * sc + tt
                    S.op("dve", lambda: nc.vector.tensor_copy(out=VA[:, kt, 8 * j:8 * j + 8, 0:64], in_=ps[b][:].rearrange("p (h q) -> p h q", q=64)),
                         reads=[pk(b)], writes=[("VA", kt)])
                done_tile()
            bf_ = nb()
            for k in range(KD):
                S.op("pe", lambda: nc.tensor.matmul(ps[bf_][0:16, :], lhsT=wf[:, k, :], rhs=xTb[:, k, :], start=(k == 0), stop=(k == KD - 1)),
                     reads=["wf", ("xTb", k)], writes=[pk(bf_)], inc=(k == KD - 1))
            S.op("dve", lambda: nc.vector.tensor_scalar(out=f_v[:], in0=ps[bf_][0:16, :], scalar1=bf_col[:, 0:1], scalar2=None, op0=ALU.add),
                 reads=[pk(bf_), "bf_col"], writes=["f_v"])
            S.op("act", lambda: nc.scalar.activation(out=f_a[:], in_=f_v[:], func=AF.Abs), reads=["f_v"], writes=["f_a"])
            S.op("act", lambda: nc.scalar.activation(out=f_a[:], in_=f_a[:], func=AF.Exp, scale=-1.0), reads=["f_a"], writes=["f_a"])
            S.op("act", lambda: nc.scalar.activation(out=f_l[:], in_=f_a[:], func=AF.Ln, bias=1.0), reads=["f_a"], writes=["f_l"])
            S.op("dve", lambda: nc.vector.scalar_tensor_tensor(out=f_l[:], in0=f_v[:], scalar=0.0, in1=f_l[:], op0=ALU.min, op1=ALU.subtract),
                 reads=["f_v", "f_l"], writes=["f_l"])
            if first_in_seq:
                S.op("pool", lambda: nc.gpsimd.memset(Fcarry[:], 0.0), writes=["Fcarry"])
            S.op("dve", lambda: nc.vector.tensor_tensor_scan(out=Frow[:], data0=bc(onesf[0:16, 0:1], [16, TS]), data1=f_l[:], initial=Fcarry[:, 0:1],
                                                             op0=ALU.mult, op1=ALU.add),
                 reads=["onesf", "f_l", "Fcarry"], writes=["Frow"])
            S.op("dve", lambda: nc.vector.tensor_copy(out=Fcarry[:], in_=Frow[:, TS - 1:TS]), reads=["Frow"], writes=["Fcarry"])
            bt_ = nb()
            for tt in range(4):
                S.op("pe", lambda: nc.tensor.transpose(out=ps[bt_][:, tt * 16:(tt + 1) * 16], in_=Frow[:, tt * 128:(tt + 1) * 128], identity=identf[0:16, 0:16]),
                     reads=["Frow", "identf"], writes=[pk(bt_)], inc=False)
            for s in range(2):
                S.op("dve", lambda: nc.vector.tensor_scalar(out=fdiag[:, s * 16:(s + 1) * 16], in0=identf[0:16, 0:16], scalar1=Frow[:, s * 256:s * 256 + 1],
                                                            scalar2=None, op0=ALU.mult),
                     reads=["identf", "Frow"], writes=["fdiag"])
            S.op("pe", lambda: nc.tensor.matmul(ps[bt_][:, 64:96], lhsT=onesf[0:16, :], rhs=fdiag[:], start=True, stop=True),
                 reads=["onesf", "fdiag"], writes=[pk(bt_)])
            S.op("dve", lambda: nc.vector.tensor_copy(out=Fcol[:, 4 * sc:4 * sc + 4, :], in_=ps[bt_][:, 0:64].rearrange("p (t h) -> p t h", h=16)),
                 reads=[pk(bt_)], writes=["Fcol"])
            S.op("dve", lambda: nc.vector.tensor_copy(out=Fq0[:], in_=ps[bt_][:, 64:96].rearrange("p (s h) -> p s h", h=16)),
                 reads=[pk(bt_)], writes=["Fq0"])
            nkt_all = 4 * sc + 4
            for s in range(2):
                S.op("dve", lambda: nc.vector.tensor_tensor(out=bcol[:, s * NKT:s * NKT + nkt_all, :], in0=bc(Fq0[:, s, :].unsqueeze(1), [128, nkt_all, 16]),
                                                            in1=Fcol[:, 0:nkt_all, :], op=ALU.subtract),
                     reads=["Fq0", "Fcol"], writes=["bcol"])
            for j in range(2):
                slot, skey = use_tile(("q", j))
                for pr in range(4):
                    qv = slot[:, pr * 1024:(pr + 1) * 1024].rearrange("p (k c) -> p k c", c=128)
                    b = nb()
                    for k in range(KD):
                        S.op("pe", lambda: nc.tensor.matmul(ps[b][:], lhsT=qv[:, k, :], rhs=xTb[:, k, :], start=(k == 0), stop=(k == KD - 1)),
                             reads=[skey, ("xTb", k)], writes=[pk(b)], inc=(k == KD - 1))
                    S.op("act", lambda: nc.scalar.activation(out=QT[:, 4 * j + pr, :], in_=ps[b][:], func=AF.Copy, scale=0.125),
                         reads=[pk(b)], writes=[("QT", 4 * j + pr)])
                done_tile()
            OT = big
            ring["n"] = 6
            ring["i"] = 0
            jobs = []
            for h in range(AH):
                for s_ in range(2):
                    nkt = 4 * sc + 2 * s_ + 2
                    for kt in range(nkt):
                        jobs.append((h, s_, kt, nkt))
            LA = 2
            NPT = 4
            pend = {}
            deferred = []

            def emit_st(i):
                h, s_, kt, nkt = jobs[i]
                pr, po = h // 2, (h % 2) * 64
                q0 = s_ * 256
                jd = kt - (4 * sc + 2 * s_)
                c0 = 128 if jd == 1 else 0
                n = 256 - c0
                b = nb()
                S.op("pe", lambda: nc.tensor.matmul(ps[b][:, 0:n], lhsT=KT[po:po + 64, pr, kt * 128:(kt + 1) * 128],
                                                    rhs=QT[po:po + 64, pr, q0 + c0:q0 + 256], start=True, stop=True),
                     reads=[("KT", pr), ("QT", pr)], writes=[pk(b)])
                pend[i] = b

            def emit_rest(i):
                h, s_, kt, nkt = jobs[i]
                ob = 6 + (h % 2)
                okey = pk(ob)
                jd = kt - (4 * sc + 2 * s_)
                c0 = 128 if jd == 1 else 0
                n = 256 - c0
                b = pend.pop(i)
                pt = PT[i % NPT]
                ptk = ("PT", i % NPT)
                oreg = ps[ob][0:65, s_ * 256:(s_ + 1) * 256]
                S.op("act", lambda: nc.scalar.activation(out=pt[:, c0:256], in_=ps[b][:, 0:n], func=AF.Exp, bias=bcol[:, s_ * NKT + kt, h:h + 1]),
                     reads=[pk(b), "bcol"], writes=[ptk])
                if jd >= 0:
                    S.op("pool", lambda: nc.gpsimd.affine_select(out=pt[:, c0:c0 + 128], in_=pt[:, c0:c0 + 128], pattern=[[1, 128]],
                                                                 compare_op=ALU.is_ge, fill=zero_reg, base=0, channel_multiplier=-1),
                         reads=[ptk], writes=[ptk])
                last = (s_ == 1 and kt == nkt - 1)
                S.op("pe", lambda: nc.tensor.matmul(oreg[:, c0:256], lhsT=VA[:, kt, h, :], rhs=pt[:, c0:256], start=(kt == 0), stop=(kt == nkt - 1)),
                     reads=[("VA", kt), ptk], writes=[okey], inc=last)
                if last:
                    rr_, Rs_ = rr[h % 2], Rs[h % 2]
                    S.op("dve", lambda: nc.vector.reciprocal(out=rr_[64:65, :], in_=ps[ob][64:65, :]), reads=[okey], writes=[("rr", h % 2)])

                    def fin(h=h, ob=ob, okey=okey, rr_=rr_, Rs_=Rs_):
                        b2 = nb()
                        S.op("pe", lambda: nc.tensor.matmul(ps[b2][0:64, :], lhsT=onesf[64:65, 0:64], rhs=rr_[64:65, :], start=True, stop=True),
                             reads=["onesf", ("rr", h % 2)], writes=[pk(b2)])
                        S.op("act", lambda: nc.scalar.copy(out=Rs_[:], in_=ps[b2][0:64, :]), reads=[pk(b2)], writes=[("Rs", h % 2)])
                        S.op("dve", lambda: nc.vector.tensor_tensor(out=OT[0:64, h, :], in0=ps[ob][0:64, :], in1=Rs_[:], op=ALU.mult),
                             reads=[okey, ("Rs", h % 2)], writes=[("big", h)])
                    deferred.append([2, fin])

            nj = len(jobs)
            for i in range(nj + LA):
                if i < nj:
                    emit_st(i)
                for dfr in list(deferred):
                    dfr[0] -= 1
                    if dfr[0] <= 0:
                        deferred.remove(dfr)
                        dfr[1]()
                if i >= LA:
                    emit_rest(i - LA)
            for dfr in deferred:
                dfr[1]()
            ring["n"] = 8
            for j in range(4):
                slot, skey = use_tile(("o", j))
                ov = slot[0:64, 0:16 * 256].rearrange("p (h c) -> p h c", c=256)
                for c in range(2):
                    k = 2 * j + c
                    b = nb()
                    for h in range(AH):
                        S.op("pe", lambda: nc.tensor.matmul(ps[b][:], lhsT=ov[:, h, c * 128:(c + 1) * 128], rhs=OT[0:64, h, :],
                                                            start=(h == 0), stop=(h == AH - 1)),
                             reads=[skey, ("big", h)], writes=[pk(b)], inc=(h == AH - 1))
                    ln_accum(k, b, 2)
                done_tile()
            ln_finish(2)

        def xk(tt):
            nm = "lnb" if tt < 2 else "lnsq"
            return [(nm, 4 * (tt % 2) + i) for i in range(4)]
        XK = xk(0) + xk(1) + xk(2) + xk(3)
        S.fence()
        gi = 0
        for bseq in range(NB if stop_after != "setup" else 0):
            for sc in range(NSC):
                tap.idx = gi
                t0 = sc * TS
                first = (sc == 0)
                S.dma("sp", xin, dr["x"][bseq, t0:t0 + TS, :].rearrange("(t p) d -> p t d", p=128), [], XK, iodom_in)
                for k in range(KD if stop_after != "xdma" else 0):
                    b = nb()
                    for tt in range(4):
                        S.op("pe", lambda: nc.tensor.transpose(out=ps[b][:, tt * 128:(tt + 1) * 128], in_=xin[:, tt, k * 128:(k + 1) * 128], identity=identf[:]),
                             reads=xk(tt) + ["identf"], writes=[pk(b)], inc=(tt == 3))
                    S.op("act", lambda: nc.scalar.copy(out=xT[:, k, :], in_=ps[b][:]), reads=[pk(b)], writes=[("xT", k)])
                    S.op("dve", lambda: nc.vector.tensor_copy(out=xTb[:, k, :], in_=ps[b][:]), reads=[pk(b)], writes=[("xTb", k)])
                S.fence()
                if stop_after not in ("xload", "xdma"):
                    ssd_phase(first)
                    tap("dbg_x1", xT[:, 0, :], ("xT", 0), [128, TS])
                if stop_after not in ("xload", "ssd", "xdma"):
                    ffn_phase(0, 1)
                    tap("dbg_x2", xT[:, 0, :], ("xT", 0), [128, TS])
                if stop_after not in ("xload", "ssd", "ffn0", "xdma"):
                    S.fence()
                    attn_phase(sc, first)
                    tap("dbg_x3", xT[:, 0, :], ("xT", 0), [128, TS])
                    ffn_phase(1, 3)
                for tt in range(4 if stop_after != "xdma" else 0):
                    for hf in range(2):
                        b = nb()
                        for kq in range(4):
                            k = hf * 4 + kq
                            S.op("pe", lambda: nc.tensor.transpose(out=ps[b][:, kq * 128:(kq + 1) * 128], in_=xT[:, k, tt * 128:(tt + 1) * 128], identity=identf[:]),
                                 reads=[("xT", k), "identf"], writes=[pk(b)], inc=(kq == 3))
                        if hf == 0:
                            S.op("act", lambda: nc.scalar.copy(out=xin[:, tt, hf * 512:(hf + 1) * 512], in_=ps[b][:]), reads=[pk(b)], writes=xk(tt))
                        else:
                            S.op("dve", lambda: nc.vector.tensor_copy(out=xin[:, tt, hf * 512:(hf + 1) * 512], in_=ps[b][:]), reads=[pk(b)], writes=xk(tt))
                S.dma("sp", out_d[bseq, t0:t0 + TS, :].rearrange("(t p) d -> p t d", p=128), xin, XK, [], iodom_out)
                gi += 1
        assert stop_after is not None or wstate["next_use"] == total_tiles, (wstate, total_tiles)
        S.wait_all("sp", [iodom_out, dbgdom])
        build.stats = dict(nins=dict(S.nins), ndma=S.ndma, counts={k: v.count for k, v in S.dom.items()})
    return nc, list(dbg.keys())


_CACHE = {}


def kernel(**inputs):
    n_cores = 8
    x = np.ascontiguousarray(inputs["x"], dtype=np.float32)
    B, SEQ, _ = x.shape
    NB = B // n_cores
    key = (NB, SEQ)
    if key not in _CACHE:
        _CACHE[key] = build(NB, SEQ)[0]
    nc = _CACHE[key]
    in_maps = []
    for c in range(n_cores):
        m = {k: np.ascontiguousarray(v, dtype=np.float32) for k, v in inputs.items() if k != "x"}
        m["x"] = x[c * NB:(c + 1) * NB]
        in_maps.append(m)
    res = run_bass_kernel_spmd(nc, in_maps, core_ids=list(range(n_cores)))
    return np.concatenate([r["out"] for r in res.results], axis=0)
```

```python
import numpy as np
from contextlib import ExitStack
from collections import defaultdict

import concourse.bass as bass
import concourse.mybir as mybir
from concourse.bass_utils import run_bass_kernel_spmd

F32 = mybir.dt.float32
BF16 = mybir.dt.bfloat16
AF = mybir.ActivationFunctionType
ALU = mybir.AluOpType

D = 1024
KD = 8
DI = 2048
NG = 8
NHEAD = 32
DFF = 2816
NF = 22
AH = 16
TS = 512
DEPTH = 2
ALPHA = (2.0 * DEPTH) ** 0.25
LN_EPS = 1e-5
RMS_EPS = 1e-5
SLOT = 4096
NSLOT = 3
NEG = -30000.0


class Dom:
    def __init__(self, nc, es, name, step, epoch):
        self.nc, self.es, self.name, self.step, self.epoch = nc, es, name, step, epoch
        self.sems = []
        self.count = 0

    def sem_for(self, cnt):
        e = (cnt - 1) // self.epoch
        while len(self.sems) <= e:
            self.sems.append(self.es.enter_context(self.nc.semaphore(f"s_{self.name}_{len(self.sems)}")))
        return self.sems[e], ((cnt - 1) % self.epoch + 1) * self.step


class Sched:
    def __init__(self, nc, es):
        self.nc, self.es = nc, es
        self.eng = {"pe": nc.tensor, "act": nc.scalar, "dve": nc.vector, "pool": nc.gpsimd, "sp": nc.sync}
        self.dom = {e: Dom(nc, es, e, 1, 4096) for e in ("pe", "act", "dve", "pool")}
        self.seen = defaultdict(int)
        self.lastw = {}
        self.readers = defaultdict(dict)
        self.ndma = 0
        self.nins = defaultdict(int)

    def new_dma_dom(self, name):
        return Dom(self.nc, self.es, name, 16, 1024)

    def _deps(self, own, reads, writes):
        deps = {}

        def need(dc, same_ok):
            dom, cnt = dc
            if dom is own and same_ok:
                return
            if deps.get(dom, 0) < cnt:
                deps[dom] = cnt

        for k in reads:
            if k in self.lastw:
                need(self.lastw[k], False)
            if isinstance(k, tuple) and k[0] == "ps":
                for dom, cnt in self.readers[k].items():
                    need((dom, cnt), True)
        for k in writes:
            if k in self.lastw:
                need(self.lastw[k], True)
            for dom, cnt in self.readers[k].items():
                need((dom, cnt), True)
        return deps

    def _wait(self, e, deps, own=None):
        for dom, cnt in deps.items():
            if self.seen[(e, dom.name)] >= cnt:
                continue
            if dom is own:
                assert cnt <= own.count, "same-engine wait on a future completion"
            sem, val = dom.sem_for(cnt)
            self.eng[e].wait_ge(sem, val)
            self.nins[e] += 1
            self.seen[(e, dom.name)] = cnt

    def op(self, e, fn, reads=(), writes=(), inc=True):
        own = self.dom[e]
        self._wait(e, self._deps(own, reads, writes), own)
        ins = fn()
        self.nins[e] += 1
        tag = own.count + 1
        if inc:
            own.count += 1
            sem, _ = own.sem_for(own.count)
            ins.then_inc(sem, 1)
        for k in reads:
            if self.readers[k].get(own, 0) < tag:
                self.readers[k][own] = tag
        for k in writes:
            self.lastw[k] = (own, tag)
            self.readers[k] = {}
        return ins

    def dma(self, q, out, in_, reads, writes, dom):
        self._wait(q, self._deps(None, reads, writes))
        ins = self.eng[q].dma_start(out=out, in_=in_)
        self.nins[q] += 1
        self.ndma += 1
        dom.count += 1
        sem, _ = dom.sem_for(dom.count)
        ins.then_inc(sem, 16)
        for k in reads:
            self.readers[k][dom] = dom.count
        for k in writes:
            self.lastw[k] = (dom, dom.count)
            self.readers[k] = {}
        return ins

    def fence(self):
        es_ = ("pe", "act", "dve", "pool")
        for e in es_:
            self._wait(e, {self.dom[f]: self.dom[f].count for f in es_ if f != e and self.dom[f].count > 0})

    def wait_all(self, e, doms):
        for dom in doms:
            if dom.count > 0:
                self._wait(e, {dom: dom.count})


def bc(ap, shape):
    return ap.to_broadcast(list(shape))


def weight_tiles():
    tiles = []
    for g in range(NG):
        tiles.append((("inA", g), 8 * 512, [("ssm_in_w", 0, g * 256, 256, "kpc", 0, 512, 0),
                                             ("ssm_in_w", 0, 2048 + g * 256, 256, "kpc", 0, 512, 256)]))
        tiles.append((("inB", g), 8 * 256, [("ssm_in_w", 0, 4096 + g * 128, 128, "kpc", 0, 256, 0),
                                             ("ssm_in_w", 0, 5120 + g * 128, 128, "kpc", 0, 256, 128)]))
    for j in range(4):
        tiles.append((("out", j), 16 * 256, [("ssm_out_w", 0, j * 256, 256, "kpc", 0, 256, 0)]))

    def ffn(l):
        for j in range(6):
            nfc = 4 if j < 5 else 2
            tiles.append((("g", l, j), 8 * 512, [("ffn_gate_w", l, j * 512, nfc * 128, "kpc", 0, 512, 0)]))
            tiles.append((("u", l, j), 8 * 512, [("ffn_up_w", l, j * 512, nfc * 128, "kpc", 0, 512, 0)]))
        for hf in range(2):
            for fg in range(3):
                nf = 8 if fg < 2 else 6
                tiles.append((("dn", l, hf, fg), nf * 512, [("ffn_down_w", l, hf * 512, 512, "kpc_rows", 0, 512, 0, fg * 8, nf)]))

    ffn(0)
    for j in range(2):
        tiles.append((("kvk", j), 4096, [("kv_w", None, j * 512, 512, "kpc", 0, 512, 0)]))
    for j in range(2):
        tiles.append((("kvv", j), 4096, [("kv_w", None, 1024 + j * 512, 512, "kpc", 0, 512, 0)]))
    for j in range(2):
        tiles.append((("q", j), 4096, [("att_q_w", 0, j * 512, 512, "kpc", 0, 512, 0)]))
    for j in range(4):
        tiles.append((("o", j), 16 * 256, [("att_o_w", 0, j * 256, 256, "hpc", 0, 256, 0)]))
    ffn(1)
    return tiles


IN_SPECS = [
    ("x", None), ("ssm_in_w", [1, 1024, 6176]), ("ssm_conv_w", [1, 4, 4096]), ("ssm_conv_b", [1, 4096]),
    ("ssm_dt_bias", [1, 32]), ("ssm_a_log", [1, 32]), ("ssm_d", [1, 32]), ("ssm_norm_w", [1, 2048]),
    ("ssm_out_w", [1, 2048, 1024]), ("kv_w", [1024, 2064]), ("kv_b_f", [16]), ("att_q_w", [1, 1024, 1024]),
    ("att_o_w", [1, 1024, 1024]), ("ffn_gate_w", [2, 1024, 2816]), ("ffn_up_w", [2, 1024, 2816]),
    ("ffn_down_w", [2, 2816, 1024]), ("ln_mix_g", [2, 1024]), ("ln_mix_b", [2, 1024]), ("ln_ffn_g", [2, 1024]),
    ("ln_ffn_b", [2, 1024]),
]


def build(NB=4, SEQ=2048, debug=False, stop_after=None):
    nc = bass.Bass("TRN2", target_bir_lowering=False)
    NSC = SEQ // TS
    NKT = SEQ // 128
    dr = {}
    for name, shp in IN_SPECS:
        if name == "x":
            shp = [NB, SEQ, D]
        dr[name] = nc.dram_tensor(name, shp, F32, kind="ExternalInput").ap()
    out_d = nc.dram_tensor("out", [NB, SEQ, D], F32, kind="ExternalOutput").ap()
    tiles = weight_tiles()
    NT = len(tiles)
    wscr = nc.dram_tensor("wscr", [NT, 128, SLOT], BF16, kind="Internal").ap()
    dbg = {}

    with ExitStack() as es:
        ec = es.enter_context
        S = Sched(nc, es)

        def sb(name, shape, dt=F32):
            return ec(nc.sbuf_tensor(name, list(shape), dt))

        identf = sb("identf", [128, 128]); identb = sb("identb", [128, 128], BF16)
        onesf = sb("onesf", [128, 128]); trif = sb("trif", [128, 128])
        lnones = sb("lnones", [128, 128], BF16)
        negmask = sb("negmask", [128, 512], BF16)
        prmA = sb("prmA", [128, 128]); prmB = sb("prmB", [128, 128])
        dtb_bc = sb("dtb_bc", [128, 32]); A_bc = sb("A_bc", [128, 32]); D_bc = sb("D_bc", [128, 32])
        bf_col = sb("bf_col", [16, 1])
        wdt = sb("wdt", [128, 8, 32], BF16); wf = sb("wf", [128, 8, 16], BF16)
        wslot = [sb(f"wslot{i}", [128, SLOT], BF16) for i in range(NSLOT)]
        scr16 = sb("scr16", [128, 4096])
        xin = scr16[:].rearrange("p (t d) -> p t d", d=D)
        lnb = scr16[:, 0:2048].bitcast(BF16).rearrange("p (k t) -> p k t", t=TS)
        lnsq = scr16[:, 2048:4096].bitcast(BF16).rearrange("p (k t) -> p k t", t=TS)
        xT = sb("xT", [128, KD, TS]); xTb = sb("xTb", [128, KD, TS], BF16)
        mean_sb = sb("mean_sb", [128, TS]); rstd_sb = sb("rstd_sb", [128, TS])
        big = sb("big", [128, NF, TS], BF16)
        acc = [sb(f"acc{i}", [128, TS]) for i in range(2)]
        lnt = acc
        stateT = sb("stateT", [128, NHEAD * 64]); stbf = sb("stbf", [128, NHEAD * 64], BF16)
        halo = sb("halo", [128, 32, 3])
        KT = sb("KT", [128, 8, SEQ], BF16)
        VA = sb("VA", [128, NKT, AH, 65], BF16)
        Fcarry = sb("Fcarry", [16, 1]); Fcol = sb("Fcol", [128, NKT, AH])
        lneps = sb("lneps", [128, 2])
        ARENA = 7360
        arena = sb("arena", [128, ARENA])
        ar = {"o": 0}

        def carve(shape, dt=F32):
            n = int(np.prod(shape[1:]))
            w = n if dt == F32 else (n + 1) // 2
            o = ar["o"]
            assert o + w <= ARENA, ("arena overflow", o, w)
            ar["o"] = o + w
            v = arena[0:shape[0], o:o + w]
            if dt != F32:
                v = v.bitcast(dt)
            if len(shape) == 3:
                v = v.rearrange("p (a b) -> p a b", b=shape[2])
            return v

        stg1 = carve([128, 128]); stg2 = carve([128, 128])
        ar["o"] = 0
        dt_sb = carve([128, 4, 32]); a_sb = carve([128, 4, 32]); cumcol = carve([128, 4, 32]); expcum = carve([128, 4, 32])
        sp_t = [carve([128, 4, 32]) for _ in range(3)]
        cumcolp = carve([128, 4, 32])
        ubuf = carve([128, 2, TS + 4])
        e4 = [carve([128, 4]) for _ in range(2)]
        ys = carve([128, 256]); ysum = carve([128, 256])
        yg = carve([128, 4, 256]); ss = carve([128, 4]); sd4 = carve([128, 4]); rstd4 = carve([128, 4])
        junk = carve([128, 256]); ygn = carve([128, 256], BF16); sttmp = carve([128, 256])
        zs = [carve([128, 4, 256], BF16), None]
        xbc = [carve([128, 4, TS], BF16), None]

        def chunk_set():
            return dict(xtok=carve([128, 256], BF16), xD=carve([128, 256], BF16), btok=carve([128, 128], BF16),
                        atri=carve([128, 512]), decayT=carve([128, 512], BF16), GT=carve([128, 512], BF16), xw=carve([128, 256], BF16))
        cset = [chunk_set(), None]
        ssd_top = ar["o"]
        sav = (arena, ar["o"])
        arena_main = arena

        def carve16(shape, dt=F32):
            n = int(np.prod(shape[1:]))
            w = n if dt == F32 else (n + 1) // 2
            o = c16["o"]
            assert o + w <= 4096, ("scr16 overflow", o, w)
            c16["o"] = o + w
            v = scr16[0:shape[0], o:o + w]
            if dt != F32:
                v = v.bitcast(dt)
            if len(shape) == 3:
                v = v.rearrange("p (a b) -> p a b", b=shape[2])
            return v
        c16 = {"o": 0}
        zs[1] = carve16([128, 4, 256], BF16)
        xbc[1] = carve16([128, 4, TS], BF16)
        cset[1] = dict(xtok=carve16([128, 256], BF16), xD=carve16([128, 256], BF16), btok=carve16([128, 128], BF16),
                       atri=carve16([128, 512]), decayT=carve16([128, 512], BF16), GT=carve16([128, 512], BF16), xw=carve16([128, 256], BF16))
        ar["o"] = 0
        QT = carve([128, 8, TS], BF16)
        f_v = carve([16, TS]); f_a = carve([16, TS]); f_l = carve([16, TS]); Frow = carve([16, TS])
        fdiag = carve([16, 16]); Fq0 = carve([128, AH]); bcol = carve([128, NKT, AH])
        PT = [carve([128, 512], BF16) for _ in range(4)]
        rr = [carve([128, 512])] * 2; Rs = [carve([64, 512])] * 2
        att_top = ar["o"]
        ps = [ec(nc.psum_tensor(f"ps{i}", [128, 512], F32)) for i in range(8)]

        def psb(i):
            return ps[i][:].bitcast(BF16)

        ring = {"i": 0, "n": 8}

        def nb():
            i = ring["i"] % ring["n"]
            ring["i"] = (i + 1) % ring["n"]
            return i

        def pk(i):
            return ("ps", i)

        P_ = "pool"
        neg_reg = nc.gpsimd.to_reg(NEG)
        zero_reg = nc.gpsimd.to_reg(0.0)
        S.op(P_, lambda: nc.gpsimd.memset(identf[:], 0.0), writes=["identf"])
        S.op(P_, lambda: nc.gpsimd.affine_select(out=identf[:], in_=identf[:], pattern=[[-1, 128]], compare_op=ALU.not_equal,
                                                 fill=1.0, base=0, channel_multiplier=1), reads=["identf"], writes=["identf"])
        S.op(P_, lambda: nc.gpsimd.tensor_copy(out=identb[:], in_=identf[:]), reads=["identf"], writes=["identb"])
        S.op(P_, lambda: nc.gpsimd.memset(onesf[:], 1.0), writes=["onesf"])
        S.op(P_, lambda: nc.gpsimd.memset(lnones[:], 1.0 / D), writes=["lnones"])
        S.op(P_, lambda: nc.gpsimd.memset(trif[:], 1.0), writes=["trif"])
        S.op(P_, lambda: nc.gpsimd.affine_select(out=trif[:], in_=trif[:], pattern=[[1, 128]], compare_op=ALU.is_ge,
                                                 fill=0.0, base=0, channel_multiplier=-1), reads=["trif"], writes=["trif"])
        S.op(P_, lambda: nc.gpsimd.memset(negmask[:], 0.0), writes=["negmask"])
        nm3 = negmask[:].rearrange("p (h l) -> p h l", h=4)
        S.op(P_, lambda: nc.gpsimd.affine_select(out=nm3, in_=nm3, pattern=[[0, 4], [1, 128]], compare_op=ALU.is_ge,
                                                 fill=NEG, base=0, channel_multiplier=-1), reads=["negmask"], writes=["negmask"])
        S.op(P_, lambda: nc.gpsimd.memset(stg2[:], 0.0), writes=["stg2"])

        cdom = S.new_dma_dom("cst")
        S.dma("sp", stg1[:], dr["ssm_conv_w"][0].rearrange("k (c p) -> (k c) p", p=128), [], ["stg1"], cdom)
        rows = [("ssm_conv_b", dr["ssm_conv_b"][0], 32, 0), ("ssm_norm_w", dr["ssm_norm_w"][0], 16, 32),
                ("ln_mix_g", dr["ln_mix_g"].rearrange("l d -> (l d)"), 16, 48), ("ln_mix_b", dr["ln_mix_b"].rearrange("l d -> (l d)"), 16, 64),
                ("ln_ffn_g", dr["ln_ffn_g"].rearrange("l d -> (l d)"), 16, 80), ("ln_ffn_b", dr["ln_ffn_b"].rearrange("l d -> (l d)"), 16, 96)]
        for (_, src, n, r0) in rows:
            S.dma("sp", stg2[r0:r0 + n, :], src.rearrange("(c p) -> c p", p=128), [], ["stg2"], cdom)
        S.dma("sp", dtb_bc[:], dr["ssm_dt_bias"].partition_broadcast(128), [], ["dtb_bc"], cdom)
        S.dma("sp", A_bc[:], dr["ssm_a_log"].partition_broadcast(128), [], ["A_bc"], cdom)
        S.dma("sp", D_bc[:], dr["ssm_d"].partition_broadcast(128), [], ["D_bc"], cdom)
        S.dma("sp", bf_col[:], dr["kv_b_f"].rearrange("(h o) -> h o", o=1), [], ["bf_col"], cdom)
        S.op("act", lambda: nc.scalar.activation(out=A_bc[:], in_=A_bc[:], func=AF.Exp), reads=["A_bc"], writes=["A_bc"])
        S.op("dve", lambda: nc.vector.tensor_scalar_mul(out=A_bc[:], in0=A_bc[:], scalar1=-1.0), reads=["A_bc"], writes=["A_bc"])
        b0 = nb()
        S.op("pe", lambda: nc.tensor.transpose(out=ps[b0][:, 0:128], in_=stg1[:], identity=identf[:]), reads=["stg1", "identf"], writes=[pk(b0)])
        S.op("dve", lambda: nc.vector.tensor_copy(out=prmA[:], in_=ps[b0][:, 0:128]), reads=[pk(b0)], writes=["prmA"])
        b1 = nb()
        S.op("pe", lambda: nc.tensor.transpose(out=ps[b1][:, 0:128], in_=stg2[:], identity=identf[:]), reads=["stg2", "identf"], writes=[pk(b1)])
        S.op("dve", lambda: nc.vector.tensor_copy(out=prmB[:], in_=ps[b1][:, 0:128]), reads=[pk(b1)], writes=["prmB"])
        S.op("dve", lambda: nc.vector.tensor_scalar_mul(out=prmA[:], in0=prmA[:], scalar1=0.5), reads=["prmA"], writes=["prmA"])
        S.op("dve", lambda: nc.vector.tensor_scalar_mul(out=prmB[:, 0:32], in0=prmB[:, 0:32], scalar1=0.5), reads=["prmB"], writes=["prmB"])
        cw = prmA[:].rearrange("p (k c) -> p k c", k=4)
        cb = prmB[:, 0:32]
        normw = prmB[:, 32:48]
        lng = {0: prmB[:, 48:56], 1: prmB[:, 80:88], 2: prmB[:, 56:64], 3: prmB[:, 88:96]}
        lnbias = {0: prmB[:, 64:72], 1: prmB[:, 96:104], 2: prmB[:, 72:80], 3: prmB[:, 104:112]}

        import os
        wcdom = S.new_dma_dom("wcv")
        stf = [scr16[:], xT[:].rearrange("p k t -> p (k t)")]
        stb = [big[:, 0:8, :].rearrange("p k t -> p (k t)"), big[:, 8:16, :].rearrange("p k t -> p (k t)")]
        ktf = KT[:].rearrange("p k t -> p (k t)").bitcast(F32)
        for i in range(ktf.shape[1] // 4096):
            stf.append(ktf[:, i * 4096:(i + 1) * 4096])
        vaf = VA[:].rearrange("p a b c -> p (a b c)")
        for i in range(vaf.shape[1] // 4096):
            stb.append(vaf[:, i * 4096:(i + 1) * 4096])
        NST = min(len(stf), len(stb), 4)
        cvl = [S.new_dma_dom(f"cvl{i}") for i in range(NST)]
        cvs = [S.new_dma_dom(f"cvs{i}") for i in range(NST)]
        cast_eng = ["dve", "act", "pool"]
        for ti, (name, nel, parts) in enumerate(tiles):
            if os.environ.get("SKIP_CONV"):
                break
            sl = ti % NST
            npart = 64 if name[0] == "o" else 128
            for part in parts:
                (src, idx, c0, n, kind, base, cstride, coff) = part[:8]
                w = dr[src] if idx is None else dr[src][idx]
                if kind == "kpc_rows":
                    k0, nk = part[8], part[9]
                    s_ap = w[k0 * 128:(k0 + nk) * 128, c0:c0 + n].rearrange("(k p) c -> p k c", p=128)
                    d_ap = stf[sl][:, base:base + nk * cstride].rearrange("p (k c) -> p k c", c=cstride)[:, :, coff:coff + n]
                elif kind == "kpc":
                    nk = w.shape[0] // 128
                    s_ap = w[:, c0:c0 + n].rearrange("(k p) c -> p k c", p=128)
                    d_ap = stf[sl][:, base:base + nk * cstride].rearrange("p (k c) -> p k c", c=cstride)[:, :, coff:coff + n]
                else:
                    s_ap = w[:, c0:c0 + n].rearrange("(h p) c -> p h c", p=64)
                    d_ap = stf[sl][0:64, base:base + 16 * cstride].rearrange("p (h c) -> p h c", c=cstride)[:, :, coff:coff + n]
                S.dma("sp", d_ap, s_ap, [], [("stf", sl)], cvl[sl])
            ce = cast_eng[ti % 3]
            if ce == "dve":
                S.op("dve", lambda: nc.vector.tensor_copy(out=stb[sl][0:npart, 0:nel], in_=stf[sl][0:npart, 0:nel]), reads=[("stf", sl)], writes=[("stb", sl)])
            elif ce == "act":
                S.op("act", lambda: nc.scalar.copy(out=stb[sl][0:npart, 0:nel], in_=stf[sl][0:npart, 0:nel]), reads=[("stf", sl)], writes=[("stb", sl)])
            else:
                S.op("pool", lambda: nc.gpsimd.tensor_copy(out=stb[sl][0:npart, 0:nel], in_=stf[sl][0:npart, 0:nel]), reads=[("stf", sl)], writes=[("stb", sl)])
            S.dma("sp", wscr[ti][0:npart, 0:nel], stb[sl][0:npart, 0:nel], [("stb", sl)], [("wscr", ti)], cvs[sl])
        S.dma("pool", wdt[:], dr["ssm_in_w"][0][:, 6144:6176].rearrange("(k p) c -> p k c", p=128), [], ["wdt"], wcdom)
        S.dma("pool", wf[:], dr["kv_w"][:, 2048:2064].rearrange("(k p) c -> p k c", p=128), [], ["wf"], wcdom)
        S.wait_all("pool", cvs + cvl)
        S.op("pool", lambda: nc.gpsimd.memset(VA[:], 1.0), reads=[("stb", i) for i in range(NST)], writes=[("VA", kt) for kt in range(NKT)])
        S.wait_all("pe", cvs + cvl)
        S.wait_all("act", cvs + cvl)
        S.wait_all("dve", cvs + cvl)
        S.wait_all("pool", cvs + cvl)

        wdoms = [S.new_dma_dom(f"w{i}") for i in range(NSLOT)]
        wstate = {"next_load": 0, "next_use": 0}
        total_tiles = NB * NSC * NT

        def prefetch():
            i = wstate["next_load"]
            if i >= total_tiles:
                return
            wstate["next_load"] += 1
            ti = i % NT
            s = i % NSLOT
            nel = tiles[ti][1]
            npart = 64 if tiles[ti][0][0] == "o" else 128
            S.dma("sp", wslot[s][0:npart, 0:nel], wscr[ti][0:npart, 0:nel], [("wscr", ti)], [("wslot", s)], wdoms[s])

        def use_tile(expect):
            if stop_after is not None:
                while tiles[wstate["next_use"] % NT][0] != expect:
                    wstate["next_use"] += 1
                    prefetch()
            i = wstate["next_use"]
            wstate["next_use"] += 1
            ti = i % NT
            assert tiles[ti][0] == expect, (tiles[ti][0], expect)
            s = i % NSLOT
            return wslot[s], ("wslot", s)

        def done_tile():
            prefetch()

        for _ in range(NSLOT):
            prefetch()

        iodom_in = S.new_dma_dom("xin")
        iodom_out = S.new_dma_dom("xout")
        dbgdom = S.new_dma_dom("dbg")

        def tap(name, ap, key, shape):
            if not debug:
                return
            if name not in dbg:
                dbg[name] = nc.dram_tensor(name, [NB * NSC] + list(shape), ap.dtype, kind="ExternalOutput").ap()
            S.dma("sp", dbg[name][tap.idx], ap, [key], [], dbgdom)
        tap.idx = 0

        def ln_accum(k, bank, ln_idx):
            S.op("dve", lambda: nc.vector.scalar_tensor_tensor(out=xT[:, k, :], in0=xT[:, k, :], scalar=ALPHA, in1=ps[bank][:],
                                                               op0=ALU.mult, op1=ALU.add),
                 reads=[("xT", k), pk(bank)], writes=[("xT", k)])
            S.op("act", lambda: nc.scalar.copy(out=lnb[:, k, :], in_=xT[:, k, :]), reads=[("xT", k)], writes=[("lnb", k)])
            S.op("act", lambda: nc.scalar.activation(out=lnsq[:, k, :], in_=xT[:, k, :], func=AF.Square), reads=[("xT", k)], writes=[("lnsq", k)])

        def ln_finish(ln_idx):
            bm, be = nb(), nb()
            for k in range(KD):
                S.op("pe", lambda: nc.tensor.matmul(ps[bm][:], lhsT=lnones[:], rhs=lnb[:, k, :], start=(k == 0), stop=(k == KD - 1)),
                     reads=[("lnb", k), "lnones"], writes=[pk(bm)], inc=(k == KD - 1))
            for k in range(KD):
                S.op("pe", lambda: nc.tensor.matmul(ps[be][:], lhsT=lnones[:], rhs=lnsq[:, k, :], start=(k == 0), stop=(k == KD - 1)),
                     reads=[("lnsq", k), "lnones"], writes=[pk(be)], inc=(k == KD - 1))
            S.op("act", lambda: nc.scalar.copy(out=mean_sb[:], in_=ps[bm][:]), reads=[pk(bm)], writes=["mean_sb"])
            S.op("dve", lambda: nc.vector.tensor_tensor(out=rstd_sb[:], in0=ps[bm][:], in1=mean_sb[:], op=ALU.mult),
                 reads=[pk(bm), "mean_sb"], writes=["rstd_sb"])
            S.op("dve", lambda: nc.vector.tensor_tensor(out=rstd_sb[:], in0=ps[be][:], in1=rstd_sb[:], op=ALU.subtract),
                 reads=[pk(be), "rstd_sb"], writes=["rstd_sb"])
            S.op("act", lambda: nc.scalar.activation(out=rstd_sb[:], in_=rstd_sb[:], func=AF.Sqrt, bias=lneps[:, 0:1]),
                 reads=["rstd_sb", "lneps"], writes=["rstd_sb"])
            S.op("dve", lambda: nc.vector.reciprocal(out=rstd_sb[:], in_=rstd_sb[:]), reads=["rstd_sb"], writes=["rstd_sb"])
            for k in range(KD):
                t1, t2 = lnt[0], lnt[1]
                k1, k2 = ("acc", 0), ("acc", 1)
                S.op("dve", lambda: nc.vector.tensor_tensor(out=t1[:], in0=xT[:, k, :], in1=mean_sb[:], op=ALU.subtract),
                     reads=[("xT", k), "mean_sb"], writes=[k1])
                S.op("pool", lambda: nc.gpsimd.tensor_tensor(out=t2[:], in0=t1[:], in1=rstd_sb[:], op=ALU.mult),
                     reads=[k1, "rstd_sb"], writes=[k2])
                S.op("act", lambda: nc.scalar.activation(out=xT[:, k, :], in_=t2[:], func=AF.Identity, scale=lng[ln_idx][:, k:k + 1],
                                                         bias=lnbias[ln_idx][:, k:k + 1]),
                     reads=[k2, "prmB"], writes=[("xT", k)])
                S.op("act", lambda: nc.scalar.activation(out=xTb[:, k, :], in_=t2[:], func=AF.Identity, scale=lng[ln_idx][:, k:k + 1],
                                                         bias=lnbias[ln_idx][:, k:k + 1]),
                     reads=[k2, "prmB"], writes=[("xTb", k)])

        S.op("pool", lambda: nc.gpsimd.memset(lneps[:, 0:1], LN_EPS), writes=["lneps"])
        S.op("pool", lambda: nc.gpsimd.memset(lneps[:, 1:2], 4.0 * RMS_EPS), writes=["lneps"])

        sg = [acc[0][:].bitcast(BF16)[:, 0:TS], acc[0][:].bitcast(BF16)[:, TS:2 * TS], acc[1][:].bitcast(BF16)[:, 0:TS], acc[1][:].bitcast(BF16)[:, TS:2 * TS]]

        def ffn_phase(l, ln_idx):
            for j in range(6):
                nfc = 4 if j < 5 else 2
                slot, skey = use_tile(("g", l, j))
                gv = slot[:, 0:4096].rearrange("p (k c) -> p k c", c=512)
                gb_ = []
                for fc in range(nfc):
                    bg = nb()
                    gb_.append(bg)
                    for k in range(KD):
                        S.op("pe", lambda: nc.tensor.matmul(ps[bg][:], lhsT=gv[:, k, fc * 128:(fc + 1) * 128], rhs=xTb[:, k, :], start=(k == 0), stop=(k == KD - 1)),
                             reads=[skey, ("xTb", k)], writes=[pk(bg)], inc=(k == KD - 1))
                    S.op("act", lambda: nc.scalar.activation(out=sg[fc], in_=ps[bg][:], func=AF.Silu), reads=[pk(bg)], writes=[("acc", fc // 2)])
                done_tile()
                slot, skey = use_tile(("u", l, j))
                uv = slot[:, 0:4096].rearrange("p (k c) -> p k c", c=512)
                for fc in range(nfc):
                    f = 4 * j + fc
                    bu = nb()
                    for k in range(KD):
                        S.op("pe", lambda: nc.tensor.matmul(ps[bu][:], lhsT=uv[:, k, fc * 128:(fc + 1) * 128], rhs=xTb[:, k, :], start=(k == 0), stop=(k == KD - 1)),
                             reads=[skey, ("xTb", k)], writes=[pk(bu)], inc=(k == KD - 1))
                    S.op("dve", lambda: nc.vector.tensor_tensor(out=big[:, f, :], in0=sg[fc], in1=ps[bu][:], op=ALU.mult),
                         reads=[("acc", fc // 2), pk(bu)], writes=[("big", f)])
                done_tile()
            for hf in range(2):
                banks = [nb() for _ in range(4)]
                for fg in range(3):
                    nf = 8 if fg < 2 else 6
                    slot, skey = use_tile(("dn", l, hf, fg))
                    dv = slot[:, 0:nf * 512].rearrange("p (f c) -> p f c", c=512)
                    for c in range(4):
                        for fl in range(nf):
                            f = fg * 8 + fl
                            S.op("pe", lambda: nc.tensor.matmul(ps[banks[c]][:], lhsT=dv[:, fl, c * 128:(c + 1) * 128], rhs=big[:, f, :],
                                                                start=(f == 0), stop=(f == NF - 1)),
                                 reads=[skey, ("big", f)], writes=[pk(banks[c])], inc=(fl == nf - 1))
                    done_tile()
                for c in range(4):
                    ln_accum(4 * hf + c, banks[c], ln_idx)
            ln_finish(ln_idx)

        def ssd_phase(first_in_seq):
            bd = nb()
            for tt in range(4):
                for k in range(KD):
                    S.op("pe", lambda: nc.tensor.matmul(ps[bd][:, tt * 32:(tt + 1) * 32], lhsT=xTb[:, k, tt * 128:(tt + 1) * 128], rhs=wdt[:, k, :],
                                                        start=(k == 0), stop=(k == KD - 1)),
                         reads=[("xTb", k), "wdt"], writes=[pk(bd)], inc=(tt == 3 and k == KD - 1))
            pd = ps[bd][:, 0:128].rearrange("p (t h) -> p t h", h=32)
            v_, av_, l_ = sp_t
            S.op("dve", lambda: nc.vector.tensor_tensor(out=v_[:], in0=pd, in1=bc(dtb_bc[:].unsqueeze(1), [128, 4, 32]), op=ALU.add),
                 reads=[pk(bd), "dtb_bc"], writes=["sp_v"])
            S.op("act", lambda: nc.scalar.activation(out=av_[:], in_=v_[:], func=AF.Abs), reads=["sp_v"], writes=["sp_a"])
            S.op("act", lambda: nc.scalar.activation(out=av_[:], in_=av_[:], func=AF.Exp, scale=-1.0), reads=["sp_a"], writes=["sp_a"])
            S.op("act", lambda: nc.scalar.activation(out=l_[:], in_=av_[:], func=AF.Ln, bias=1.0), reads=["sp_a"], writes=["sp_l"])
            S.op("dve", lambda: nc.vector.scalar_tensor_tensor(out=dt_sb[:], in0=v_[:], scalar=0.0, in1=l_[:], op0=ALU.max, op1=ALU.add),
                 reads=["sp_v", "sp_l"], writes=["dt_sb"])
            S.op("act", lambda: nc.scalar.activation(out=l_[:], in_=dt_sb[:], func=AF.Ln), reads=["dt_sb"], writes=["sp_l"])
            S.op("dve", lambda: nc.vector.tensor_tensor(out=a_sb[:], in0=dt_sb[:], in1=bc(A_bc[:].unsqueeze(1), [128, 4, 32]), op=ALU.mult),
                 reads=["dt_sb", "A_bc"], writes=["a_sb"])
            bcu = nb()
            for c in range(4):
                S.op("pe", lambda: nc.tensor.matmul(ps[bcu][:, c * 32:(c + 1) * 32], lhsT=trif[:], rhs=a_sb[:, c, :], start=True, stop=True),
                     reads=["trif", "a_sb"], writes=[pk(bcu)], inc=(c == 3))
            pc = ps[bcu][:, 0:128].rearrange("p (t h) -> p t h", h=32)
            S.op("dve", lambda: nc.vector.tensor_tensor(out=cumcolp[:], in0=pc, in1=l_[:], op=ALU.subtract), reads=[pk(bcu), "sp_l"], writes=["cumcolp"])
            S.op("act", lambda: nc.scalar.activation(out=expcum[:], in_=pc, func=AF.Exp), reads=[pk(bcu)], writes=["expcum"])
            if first_in_seq:
                S.op("pool", lambda: nc.gpsimd.memset(stateT[:], 0.0), writes=[("stateT", g) for g in range(NG)])
                S.op("pool", lambda: nc.gpsimd.memset(stbf[:], 0.0), writes=[("stbf", g) for g in range(NG)])
                S.op("pool", lambda: nc.gpsimd.memset(halo[:], 0.0), writes=[("halo", ci) for ci in range(32)])

            def inproj_pieces(g):
                gb = g % 2
                zs_, xbc_ = zs[gb], xbc[gb]
                st = {}

                def p_open_a():
                    st["slot"], st["skey"] = use_tile(("inA", g))
                    st["Wv"] = st["slot"][:, 0:8 * 512].rearrange("p (k c) -> p k c", c=512)

                def p_z(half):
                    def f():
                        if half == 0:
                            p_open_a()
                        Wv, skey = st["Wv"], st["skey"]
                        bz = nb()
                        for t2 in range(2):
                            tt = 2 * half + t2
                            for k in range(KD):
                                S.op("pe", lambda: nc.tensor.matmul(ps[bz][:, t2 * 256:(t2 + 1) * 256], lhsT=xTb[:, k, tt * 128:(tt + 1) * 128],
                                                                    rhs=Wv[:, k, 0:256], start=(k == 0), stop=(k == KD - 1)),
                                     reads=[skey, ("xTb", k)], writes=[pk(bz)], inc=(t2 == 1 and k == KD - 1))
                        S.op("act", lambda: nc.scalar.activation(out=acc[half][:], in_=ps[bz][:], func=AF.Tanh, scale=0.5), reads=[pk(bz)], writes=[("acc", half)])
                        S.op("dve", lambda: nc.vector.scalar_tensor_tensor(out=zs_[:, 2 * half:2 * half + 2, :].rearrange("p t c -> p (t c)"), in0=acc[half][:], scalar=1.0,
                                                                           in1=ps[bz][:], op0=ALU.add, op1=ALU.mult),
                             reads=[("acc", half), pk(bz)], writes=[("zs", gb)])
                    return f

                def p_x(r):
                    def f():
                        ci = (2 * g + r) if r < 2 else (16 + g if r == 2 else 24 + g)
                        if r == 2:
                            done_tile()
                            st["slot"], st["skey"] = use_tile(("inB", g))
                            st["Wv"] = st["slot"][:, 0:8 * 256].rearrange("p (k c) -> p k c", c=256)
                        Wv, skey = st["Wv"], st["skey"]
                        wcol = (256 + r * 128) if r < 2 else (r - 2) * 128
                        bx = nb()
                        for k in range(KD):
                            S.op("pe", lambda: nc.tensor.matmul(ps[bx][:], lhsT=Wv[:, k, wcol:wcol + 128], rhs=xTb[:, k, :],
                                                                start=(k == 0), stop=(k == KD - 1)),
                                 reads=[skey, ("xTb", k)], writes=[pk(bx)], inc=(k == KD - 1))
                        ur = r % 2
                        uk = ("ubuf", ur)
                        S.op("pool", lambda: nc.gpsimd.tensor_copy(out=ubuf[:, ur, 0:3], in_=halo[:, ci, :]), reads=[("halo", ci)], writes=[uk])
                        S.op("act", lambda: nc.scalar.copy(out=ubuf[:, ur, 3:TS + 3], in_=ps[bx][:]), reads=[pk(bx)], writes=[uk])
                        a_ = acc[r % 2]
                        ak = ("acc", r % 2)
                        S.op("act", lambda: nc.scalar.activation(out=a_[:], in_=ps[bx][:], func=AF.Identity, scale=cw[:, 3, ci:ci + 1], bias=cb[:, ci:ci + 1]),
                             reads=[pk(bx), "prmA", "prmB"], writes=[ak])
                        for kk in range(3):
                            S.op("dve", lambda: nc.vector.scalar_tensor_tensor(out=a_[:], in0=ubuf[:, ur, kk:kk + TS], scalar=cw[:, kk, ci:ci + 1], in1=a_[:],
                                                                               op0=ALU.mult, op1=ALU.add), reads=[uk, ak, "prmA"], writes=[ak])
                        S.op("pool", lambda: nc.gpsimd.tensor_copy(out=halo[:, ci, :], in_=ubuf[:, ur, TS:TS + 3]), reads=[uk], writes=[("halo", ci)])
                        S.op("act", lambda: nc.scalar.activation(out=ubuf[:, ur, 0:TS], in_=a_[:], func=AF.Tanh), reads=[ak], writes=[uk])
                        S.op("dve", lambda: nc.vector.scalar_tensor_tensor(out=xbc_[:, r, :], in0=ubuf[:, ur, 0:TS], scalar=1.0, in1=a_[:],
                                                                           op0=ALU.add, op1=ALU.mult), reads=[uk, ak], writes=[("xbc", gb, r)])
                        if r == 3:
                            done_tile()
                    return f
                return [p_z(0), p_z(1), p_x(0), p_x(1), p_x(2), p_x(3)]

            cst = {}

            def stage_A(g, c):
                gb = g % 2
                xbc_ = xbc[gb]
                cs_ = cset[c % 2]
                ck = ("cs", c % 2)
                hs = slice(4 * g, 4 * g + 4)
                cs = slice(c * 128, (c + 1) * 128)
                bt = nb()
                T1 = psb(bt)
                for j, r in enumerate((0, 1, 2)):
                    S.op("pe", lambda: nc.tensor.transpose(out=T1[:, j * 128:(j + 1) * 128], in_=xbc_[:, r, cs], identity=identb[:]),
                         reads=[("xbc", gb, r), "identb"], writes=[pk(bt)], inc=(j == 2))
                T1x = T1[:, 0:256].rearrange("p (h q) -> p h q", q=64)
                S.op("act", lambda: nc.scalar.copy(out=cs_["xtok"][:], in_=T1[:, 0:256]), reads=[pk(bt)], writes=[(ck, "xtok")])
                S.op("dve", lambda: nc.vector.tensor_tensor(out=cs_["xD"][:].rearrange("p (h q) -> p h q", q=64), in0=T1x,
                                                            in1=bc(D_bc[:, hs].unsqueeze(2), [128, 4, 64]), op=ALU.mult),
                     reads=[pk(bt), "D_bc"], writes=[(ck, "xD")])
                S.op("act", lambda: nc.scalar.copy(out=cs_["btok"][:], in_=T1[:, 256:384]), reads=[pk(bt)], writes=[(ck, "btok")])
                b1_ = nb()
                S.op("pe", lambda: nc.tensor.matmul(ps[b1_][:], lhsT=onesf[:], rhs=cs_["atri"][:], start=True, stop=True),
                     reads=["onesf", (ck, "atri")], writes=[pk(b1_)])
                cst[(g, c)] = dict(b1=b1_)

            def stage_A1(g, c):
                cs_ = cset[c % 2]
                ck = ("cs", c % 2)
                hs = slice(4 * g, 4 * g + 4)
                S.op("pool", lambda: nc.gpsimd.tensor_tensor(out=cs_["atri"][:].rearrange("p (h l) -> p h l", l=128),
                                                             in0=bc(trif[:].unsqueeze(1), [128, 4, 128]),
                                                             in1=bc(a_sb[:, c, hs].unsqueeze(2), [128, 4, 128]), op=ALU.mult),
                     reads=["trif", "a_sb"], writes=[(ck, "atri")])

            def stage_B1(g, c):
                gb = g % 2
                xbc_ = xbc[gb]
                cs_ = cset[c % 2]
                ck = ("cs", c % 2)
                hs = slice(4 * g, 4 * g + 4)
                cs = slice(c * 128, (c + 1) * 128)
                b1_ = cst[(g, c)]["b1"]
                X1 = ps[b1_][:].rearrange("p (h l) -> p h l", l=128)
                seg = cs_["atri"]
                S.op("dve", lambda: nc.vector.tensor_tensor(out=seg[:].rearrange("p (h l) -> p h l", l=128), in0=X1,
                                                            in1=bc(cumcolp[:, c, hs].unsqueeze(2), [128, 4, 128]), op=ALU.subtract),
                     reads=[pk(b1_), "cumcolp"], writes=[(ck, "atri")])
                S.op("act", lambda: nc.scalar.activation(out=e4[c % 2][:], in_=X1[:, :, 127], func=AF.Exp), reads=[pk(b1_)], writes=[("e4", c % 2)])
                seg3 = seg[:].rearrange("p (h l) -> p h l", l=128)
                S.op("pool", lambda: nc.gpsimd.affine_select(out=seg3, in_=seg3, pattern=[[0, 4], [1, 128]], compare_op=ALU.is_ge,
                                                             fill=neg_reg, base=0, channel_multiplier=-1), reads=[(ck, "atri")], writes=[(ck, "atri")])
                S.op("act", lambda: nc.scalar.activation(out=cs_["decayT"][:], in_=seg[:], func=AF.Exp), reads=[(ck, "atri")], writes=[(ck, "decayT")])
                b2_ = nb()
                S.op("pe", lambda: nc.tensor.matmul(ps[b2_][:, 0:128], lhsT=xbc_[:, 2, cs], rhs=xbc_[:, 3, cs], start=True, stop=True),
                     reads=[("xbc", gb, 2), ("xbc", gb, 3)], writes=[pk(b2_)])
                cst[(g, c)]["b2"] = b2_

            def stage_B2(g, c):
                cs_ = cset[c % 2]
                ck = ("cs", c % 2)
                b2_ = cst[(g, c)]["b2"]
                S.op("dve", lambda: nc.vector.tensor_tensor(out=cs_["GT"][:].rearrange("p (h l) -> p h l", l=128),
                                                            in0=cs_["decayT"][:].rearrange("p (h l) -> p h l", l=128),
                                                            in1=bc(ps[b2_][:, 0:128].unsqueeze(1), [128, 4, 128]), op=ALU.mult),
                     reads=[(ck, "decayT"), pk(b2_)], writes=[(ck, "GT")])
                dlast = cs_["decayT"][:].rearrange("p (h l) -> p h l", l=128)[:, :, 127:128]
                S.op("dve", lambda: nc.vector.tensor_tensor(out=cs_["xw"][:].rearrange("p (h q) -> p h q", q=64),
                                                            in0=cs_["xtok"][:].rearrange("p (h q) -> p h q", q=64),
                                                            in1=bc(dlast, [128, 4, 64]), op=ALU.mult),
                     reads=[(ck, "xtok"), (ck, "decayT")], writes=[(ck, "xw")])
                b3_ = nb()
                S.op("pe", lambda: nc.tensor.matmul(ps[b3_][:, 0:256], lhsT=identb[:], rhs=cs_["xD"][:], start=True, stop=False),
                     reads=["identb", (ck, "xD")], writes=[pk(b3_)], inc=False)
                for h in range(4):
                    S.op("pe", lambda: nc.tensor.matmul(ps[b3_][:, h * 64:(h + 1) * 64], lhsT=cs_["GT"][:, h * 128:(h + 1) * 128],
                                                        rhs=cs_["xtok"][:, h * 64:(h + 1) * 64], start=False, stop=True),
                         reads=[(ck, "GT"), (ck, "xtok")], writes=[pk(b3_)], inc=False)
                S.op("pe", lambda: nc.tensor.matmul(ps[b3_][:, 256:512], lhsT=cs_["btok"][:], rhs=cs_["xw"][:], start=True, stop=True),
                     reads=[(ck, "btok"), (ck, "xw")], writes=[pk(b3_)])
                cst[(g, c)]["b3"] = b3_

            def stage_C(g, c):
                gb = g % 2
                xbc_ = xbc[gb]
                hs = slice(4 * g, 4 * g + 4)
                cs = slice(c * 128, (c + 1) * 128)
                b3_ = cst[(g, c)]["b3"]
                st_g = stateT[:, g * 256:(g + 1) * 256]
                stb_g = stbf[:, g * 256:(g + 1) * 256]
                b4_ = nb()
                S.op("pe", lambda: nc.tensor.matmul(ps[b4_][:, 0:256], lhsT=xbc_[:, 3, cs], rhs=stb_g, start=True, stop=True),
                     reads=[("xbc", gb, 3), ("stbf", g)], writes=[pk(b4_)])
                S.op("dve", lambda: nc.vector.tensor_tensor(out=sttmp[:].rearrange("p (h q) -> p h q", q=64),
                                                            in0=st_g.rearrange("p (h q) -> p h q", q=64),
                                                            in1=bc(e4[c % 2][:].unsqueeze(2), [128, 4, 64]), op=ALU.mult),
                     reads=[("stateT", g), ("e4", c % 2)], writes=["sttmp"])
                S.op("dve", lambda: nc.vector.tensor_tensor(out=st_g, in0=sttmp[:], in1=ps[b3_][:, 256:512], op=ALU.add),
                     reads=["sttmp", pk(b3_)], writes=[("stateT", g)])
                S.op("dve", lambda: nc.vector.tensor_tensor(out=ys[:].rearrange("p (h q) -> p h q", q=64),
                                                            in0=ps[b4_][:, 0:256].rearrange("p (h q) -> p h q", q=64),
                                                            in1=bc(expcum[:, c, hs].unsqueeze(2), [128, 4, 64]), op=ALU.mult),
                     reads=[pk(b4_), "expcum"], writes=["ys"])
                S.op("act", lambda: nc.scalar.copy(out=stb_g, in_=st_g), reads=[("stateT", g)], writes=[("stbf", g)])
                S.op("dve", lambda: nc.vector.tensor_tensor(out=ysum[:], in0=ps[b3_][:, 0:256], in1=ys[:], op=ALU.add),
                     reads=[pk(b3_), "ys"], writes=["ysum"])
                S.op("pool", lambda: nc.gpsimd.tensor_tensor(out=yg[:, c, :], in0=ysum[:], in1=zs[gb][:, c, :], op=ALU.mult),
                     reads=["ysum", ("zs", gb)], writes=[("yg", c)])
                S.op("act", lambda: nc.scalar.activation(out=junk[:], in_=yg[:, c, :], func=AF.Square, accum_out=ss[:, c:c + 1]),
                     reads=[("yg", c)], writes=["junk", ("ss", c)])

            def group_end(g):
                S.op("act", lambda: nc.scalar.activation(out=sd4[:], in_=ss[:], func=AF.Sqrt, scale=1.0 / 256.0, bias=lneps[:, 1:2]),
                     reads=[("ss", c) for c in range(4)] + ["lneps"], writes=["sd4"])
                S.op("dve", lambda: nc.vector.reciprocal(out=rstd4[:], in_=sd4[:]), reads=["sd4"], writes=["rstd4"])
                bn_ = nb()
                Tn = psb(bn_)
                for c in range(4):
                    S.op("act", lambda: nc.scalar.activation(out=ygn[:], in_=yg[:, c, :], func=AF.Copy, scale=rstd4[:, c:c + 1]),
                         reads=[("yg", c), "rstd4"], writes=["ygn"])
                    for j in range(2):
                        S.op("pe", lambda: nc.tensor.transpose(out=Tn[:, (j * 4 + c) * 128:(j * 4 + c + 1) * 128], in_=ygn[:, j * 128:(j + 1) * 128],
                                                               identity=identb[:]),
                             reads=["ygn", "identb"], writes=[pk(bn_)], inc=(j == 1))
                for j in range(2):
                    kc = 2 * g + j
                    S.op("dve", lambda: nc.vector.tensor_scalar(out=big[:, kc, :], in0=Tn[:, j * 512:(j + 1) * 512], scalar1=normw[:, kc:kc + 1],
                                                                scalar2=None, op0=ALU.mult),
                         reads=[pk(bn_), "prmB"], writes=[("big", kc)])

            for f in inproj_pieces(0):
                f()
            for g in range(NG):
                P = inproj_pieces(g + 1) if g + 1 < NG else []
                P = P + [lambda: None] * (6 - len(P))
                for f in P:
                    f()
                order = []
                for it in range(-3, 5):
                    for (fn, c) in ((stage_C, it - 1), (stage_B2, it), (stage_B1, it + 1), (stage_A, it + 2), (stage_A1, it + 3)):
                        if 0 <= c <= 3:
                            order.append((lambda fn=fn, c=c: fn(g, c)))
                order.append(lambda: group_end(g))
                for f in order:
                    f()
            S.fence()
            for j in range(4):
                slot, skey = use_tile(("out", j))
                ov = slot[:, 0:16 * 256].rearrange("p (k c) -> p k c", c=256)
                for c in range(2):
                    k = 2 * j + c
                    b = nb()
                    for kk in range(16):
                        S.op("pe", lambda: nc.tensor.matmul(ps[b][:], lhsT=ov[:, kk, c * 128:(c + 1) * 128], rhs=big[:, kk, :],
                                                            start=(kk == 0), stop=(kk == 15)),
                             reads=[skey, ("big", kk)], writes=[pk(b)], inc=(kk == 15))
                    ln_accum(k, b, 0)
                done_tile()
            ln_finish(0)

        def attn_phase(sc, first_in_seq):
            t0 = sc * TS
            for j in range(2):
                slot, skey = use_tile(("kvk", j))
                kvv_ = slot[:, 0:4096].rearrange("p (k c) -> p k c", c=512)
                for pr in range(4):
                    b = nb()
                    for k in range(KD):
                        S.op("pe", lambda: nc.tensor.matmul(ps[b][:], lhsT=kvv_[:, k, pr * 128:(pr + 1) * 128], rhs=xTb[:, k, :], start=(k == 0), stop=(k == KD - 1)),
                             reads=[skey, ("xTb", k)], writes=[pk(b)], inc=(k == KD - 1))
                    S.op("act", lambda: nc.scalar.copy(out=KT[:, 4 * j + pr, t0:t0 + TS], in_=ps[b][:]), reads=[pk(b)], writes=[("KT", 4 * j + pr)])
                done_tile()
            for j in range(2):
                slot, skey = use_tile(("kvv", j))
                vv = slot[:, 0:4096].rearrange("p (k c) -> p k c", c=512)
                for tt in range(4):
                    b = nb()
                    for k in range(KD):
                        S.op("pe", lambda: nc.tensor.matmul(ps[b][:], lhsT=xTb[:, k, tt * 128:(tt + 1) * 128], rhs=vv[:, k, :],
                                                            start=(k == 0), stop=(k == KD - 1)),
                             reads=[skey, ("xTb", k)], writes=[pk(b)], inc=(k == KD - 1))
                    kt = 4 * sc + tt
                    S.op("dve", lambda: nc.vector.tensor_copy(out=VA[:, kt, 8 * j:8 * j + 8, 0:64], in_=ps[b][:].rearrange("p (h q) -> p h q", q=64)),
                         reads=[pk(b)], writes=[("VA", kt)])
                done_tile()
            bf_ = nb()
            for k in range(KD):
                S.op("pe", lambda: nc.tensor.matmul(ps[bf_][0:16, :], lhsT=wf[:, k, :], rhs=xTb[:, k, :], start=(k == 0), stop=(k == KD - 1)),
                     reads=["wf", ("xTb", k)], writes=[pk(bf_)], inc=(k == KD - 1))
            S.op("dve", lambda: nc.vector.tensor_scalar(out=f_v[:], in0=ps[bf_][0:16, :], scalar1=bf_col[:, 0:1], scalar2=None, op0=ALU.add),
                 reads=[pk(bf_), "bf_col"], writes=["f_v"])
            S.op("act", lambda: nc.scalar.activation(out=f_a[:], in_=f_v[:], func=AF.Abs), reads=["f_v"], writes=["f_a"])
            S.op("act", lambda: nc.scalar.activation(out=f_a[:], in_=f_a[:], func=AF.Exp, scale=-1.0), reads=["f_a"], writes=["f_a"])
            S.op("act", lambda: nc.scalar.activation(out=f_l[:], in_=f_a[:], func=AF.Ln, bias=1.0), reads=["f_a"], writes=["f_l"])
            S.op("dve", lambda: nc.vector.scalar_tensor_tensor(out=f_l[:], in0=f_v[:], scalar=0.0, in1=f_l[:], op0=ALU.min, op1=ALU.subtract),
                 reads=["f_v", "f_l"], writes=["f_l"])
            if first_in_seq:
                S.op("pool", lambda: nc.gpsimd.memset(Fcarry[:], 0.0), writes=["Fcarry"])
            S.op("dve", lambda: nc.vector.tensor_tensor_scan(out=Frow[:], data0=bc(onesf[0:16, 0:1], [16, TS]), data1=f_l[:], initial=Fcarry[:, 0:1],
                                                             op0=ALU.mult, op1=ALU.add),
                 reads=["onesf", "f_l", "Fcarry"], writes=["Frow"])
            S.op("dve", lambda: nc.vector.tensor_copy(out=Fcarry[:], in_=Frow[:, TS - 1:TS]), reads=["Frow"], writes=["Fcarry"])
            bt_ = nb()
            for tt in range(4):
                S.op("pe", lambda: nc.tensor.transpose(out=ps[bt_][:, tt * 16:(tt + 1) * 16], in_=Frow[:, tt * 128:(tt + 1) * 128], identity=identf[0:16, 0:16]),
                     reads=["Frow", "identf"], writes=[pk(bt_)], inc=False)
            S.op("dve", lambda: nc.vector.tensor_scalar(out=fdiag[:], in0=identf[0:16, 0:16], scalar1=Frow[:, 255:256], scalar2=None, op0=ALU.mult),
                 reads=["identf", "Frow"], writes=["fdiag"])
            S.op("pe", lambda: nc.tensor.matmul(ps[bt_][:, 64:80], lhsT=onesf[0:16, :], rhs=fdiag[:], start=True, stop=True),
                 reads=["onesf", "fdiag"], writes=[pk(bt_)])
            S.op("dve", lambda: nc.vector.tensor_copy(out=Fcol[:, 4 * sc:4 * sc + 4, :], in_=ps[bt_][:, 0:64].rearrange("p (t h) -> p t h", h=16)),
                 reads=[pk(bt_)], writes=["Fcol"])
            S.op("dve", lambda: nc.vector.tensor_copy(out=Fq0[:], in_=ps[bt_][:, 64:80]), reads=[pk(bt_)], writes=["Fq0"])
            nkt_all = 4 * sc + 4
            S.op("dve", lambda: nc.vector.tensor_tensor(out=bcol[:, 0:nkt_all, :], in0=bc(Fq0[:].unsqueeze(1), [128, nkt_all, 16]),
                                                        in1=Fcol[:, 0:nkt_all, :], op=ALU.subtract),
                 reads=["Fq0", "Fcol"], writes=["bcol"])
            for j in range(2):
                slot, skey = use_tile(("q", j))
                qvv_ = slot[:, 0:4096].rearrange("p (k c) -> p k c", c=512)
                for pr in range(4):
                    b = nb()
                    for k in range(KD):
                        S.op("pe", lambda: nc.tensor.matmul(ps[b][:], lhsT=qvv_[:, k, pr * 128:(pr + 1) * 128], rhs=xTb[:, k, :], start=(k == 0), stop=(k == KD - 1)),
                             reads=[skey, ("xTb", k)], writes=[pk(b)], inc=(k == KD - 1))
                    S.op("act", lambda: nc.scalar.activation(out=QT[:, 4 * j + pr, :], in_=ps[b][:], func=AF.Copy, scale=0.125),
                         reads=[pk(b)], writes=[("QT", 4 * j + pr)])
                done_tile()
            OT = big
            ring["n"] = 6
            ring["i"] = 0
            jobs = []
            nkt = 4 * sc + 4
            for h in range(AH):
                for kt in range(nkt):
                    jobs.append((h, kt))
            LA = 2
            NPT = 4
            pend = {}
            deferred = []

            def emit_st(i):
                h, kt = jobs[i]
                pr, po = h // 2, (h % 2) * 64
                jd = kt - 4 * sc
                c0 = 128 * jd if jd > 0 else 0
                n = TS - c0
                b = nb()
                S.op("pe", lambda: nc.tensor.matmul(ps[b][:, 0:n], lhsT=KT[po:po + 64, pr, kt * 128:(kt + 1) * 128],
                                                    rhs=QT[po:po + 64, pr, c0:TS], start=True, stop=True),
                     reads=[("KT", pr), ("QT", pr)], writes=[pk(b)])
                pend[i] = b

            def emit_rest(i):
                h, kt = jobs[i]
                ob = 6 + (h % 2)
                okey = pk(ob)
                jd = kt - 4 * sc
                c0 = 128 * jd if jd > 0 else 0
                n = TS - c0
                b = pend.pop(i)
                pt = PT[i % NPT]
                ptk = ("PT", i % NPT)
                oreg = ps[ob][0:65, :]
                S.op("act", lambda: nc.scalar.activation(out=pt[:, c0:TS], in_=ps[b][:, 0:n], func=AF.Exp, bias=bcol[:, kt, h:h + 1]),
                     reads=[pk(b), "bcol"], writes=[ptk])
                if jd >= 0:
                    S.op("pool", lambda: nc.gpsimd.affine_select(out=pt[:, c0:c0 + 128], in_=pt[:, c0:c0 + 128], pattern=[[1, 128]],
                                                                 compare_op=ALU.is_ge, fill=zero_reg, base=0, channel_multiplier=-1),
                         reads=[ptk], writes=[ptk])
                last = (kt == nkt - 1)
                S.op("pe", lambda: nc.tensor.matmul(oreg[:, c0:TS], lhsT=VA[:, kt, h, :], rhs=pt[:, c0:TS], start=(kt == 0), stop=last),
                     reads=[("VA", kt), ptk], writes=[okey], inc=last)
                if last:
                    rr_, Rs_ = rr[h % 2], Rs[h % 2]
                    S.op("dve", lambda: nc.vector.reciprocal(out=rr_[64:65, :], in_=ps[ob][64:65, :]), reads=[okey], writes=[("rr", 0)])

                    def fin(h=h, ob=ob, okey=okey, rr_=rr_, Rs_=Rs_):
                        b2 = nb()
                        S.op("pe", lambda: nc.tensor.matmul(ps[b2][0:64, :], lhsT=onesf[64:65, 0:64], rhs=rr_[64:65, :], start=True, stop=True),
                             reads=["onesf", ("rr", 0)], writes=[pk(b2)])
                        S.op("act", lambda: nc.scalar.copy(out=Rs_[:], in_=ps[b2][0:64, :]), reads=[pk(b2)], writes=[("Rs", 0)])
                        S.op("dve", lambda: nc.vector.tensor_tensor(out=OT[0:64, h, :], in0=ps[ob][0:64, :], in1=Rs_[:], op=ALU.mult),
                             reads=[okey, ("Rs", 0)], writes=[("big", h)])
                    deferred.append([2, fin])

            nj = len(jobs)
            for i in range(nj + LA):
                if i < nj:
                    emit_st(i)
                for dfr in list(deferred):
                    dfr[0] -= 1
                    if dfr[0] <= 0:
                        deferred.remove(dfr)
                        dfr[1]()
                if i >= LA:
                    emit_rest(i - LA)
            for dfr in deferred:
                dfr[1]()
            ring["n"] = 8
            for j in range(4):
                slot, skey = use_tile(("o", j))
                ov = slot[0:64, 0:16 * 256].rearrange("p (h c) -> p h c", c=256)
                for c in range(2):
                    k = 2 * j + c
                    b = nb()
                    for h in range(AH):
                        S.op("pe", lambda: nc.tensor.matmul(ps[b][:], lhsT=ov[:, h, c * 128:(c + 1) * 128], rhs=OT[0:64, h, :],
                                                            start=(h == 0), stop=(h == AH - 1)),
                             reads=[skey, ("big", h)], writes=[pk(b)], inc=(h == AH - 1))
                    ln_accum(k, b, 2)
                done_tile()
            ln_finish(2)

        def xk(tt):
            nm = "lnb" if tt < 2 else "lnsq"
            return [(nm, 4 * (tt % 2) + i) for i in range(4)]
        XK = xk(0) + xk(1) + xk(2) + xk(3)
        S.fence()
        gi = 0
        for bseq in range(NB if stop_after != "setup" else 0):
            for sc in range(NSC):
                tap.idx = gi
                t0 = sc * TS
                first = (sc == 0)
                S.dma("sp", xin, dr["x"][bseq, t0:t0 + TS, :].rearrange("(t p) d -> p t d", p=128), [], XK, iodom_in)
                for k in range(KD if stop_after != "xdma" else 0):
                    b = nb()
                    for tt in range(4):
                        S.op("pe", lambda: nc.tensor.transpose(out=ps[b][:, tt * 128:(tt + 1) * 128], in_=xin[:, tt, k * 128:(k + 1) * 128], identity=identf[:]),
                             reads=xk(tt) + ["identf"], writes=[pk(b)], inc=(tt == 3))
                    S.op("act", lambda: nc.scalar.copy(out=xT[:, k, :], in_=ps[b][:]), reads=[pk(b)], writes=[("xT", k)])
                    S.op("dve", lambda: nc.vector.tensor_copy(out=xTb[:, k, :], in_=ps[b][:]), reads=[pk(b)], writes=[("xTb", k)])
                S.fence()
                if stop_after not in ("xload", "xdma"):
                    ssd_phase(first)
                    tap("dbg_x1", xT[:, 0, :], ("xT", 0), [128, TS])
                if stop_after not in ("xload", "ssd", "xdma"):
                    ffn_phase(0, 1)
                    tap("dbg_x2", xT[:, 0, :], ("xT", 0), [128, TS])
                if stop_after not in ("xload", "ssd", "ffn0", "xdma"):
                    S.fence()
                    attn_phase(sc, first)
                    tap("dbg_x3", xT[:, 0, :], ("xT", 0), [128, TS])
                    ffn_phase(1, 3)
                for tt in range(4 if stop_after != "xdma" else 0):
                    for hf in range(2):
                        b = nb()
                        for kq in range(4):
                            k = hf * 4 + kq
                            S.op("pe", lambda: nc.tensor.transpose(out=ps[b][:, kq * 128:(kq + 1) * 128], in_=xT[:, k, tt * 128:(tt + 1) * 128], identity=identf[:]),
                                 reads=[("xT", k), "identf"], writes=[pk(b)], inc=(kq == 3))
                        if hf == 0:
                            S.op("act", lambda: nc.scalar.copy(out=xin[:, tt, hf * 512:(hf + 1) * 512], in_=ps[b][:]), reads=[pk(b)], writes=xk(tt))
                        else:
                            S.op("dve", lambda: nc.vector.tensor_copy(out=xin[:, tt, hf * 512:(hf + 1) * 512], in_=ps[b][:]), reads=[pk(b)], writes=xk(tt))
                S.dma("sp", out_d[bseq, t0:t0 + TS, :].rearrange("(t p) d -> p t d", p=128), xin, XK, [], iodom_out)
                gi += 1
        assert stop_after is not None or wstate["next_use"] == total_tiles, (wstate, total_tiles)
        S.wait_all("sp", [iodom_out, dbgdom])
        build.stats = dict(nins=dict(S.nins), ndma=S.ndma, counts={k: v.count for k, v in S.dom.items()})
    return nc, list(dbg.keys())


_CACHE = {}


def kernel(**inputs):
    n_cores = 8
    x = np.ascontiguousarray(inputs["x"], dtype=np.float32)
    B, SEQ, _ = x.shape
    NB = B // n_cores
    key = (NB, SEQ)
    if key not in _CACHE:
        _CACHE[key] = build(NB, SEQ)[0]
    nc = _CACHE[key]
    in_maps = []
    for c in range(n_cores):
        m = {k: np.ascontiguousarray(v, dtype=np.float32) for k, v in inputs.items() if k != "x"}
        m["x"] = x[c * NB:(c + 1) * NB]
        in_maps.append(m)
    res = run_bass_kernel_spmd(nc, in_maps, core_ids=list(range(n_cores)))
    return np.concatenate([r["out"] for r in res.results], axis=0)
```

```python
import numpy as np
from contextlib import ExitStack
from collections import defaultdict

import concourse.bass as bass
import concourse.mybir as mybir
from concourse.bass_utils import run_bass_kernel_spmd

F32 = mybir.dt.float32
BF16 = mybir.dt.bfloat16
AF = mybir.ActivationFunctionType
ALU = mybir.AluOpType

D = 1024
KD = 8
DI = 2048
NG = 8
NHEAD = 32
DFF = 2816
NF = 22
AH = 16
TS = 512
DEPTH = 2
ALPHA = (2.0 * DEPTH) ** 0.25
LN_EPS = 1e-5
RMS_EPS = 1e-5
SLOT = 4096
NSLOT = 3
NEG = -30000.0


class Dom:
    def __init__(self, nc, es, name, step, epoch):
        self.nc, self.es, self.name, self.step, self.epoch = nc, es, name, step, epoch
        self.sems = []
        self.count = 0

    def sem_for(self, cnt):
        e = (cnt - 1) // self.epoch
        while len(self.sems) <= e:
            self.sems.append(self.es.enter_context(self.nc.semaphore(f"s_{self.name}_{len(self.sems)}")))
        return self.sems[e], ((cnt - 1) % self.epoch + 1) * self.step


class Sched:
    def __init__(self, nc, es):
        self.nc, self.es = nc, es
        self.eng = {"pe": nc.tensor, "act": nc.scalar, "dve": nc.vector, "pool": nc.gpsimd, "sp": nc.sync}
        self.dom = {e: Dom(nc, es, e, 1, 4096) for e in ("pe", "act", "dve", "pool")}
        self.seen = defaultdict(int)
        self.lastw = {}
        self.readers = defaultdict(dict)
        self.ndma = 0
        self.nins = defaultdict(int)

    def new_dma_dom(self, name):
        return Dom(self.nc, self.es, name, 16, 1024)

    def _deps(self, own, reads, writes):
        deps = {}

        def need(dc, same_ok):
            dom, cnt = dc
            if dom is own and same_ok:
                return
            if deps.get(dom, 0) < cnt:
                deps[dom] = cnt

        for k in reads:
            if k in self.lastw:
                need(self.lastw[k], False)
            if isinstance(k, tuple) and k[0] == "ps":
                for dom, cnt in self.readers[k].items():
                    need((dom, cnt), True)
        for k in writes:
            if k in self.lastw:
                need(self.lastw[k], True)
            for dom, cnt in self.readers[k].items():
                need((dom, cnt), True)
        return deps

    def _wait(self, e, deps, own=None):
        for dom, cnt in deps.items():
            if self.seen[(e, dom.name)] >= cnt:
                continue
            if dom is own:
                assert cnt <= own.count, "same-engine wait on a future completion"
            sem, val = dom.sem_for(cnt)
            self.eng[e].wait_ge(sem, val)
            self.nins[e] += 1
            self.seen[(e, dom.name)] = cnt

    def op(self, e, fn, reads=(), writes=(), inc=True):
        own = self.dom[e]
        self._wait(e, self._deps(own, reads, writes), own)
        ins = fn()
        self.nins[e] += 1
        tag = own.count + 1
        if inc:
            own.count += 1
            sem, _ = own.sem_for(own.count)
            ins.then_inc(sem, 1)
        for k in reads:
            if self.readers[k].get(own, 0) < tag:
                self.readers[k][own] = tag
        for k in writes:
            self.lastw[k] = (own, tag)
            self.readers[k] = {}
        return ins

    def dma(self, q, out, in_, reads, writes, dom):
        self._wait(q, self._deps(None, reads, writes))
        ins = self.eng[q].dma_start(out=out, in_=in_)
        self.nins[q] += 1
        self.ndma += 1
        dom.count += 1
        sem, _ = dom.sem_for(dom.count)
        ins.then_inc(sem, 16)
        for k in reads:
            self.readers[k][dom] = dom.count
        for k in writes:
            self.lastw[k] = (dom, dom.count)
            self.readers[k] = {}
        return ins

    def fence(self):
        es_ = ("pe", "act", "dve", "pool")
        for e in es_:
            self._wait(e, {self.dom[f]: self.dom[f].count for f in es_ if f != e and self.dom[f].count > 0})

    def wait_all(self, e, doms):
        for dom in doms:
            if dom.count > 0:
                self._wait(e, {dom: dom.count})


def bc(ap, shape):
    return ap.to_broadcast(list(shape))


def weight_tiles():
    tiles = []
    for g in range(NG):
        tiles.append((("inA", g), 8 * 512, [("ssm_in_w", 0, g * 256, 256, "kpc", 0, 512, 0),
                                             ("ssm_in_w", 0, 2048 + g * 256, 256, "kpc", 0, 512, 256)]))
        tiles.append((("inB", g), 8 * 256, [("ssm_in_w", 0, 4096 + g * 128, 128, "kpc", 0, 256, 0),
                                             ("ssm_in_w", 0, 5120 + g * 128, 128, "kpc", 0, 256, 128)]))
    for j in range(4):
        tiles.append((("out", j), 16 * 256, [("ssm_out_w", 0, j * 256, 256, "kpc", 0, 256, 0)]))

    def ffn(l):
        for j in range(6):
            nfc = 4 if j < 5 else 2
            tiles.append((("g", l, j), 8 * 512, [("ffn_gate_w", l, j * 512, nfc * 128, "kpc", 0, 512, 0)]))
            tiles.append((("u", l, j), 8 * 512, [("ffn_up_w", l, j * 512, nfc * 128, "kpc", 0, 512, 0)]))
        for hf in range(2):
            for fg in range(3):
                nf = 8 if fg < 2 else 6
                tiles.append((("dn", l, hf, fg), nf * 512, [("ffn_down_w", l, hf * 512, 512, "kpc_rows", 0, 512, 0, fg * 8, nf)]))

    ffn(0)
    for j in range(2):
        tiles.append((("kvk", j), 4096, [("kv_w", None, j * 512, 512, "kpc", 0, 512, 0)]))
    for j in range(2):
        tiles.append((("kvv", j), 4096, [("kv_w", None, 1024 + j * 512, 512, "kpc", 0, 512, 0)]))
    for j in range(2):
        tiles.append((("q", j), 4096, [("att_q_w", 0, j * 512, 512, "kpc", 0, 512, 0)]))
    for j in range(4):
        tiles.append((("o", j), 16 * 256, [("att_o_w", 0, j * 256, 256, "hpc", 0, 256, 0)]))
    ffn(1)
    return tiles


IN_SPECS = [
    ("x", None), ("ssm_in_w", [1, 1024, 6176]), ("ssm_conv_w", [1, 4, 4096]), ("ssm_conv_b", [1, 4096]),
    ("ssm_dt_bias", [1, 32]), ("ssm_a_log", [1, 32]), ("ssm_d", [1, 32]), ("ssm_norm_w", [1, 2048]),
    ("ssm_out_w", [1, 2048, 1024]), ("kv_w", [1024, 2064]), ("kv_b_f", [16]), ("att_q_w", [1, 1024, 1024]),
    ("att_o_w", [1, 1024, 1024]), ("ffn_gate_w", [2, 1024, 2816]), ("ffn_up_w", [2, 1024, 2816]),
    ("ffn_down_w", [2, 2816, 1024]), ("ln_mix_g", [2, 1024]), ("ln_mix_b", [2, 1024]), ("ln_ffn_g", [2, 1024]),
    ("ln_ffn_b", [2, 1024]),
]


def build(NB=4, SEQ=2048, debug=False, stop_after=None):
    nc = bass.Bass("TRN2", target_bir_lowering=False)
    NSC = SEQ // TS
    NKT = SEQ // 128
    dr = {}
    for name, shp in IN_SPECS:
        if name == "x":
            shp = [NB, SEQ, D]
        dr[name] = nc.dram_tensor(name, shp, F32, kind="ExternalInput").ap()
    out_d = nc.dram_tensor("out", [NB, SEQ, D], F32, kind="ExternalOutput").ap()
    tiles = weight_tiles()
    NT = len(tiles)
    wscr = nc.dram_tensor("wscr", [NT, 128, SLOT], BF16, kind="Internal").ap()
    dbg = {}

    with ExitStack() as es:
        ec = es.enter_context
        S = Sched(nc, es)

        def sb(name, shape, dt=F32):
            return ec(nc.sbuf_tensor(name, list(shape), dt))

        identf = sb("identf", [128, 128]); identb = sb("identb", [128, 128], BF16)
        onesf = sb("onesf", [128, 128]); trif = sb("trif", [128, 128])
        lnones = sb("lnones", [128, 128], BF16)
        negmask = sb("negmask", [128, 512], BF16)
        prmA = sb("prmA", [128, 128]); prmB = sb("prmB", [128, 128])
        dtb_bc = sb("dtb_bc", [128, 32]); A_bc = sb("A_bc", [128, 32]); D_bc = sb("D_bc", [128, 32])
        bf_col = sb("bf_col", [16, 1])
        wdt = sb("wdt", [128, 8, 32], BF16); wf = sb("wf", [128, 8, 16], BF16)
        wslot = [sb(f"wslot{i}", [128, SLOT], BF16) for i in range(NSLOT)]
        scr16 = sb("scr16", [128, 4096])
        xin = scr16[:].rearrange("p (t d) -> p t d", d=D)
        lnb = scr16[:, 0:2048].bitcast(BF16).rearrange("p (k t) -> p k t", t=TS)
        lnsq = scr16[:, 2048:4096].bitcast(BF16).rearrange("p (k t) -> p k t", t=TS)
        xT = sb("xT", [128, KD, TS]); xTb = sb("xTb", [128, KD, TS], BF16)
        mean_sb = sb("mean_sb", [128, TS]); rstd_sb = sb("rstd_sb", [128, TS])
        big = sb("big", [128, NF, TS], BF16)
        acc = [sb(f"acc{i}", [128, TS]) for i in range(2)]
        lnt = acc
        stateT = sb("stateT", [128, NHEAD * 64]); stbf = sb("stbf", [128, NHEAD * 64], BF16)
        halo = sb("halo", [128, 32, 3])
        KT = sb("KT", [128, 8, SEQ], BF16)
        VA = sb("VA", [128, NKT, AH, 65], BF16)
        Fcarry = sb("Fcarry", [16, 1]); Fcol = sb("Fcol", [128, NKT, AH])
        lneps = sb("lneps", [128, 2])
        ARENA = 7750
        arena = sb("arena", [128, ARENA])
        ar = {"o": 0}

        def carve(shape, dt=F32):
            n = int(np.prod(shape[1:]))
            w = n if dt == F32 else (n + 1) // 2
            o = ar["o"]
            assert o + w <= ARENA, ("arena overflow", o, w)
            ar["o"] = o + w
            v = arena[0:shape[0], o:o + w]
            if dt != F32:
                v = v.bitcast(dt)
            if len(shape) == 3:
                v = v.rearrange("p (a b) -> p a b", b=shape[2])
            return v

        stg1 = carve([128, 128]); stg2 = carve([128, 128])
        ar["o"] = 0
        dt_sb = carve([128, 4, 32]); a_sb = carve([128, 4, 32]); cumcol = carve([128, 4, 32]); expcum = carve([128, 4, 32])
        sp_t = [carve([128, 4, 32]) for _ in range(3)]
        cumcolp = carve([128, 4, 32])
        ubuf = carve([128, 2, TS + 4])
        e4 = [carve([128, 4]) for _ in range(2)]
        ys = carve([128, 256]); ysum = carve([128, 256])
        yg = carve([128, 4, 256]); ss = carve([128, 4]); sd4 = carve([128, 4]); rstd4 = carve([128, 4])
        junk = carve([128, 256]); ygn = carve([128, 4, 256], BF16); sttmp = carve([128, 256])
        zs = [carve([128, 4, 256], BF16), None]
        xbc = [carve([128, 4, TS], BF16), None]

        def chunk_set():
            return dict(xtok=carve([128, 256], BF16), xD=carve([128, 256], BF16), btok=carve([128, 128], BF16),
                        atri=carve([128, 512]), decayT=carve([128, 512], BF16), GT=carve([128, 512], BF16), xw=carve([128, 256], BF16))
        cset = [chunk_set(), None]
        ssd_top = ar["o"]
        sav = (arena, ar["o"])
        arena_main = arena

        def carve16(shape, dt=F32):
            n = int(np.prod(shape[1:]))
            w = n if dt == F32 else (n + 1) // 2
            o = c16["o"]
            assert o + w <= 4096, ("scr16 overflow", o, w)
            c16["o"] = o + w
            v = scr16[0:shape[0], o:o + w]
            if dt != F32:
                v = v.bitcast(dt)
            if len(shape) == 3:
                v = v.rearrange("p (a b) -> p a b", b=shape[2])
            return v
        c16 = {"o": 0}
        zs[1] = carve16([128, 4, 256], BF16)
        xbc[1] = carve16([128, 4, TS], BF16)
        cset[1] = dict(xtok=carve16([128, 256], BF16), xD=carve16([128, 256], BF16), btok=carve16([128, 128], BF16),
                       atri=carve16([128, 512]), decayT=carve16([128, 512], BF16), GT=carve16([128, 512], BF16), xw=carve16([128, 256], BF16))
        ar["o"] = 0
        QT = carve([128, 8, TS], BF16)
        f_v = carve([16, TS]); f_a = carve([16, TS]); f_l = carve([16, TS]); Frow = carve([16, TS])
        fdiag = carve([16, 16]); Fq0 = carve([128, AH]); bcol = carve([128, NKT, AH])
        PT = [carve([128, 512], BF16) for _ in range(4)]
        rr = [carve([128, 512])] * 2; Rs = [carve([64, 512])] * 2
        att_top = ar["o"]
        ps = [ec(nc.psum_tensor(f"ps{i}", [128, 512], F32)) for i in range(8)]

        def psb(i):
            return ps[i][:].bitcast(BF16)

        ring = {"i": 0, "n": 8}

        def nb():
            i = ring["i"] % ring["n"]
            ring["i"] = (i + 1) % ring["n"]
            return i

        def pk(i):
            return ("ps", i)

        P_ = "pool"
        neg_reg = nc.gpsimd.to_reg(NEG)
        zero_reg = nc.gpsimd.to_reg(0.0)
        S.op(P_, lambda: nc.gpsimd.memset(identf[:], 0.0), writes=["identf"])
        S.op(P_, lambda: nc.gpsimd.affine_select(out=identf[:], in_=identf[:], pattern=[[-1, 128]], compare_op=ALU.not_equal,
                                                 fill=1.0, base=0, channel_multiplier=1), reads=["identf"], writes=["identf"])
        S.op(P_, lambda: nc.gpsimd.tensor_copy(out=identb[:], in_=identf[:]), reads=["identf"], writes=["identb"])
        S.op(P_, lambda: nc.gpsimd.memset(onesf[:], 1.0), writes=["onesf"])
        S.op(P_, lambda: nc.gpsimd.memset(lnones[:], 1.0 / D), writes=["lnones"])
        S.op(P_, lambda: nc.gpsimd.memset(trif[:], 1.0), writes=["trif"])
        S.op(P_, lambda: nc.gpsimd.affine_select(out=trif[:], in_=trif[:], pattern=[[1, 128]], compare_op=ALU.is_ge,
                                                 fill=0.0, base=0, channel_multiplier=-1), reads=["trif"], writes=["trif"])
        S.op(P_, lambda: nc.gpsimd.memset(negmask[:], 0.0), writes=["negmask"])
        nm3 = negmask[:].rearrange("p (h l) -> p h l", h=4)
        S.op(P_, lambda: nc.gpsimd.affine_select(out=nm3, in_=nm3, pattern=[[0, 4], [1, 128]], compare_op=ALU.is_ge,
                                                 fill=NEG, base=0, channel_multiplier=-1), reads=["negmask"], writes=["negmask"])
        S.op(P_, lambda: nc.gpsimd.memset(stg2[:], 0.0), writes=["stg2"])

        cdom = S.new_dma_dom("cst")
        S.dma("sp", stg1[:], dr["ssm_conv_w"][0].rearrange("k (c p) -> (k c) p", p=128), [], ["stg1"], cdom)
        rows = [("ssm_conv_b", dr["ssm_conv_b"][0], 32, 0), ("ssm_norm_w", dr["ssm_norm_w"][0], 16, 32),
                ("ln_mix_g", dr["ln_mix_g"].rearrange("l d -> (l d)"), 16, 48), ("ln_mix_b", dr["ln_mix_b"].rearrange("l d -> (l d)"), 16, 64),
                ("ln_ffn_g", dr["ln_ffn_g"].rearrange("l d -> (l d)"), 16, 80), ("ln_ffn_b", dr["ln_ffn_b"].rearrange("l d -> (l d)"), 16, 96)]
        for (_, src, n, r0) in rows:
            S.dma("sp", stg2[r0:r0 + n, :], src.rearrange("(c p) -> c p", p=128), [], ["stg2"], cdom)
        S.dma("sp", dtb_bc[:], dr["ssm_dt_bias"].partition_broadcast(128), [], ["dtb_bc"], cdom)
        S.dma("sp", A_bc[:], dr["ssm_a_log"].partition_broadcast(128), [], ["A_bc"], cdom)
        S.dma("sp", D_bc[:], dr["ssm_d"].partition_broadcast(128), [], ["D_bc"], cdom)
        S.dma("sp", bf_col[:], dr["kv_b_f"].rearrange("(h o) -> h o", o=1), [], ["bf_col"], cdom)
        S.op("act", lambda: nc.scalar.activation(out=A_bc[:], in_=A_bc[:], func=AF.Exp), reads=["A_bc"], writes=["A_bc"])
        S.op("dve", lambda: nc.vector.tensor_scalar_mul(out=A_bc[:], in0=A_bc[:], scalar1=-1.0), reads=["A_bc"], writes=["A_bc"])
        b0 = nb()
        S.op("pe", lambda: nc.tensor.transpose(out=ps[b0][:, 0:128], in_=stg1[:], identity=identf[:]), reads=["stg1", "identf"], writes=[pk(b0)])
        S.op("dve", lambda: nc.vector.tensor_copy(out=prmA[:], in_=ps[b0][:, 0:128]), reads=[pk(b0)], writes=["prmA"])
        b1 = nb()
        S.op("pe", lambda: nc.tensor.transpose(out=ps[b1][:, 0:128], in_=stg2[:], identity=identf[:]), reads=["stg2", "identf"], writes=[pk(b1)])
        S.op("dve", lambda: nc.vector.tensor_copy(out=prmB[:], in_=ps[b1][:, 0:128]), reads=[pk(b1)], writes=["prmB"])
        S.op("dve", lambda: nc.vector.tensor_scalar_mul(out=prmA[:], in0=prmA[:], scalar1=0.5), reads=["prmA"], writes=["prmA"])
        S.op("dve", lambda: nc.vector.tensor_scalar_mul(out=prmB[:, 0:32], in0=prmB[:, 0:32], scalar1=0.5), reads=["prmB"], writes=["prmB"])
        cw = prmA[:].rearrange("p (k c) -> p k c", k=4)
        cb = prmB[:, 0:32]
        normw = prmB[:, 32:48]
        lng = {0: prmB[:, 48:56], 1: prmB[:, 80:88], 2: prmB[:, 56:64], 3: prmB[:, 88:96]}
        lnbias = {0: prmB[:, 64:72], 1: prmB[:, 96:104], 2: prmB[:, 72:80], 3: prmB[:, 104:112]}

        import os
        wcdom = S.new_dma_dom("wcv")
        stf = [scr16[:], xT[:].rearrange("p k t -> p (k t)")]
        stb = [big[:, 0:8, :].rearrange("p k t -> p (k t)"), big[:, 8:16, :].rearrange("p k t -> p (k t)")]
        ktf = KT[:].rearrange("p k t -> p (k t)").bitcast(F32)
        for i in range(ktf.shape[1] // 4096):
            stf.append(ktf[:, i * 4096:(i + 1) * 4096])
        vaf = VA[:].rearrange("p a b c -> p (a b c)")
        for i in range(vaf.shape[1] // 4096):
            stb.append(vaf[:, i * 4096:(i + 1) * 4096])
        NST = min(len(stf), len(stb), 4)
        cvl = [S.new_dma_dom(f"cvl{i}") for i in range(NST)]
        cvs = [S.new_dma_dom(f"cvs{i}") for i in range(NST)]
        cast_eng = ["dve", "act", "pool"]
        for ti, (name, nel, parts) in enumerate(tiles):
            if os.environ.get("SKIP_CONV"):
                break
            sl = ti % NST
            npart = 64 if name[0] == "o" else 128
            for part in parts:
                (src, idx, c0, n, kind, base, cstride, coff) = part[:8]
                w = dr[src] if idx is None else dr[src][idx]
                if kind == "kpc_rows":
                    k0, nk = part[8], part[9]
                    s_ap = w[k0 * 128:(k0 + nk) * 128, c0:c0 + n].rearrange("(k p) c -> p k c", p=128)
                    d_ap = stf[sl][:, base:base + nk * cstride].rearrange("p (k c) -> p k c", c=cstride)[:, :, coff:coff + n]
                elif kind == "kpc":
                    nk = w.shape[0] // 128
                    s_ap = w[:, c0:c0 + n].rearrange("(k p) c -> p k c", p=128)
                    d_ap = stf[sl][:, base:base + nk * cstride].rearrange("p (k c) -> p k c", c=cstride)[:, :, coff:coff + n]
                else:
                    s_ap = w[:, c0:c0 + n].rearrange("(h p) c -> p h c", p=64)
                    d_ap = stf[sl][0:64, base:base + 16 * cstride].rearrange("p (h c) -> p h c", c=cstride)[:, :, coff:coff + n]
                S.dma("sp", d_ap, s_ap, [], [("stf", sl)], cvl[sl])
            ce = cast_eng[ti % 3]
            if ce == "dve":
                S.op("dve", lambda: nc.vector.tensor_copy(out=stb[sl][0:npart, 0:nel], in_=stf[sl][0:npart, 0:nel]), reads=[("stf", sl)], writes=[("stb", sl)])
            elif ce == "act":
                S.op("act", lambda: nc.scalar.copy(out=stb[sl][0:npart, 0:nel], in_=stf[sl][0:npart, 0:nel]), reads=[("stf", sl)], writes=[("stb", sl)])
            else:
                S.op("pool", lambda: nc.gpsimd.tensor_copy(out=stb[sl][0:npart, 0:nel], in_=stf[sl][0:npart, 0:nel]), reads=[("stf", sl)], writes=[("stb", sl)])
            S.dma("sp", wscr[ti][0:npart, 0:nel], stb[sl][0:npart, 0:nel], [("stb", sl)], [("wscr", ti)], cvs[sl])
        S.dma("pool", wdt[:], dr["ssm_in_w"][0][:, 6144:6176].rearrange("(k p) c -> p k c", p=128), [], ["wdt"], wcdom)
        S.dma("pool", wf[:], dr["kv_w"][:, 2048:2064].rearrange("(k p) c -> p k c", p=128), [], ["wf"], wcdom)
        S.wait_all("pool", cvs + cvl)
        S.op("pool", lambda: nc.gpsimd.memset(VA[:], 1.0), reads=[("stb", i) for i in range(NST)], writes=[("VA", kt) for kt in range(NKT)])
        S.wait_all("pe", cvs + cvl)
        S.wait_all("act", cvs + cvl)
        S.wait_all("dve", cvs + cvl)
        S.wait_all("pool", cvs + cvl)

        wdoms = [S.new_dma_dom(f"w{i}") for i in range(NSLOT)]
        wstate = {"next_load": 0, "next_use": 0}
        total_tiles = NB * NSC * NT

        def prefetch():
            i = wstate["next_load"]
            if i >= total_tiles:
                return
            wstate["next_load"] += 1
            ti = i % NT
            s = i % NSLOT
            nel = tiles[ti][1]
            npart = 64 if tiles[ti][0][0] == "o" else 128
            S.dma("sp", wslot[s][0:npart, 0:nel], wscr[ti][0:npart, 0:nel], [("wscr", ti)], [("wslot", s)], wdoms[s])

        def use_tile(expect):
            if stop_after is not None:
                while tiles[wstate["next_use"] % NT][0] != expect:
                    wstate["next_use"] += 1
                    prefetch()
            i = wstate["next_use"]
            wstate["next_use"] += 1
            ti = i % NT
            assert tiles[ti][0] == expect, (tiles[ti][0], expect)
            s = i % NSLOT
            return wslot[s], ("wslot", s)

        def done_tile():
            prefetch()

        for _ in range(NSLOT):
            prefetch()

        iodom_in = S.new_dma_dom("xin")
        iodom_out = S.new_dma_dom("xout")
        dbgdom = S.new_dma_dom("dbg")

        def tap(name, ap, key, shape):
            if not debug:
                return
            if name not in dbg:
                dbg[name] = nc.dram_tensor(name, [NB * NSC] + list(shape), ap.dtype, kind="ExternalOutput").ap()
            S.dma("sp", dbg[name][tap.idx], ap, [key], [], dbgdom)
        tap.idx = 0

        def ln_accum(k, bank, ln_idx):
            S.op("dve", lambda: nc.vector.scalar_tensor_tensor(out=xT[:, k, :], in0=xT[:, k, :], scalar=ALPHA, in1=ps[bank][:],
                                                               op0=ALU.mult, op1=ALU.add),
                 reads=[("xT", k), pk(bank)], writes=[("xT", k)])
            S.op("act", lambda: nc.scalar.copy(out=lnb[:, k, :], in_=xT[:, k, :]), reads=[("xT", k)], writes=[("lnb", k)])
            S.op("act", lambda: nc.scalar.activation(out=lnsq[:, k, :], in_=xT[:, k, :], func=AF.Square), reads=[("xT", k)], writes=[("lnsq", k)])

        def ln_finish(ln_idx):
            bm, be = nb(), nb()
            for k in range(KD):
                S.op("pe", lambda: nc.tensor.matmul(ps[bm][:], lhsT=lnones[:], rhs=lnb[:, k, :], start=(k == 0), stop=(k == KD - 1)),
                     reads=[("lnb", k), "lnones"], writes=[pk(bm)], inc=(k == KD - 1))
            for k in range(KD):
                S.op("pe", lambda: nc.tensor.matmul(ps[be][:], lhsT=lnones[:], rhs=lnsq[:, k, :], start=(k == 0), stop=(k == KD - 1)),
                     reads=[("lnsq", k), "lnones"], writes=[pk(be)], inc=(k == KD - 1))
            S.op("act", lambda: nc.scalar.copy(out=mean_sb[:], in_=ps[bm][:]), reads=[pk(bm)], writes=["mean_sb"])
            S.op("dve", lambda: nc.vector.tensor_tensor(out=rstd_sb[:], in0=ps[bm][:], in1=mean_sb[:], op=ALU.mult),
                 reads=[pk(bm), "mean_sb"], writes=["rstd_sb"])
            S.op("dve", lambda: nc.vector.tensor_tensor(out=rstd_sb[:], in0=ps[be][:], in1=rstd_sb[:], op=ALU.subtract),
                 reads=[pk(be), "rstd_sb"], writes=["rstd_sb"])
            S.op("act", lambda: nc.scalar.activation(out=rstd_sb[:], in_=rstd_sb[:], func=AF.Sqrt, bias=lneps[:, 0:1]),
                 reads=["rstd_sb", "lneps"], writes=["rstd_sb"])
            S.op("dve", lambda: nc.vector.reciprocal(out=rstd_sb[:], in_=rstd_sb[:]), reads=["rstd_sb"], writes=["rstd_sb"])
            for k in range(KD):
                t1, t2 = lnt[0], lnt[1]
                k1, k2 = ("acc", 0), ("acc", 1)
                S.op("dve", lambda: nc.vector.tensor_tensor(out=t1[:], in0=xT[:, k, :], in1=mean_sb[:], op=ALU.subtract),
                     reads=[("xT", k), "mean_sb"], writes=[k1])
                S.op("pool", lambda: nc.gpsimd.tensor_tensor(out=t2[:], in0=t1[:], in1=rstd_sb[:], op=ALU.mult),
                     reads=[k1, "rstd_sb"], writes=[k2])
                S.op("act", lambda: nc.scalar.activation(out=xT[:, k, :], in_=t2[:], func=AF.Identity, scale=lng[ln_idx][:, k:k + 1],
                                                         bias=lnbias[ln_idx][:, k:k + 1]),
                     reads=[k2, "prmB"], writes=[("xT", k)])
                S.op("act", lambda: nc.scalar.activation(out=xTb[:, k, :], in_=t2[:], func=AF.Identity, scale=lng[ln_idx][:, k:k + 1],
                                                         bias=lnbias[ln_idx][:, k:k + 1]),
                     reads=[k2, "prmB"], writes=[("xTb", k)])

        S.op("pool", lambda: nc.gpsimd.memset(lneps[:, 0:1], LN_EPS), writes=["lneps"])
        S.op("pool", lambda: nc.gpsimd.memset(lneps[:, 1:2], 4.0 * RMS_EPS), writes=["lneps"])

        sg = [acc[0][:].bitcast(BF16)[:, 0:TS], acc[0][:].bitcast(BF16)[:, TS:2 * TS], acc[1][:].bitcast(BF16)[:, 0:TS], acc[1][:].bitcast(BF16)[:, TS:2 * TS]]

        def ffn_phase(l, ln_idx):
            for j in range(6):
                nfc = 4 if j < 5 else 2
                slot, skey = use_tile(("g", l, j))
                gv = slot[:, 0:4096].rearrange("p (k c) -> p k c", c=512)
                gb_ = []
                for fc in range(nfc):
                    bg = nb()
                    gb_.append(bg)
                    for k in range(KD):
                        S.op("pe", lambda: nc.tensor.matmul(ps[bg][:], lhsT=gv[:, k, fc * 128:(fc + 1) * 128], rhs=xTb[:, k, :], start=(k == 0), stop=(k == KD - 1)),
                             reads=[skey, ("xTb", k)], writes=[pk(bg)], inc=(k == KD - 1))
                    S.op("act", lambda: nc.scalar.activation(out=sg[fc], in_=ps[bg][:], func=AF.Silu), reads=[pk(bg)], writes=[("acc", fc // 2)])
                done_tile()
                slot, skey = use_tile(("u", l, j))
                uv = slot[:, 0:4096].rearrange("p (k c) -> p k c", c=512)
                for fc in range(nfc):
                    f = 4 * j + fc
                    bu = nb()
                    for k in range(KD):
                        S.op("pe", lambda: nc.tensor.matmul(ps[bu][:], lhsT=uv[:, k, fc * 128:(fc + 1) * 128], rhs=xTb[:, k, :], start=(k == 0), stop=(k == KD - 1)),
                             reads=[skey, ("xTb", k)], writes=[pk(bu)], inc=(k == KD - 1))
                    S.op("dve", lambda: nc.vector.tensor_tensor(out=big[:, f, :], in0=sg[fc], in1=ps[bu][:], op=ALU.mult),
                         reads=[("acc", fc // 2), pk(bu)], writes=[("big", f)])
                done_tile()
            for hf in range(2):
                banks = [nb() for _ in range(4)]
                for fg in range(3):
                    nf = 8 if fg < 2 else 6
                    slot, skey = use_tile(("dn", l, hf, fg))
                    dv = slot[:, 0:nf * 512].rearrange("p (f c) -> p f c", c=512)
                    for c in range(4):
                        for fl in range(nf):
                            f = fg * 8 + fl
                            S.op("pe", lambda: nc.tensor.matmul(ps[banks[c]][:], lhsT=dv[:, fl, c * 128:(c + 1) * 128], rhs=big[:, f, :],
                                                                start=(f == 0), stop=(f == NF - 1)),
                                 reads=[skey, ("big", f)], writes=[pk(banks[c])], inc=(fl == nf - 1))
                    done_tile()
                for c in range(4):
                    ln_accum(4 * hf + c, banks[c], ln_idx)
            ln_finish(ln_idx)

        def ssd_phase(first_in_seq):
            bd = nb()
            for tt in range(4):
                for k in range(KD):
                    S.op("pe", lambda: nc.tensor.matmul(ps[bd][:, tt * 32:(tt + 1) * 32], lhsT=xTb[:, k, tt * 128:(tt + 1) * 128], rhs=wdt[:, k, :],
                                                        start=(k == 0), stop=(k == KD - 1)),
                         reads=[("xTb", k), "wdt"], writes=[pk(bd)], inc=(tt == 3 and k == KD - 1))
            pd = ps[bd][:, 0:128].rearrange("p (t h) -> p t h", h=32)
            v_, av_, l_ = sp_t
            S.op("dve", lambda: nc.vector.tensor_tensor(out=v_[:], in0=pd, in1=bc(dtb_bc[:].unsqueeze(1), [128, 4, 32]), op=ALU.add),
                 reads=[pk(bd), "dtb_bc"], writes=["sp_v"])
            S.op("act", lambda: nc.scalar.activation(out=av_[:], in_=v_[:], func=AF.Abs), reads=["sp_v"], writes=["sp_a"])
            S.op("act", lambda: nc.scalar.activation(out=av_[:], in_=av_[:], func=AF.Exp, scale=-1.0), reads=["sp_a"], writes=["sp_a"])
            S.op("act", lambda: nc.scalar.activation(out=l_[:], in_=av_[:], func=AF.Ln, bias=1.0), reads=["sp_a"], writes=["sp_l"])
            S.op("dve", lambda: nc.vector.scalar_tensor_tensor(out=dt_sb[:], in0=v_[:], scalar=0.0, in1=l_[:], op0=ALU.max, op1=ALU.add),
                 reads=["sp_v", "sp_l"], writes=["dt_sb"])
            S.op("act", lambda: nc.scalar.activation(out=l_[:], in_=dt_sb[:], func=AF.Ln), reads=["dt_sb"], writes=["sp_l"])
            S.op("dve", lambda: nc.vector.tensor_tensor(out=a_sb[:], in0=dt_sb[:], in1=bc(A_bc[:].unsqueeze(1), [128, 4, 32]), op=ALU.mult),
                 reads=["dt_sb", "A_bc"], writes=["a_sb"])
            bcu = nb()
            for c in range(4):
                S.op("pe", lambda: nc.tensor.matmul(ps[bcu][:, c * 32:(c + 1) * 32], lhsT=trif[:], rhs=a_sb[:, c, :], start=True, stop=True),
                     reads=["trif", "a_sb"], writes=[pk(bcu)], inc=(c == 3))
            pc = ps[bcu][:, 0:128].rearrange("p (t h) -> p t h", h=32)
            S.op("dve", lambda: nc.vector.tensor_tensor(out=cumcolp[:], in0=pc, in1=l_[:], op=ALU.subtract), reads=[pk(bcu), "sp_l"], writes=["cumcolp"])
            S.op("act", lambda: nc.scalar.activation(out=expcum[:], in_=pc, func=AF.Exp), reads=[pk(bcu)], writes=["expcum"])
            if first_in_seq:
                S.op("pool", lambda: nc.gpsimd.memset(stateT[:], 0.0), writes=[("stateT", g) for g in range(NG)])
                S.op("pool", lambda: nc.gpsimd.memset(stbf[:], 0.0), writes=[("stbf", g) for g in range(NG)])
                S.op("pool", lambda: nc.gpsimd.memset(halo[:], 0.0), writes=[("halo", ci) for ci in range(32)])

            def inproj_pieces(g):
                gb = g % 2
                zs_, xbc_ = zs[gb], xbc[gb]
                st = {}

                def p_open_a():
                    st["slot"], st["skey"] = use_tile(("inA", g))
                    st["Wv"] = st["slot"][:, 0:8 * 512].rearrange("p (k c) -> p k c", c=512)

                def p_z(half):
                    def f():
                        Wv, skey = st["WvA"], st["skeyA"]
                        bz = nb()
                        for t2 in range(2):
                            tt = 2 * half + t2
                            for k in range(KD):
                                S.op("pe", lambda: nc.tensor.matmul(ps[bz][:, t2 * 256:(t2 + 1) * 256], lhsT=xTb[:, k, tt * 128:(tt + 1) * 128],
                                                                    rhs=Wv[:, k, 0:256], start=(k == 0), stop=(k == KD - 1)),
                                     reads=[skey, ("xTb", k)], writes=[pk(bz)], inc=(t2 == 1 and k == KD - 1))
                        S.op("act", lambda: nc.scalar.activation(out=acc[half][:], in_=ps[bz][:], func=AF.Tanh, scale=0.5), reads=[pk(bz)], writes=[("acc", half)])
                        S.op("dve", lambda: nc.vector.scalar_tensor_tensor(out=zs_[:, 2 * half:2 * half + 2, :].rearrange("p t c -> p (t c)"), in0=acc[half][:], scalar=1.0,
                                                                           in1=ps[bz][:], op0=ALU.add, op1=ALU.mult),
                             reads=[("acc", half), pk(bz)], writes=[("zs", gb)])
                        if half == 1:
                            done_tile()
                            done_tile()
                    return f

                def p_x(r):
                    def f():
                        ci = (2 * g + r) if r < 2 else (16 + g if r == 2 else 24 + g)
                        if r == 0:
                            p_open_a()
                            st["WvA"], st["skeyA"] = st["Wv"], st["skey"]
                        if r == 2:
                            st["slot"], st["skey"] = use_tile(("inB", g))
                            st["Wv"] = st["slot"][:, 0:8 * 256].rearrange("p (k c) -> p k c", c=256)
                        Wv, skey = st["Wv"], st["skey"]
                        wcol = (256 + r * 128) if r < 2 else (r - 2) * 128
                        bx = nb()
                        for k in range(KD):
                            S.op("pe", lambda: nc.tensor.matmul(ps[bx][:], lhsT=Wv[:, k, wcol:wcol + 128], rhs=xTb[:, k, :],
                                                                start=(k == 0), stop=(k == KD - 1)),
                                 reads=[skey, ("xTb", k)], writes=[pk(bx)], inc=(k == KD - 1))
                        ur = r % 2
                        uk = ("ubuf", ur)
                        S.op("pool", lambda: nc.gpsimd.tensor_copy(out=ubuf[:, ur, 0:3], in_=halo[:, ci, :]), reads=[("halo", ci)], writes=[uk])
                        S.op("act", lambda: nc.scalar.copy(out=ubuf[:, ur, 3:TS + 3], in_=ps[bx][:]), reads=[pk(bx)], writes=[uk])
                        a_ = acc[r % 2]
                        ak = ("acc", r % 2)
                        S.op("act", lambda: nc.scalar.activation(out=a_[:], in_=ps[bx][:], func=AF.Identity, scale=cw[:, 3, ci:ci + 1], bias=cb[:, ci:ci + 1]),
                             reads=[pk(bx), "prmA", "prmB"], writes=[ak])
                        for kk in range(3):
                            S.op("dve", lambda: nc.vector.scalar_tensor_tensor(out=a_[:], in0=ubuf[:, ur, kk:kk + TS], scalar=cw[:, kk, ci:ci + 1], in1=a_[:],
                                                                               op0=ALU.mult, op1=ALU.add), reads=[uk, ak, "prmA"], writes=[ak])
                        S.op("pool", lambda: nc.gpsimd.tensor_copy(out=halo[:, ci, :], in_=ubuf[:, ur, TS:TS + 3]), reads=[uk], writes=[("halo", ci)])
                        S.op("act", lambda: nc.scalar.activation(out=ubuf[:, ur, 0:TS], in_=a_[:], func=AF.Tanh), reads=[ak], writes=[uk])
                        S.op("dve", lambda: nc.vector.scalar_tensor_tensor(out=xbc_[:, r, :], in0=ubuf[:, ur, 0:TS], scalar=1.0, in1=a_[:],
                                                                           op0=ALU.add, op1=ALU.mult), reads=[uk, ak], writes=[("xbc", gb, r)])
                    return f
                return [p_x(0), p_x(1), p_x(2), p_x(3), p_z(0), p_z(1)]

            cst = {}

            def stage_A(g, c):
                gb = g % 2
                xbc_ = xbc[gb]
                cs_ = cset[c % 2]
                ck = ("cs", c % 2)
                hs = slice(4 * g, 4 * g + 4)
                cs = slice(c * 128, (c + 1) * 128)
                bt = nb()
                T1 = psb(bt)
                for j, r in enumerate((0, 1, 2)):
                    S.op("pe", lambda: nc.tensor.transpose(out=T1[:, j * 128:(j + 1) * 128], in_=xbc_[:, r, cs], identity=identb[:]),
                         reads=[("xbc", gb, r), "identb"], writes=[pk(bt)], inc=(j == 2))
                T1x = T1[:, 0:256].rearrange("p (h q) -> p h q", q=64)
                S.op("act", lambda: nc.scalar.copy(out=cs_["xtok"][:], in_=T1[:, 0:256]), reads=[pk(bt)], writes=[(ck, "xtok")])
                S.op("pool", lambda: nc.gpsimd.tensor_tensor(out=cs_["xD"][:].rearrange("p (h q) -> p h q", q=64),
                                                             in0=cs_["xtok"][:].rearrange("p (h q) -> p h q", q=64),
                                                             in1=bc(D_bc[:, hs].unsqueeze(2), [128, 4, 64]), op=ALU.mult),
                     reads=[(ck, "xtok"), "D_bc"], writes=[(ck, "xD")])
                S.op("act", lambda: nc.scalar.copy(out=cs_["btok"][:], in_=T1[:, 256:384]), reads=[pk(bt)], writes=[(ck, "btok")])
                b1_ = nb()
                S.op("pe", lambda: nc.tensor.matmul(ps[b1_][:], lhsT=onesf[:], rhs=cs_["atri"][:], start=True, stop=True),
                     reads=["onesf", (ck, "atri")], writes=[pk(b1_)])
                cst[(g, c)] = dict(b1=b1_)

            def stage_A1(g, c):
                cs_ = cset[c % 2]
                ck = ("cs", c % 2)
                hs = slice(4 * g, 4 * g + 4)
                S.op("pool", lambda: nc.gpsimd.tensor_tensor(out=cs_["atri"][:].rearrange("p (h l) -> p h l", l=128),
                                                             in0=bc(trif[:].unsqueeze(1), [128, 4, 128]),
                                                             in1=bc(a_sb[:, c, hs].unsqueeze(2), [128, 4, 128]), op=ALU.mult),
                     reads=["trif", "a_sb"], writes=[(ck, "atri")])

            def stage_B1(g, c):
                gb = g % 2
                xbc_ = xbc[gb]
                cs_ = cset[c % 2]
                ck = ("cs", c % 2)
                hs = slice(4 * g, 4 * g + 4)
                cs = slice(c * 128, (c + 1) * 128)
                b1_ = cst[(g, c)]["b1"]
                X1 = ps[b1_][:].rearrange("p (h l) -> p h l", l=128)
                seg = cs_["atri"]
                S.op("dve", lambda: nc.vector.tensor_tensor(out=seg[:].rearrange("p (h l) -> p h l", l=128), in0=X1,
                                                            in1=bc(cumcolp[:, c, hs].unsqueeze(2), [128, 4, 128]), op=ALU.subtract),
                     reads=[pk(b1_), "cumcolp"], writes=[(ck, "atri")])
                S.op("act", lambda: nc.scalar.activation(out=e4[c % 2][:], in_=X1[:, :, 127], func=AF.Exp), reads=[pk(b1_)], writes=[("e4", c % 2)])
                seg3 = seg[:].rearrange("p (h l) -> p h l", l=128)
                S.op("pool", lambda: nc.gpsimd.affine_select(out=seg3, in_=seg3, pattern=[[0, 4], [1, 128]], compare_op=ALU.is_ge,
                                                             fill=neg_reg, base=0, channel_multiplier=-1), reads=[(ck, "atri")], writes=[(ck, "atri")])
                S.op("act", lambda: nc.scalar.activation(out=cs_["decayT"][:], in_=seg[:], func=AF.Exp), reads=[(ck, "atri")], writes=[(ck, "decayT")])
                b2_ = nb()
                S.op("pe", lambda: nc.tensor.matmul(ps[b2_][:, 0:128], lhsT=xbc_[:, 2, cs], rhs=xbc_[:, 3, cs], start=True, stop=True),
                     reads=[("xbc", gb, 2), ("xbc", gb, 3)], writes=[pk(b2_)])
                cst[(g, c)]["b2"] = b2_

            def stage_B2(g, c):
                cs_ = cset[c % 2]
                ck = ("cs", c % 2)
                b2_ = cst[(g, c)]["b2"]
                S.op("dve", lambda: nc.vector.tensor_tensor(out=cs_["GT"][:].rearrange("p (h l) -> p h l", l=128),
                                                            in0=cs_["decayT"][:].rearrange("p (h l) -> p h l", l=128),
                                                            in1=bc(ps[b2_][:, 0:128].unsqueeze(1), [128, 4, 128]), op=ALU.mult),
                     reads=[(ck, "decayT"), pk(b2_)], writes=[(ck, "GT")])
                dlast = cs_["decayT"][:].rearrange("p (h l) -> p h l", l=128)[:, :, 127:128]
                S.op("dve", lambda: nc.vector.tensor_tensor(out=cs_["xw"][:].rearrange("p (h q) -> p h q", q=64),
                                                            in0=cs_["xtok"][:].rearrange("p (h q) -> p h q", q=64),
                                                            in1=bc(dlast, [128, 4, 64]), op=ALU.mult),
                     reads=[(ck, "xtok"), (ck, "decayT")], writes=[(ck, "xw")])
                b3_ = nb()
                S.op("pe", lambda: nc.tensor.matmul(ps[b3_][:, 0:256], lhsT=identb[:], rhs=cs_["xD"][:], start=True, stop=False),
                     reads=["identb", (ck, "xD")], writes=[pk(b3_)], inc=False)
                for h in range(4):
                    S.op("pe", lambda: nc.tensor.matmul(ps[b3_][:, h * 64:(h + 1) * 64], lhsT=cs_["GT"][:, h * 128:(h + 1) * 128],
                                                        rhs=cs_["xtok"][:, h * 64:(h + 1) * 64], start=False, stop=True),
                         reads=[(ck, "GT"), (ck, "xtok")], writes=[pk(b3_)], inc=False)
                S.op("pe", lambda: nc.tensor.matmul(ps[b3_][:, 256:512], lhsT=cs_["btok"][:], rhs=cs_["xw"][:], start=True, stop=True),
                     reads=[(ck, "btok"), (ck, "xw")], writes=[pk(b3_)])
                cst[(g, c)]["b3"] = b3_

            def stage_C(g, c):
                gb = g % 2
                xbc_ = xbc[gb]
                hs = slice(4 * g, 4 * g + 4)
                cs = slice(c * 128, (c + 1) * 128)
                b3_ = cst[(g, c)]["b3"]
                st_g = stateT[:, g * 256:(g + 1) * 256]
                stb_g = stbf[:, g * 256:(g + 1) * 256]
                b4_ = nb()
                S.op("pe", lambda: nc.tensor.matmul(ps[b4_][:, 0:256], lhsT=xbc_[:, 3, cs], rhs=stb_g, start=True, stop=True),
                     reads=[("xbc", gb, 3), ("stbf", g)], writes=[pk(b4_)])
                S.op("pool", lambda: nc.gpsimd.tensor_tensor(out=sttmp[:].rearrange("p (h q) -> p h q", q=64),
                                                             in0=st_g.rearrange("p (h q) -> p h q", q=64),
                                                             in1=bc(e4[c % 2][:].unsqueeze(2), [128, 4, 64]), op=ALU.mult),
                     reads=[("stateT", g), ("e4", c % 2)], writes=["sttmp"])
                S.op("dve", lambda: nc.vector.tensor_tensor(out=st_g, in0=sttmp[:], in1=ps[b3_][:, 256:512], op=ALU.add),
                     reads=["sttmp", pk(b3_)], writes=[("stateT", g)])
                S.op("dve", lambda: nc.vector.tensor_tensor(out=ys[:].rearrange("p (h q) -> p h q", q=64),
                                                            in0=ps[b4_][:, 0:256].rearrange("p (h q) -> p h q", q=64),
                                                            in1=bc(expcum[:, c, hs].unsqueeze(2), [128, 4, 64]), op=ALU.mult),
                     reads=[pk(b4_), "expcum"], writes=["ys"])
                S.op("act", lambda: nc.scalar.copy(out=stb_g, in_=st_g), reads=[("stateT", g)], writes=[("stbf", g)])
                S.op("dve", lambda: nc.vector.tensor_tensor(out=ysum[:], in0=ps[b3_][:, 0:256], in1=ys[:], op=ALU.add),
                     reads=[pk(b3_), "ys"], writes=["ysum"])
                S.op("pool", lambda: nc.gpsimd.tensor_tensor(out=yg[:, c, :], in0=ysum[:], in1=zs[gb][:, c, :], op=ALU.mult),
                     reads=["ysum", ("zs", gb)], writes=[("yg", c)])
                S.op("act", lambda: nc.scalar.activation(out=junk[:], in_=yg[:, c, :], func=AF.Square, accum_out=ss[:, c:c + 1]),
                     reads=[("yg", c)], writes=["junk", ("ss", c)])

            def group_end1(g):
                S.op("act", lambda: nc.scalar.activation(out=sd4[:], in_=ss[:], func=AF.Sqrt, scale=1.0 / 256.0, bias=lneps[:, 1:2]),
                     reads=[("ss", c) for c in range(4)] + ["lneps"], writes=["sd4"])
                S.op("dve", lambda: nc.vector.reciprocal(out=rstd4[:], in_=sd4[:]), reads=["sd4"], writes=["rstd4"])
                for c in range(4):
                    S.op("act", lambda: nc.scalar.activation(out=ygn[:, c, :], in_=yg[:, c, :], func=AF.Copy, scale=rstd4[:, c:c + 1]),
                         reads=[("yg", c), "rstd4"], writes=[("ygn", c)])

            def group_end2(g):
                bn_ = nb()
                Tn = psb(bn_)
                for c in range(4):
                    for j in range(2):
                        S.op("pe", lambda: nc.tensor.transpose(out=Tn[:, (j * 4 + c) * 128:(j * 4 + c + 1) * 128], in_=ygn[:, c, j * 128:(j + 1) * 128],
                                                               identity=identb[:]),
                             reads=[("ygn", c), "identb"], writes=[pk(bn_)], inc=(c == 3 and j == 1))
                for j in range(2):
                    kc = 2 * g + j
                    S.op("dve", lambda: nc.vector.tensor_scalar(out=big[:, kc, :], in0=Tn[:, j * 512:(j + 1) * 512], scalar1=normw[:, kc:kc + 1],
                                                                scalar2=None, op0=ALU.mult),
                         reads=[pk(bn_), "prmB"], writes=[("big", kc)])

            for f in inproj_pieces(0):
                f()
            for g in range(NG):
                P = inproj_pieces(g + 1) if g + 1 < NG else []
                P = P + [lambda: None] * (6 - len(P))
                for f in P:
                    f()
                if g > 0:
                    group_end2(g - 1)
                order = []
                for it in range(-3, 5):
                    for (fn, c) in ((stage_C, it - 1), (stage_B2, it), (stage_B1, it + 1), (stage_A, it + 2), (stage_A1, it + 3)):
                        if 0 <= c <= 3:
                            order.append((lambda fn=fn, c=c: fn(g, c)))
                order.append(lambda: group_end1(g))
                for f in order:
                    f()
            group_end2(NG - 1)
            S.fence()
            for j in range(4):
                slot, skey = use_tile(("out", j))
                ov = slot[:, 0:16 * 256].rearrange("p (k c) -> p k c", c=256)
                for c in range(2):
                    k = 2 * j + c
                    b = nb()
                    for kk in range(16):
                        S.op("pe", lambda: nc.tensor.matmul(ps[b][:], lhsT=ov[:, kk, c * 128:(c + 1) * 128], rhs=big[:, kk, :],
                                                            start=(kk == 0), stop=(kk == 15)),
                             reads=[skey, ("big", kk)], writes=[pk(b)], inc=(kk == 15))
                    ln_accum(k, b, 0)
                done_tile()
            ln_finish(0)

        def attn_phase(sc, first_in_seq):
            t0 = sc * TS
            for j in range(2):
                slot, skey = use_tile(("kvk", j))
                kvv_ = slot[:, 0:4096].rearrange("p (k c) -> p k c", c=512)
                for pr in range(4):
                    b = nb()
                    for k in range(KD):
                        S.op("pe", lambda: nc.tensor.matmul(ps[b][:], lhsT=kvv_[:, k, pr * 128:(pr + 1) * 128], rhs=xTb[:, k, :], start=(k == 0), stop=(k == KD - 1)),
                             reads=[skey, ("xTb", k)], writes=[pk(b)], inc=(k == KD - 1))
                    S.op("act", lambda: nc.scalar.copy(out=KT[:, 4 * j + pr, t0:t0 + TS], in_=ps[b][:]), reads=[pk(b)], writes=[("KT", 4 * j + pr)])
                done_tile()
            for j in range(2):
                slot, skey = use_tile(("kvv", j))
                vv = slot[:, 0:4096].rearrange("p (k c) -> p k c", c=512)
                for tt in range(4):
                    b = nb()
                    for k in range(KD):
                        S.op("pe", lambda: nc.tensor.matmul(ps[b][:], lhsT=xTb[:, k, tt * 128:(tt + 1) * 128], rhs=vv[:, k, :],
                                                            start=(k == 0), stop=(k == KD - 1)),
                             reads=[skey, ("xTb", k)], writes=[pk(b)], inc=(k == KD - 1))
                    kt = 4 * sc + tt
                    S.op("dve", lambda: nc.vector.tensor_copy(out=VA[:, kt, 8 * j:8 * j + 8, 0:64], in_=ps[b][:].rearrange("p (h q) -> p h q", q=64)),
                         reads=[pk(b)], writes=[("VA", kt)])
                done_tile()
            bf_ = nb()
            for k in range(KD):
                S.op("pe", lambda: nc.tensor.matmul(ps[bf_][0:16, :], lhsT=wf[:, k, :], rhs=xTb[:, k, :], start=(k == 0), stop=(k == KD - 1)),
                     reads=["wf", ("xTb", k)], writes=[pk(bf_)], inc=(k == KD - 1))
            S.op("dve", lambda: nc.vector.tensor_scalar(out=f_v[:], in0=ps[bf_][0:16, :], scalar1=bf_col[:, 0:1], scalar2=None, op0=ALU.add),
                 reads=[pk(bf_), "bf_col"], writes=["f_v"])
            S.op("act", lambda: nc.scalar.activation(out=f_a[:], in_=f_v[:], func=AF.Abs), reads=["f_v"], writes=["f_a"])
            S.op("act", lambda: nc.scalar.activation(out=f_a[:], in_=f_a[:], func=AF.Exp, scale=-1.0), reads=["f_a"], writes=["f_a"])
            S.op("act", lambda: nc.scalar.activation(out=f_l[:], in_=f_a[:], func=AF.Ln, bias=1.0), reads=["f_a"], writes=["f_l"])
            S.op("dve", lambda: nc.vector.scalar_tensor_tensor(out=f_l[:], in0=f_v[:], scalar=0.0, in1=f_l[:], op0=ALU.min, op1=ALU.subtract),
                 reads=["f_v", "f_l"], writes=["f_l"])
            if first_in_seq:
                S.op("pool", lambda: nc.gpsimd.memset(Fcarry[:], 0.0), writes=["Fcarry"])
            S.op("dve", lambda: nc.vector.tensor_tensor_scan(out=Frow[:], data0=bc(onesf[0:16, 0:1], [16, TS]), data1=f_l[:], initial=Fcarry[:, 0:1],
                                                             op0=ALU.mult, op1=ALU.add),
                 reads=["onesf", "f_l", "Fcarry"], writes=["Frow"])
            S.op("dve", lambda: nc.vector.tensor_copy(out=Fcarry[:], in_=Frow[:, TS - 1:TS]), reads=["Frow"], writes=["Fcarry"])
            bt_ = nb()
            for tt in range(4):
                S.op("pe", lambda: nc.tensor.transpose(out=ps[bt_][:, tt * 16:(tt + 1) * 16], in_=Frow[:, tt * 128:(tt + 1) * 128], identity=identf[0:16, 0:16]),
                     reads=["Frow", "identf"], writes=[pk(bt_)], inc=False)
            S.op("dve", lambda: nc.vector.tensor_scalar(out=fdiag[:], in0=identf[0:16, 0:16], scalar1=Frow[:, 255:256], scalar2=None, op0=ALU.mult),
                 reads=["identf", "Frow"], writes=["fdiag"])
            S.op("pe", lambda: nc.tensor.matmul(ps[bt_][:, 64:80], lhsT=onesf[0:16, :], rhs=fdiag[:], start=True, stop=True),
                 reads=["onesf", "fdiag"], writes=[pk(bt_)])
            S.op("dve", lambda: nc.vector.tensor_copy(out=Fcol[:, 4 * sc:4 * sc + 4, :], in_=ps[bt_][:, 0:64].rearrange("p (t h) -> p t h", h=16)),
                 reads=[pk(bt_)], writes=["Fcol"])
            S.op("dve", lambda: nc.vector.tensor_copy(out=Fq0[:], in_=ps[bt_][:, 64:80]), reads=[pk(bt_)], writes=["Fq0"])
            nkt_all = 4 * sc + 4
            S.op("dve", lambda: nc.vector.tensor_tensor(out=bcol[:, 0:nkt_all, :], in0=bc(Fq0[:].unsqueeze(1), [128, nkt_all, 16]),
                                                        in1=Fcol[:, 0:nkt_all, :], op=ALU.subtract),
                 reads=["Fq0", "Fcol"], writes=["bcol"])
            for j in range(2):
                slot, skey = use_tile(("q", j))
                qvv_ = slot[:, 0:4096].rearrange("p (k c) -> p k c", c=512)
                for pr in range(4):
                    b = nb()
                    for k in range(KD):
                        S.op("pe", lambda: nc.tensor.matmul(ps[b][:], lhsT=qvv_[:, k, pr * 128:(pr + 1) * 128], rhs=xTb[:, k, :], start=(k == 0), stop=(k == KD - 1)),
                             reads=[skey, ("xTb", k)], writes=[pk(b)], inc=(k == KD - 1))
                    S.op("act", lambda: nc.scalar.activation(out=QT[:, 4 * j + pr, :], in_=ps[b][:], func=AF.Copy, scale=0.125),
                         reads=[pk(b)], writes=[("QT", 4 * j + pr)])
                done_tile()
            OT = big
            ring["n"] = 6
            ring["i"] = 0
            jobs = []
            nkt = 4 * sc + 4
            for h in range(AH):
                for kt in range(nkt):
                    jobs.append((h, kt))
            LA = 2
            NPT = 4
            pend = {}
            deferred = []

            def emit_st(i):
                h, kt = jobs[i]
                pr, po = h // 2, (h % 2) * 64
                jd = kt - 4 * sc
                c0 = 128 * jd if jd > 0 else 0
                n = TS - c0
                b = nb()
                S.op("pe", lambda: nc.tensor.matmul(ps[b][:, 0:n], lhsT=KT[po:po + 64, pr, kt * 128:(kt + 1) * 128],
                                                    rhs=QT[po:po + 64, pr, c0:TS], start=True, stop=True),
                     reads=[("KT", pr), ("QT", pr)], writes=[pk(b)])
                pend[i] = b

            def emit_rest(i):
                h, kt = jobs[i]
                ob = 6 + (h % 2)
                okey = pk(ob)
                jd = kt - 4 * sc
                c0 = 128 * jd if jd > 0 else 0
                n = TS - c0
                b = pend.pop(i)
                pt = PT[i % NPT]
                ptk = ("PT", i % NPT)
                oreg = ps[ob][0:65, :]
                S.op("act", lambda: nc.scalar.activation(out=pt[:, c0:TS], in_=ps[b][:, 0:n], func=AF.Exp, bias=bcol[:, kt, h:h + 1]),
                     reads=[pk(b), "bcol"], writes=[ptk])
                if jd >= 0:
                    S.op("pool", lambda: nc.gpsimd.affine_select(out=pt[:, c0:c0 + 128], in_=pt[:, c0:c0 + 128], pattern=[[1, 128]],
                                                                 compare_op=ALU.is_ge, fill=zero_reg, base=0, channel_multiplier=-1),
                         reads=[ptk], writes=[ptk])
                last = (kt == nkt - 1)
                S.op("pe", lambda: nc.tensor.matmul(oreg[:, c0:TS], lhsT=VA[:, kt, h, :], rhs=pt[:, c0:TS], start=(kt == 0), stop=last),
                     reads=[("VA", kt), ptk], writes=[okey], inc=last)
                if last:
                    rr_, Rs_ = rr[h % 2], Rs[h % 2]
                    S.op("dve", lambda: nc.vector.reciprocal(out=rr_[64:65, :], in_=ps[ob][64:65, :]), reads=[okey], writes=[("rr", 0)])

                    def fin(h=h, ob=ob, okey=okey, rr_=rr_, Rs_=Rs_):
                        b2 = nb()
                        S.op("pe", lambda: nc.tensor.matmul(ps[b2][0:64, :], lhsT=onesf[64:65, 0:64], rhs=rr_[64:65, :], start=True, stop=True),
                             reads=["onesf", ("rr", 0)], writes=[pk(b2)])
                        S.op("act", lambda: nc.scalar.copy(out=Rs_[:], in_=ps[b2][0:64, :]), reads=[pk(b2)], writes=[("Rs", 0)])
                        S.op("dve", lambda: nc.vector.tensor_tensor(out=OT[0:64, h, :], in0=ps[ob][0:64, :], in1=Rs_[:], op=ALU.mult),
                             reads=[okey, ("Rs", 0)], writes=[("big", h)])
                    deferred.append([2, fin])

            nj = len(jobs)
            for i in range(nj + LA):
                if i < nj:
                    emit_st(i)
                for dfr in list(deferred):
                    dfr[0] -= 1
                    if dfr[0] <= 0:
                        deferred.remove(dfr)
                        dfr[1]()
                if i >= LA:
                    emit_rest(i - LA)
            for dfr in deferred:
                dfr[1]()
            ring["n"] = 8
            for j in range(4):
                slot, skey = use_tile(("o", j))
                ov = slot[0:64, 0:16 * 256].rearrange("p (h c) -> p h c", c=256)
                for c in range(2):
                    k = 2 * j + c
                    b = nb()
                    for h in range(AH):
                        S.op("pe", lambda: nc.tensor.matmul(ps[b][:], lhsT=ov[:, h, c * 128:(c + 1) * 128], rhs=OT[0:64, h, :],
                                                            start=(h == 0), stop=(h == AH - 1)),
                             reads=[skey, ("big", h)], writes=[pk(b)], inc=(h == AH - 1))
                    ln_accum(k, b, 2)
                done_tile()
            ln_finish(2)

        def xk(tt):
            nm = "lnb" if tt < 2 else "lnsq"
            return [(nm, 4 * (tt % 2) + i) for i in range(4)]
        XK = xk(0) + xk(1) + xk(2) + xk(3)
        S.fence()
        gi = 0
        for bseq in range(NB if stop_after != "setup" else 0):
            for sc in range(NSC):
                tap.idx = gi
                t0 = sc * TS
                first = (sc == 0)
                S.dma("sp", xin, dr["x"][bseq, t0:t0 + TS, :].rearrange("(t p) d -> p t d", p=128), [], XK, iodom_in)
                for k in range(KD if stop_after != "xdma" else 0):
                    b = nb()
                    for tt in range(4):
                        S.op("pe", lambda: nc.tensor.transpose(out=ps[b][:, tt * 128:(tt + 1) * 128], in_=xin[:, tt, k * 128:(k + 1) * 128], identity=identf[:]),
                             reads=xk(tt) + ["identf"], writes=[pk(b)], inc=(tt == 3))
                    S.op("act", lambda: nc.scalar.copy(out=xT[:, k, :], in_=ps[b][:]), reads=[pk(b)], writes=[("xT", k)])
                    S.op("dve", lambda: nc.vector.tensor_copy(out=xTb[:, k, :], in_=ps[b][:]), reads=[pk(b)], writes=[("xTb", k)])
                S.fence()
                if stop_after not in ("xload", "xdma"):
                    ssd_phase(first)
                    tap("dbg_x1", xT[:, 0, :], ("xT", 0), [128, TS])
                if stop_after not in ("xload", "ssd", "xdma"):
                    ffn_phase(0, 1)
                    tap("dbg_x2", xT[:, 0, :], ("xT", 0), [128, TS])
                if stop_after not in ("xload", "ssd", "ffn0", "xdma"):
                    S.fence()
                    attn_phase(sc, first)
                    tap("dbg_x3", xT[:, 0, :], ("xT", 0), [128, TS])
                    ffn_phase(1, 3)
                for tt in range(4 if stop_after != "xdma" else 0):
                    for hf in range(2):
                        b = nb()
                        for kq in range(4):
                            k = hf * 4 + kq
                            S.op("pe", lambda: nc.tensor.transpose(out=ps[b][:, kq * 128:(kq + 1) * 128], in_=xT[:, k, tt * 128:(tt + 1) * 128], identity=identf[:]),
                                 reads=[("xT", k), "identf"], writes=[pk(b)], inc=(kq == 3))
                        if hf == 0:
                            S.op("act", lambda: nc.scalar.copy(out=xin[:, tt, hf * 512:(hf + 1) * 512], in_=ps[b][:]), reads=[pk(b)], writes=xk(tt))
                        else:
                            S.op("dve", lambda: nc.vector.tensor_copy(out=xin[:, tt, hf * 512:(hf + 1) * 512], in_=ps[b][:]), reads=[pk(b)], writes=xk(tt))
                S.dma("sp", out_d[bseq, t0:t0 + TS, :].rearrange("(t p) d -> p t d", p=128), xin, XK, [], iodom_out)
                gi += 1
        assert stop_after is not None or wstate["next_use"] == total_tiles, (wstate, total_tiles)
        S.wait_all("sp", [iodom_out, dbgdom])
        build.stats = dict(nins=dict(S.nins), ndma=S.ndma, counts={k: v.count for k, v in S.dom.items()})
    return nc, list(dbg.keys())


_CACHE = {}


def kernel(**inputs):
    n_cores = 8
    x = np.ascontiguousarray(inputs["x"], dtype=np.float32)
    B, SEQ, _ = x.shape
    NB = B // n_cores
    key = (NB, SEQ)
    if key not in _CACHE:
        _CACHE[key] = build(NB, SEQ)[0]
    nc = _CACHE[key]
    in_maps = []
    for c in range(n_cores):
        m = {k: np.ascontiguousarray(v, dtype=np.float32) for k, v in inputs.items() if k != "x"}
        m["x"] = x[c * NB:(c + 1) * NB]
        in_maps.append(m)
    res = run_bass_kernel_spmd(nc, in_maps, core_ids=list(range(n_cores)))
    return np.concatenate([r["out"] for r in res.results], axis=0)
```

```python
import numpy as np
from contextlib import ExitStack
from collections import defaultdict

import concourse.bass as bass
import concourse.mybir as mybir
from concourse.bass_utils import run_bass_kernel_spmd

F32 = mybir.dt.float32
BF16 = mybir.dt.bfloat16
AF = mybir.ActivationFunctionType
ALU = mybir.AluOpType

D = 1024
KD = 8
DI = 2048
NG = 8
NHEAD = 32
DFF = 2816
NF = 22
AH = 16
TS = 512
DEPTH = 2
ALPHA = (2.0 * DEPTH) ** 0.25
LN_EPS = 1e-5
RMS_EPS = 1e-5
SLOT = 4096
NSLOT = 3
NEG = -30000.0


class Dom:
    def __init__(self, nc, es, name, step, epoch):
        self.nc, self.es, self.name, self.step, self.epoch = nc, es, name, step, epoch
        self.sems = []
        self.count = 0

    def sem_for(self, cnt):
        e = (cnt - 1) // self.epoch
        while len(self.sems) <= e:
            self.sems.append(self.es.enter_context(self.nc.semaphore(f"s_{self.name}_{len(self.sems)}")))
        return self.sems[e], ((cnt - 1) % self.epoch + 1) * self.step


class Sched:
    def __init__(self, nc, es):
        self.nc, self.es = nc, es
        self.eng = {"pe": nc.tensor, "act": nc.scalar, "dve": nc.vector, "pool": nc.gpsimd, "sp": nc.sync}
        self.dom = {e: Dom(nc, es, e, 1, 4096) for e in ("pe", "act", "dve", "pool")}
        self.seen = defaultdict(int)
        self.lastw = {}
        self.readers = defaultdict(dict)
        self.ndma = 0
        self.nins = defaultdict(int)

    def new_dma_dom(self, name):
        return Dom(self.nc, self.es, name, 16, 1024)

    def _deps(self, own, reads, writes):
        deps = {}

        def need(dc, same_ok):
            dom, cnt = dc
            if dom is own and same_ok:
                return
            if deps.get(dom, 0) < cnt:
                deps[dom] = cnt

        for k in reads:
            if k in self.lastw:
                need(self.lastw[k], False)
            if isinstance(k, tuple) and k[0] == "ps":
                for dom, cnt in self.readers[k].items():
                    need((dom, cnt), True)
        for k in writes:
            if k in self.lastw:
                need(self.lastw[k], True)
            for dom, cnt in self.readers[k].items():
                need((dom, cnt), True)
        return deps

    def _wait(self, e, deps, own=None):
        for dom, cnt in deps.items():
            if self.seen[(e, dom.name)] >= cnt:
                continue
            if dom is own:
                assert cnt <= own.count, "same-engine wait on a future completion"
            sem, val = dom.sem_for(cnt)
            self.eng[e].wait_ge(sem, val)
            self.nins[e] += 1
            self.seen[(e, dom.name)] = cnt

    def op(self, e, fn, reads=(), writes=(), inc=True):
        own = self.dom[e]
        self._wait(e, self._deps(own, reads, writes), own)
        ins = fn()
        self.nins[e] += 1
        tag = own.count + 1
        if inc:
            own.count += 1
            sem, _ = own.sem_for(own.count)
            ins.then_inc(sem, 1)
        for k in reads:
            if self.readers[k].get(own, 0) < tag:
                self.readers[k][own] = tag
        for k in writes:
            self.lastw[k] = (own, tag)
            self.readers[k] = {}
        return ins

    def dma(self, q, out, in_, reads, writes, dom):
        self._wait(q, self._deps(None, reads, writes))
        ins = self.eng[q].dma_start(out=out, in_=in_)
        self.nins[q] += 1
        self.ndma += 1
        dom.count += 1
        sem, _ = dom.sem_for(dom.count)
        ins.then_inc(sem, 16)
        for k in reads:
            self.readers[k][dom] = dom.count
        for k in writes:
            self.lastw[k] = (dom, dom.count)
            self.readers[k] = {}
        return ins

    def fence(self):
        es_ = ("pe", "act", "dve", "pool")
        for e in es_:
            self._wait(e, {self.dom[f]: self.dom[f].count for f in es_ if f != e and self.dom[f].count > 0})

    def wait_all(self, e, doms):
        for dom in doms:
            if dom.count > 0:
                self._wait(e, {dom: dom.count})


def bc(ap, shape):
    return ap.to_broadcast(list(shape))


def weight_tiles():
    tiles = []
    for g in range(NG):
        tiles.append((("inA", g), 8 * 512, [("ssm_in_w", 0, g * 256, 256, "kpc", 0, 512, 0),
                                             ("ssm_in_w", 0, 2048 + g * 256, 256, "kpc", 0, 512, 256)]))
        tiles.append((("inB", g), 8 * 256, [("ssm_in_w", 0, 4096 + g * 128, 128, "kpc", 0, 256, 0),
                                             ("ssm_in_w", 0, 5120 + g * 128, 128, "kpc", 0, 256, 128)]))
    for j in range(4):
        tiles.append((("out", j), 16 * 256, [("ssm_out_w", 0, j * 256, 256, "kpc", 0, 256, 0)]))

    def ffn(l):
        for j in range(6):
            nfc = 4 if j < 5 else 2
            tiles.append((("g", l, j), 8 * 512, [("ffn_gate_w", l, j * 512, nfc * 128, "kpc", 0, 512, 0)]))
            tiles.append((("u", l, j), 8 * 512, [("ffn_up_w", l, j * 512, nfc * 128, "kpc", 0, 512, 0)]))
        for hf in range(2):
            for fg in range(3):
                nf = 8 if fg < 2 else 6
                tiles.append((("dn", l, hf, fg), nf * 512, [("ffn_down_w", l, hf * 512, 512, "kpc_rows", 0, 512, 0, fg * 8, nf)]))

    ffn(0)
    for j in range(2):
        tiles.append((("kvk", j), 4096, [("kv_w", None, j * 512, 512, "kpc", 0, 512, 0)]))
    for j in range(2):
        tiles.append((("kvv", j), 4096, [("kv_w", None, 1024 + j * 512, 512, "kpc", 0, 512, 0)]))
    for j in range(2):
        tiles.append((("q", j), 4096, [("att_q_w", 0, j * 512, 512, "kpc", 0, 512, 0)]))
    for j in range(4):
        tiles.append((("o", j), 16 * 256, [("att_o_w", 0, j * 256, 256, "hpc", 0, 256, 0)]))
    ffn(1)
    return tiles


IN_SPECS = [
    ("x", None), ("ssm_in_w", [1, 1024, 6176]), ("ssm_conv_w", [1, 4, 4096]), ("ssm_conv_b", [1, 4096]),
    ("ssm_dt_bias", [1, 32]), ("ssm_a_log", [1, 32]), ("ssm_d", [1, 32]), ("ssm_norm_w", [1, 2048]),
    ("ssm_out_w", [1, 2048, 1024]), ("kv_w", [1024, 2064]), ("kv_b_f", [16]), ("att_q_w", [1, 1024, 1024]),
    ("att_o_w", [1, 1024, 1024]), ("ffn_gate_w", [2, 1024, 2816]), ("ffn_up_w", [2, 1024, 2816]),
    ("ffn_down_w", [2, 2816, 1024]), ("ln_mix_g", [2, 1024]), ("ln_mix_b", [2, 1024]), ("ln_ffn_g", [2, 1024]),
    ("ln_ffn_b", [2, 1024]),
]


def build(NB=4, SEQ=2048, debug=False, stop_after=None):
    nc = bass.Bass("TRN2", target_bir_lowering=False)
    NSC = SEQ // TS
    NKT = SEQ // 128
    dr = {}
    for name, shp in IN_SPECS:
        if name == "x":
            shp = [NB, SEQ, D]
        dr[name] = nc.dram_tensor(name, shp, F32, kind="ExternalInput").ap()
    out_d = nc.dram_tensor("out", [NB, SEQ, D], F32, kind="ExternalOutput").ap()
    tiles = weight_tiles()
    NT = len(tiles)
    wscr = nc.dram_tensor("wscr", [NT, 128, SLOT], BF16, kind="Internal").ap()
    dbg = {}

    with ExitStack() as es:
        ec = es.enter_context
        S = Sched(nc, es)

        def sb(name, shape, dt=F32):
            return ec(nc.sbuf_tensor(name, list(shape), dt))

        identf = sb("identf", [128, 128]); identb = sb("identb", [128, 128], BF16)
        onesf = sb("onesf", [128, 128]); trif = sb("trif", [128, 128])
        lnones = sb("lnones", [128, 128], BF16)
        negmask = sb("negmask", [128, 512], BF16)
        prmA = sb("prmA", [128, 128]); prmB = sb("prmB", [128, 128])
        dtb_bc = sb("dtb_bc", [128, 32]); A_bc = sb("A_bc", [128, 32]); D_bc = sb("D_bc", [128, 32])
        bf_col = sb("bf_col", [16, 1])
        wdt = sb("wdt", [128, 8, 32], BF16); wf = sb("wf", [128, 8, 16], BF16)
        wslot = [sb(f"wslot{i}", [128, SLOT], BF16) for i in range(NSLOT)]
        scr16 = sb("scr16", [128, 4096])
        xin = scr16[:].rearrange("p (t d) -> p t d", d=D)
        lnb = scr16[:, 0:2048].bitcast(BF16).rearrange("p (k t) -> p k t", t=TS)
        lnsq = scr16[:, 2048:4096].bitcast(BF16).rearrange("p (k t) -> p k t", t=TS)
        xT = sb("xT", [128, KD, TS]); xTb = sb("xTb", [128, KD, TS], BF16)
        mean_sb = sb("mean_sb", [128, TS]); rstd_sb = sb("rstd_sb", [128, TS])
        big = sb("big", [128, NF, TS], BF16)
        acc = [sb(f"acc{i}", [128, TS]) for i in range(2)]
        lnt = acc
        stateT = sb("stateT", [128, NHEAD * 64]); stbf = sb("stbf", [128, NHEAD * 64], BF16)
        halo = sb("halo", [128, 32, 3])
        KT = sb("KT", [128, 8, SEQ], BF16)
        VA = sb("VA", [128, NKT, AH, 65], BF16)
        Fcarry = sb("Fcarry", [16, 1]); Fcol = sb("Fcol", [128, NKT, AH])
        lneps = sb("lneps", [128, 2])
        ARENA = 7750
        arena = sb("arena", [128, ARENA])
        ar = {"o": 0}

        def carve(shape, dt=F32):
            n = int(np.prod(shape[1:]))
            w = n if dt == F32 else (n + 1) // 2
            o = ar["o"]
            assert o + w <= ARENA, ("arena overflow", o, w)
            ar["o"] = o + w
            v = arena[0:shape[0], o:o + w]
            if dt != F32:
                v = v.bitcast(dt)
            if len(shape) == 3:
                v = v.rearrange("p (a b) -> p a b", b=shape[2])
            return v

        stg1 = carve([128, 128]); stg2 = carve([128, 128])
        ar["o"] = 0
        dt_sb = carve([128, 4, 32]); a_sb = carve([128, 4, 32]); cumcol = carve([128, 4, 32]); expcum = carve([128, 4, 32])
        sp_t = [carve([128, 4, 32]) for _ in range(3)]
        cumcolp = carve([128, 4, 32])
        ubuf = carve([128, 2, TS + 4])
        e4 = [carve([128, 4]) for _ in range(2)]
        ys = carve([128, 256]); ysum = carve([128, 256])
        yg = carve([128, 4, 256]); ss = carve([128, 4]); sd4 = carve([128, 4]); rstd4 = carve([128, 4])
        junk = carve([128, 256]); ygn = carve([128, 4, 256], BF16); sttmp = carve([128, 256])
        zs = [carve([128, 4, 256], BF16), None]
        xbc = [carve([128, 4, TS], BF16), None]

        def chunk_set():
            return dict(xtok=carve([128, 256], BF16), xD=carve([128, 256], BF16), btok=carve([128, 128], BF16),
                        atri=carve([128, 512]), decayT=carve([128, 512], BF16), GT=carve([128, 512], BF16), xw=carve([128, 256], BF16))
        cset = [chunk_set(), None]
        ssd_top = ar["o"]
        sav = (arena, ar["o"])
        arena_main = arena

        def carve16(shape, dt=F32):
            n = int(np.prod(shape[1:]))
            w = n if dt == F32 else (n + 1) // 2
            o = c16["o"]
            assert o + w <= 4096, ("scr16 overflow", o, w)
            c16["o"] = o + w
            v = scr16[0:shape[0], o:o + w]
            if dt != F32:
                v = v.bitcast(dt)
            if len(shape) == 3:
                v = v.rearrange("p (a b) -> p a b", b=shape[2])
            return v
        c16 = {"o": 0}
        zs[1] = carve16([128, 4, 256], BF16)
        xbc[1] = carve16([128, 4, TS], BF16)
        cset[1] = dict(xtok=carve16([128, 256], BF16), xD=carve16([128, 256], BF16), btok=carve16([128, 128], BF16),
                       atri=carve16([128, 512]), decayT=carve16([128, 512], BF16), GT=carve16([128, 512], BF16), xw=carve16([128, 256], BF16))
        ar["o"] = 0
        QT = carve([128, 8, TS], BF16)
        f_v = carve([16, TS]); f_a = carve([16, TS]); f_l = carve([16, TS]); Frow = carve([16, TS])
        fdiag = carve([16, 16]); Fq0 = carve([128, AH]); bcol = carve([128, NKT, AH])
        PT = [carve([128, 512], BF16) for _ in range(4)]
        rr = [carve([128, 512])] * 2; Rs = [carve([64, 512])] * 2
        att_top = ar["o"]
        ps = [ec(nc.psum_tensor(f"ps{i}", [128, 512], F32)) for i in range(8)]

        def psb(i):
            return ps[i][:].bitcast(BF16)

        ring = {"i": 0, "n": 8}

        def nb():
            i = ring["i"] % ring["n"]
            ring["i"] = (i + 1) % ring["n"]
            return i

        def pk(i):
            return ("ps", i)

        P_ = "pool"
        neg_reg = nc.gpsimd.to_reg(NEG)
        zero_reg = nc.gpsimd.to_reg(0.0)
        S.op(P_, lambda: nc.gpsimd.memset(identf[:], 0.0), writes=["identf"])
        S.op(P_, lambda: nc.gpsimd.affine_select(out=identf[:], in_=identf[:], pattern=[[-1, 128]], compare_op=ALU.not_equal,
                                                 fill=1.0, base=0, channel_multiplier=1), reads=["identf"], writes=["identf"])
        S.op(P_, lambda: nc.gpsimd.tensor_copy(out=identb[:], in_=identf[:]), reads=["identf"], writes=["identb"])
        S.op(P_, lambda: nc.gpsimd.memset(onesf[:], 1.0), writes=["onesf"])
        S.op(P_, lambda: nc.gpsimd.memset(lnones[:], 1.0 / D), writes=["lnones"])
        S.op(P_, lambda: nc.gpsimd.memset(trif[:], 1.0), writes=["trif"])
        S.op(P_, lambda: nc.gpsimd.affine_select(out=trif[:], in_=trif[:], pattern=[[1, 128]], compare_op=ALU.is_ge,
                                                 fill=0.0, base=0, channel_multiplier=-1), reads=["trif"], writes=["trif"])
        S.op(P_, lambda: nc.gpsimd.memset(negmask[:], 0.0), writes=["negmask"])
        nm3 = negmask[:].rearrange("p (h l) -> p h l", h=4)
        S.op(P_, lambda: nc.gpsimd.affine_select(out=nm3, in_=nm3, pattern=[[0, 4], [1, 128]], compare_op=ALU.is_ge,
                                                 fill=NEG, base=0, channel_multiplier=-1), reads=["negmask"], writes=["negmask"])
        S.op(P_, lambda: nc.gpsimd.memset(stg2[:], 0.0), writes=["stg2"])

        cdom = S.new_dma_dom("cst")
        S.dma("sp", stg1[:], dr["ssm_conv_w"][0].rearrange("k (c p) -> (k c) p", p=128), [], ["stg1"], cdom)
        rows = [("ssm_conv_b", dr["ssm_conv_b"][0], 32, 0), ("ssm_norm_w", dr["ssm_norm_w"][0], 16, 32),
                ("ln_mix_g", dr["ln_mix_g"].rearrange("l d -> (l d)"), 16, 48), ("ln_mix_b", dr["ln_mix_b"].rearrange("l d -> (l d)"), 16, 64),
                ("ln_ffn_g", dr["ln_ffn_g"].rearrange("l d -> (l d)"), 16, 80), ("ln_ffn_b", dr["ln_ffn_b"].rearrange("l d -> (l d)"), 16, 96)]
        for (_, src, n, r0) in rows:
            S.dma("sp", stg2[r0:r0 + n, :], src.rearrange("(c p) -> c p", p=128), [], ["stg2"], cdom)
        S.dma("sp", dtb_bc[:], dr["ssm_dt_bias"].partition_broadcast(128), [], ["dtb_bc"], cdom)
        S.dma("sp", A_bc[:], dr["ssm_a_log"].partition_broadcast(128), [], ["A_bc"], cdom)
        S.dma("sp", D_bc[:], dr["ssm_d"].partition_broadcast(128), [], ["D_bc"], cdom)
        S.dma("sp", bf_col[:], dr["kv_b_f"].rearrange("(h o) -> h o", o=1), [], ["bf_col"], cdom)
        S.op("act", lambda: nc.scalar.activation(out=A_bc[:], in_=A_bc[:], func=AF.Exp), reads=["A_bc"], writes=["A_bc"])
        S.op("dve", lambda: nc.vector.tensor_scalar_mul(out=A_bc[:], in0=A_bc[:], scalar1=-1.0), reads=["A_bc"], writes=["A_bc"])
        b0 = nb()
        S.op("pe", lambda: nc.tensor.transpose(out=ps[b0][:, 0:128], in_=stg1[:], identity=identf[:]), reads=["stg1", "identf"], writes=[pk(b0)])
        S.op("dve", lambda: nc.vector.tensor_copy(out=prmA[:], in_=ps[b0][:, 0:128]), reads=[pk(b0)], writes=["prmA"])
        b1 = nb()
        S.op("pe", lambda: nc.tensor.transpose(out=ps[b1][:, 0:128], in_=stg2[:], identity=identf[:]), reads=["stg2", "identf"], writes=[pk(b1)])
        S.op("dve", lambda: nc.vector.tensor_copy(out=prmB[:], in_=ps[b1][:, 0:128]), reads=[pk(b1)], writes=["prmB"])
        S.op("dve", lambda: nc.vector.tensor_scalar_mul(out=prmA[:], in0=prmA[:], scalar1=0.5), reads=["prmA"], writes=["prmA"])
        S.op("dve", lambda: nc.vector.tensor_scalar_mul(out=prmB[:, 0:32], in0=prmB[:, 0:32], scalar1=0.5), reads=["prmB"], writes=["prmB"])
        cw = prmA[:].rearrange("p (k c) -> p k c", k=4)
        cb = prmB[:, 0:32]
        normw = prmB[:, 32:48]
        lng = {0: prmB[:, 48:56], 1: prmB[:, 80:88], 2: prmB[:, 56:64], 3: prmB[:, 88:96]}
        lnbias = {0: prmB[:, 64:72], 1: prmB[:, 96:104], 2: prmB[:, 72:80], 3: prmB[:, 104:112]}

        import os
        wcdom = S.new_dma_dom("wcv")
        stf = [scr16[:], xT[:].rearrange("p k t -> p (k t)")]
        stb = [big[:, 0:8, :].rearrange("p k t -> p (k t)"), big[:, 8:16, :].rearrange("p k t -> p (k t)")]
        ktf = KT[:].rearrange("p k t -> p (k t)").bitcast(F32)
        for i in range(ktf.shape[1] // 4096):
            stf.append(ktf[:, i * 4096:(i + 1) * 4096])
        vaf = VA[:].rearrange("p a b c -> p (a b c)")
        for i in range(vaf.shape[1] // 4096):
            stb.append(vaf[:, i * 4096:(i + 1) * 4096])
        NST = min(len(stf), len(stb), 4)
        cvl = [S.new_dma_dom(f"cvl{i}") for i in range(NST)]
        cvs = [S.new_dma_dom(f"cvs{i}") for i in range(NST)]
        cast_eng = ["dve", "act", "pool"]
        for ti, (name, nel, parts) in enumerate(tiles):
            if os.environ.get("SKIP_CONV"):
                break
            sl = ti % NST
            npart = 64 if name[0] == "o" else 128
            for part in parts:
                (src, idx, c0, n, kind, base, cstride, coff) = part[:8]
                w = dr[src] if idx is None else dr[src][idx]
                if kind == "kpc_rows":
                    k0, nk = part[8], part[9]
                    s_ap = w[k0 * 128:(k0 + nk) * 128, c0:c0 + n].rearrange("(k p) c -> p k c", p=128)
                    d_ap = stf[sl][:, base:base + nk * cstride].rearrange("p (k c) -> p k c", c=cstride)[:, :, coff:coff + n]
                elif kind == "kpc":
                    nk = w.shape[0] // 128
                    s_ap = w[:, c0:c0 + n].rearrange("(k p) c -> p k c", p=128)
                    d_ap = stf[sl][:, base:base + nk * cstride].rearrange("p (k c) -> p k c", c=cstride)[:, :, coff:coff + n]
                else:
                    s_ap = w[:, c0:c0 + n].rearrange("(h p) c -> p h c", p=64)
                    d_ap = stf[sl][0:64, base:base + 16 * cstride].rearrange("p (h c) -> p h c", c=cstride)[:, :, coff:coff + n]
                S.dma("sp", d_ap, s_ap, [], [("stf", sl)], cvl[sl])
            ce = cast_eng[ti % 3]
            if ce == "dve":
                S.op("dve", lambda: nc.vector.tensor_copy(out=stb[sl][0:npart, 0:nel], in_=stf[sl][0:npart, 0:nel]), reads=[("stf", sl)], writes=[("stb", sl)])
            elif ce == "act":
                S.op("act", lambda: nc.scalar.copy(out=stb[sl][0:npart, 0:nel], in_=stf[sl][0:npart, 0:nel]), reads=[("stf", sl)], writes=[("stb", sl)])
            else:
                S.op("pool", lambda: nc.gpsimd.tensor_copy(out=stb[sl][0:npart, 0:nel], in_=stf[sl][0:npart, 0:nel]), reads=[("stf", sl)], writes=[("stb", sl)])
            S.dma("sp", wscr[ti][0:npart, 0:nel], stb[sl][0:npart, 0:nel], [("stb", sl)], [("wscr", ti)], cvs[sl])
        S.dma("pool", wdt[:], dr["ssm_in_w"][0][:, 6144:6176].rearrange("(k p) c -> p k c", p=128), [], ["wdt"], wcdom)
        S.dma("pool", wf[:], dr["kv_w"][:, 2048:2064].rearrange("(k p) c -> p k c", p=128), [], ["wf"], wcdom)
        S.wait_all("pool", cvs + cvl)
        S.op("pool", lambda: nc.gpsimd.memset(VA[:], 1.0), reads=[("stb", i) for i in range(NST)], writes=[("VA", kt) for kt in range(NKT)])
        S.wait_all("sp", cvs + cvl)
        S.wait_all("pe", cvs + cvl)
        S.wait_all("act", cvs + cvl)
        S.wait_all("dve", cvs + cvl)
        S.wait_all("pool", cvs + cvl)

        wdoms = [S.new_dma_dom(f"w{i}") for i in range(NSLOT)]
        wstate = {"next_load": 0, "next_use": 0}
        total_tiles = NB * NSC * NT

        def prefetch():
            i = wstate["next_load"]
            if i >= total_tiles:
                return
            wstate["next_load"] += 1
            ti = i % NT
            s = i % NSLOT
            nel = tiles[ti][1]
            npart = 64 if tiles[ti][0][0] == "o" else 128
            S.dma("sp", wslot[s][0:npart, 0:nel], wscr[ti][0:npart, 0:nel], [("wscr", ti)], [("wslot", s)], wdoms[s])

        def use_tile(expect):
            if stop_after is not None:
                while tiles[wstate["next_use"] % NT][0] != expect:
                    wstate["next_use"] += 1
                    prefetch()
            i = wstate["next_use"]
            wstate["next_use"] += 1
            ti = i % NT
            assert tiles[ti][0] == expect, (tiles[ti][0], expect)
            s = i % NSLOT
            return wslot[s], ("wslot", s)

        def done_tile():
            prefetch()

        for _ in range(NSLOT):
            prefetch()

        iodom_in = S.new_dma_dom("xin")
        iodom_out = S.new_dma_dom("xout")
        dbgdom = S.new_dma_dom("dbg")

        def tap(name, ap, key, shape):
            if not debug:
                return
            if name not in dbg:
                dbg[name] = nc.dram_tensor(name, [NB * NSC] + list(shape), ap.dtype, kind="ExternalOutput").ap()
            S.dma("sp", dbg[name][tap.idx], ap, [key], [], dbgdom)
        tap.idx = 0

        def ln_accum(k, bank, ln_idx):
            S.op("dve", lambda: nc.vector.scalar_tensor_tensor(out=xT[:, k, :], in0=xT[:, k, :], scalar=ALPHA, in1=ps[bank][:],
                                                               op0=ALU.mult, op1=ALU.add),
                 reads=[("xT", k), pk(bank)], writes=[("xT", k)])
            S.op("act", lambda: nc.scalar.copy(out=lnb[:, k, :], in_=xT[:, k, :]), reads=[("xT", k)], writes=[("lnb", k)])
            S.op("act", lambda: nc.scalar.activation(out=lnsq[:, k, :], in_=xT[:, k, :], func=AF.Square), reads=[("xT", k)], writes=[("lnsq", k)])

        def ln_finish(ln_idx):
            bm, be = nb(), nb()
            for k in range(KD):
                S.op("pe", lambda: nc.tensor.matmul(ps[bm][:], lhsT=lnones[:], rhs=lnb[:, k, :], start=(k == 0), stop=(k == KD - 1)),
                     reads=[("lnb", k), "lnones"], writes=[pk(bm)], inc=(k == KD - 1))
            for k in range(KD):
                S.op("pe", lambda: nc.tensor.matmul(ps[be][:], lhsT=lnones[:], rhs=lnsq[:, k, :], start=(k == 0), stop=(k == KD - 1)),
                     reads=[("lnsq", k), "lnones"], writes=[pk(be)], inc=(k == KD - 1))
            S.op("act", lambda: nc.scalar.copy(out=mean_sb[:], in_=ps[bm][:]), reads=[pk(bm)], writes=["mean_sb"])
            S.op("dve", lambda: nc.vector.tensor_tensor(out=rstd_sb[:], in0=ps[bm][:], in1=mean_sb[:], op=ALU.mult),
                 reads=[pk(bm), "mean_sb"], writes=["rstd_sb"])
            S.op("dve", lambda: nc.vector.tensor_tensor(out=rstd_sb[:], in0=ps[be][:], in1=rstd_sb[:], op=ALU.subtract),
                 reads=[pk(be), "rstd_sb"], writes=["rstd_sb"])
            S.op("act", lambda: nc.scalar.activation(out=rstd_sb[:], in_=rstd_sb[:], func=AF.Sqrt, bias=lneps[:, 0:1]),
                 reads=["rstd_sb", "lneps"], writes=["rstd_sb"])
            S.op("dve", lambda: nc.vector.reciprocal(out=rstd_sb[:], in_=rstd_sb[:]), reads=["rstd_sb"], writes=["rstd_sb"])
            for k in range(KD):
                t1, t2 = lnt[0], lnt[1]
                k1, k2 = ("acc", 0), ("acc", 1)
                S.op("dve", lambda: nc.vector.tensor_tensor(out=t1[:], in0=xT[:, k, :], in1=mean_sb[:], op=ALU.subtract),
                     reads=[("xT", k), "mean_sb"], writes=[k1])
                S.op("pool", lambda: nc.gpsimd.tensor_tensor(out=t2[:], in0=t1[:], in1=rstd_sb[:], op=ALU.mult),
                     reads=[k1, "rstd_sb"], writes=[k2])
                S.op("act", lambda: nc.scalar.activation(out=xT[:, k, :], in_=t2[:], func=AF.Identity, scale=lng[ln_idx][:, k:k + 1],
                                                         bias=lnbias[ln_idx][:, k:k + 1]),
                     reads=[k2, "prmB"], writes=[("xT", k)])
                S.op("act", lambda: nc.scalar.activation(out=xTb[:, k, :], in_=t2[:], func=AF.Identity, scale=lng[ln_idx][:, k:k + 1],
                                                         bias=lnbias[ln_idx][:, k:k + 1]),
                     reads=[k2, "prmB"], writes=[("xTb", k)])

        S.op("pool", lambda: nc.gpsimd.memset(lneps[:, 0:1], LN_EPS), writes=["lneps"])
        S.op("pool", lambda: nc.gpsimd.memset(lneps[:, 1:2], 4.0 * RMS_EPS), writes=["lneps"])

        sg = [acc[0][:].bitcast(BF16)[:, 0:TS], acc[0][:].bitcast(BF16)[:, TS:2 * TS], acc[1][:].bitcast(BF16)[:, 0:TS], acc[1][:].bitcast(BF16)[:, TS:2 * TS]]

        def ffn_phase(l, ln_idx):
            for j in range(6):
                nfc = 4 if j < 5 else 2
                slot, skey = use_tile(("g", l, j))
                gv = slot[:, 0:4096].rearrange("p (k c) -> p k c", c=512)
                gb_ = []
                for fc in range(nfc):
                    bg = nb()
                    gb_.append(bg)
                    for k in range(KD):
                        S.op("pe", lambda: nc.tensor.matmul(ps[bg][:], lhsT=gv[:, k, fc * 128:(fc + 1) * 128], rhs=xTb[:, k, :], start=(k == 0), stop=(k == KD - 1)),
                             reads=[skey, ("xTb", k)], writes=[pk(bg)], inc=(k == KD - 1))
                    S.op("act", lambda: nc.scalar.activation(out=sg[fc], in_=ps[bg][:], func=AF.Silu), reads=[pk(bg)], writes=[("acc", fc // 2)])
                done_tile()
                slot, skey = use_tile(("u", l, j))
                uv = slot[:, 0:4096].rearrange("p (k c) -> p k c", c=512)
                for fc in range(nfc):
                    f = 4 * j + fc
                    bu = nb()
                    for k in range(KD):
                        S.op("pe", lambda: nc.tensor.matmul(ps[bu][:], lhsT=uv[:, k, fc * 128:(fc + 1) * 128], rhs=xTb[:, k, :], start=(k == 0), stop=(k == KD - 1)),
                             reads=[skey, ("xTb", k)], writes=[pk(bu)], inc=(k == KD - 1))
                    S.op("dve", lambda: nc.vector.tensor_tensor(out=big[:, f, :], in0=sg[fc], in1=ps[bu][:], op=ALU.mult),
                         reads=[("acc", fc // 2), pk(bu)], writes=[("big", f)])
                done_tile()
            for hf in range(2):
                banks = [nb() for _ in range(4)]
                for fg in range(3):
                    nf = 8 if fg < 2 else 6
                    slot, skey = use_tile(("dn", l, hf, fg))
                    dv = slot[:, 0:nf * 512].rearrange("p (f c) -> p f c", c=512)
                    for c in range(4):
                        for fl in range(nf):
                            f = fg * 8 + fl
                            S.op("pe", lambda: nc.tensor.matmul(ps[banks[c]][:], lhsT=dv[:, fl, c * 128:(c + 1) * 128], rhs=big[:, f, :],
                                                                start=(f == 0), stop=(f == NF - 1)),
                                 reads=[skey, ("big", f)], writes=[pk(banks[c])], inc=(fl == nf - 1))
                    done_tile()
                for c in range(4):
                    ln_accum(4 * hf + c, banks[c], ln_idx)
            ln_finish(ln_idx)

        def ssd_phase(first_in_seq):
            bd = nb()
            for tt in range(4):
                for k in range(KD):
                    S.op("pe", lambda: nc.tensor.matmul(ps[bd][:, tt * 32:(tt + 1) * 32], lhsT=xTb[:, k, tt * 128:(tt + 1) * 128], rhs=wdt[:, k, :],
                                                        start=(k == 0), stop=(k == KD - 1)),
                         reads=[("xTb", k), "wdt"], writes=[pk(bd)], inc=(tt == 3 and k == KD - 1))
            pd = ps[bd][:, 0:128].rearrange("p (t h) -> p t h", h=32)
            v_, av_, l_ = sp_t
            S.op("dve", lambda: nc.vector.tensor_tensor(out=v_[:], in0=pd, in1=bc(dtb_bc[:].unsqueeze(1), [128, 4, 32]), op=ALU.add),
                 reads=[pk(bd), "dtb_bc"], writes=["sp_v"])
            S.op("act", lambda: nc.scalar.activation(out=av_[:], in_=v_[:], func=AF.Abs), reads=["sp_v"], writes=["sp_a"])
            S.op("act", lambda: nc.scalar.activation(out=av_[:], in_=av_[:], func=AF.Exp, scale=-1.0), reads=["sp_a"], writes=["sp_a"])
            S.op("act", lambda: nc.scalar.activation(out=l_[:], in_=av_[:], func=AF.Ln, bias=1.0), reads=["sp_a"], writes=["sp_l"])
            S.op("dve", lambda: nc.vector.scalar_tensor_tensor(out=dt_sb[:], in0=v_[:], scalar=0.0, in1=l_[:], op0=ALU.max, op1=ALU.add),
                 reads=["sp_v", "sp_l"], writes=["dt_sb"])
            S.op("act", lambda: nc.scalar.activation(out=l_[:], in_=dt_sb[:], func=AF.Ln), reads=["dt_sb"], writes=["sp_l"])
            S.op("dve", lambda: nc.vector.tensor_tensor(out=a_sb[:], in0=dt_sb[:], in1=bc(A_bc[:].unsqueeze(1), [128, 4, 32]), op=ALU.mult),
                 reads=["dt_sb", "A_bc"], writes=["a_sb"])
            bcu = nb()
            for c in range(4):
                S.op("pe", lambda: nc.tensor.matmul(ps[bcu][:, c * 32:(c + 1) * 32], lhsT=trif[:], rhs=a_sb[:, c, :], start=True, stop=True),
                     reads=["trif", "a_sb"], writes=[pk(bcu)], inc=(c == 3))
            pc = ps[bcu][:, 0:128].rearrange("p (t h) -> p t h", h=32)
            S.op("dve", lambda: nc.vector.tensor_tensor(out=cumcolp[:], in0=pc, in1=l_[:], op=ALU.subtract), reads=[pk(bcu), "sp_l"], writes=["cumcolp"])
            S.op("act", lambda: nc.scalar.activation(out=expcum[:], in_=pc, func=AF.Exp), reads=[pk(bcu)], writes=["expcum"])
            if first_in_seq:
                S.op("pool", lambda: nc.gpsimd.memset(stateT[:], 0.0), writes=[("stateT", g) for g in range(NG)])
                S.op("pool", lambda: nc.gpsimd.memset(stbf[:], 0.0), writes=[("stbf", g) for g in range(NG)])
                S.op("pool", lambda: nc.gpsimd.memset(halo[:], 0.0), writes=[("halo", ci) for ci in range(32)])

            def inproj_pieces(g):
                gb = g % 2
                zs_, xbc_ = zs[gb], xbc[gb]
                st = {}

                def p_open_a():
                    st["slot"], st["skey"] = use_tile(("inA", g))
                    st["Wv"] = st["slot"][:, 0:8 * 512].rearrange("p (k c) -> p k c", c=512)

                def p_z(half):
                    def f():
                        Wv, skey = st["WvA"], st["skeyA"]
                        bz = nb()
                        for t2 in range(2):
                            tt = 2 * half + t2
                            for k in range(KD):
                                S.op("pe", lambda: nc.tensor.matmul(ps[bz][:, t2 * 256:(t2 + 1) * 256], lhsT=xTb[:, k, tt * 128:(tt + 1) * 128],
                                                                    rhs=Wv[:, k, 0:256], start=(k == 0), stop=(k == KD - 1)),
                                     reads=[skey, ("xTb", k)], writes=[pk(bz)], inc=(t2 == 1 and k == KD - 1))
                        S.op("act", lambda: nc.scalar.activation(out=acc[half][:], in_=ps[bz][:], func=AF.Tanh, scale=0.5), reads=[pk(bz)], writes=[("acc", half)])
                        S.op("dve", lambda: nc.vector.scalar_tensor_tensor(out=zs_[:, 2 * half:2 * half + 2, :].rearrange("p t c -> p (t c)"), in0=acc[half][:], scalar=1.0,
                                                                           in1=ps[bz][:], op0=ALU.add, op1=ALU.mult),
                             reads=[("acc", half), pk(bz)], writes=[("zs", gb)])
                        if half == 1:
                            done_tile()
                            done_tile()
                    return f

                def p_x(r0):
                    def f():
                        if r0 == 0:
                            p_open_a()
                            st["WvA"], st["skeyA"] = st["Wv"], st["skey"]
                        if r0 == 2:
                            st["slot"], st["skey"] = use_tile(("inB", g))
                            st["Wv"] = st["slot"][:, 0:8 * 256].rearrange("p (k c) -> p k c", c=256)
                        Wv, skey = st["Wv"], st["skey"]
                        rows = []
                        for r in (r0, r0 + 1):
                            ci = (2 * g + r) if r < 2 else (16 + g if r == 2 else 24 + g)
                            wcol = (256 + r * 128) if r < 2 else (r - 2) * 128
                            bx = nb()
                            for k in range(KD):
                                S.op("pe", lambda: nc.tensor.matmul(ps[bx][:], lhsT=Wv[:, k, wcol:wcol + 128], rhs=xTb[:, k, :],
                                                                    start=(k == 0), stop=(k == KD - 1)),
                                     reads=[skey, ("xTb", k)], writes=[pk(bx)], inc=(k == KD - 1))
                            rows.append((r, ci, bx, r % 2))
                        for (r, ci, bx, ur) in rows:
                            uk = ("ubuf", ur)
                            S.op("pool", lambda: nc.gpsimd.tensor_copy(out=ubuf[:, ur, 0:3], in_=halo[:, ci, :]), reads=[("halo", ci)], writes=[uk])
                            S.op("act", lambda: nc.scalar.copy(out=ubuf[:, ur, 3:TS + 3], in_=ps[bx][:]), reads=[pk(bx)], writes=[uk])
                            S.op("act", lambda: nc.scalar.activation(out=acc[ur][:], in_=ps[bx][:], func=AF.Identity, scale=cw[:, 3, ci:ci + 1], bias=cb[:, ci:ci + 1]),
                                 reads=[pk(bx), "prmA", "prmB"], writes=[("acc", ur)])
                        for kk in range(3):
                            for (r, ci, bx, ur) in rows:
                                S.op("dve", lambda: nc.vector.scalar_tensor_tensor(out=acc[ur][:], in0=ubuf[:, ur, kk:kk + TS], scalar=cw[:, kk, ci:ci + 1], in1=acc[ur][:],
                                                                                   op0=ALU.mult, op1=ALU.add), reads=[("ubuf", ur), ("acc", ur), "prmA"], writes=[("acc", ur)])
                        for (r, ci, bx, ur) in rows:
                            S.op("pool", lambda: nc.gpsimd.tensor_copy(out=halo[:, ci, :], in_=ubuf[:, ur, TS:TS + 3]), reads=[("ubuf", ur)], writes=[("halo", ci)])
                        for (r, ci, bx, ur) in rows:
                            S.op("act", lambda: nc.scalar.activation(out=ubuf[:, ur, 0:TS], in_=acc[ur][:], func=AF.Tanh), reads=[("acc", ur)], writes=[("ubuf", ur)])
                        for (r, ci, bx, ur) in rows:
                            S.op("dve", lambda: nc.vector.scalar_tensor_tensor(out=xbc_[:, r, :], in0=ubuf[:, ur, 0:TS], scalar=1.0, in1=acc[ur][:],
                                                                               op0=ALU.add, op1=ALU.mult), reads=[("ubuf", ur), ("acc", ur)], writes=[("xbc", gb, r)])
                    return f
                return [p_x(0), p_x(2), p_z(0), p_z(1)]

            cst = {}

            def stage_A(g, c):
                gb = g % 2
                xbc_ = xbc[gb]
                cs_ = cset[c % 2]
                ck = ("cs", c % 2)
                hs = slice(4 * g, 4 * g + 4)
                cs = slice(c * 128, (c + 1) * 128)
                bt = nb()
                T1 = psb(bt)
                for j, r in enumerate((0, 1, 2)):
                    S.op("pe", lambda: nc.tensor.transpose(out=T1[:, j * 128:(j + 1) * 128], in_=xbc_[:, r, cs], identity=identb[:]),
                         reads=[("xbc", gb, r), "identb"], writes=[pk(bt)], inc=(j == 2))
                T1x = T1[:, 0:256].rearrange("p (h q) -> p h q", q=64)
                S.op("act", lambda: nc.scalar.copy(out=cs_["xtok"][:], in_=T1[:, 0:256]), reads=[pk(bt)], writes=[(ck, "xtok")])
                S.op("pool", lambda: nc.gpsimd.tensor_tensor(out=cs_["xD"][:].rearrange("p (h q) -> p h q", q=64),
                                                             in0=cs_["xtok"][:].rearrange("p (h q) -> p h q", q=64),
                                                             in1=bc(D_bc[:, hs].unsqueeze(2), [128, 4, 64]), op=ALU.mult),
                     reads=[(ck, "xtok"), "D_bc"], writes=[(ck, "xD")])
                S.op("act", lambda: nc.scalar.copy(out=cs_["btok"][:], in_=T1[:, 256:384]), reads=[pk(bt)], writes=[(ck, "btok")])
                b1_ = nb()
                S.op("pe", lambda: nc.tensor.matmul(ps[b1_][:], lhsT=onesf[:], rhs=cs_["atri"][:], start=True, stop=True),
                     reads=["onesf", (ck, "atri")], writes=[pk(b1_)])
                cst[(g, c)] = dict(b1=b1_)

            def stage_A1(g, c):
                cs_ = cset[c % 2]
                ck = ("cs", c % 2)
                hs = slice(4 * g, 4 * g + 4)
                S.op("pool", lambda: nc.gpsimd.tensor_tensor(out=cs_["atri"][:].rearrange("p (h l) -> p h l", l=128),
                                                             in0=bc(trif[:].unsqueeze(1), [128, 4, 128]),
                                                             in1=bc(a_sb[:, c, hs].unsqueeze(2), [128, 4, 128]), op=ALU.mult),
                     reads=["trif", "a_sb"], writes=[(ck, "atri")])

            def stage_B1(g, c):
                gb = g % 2
                xbc_ = xbc[gb]
                cs_ = cset[c % 2]
                ck = ("cs", c % 2)
                hs = slice(4 * g, 4 * g + 4)
                cs = slice(c * 128, (c + 1) * 128)
                b1_ = cst[(g, c)]["b1"]
                X1 = ps[b1_][:].rearrange("p (h l) -> p h l", l=128)
                seg = cs_["atri"]
                S.op("dve", lambda: nc.vector.tensor_tensor(out=seg[:].rearrange("p (h l) -> p h l", l=128), in0=X1,
                                                            in1=bc(cumcolp[:, c, hs].unsqueeze(2), [128, 4, 128]), op=ALU.subtract),
                     reads=[pk(b1_), "cumcolp"], writes=[(ck, "atri")])
                S.op("act", lambda: nc.scalar.activation(out=e4[c % 2][:], in_=X1[:, :, 127], func=AF.Exp), reads=[pk(b1_)], writes=[("e4", c % 2)])
                seg3 = seg[:].rearrange("p (h l) -> p h l", l=128)
                S.op("pool", lambda: nc.gpsimd.affine_select(out=seg3, in_=seg3, pattern=[[0, 4], [1, 128]], compare_op=ALU.is_ge,
                                                             fill=neg_reg, base=0, channel_multiplier=-1), reads=[(ck, "atri")], writes=[(ck, "atri")])
                S.op("act", lambda: nc.scalar.activation(out=cs_["decayT"][:], in_=seg[:], func=AF.Exp), reads=[(ck, "atri")], writes=[(ck, "decayT")])
                b2_ = nb()
                S.op("pe", lambda: nc.tensor.matmul(ps[b2_][:, 0:128], lhsT=xbc_[:, 2, cs], rhs=xbc_[:, 3, cs], start=True, stop=True),
                     reads=[("xbc", gb, 2), ("xbc", gb, 3)], writes=[pk(b2_)])
                cst[(g, c)]["b2"] = b2_

            def stage_B2(g, c):
                cs_ = cset[c % 2]
                ck = ("cs", c % 2)
                b2_ = cst[(g, c)]["b2"]
                S.op("dve", lambda: nc.vector.tensor_tensor(out=cs_["GT"][:].rearrange("p (h l) -> p h l", l=128),
                                                            in0=cs_["decayT"][:].rearrange("p (h l) -> p h l", l=128),
                                                            in1=bc(ps[b2_][:, 0:128].unsqueeze(1), [128, 4, 128]), op=ALU.mult),
                     reads=[(ck, "decayT"), pk(b2_)], writes=[(ck, "GT")])
                dlast = cs_["decayT"][:].rearrange("p (h l) -> p h l", l=128)[:, :, 127:128]
                S.op("dve", lambda: nc.vector.tensor_tensor(out=cs_["xw"][:].rearrange("p (h q) -> p h q", q=64),
                                                            in0=cs_["xtok"][:].rearrange("p (h q) -> p h q", q=64),
                                                            in1=bc(dlast, [128, 4, 64]), op=ALU.mult),
                     reads=[(ck, "xtok"), (ck, "decayT")], writes=[(ck, "xw")])
                b3_ = nb()
                S.op("pe", lambda: nc.tensor.matmul(ps[b3_][:, 0:256], lhsT=identb[:], rhs=cs_["xD"][:], start=True, stop=False),
                     reads=["identb", (ck, "xD")], writes=[pk(b3_)], inc=False)
                for h in range(4):
                    S.op("pe", lambda: nc.tensor.matmul(ps[b3_][:, h * 64:(h + 1) * 64], lhsT=cs_["GT"][:, h * 128:(h + 1) * 128],
                                                        rhs=cs_["xtok"][:, h * 64:(h + 1) * 64], start=False, stop=True),
                         reads=[(ck, "GT"), (ck, "xtok")], writes=[pk(b3_)], inc=False)
                S.op("pe", lambda: nc.tensor.matmul(ps[b3_][:, 256:512], lhsT=cs_["btok"][:], rhs=cs_["xw"][:], start=True, stop=True),
                     reads=[(ck, "btok"), (ck, "xw")], writes=[pk(b3_)])
                cst[(g, c)]["b3"] = b3_

            def stage_C(g, c):
                gb = g % 2
                xbc_ = xbc[gb]
                hs = slice(4 * g, 4 * g + 4)
                cs = slice(c * 128, (c + 1) * 128)
                b3_ = cst[(g, c)]["b3"]
                st_g = stateT[:, g * 256:(g + 1) * 256]
                stb_g = stbf[:, g * 256:(g + 1) * 256]
                b4_ = nb()
                S.op("pe", lambda: nc.tensor.matmul(ps[b4_][:, 0:256], lhsT=xbc_[:, 3, cs], rhs=stb_g, start=True, stop=True),
                     reads=[("xbc", gb, 3), ("stbf", g)], writes=[pk(b4_)])
                S.op("pool", lambda: nc.gpsimd.tensor_tensor(out=sttmp[:].rearrange("p (h q) -> p h q", q=64),
                                                             in0=st_g.rearrange("p (h q) -> p h q", q=64),
                                                             in1=bc(e4[c % 2][:].unsqueeze(2), [128, 4, 64]), op=ALU.mult),
                     reads=[("stateT", g), ("e4", c % 2)], writes=["sttmp"])
                S.op("dve", lambda: nc.vector.tensor_tensor(out=ys[:].rearrange("p (h q) -> p h q", q=64),
                                                            in0=ps[b4_][:, 0:256].rearrange("p (h q) -> p h q", q=64),
                                                            in1=bc(expcum[:, c, hs].unsqueeze(2), [128, 4, 64]), op=ALU.mult),
                     reads=[pk(b4_), "expcum"], writes=["ys"])
                S.op("dve", lambda: nc.vector.tensor_tensor(out=st_g, in0=sttmp[:], in1=ps[b3_][:, 256:512], op=ALU.add),
                     reads=["sttmp", pk(b3_)], writes=[("stateT", g)])
                S.op("act", lambda: nc.scalar.copy(out=stb_g, in_=st_g), reads=[("stateT", g)], writes=[("stbf", g)])
                S.op("dve", lambda: nc.vector.tensor_tensor(out=ysum[:], in0=ps[b3_][:, 0:256], in1=ys[:], op=ALU.add),
                     reads=[pk(b3_), "ys"], writes=["ysum"])
                S.op("pool", lambda: nc.gpsimd.tensor_tensor(out=yg[:, c, :], in0=ysum[:], in1=zs[gb][:, c, :], op=ALU.mult),
                     reads=["ysum", ("zs", gb)], writes=[("yg", c)])
                S.op("act", lambda: nc.scalar.activation(out=junk[:], in_=yg[:, c, :], func=AF.Square, accum_out=ss[:, c:c + 1]),
                     reads=[("yg", c)], writes=["junk", ("ss", c)])

            def group_end1(g):
                S.op("act", lambda: nc.scalar.activation(out=sd4[:], in_=ss[:], func=AF.Sqrt, scale=1.0 / 256.0, bias=lneps[:, 1:2]),
                     reads=[("ss", c) for c in range(4)] + ["lneps"], writes=["sd4"])
                S.op("dve", lambda: nc.vector.reciprocal(out=rstd4[:], in_=sd4[:]), reads=["sd4"], writes=["rstd4"])
                for c in range(4):
                    S.op("act", lambda: nc.scalar.activation(out=ygn[:, c, :], in_=yg[:, c, :], func=AF.Copy, scale=rstd4[:, c:c + 1]),
                         reads=[("yg", c), "rstd4"], writes=[("ygn", c)])

            def group_end2(g):
                bn_ = nb()
                Tn = psb(bn_)
                for c in range(4):
                    for j in range(2):
                        S.op("pe", lambda: nc.tensor.transpose(out=Tn[:, (j * 4 + c) * 128:(j * 4 + c + 1) * 128], in_=ygn[:, c, j * 128:(j + 1) * 128],
                                                               identity=identb[:]),
                             reads=[("ygn", c), "identb"], writes=[pk(bn_)], inc=(c == 3 and j == 1))
                for j in range(2):
                    kc = 2 * g + j
                    S.op("dve", lambda: nc.vector.tensor_scalar(out=big[:, kc, :], in0=Tn[:, j * 512:(j + 1) * 512], scalar1=normw[:, kc:kc + 1],
                                                                scalar2=None, op0=ALU.mult),
                         reads=[pk(bn_), "prmB"], writes=[("big", kc)])

            for f in inproj_pieces(0):
                f()
            for g in range(NG):
                if g == 0:
                    for e_ in ("pe", "act", "dve", "pool"):
                        S.wait_all(e_, [iodom_out])
                P = inproj_pieces(g + 1) if g + 1 < NG else []
                P = P + [lambda: None] * (4 - len(P))
                order = list(P)
                if g > 0:
                    order.append(lambda: group_end2(g - 1))
                for it in range(-3, 5):
                    for (fn, c) in ((stage_C, it - 1), (stage_B2, it), (stage_B1, it + 1), (stage_A, it + 2), (stage_A1, it + 3)):
                        if 0 <= c <= 3:
                            order.append((lambda fn=fn, c=c: fn(g, c)))
                order.append(lambda: group_end1(g))
                for f in order:
                    f()
            group_end2(NG - 1)
            S.fence()
            for j in range(4):
                slot, skey = use_tile(("out", j))
                ov = slot[:, 0:16 * 256].rearrange("p (k c) -> p k c", c=256)
                for c in range(2):
                    k = 2 * j + c
                    b = nb()
                    for kk in range(16):
                        S.op("pe", lambda: nc.tensor.matmul(ps[b][:], lhsT=ov[:, kk, c * 128:(c + 1) * 128], rhs=big[:, kk, :],
                                                            start=(kk == 0), stop=(kk == 15)),
                             reads=[skey, ("big", kk)], writes=[pk(b)], inc=(kk == 15))
                    ln_accum(k, b, 0)
                done_tile()
            ln_finish(0)

        def attn_phase(sc, first_in_seq):
            t0 = sc * TS
            for j in range(2):
                slot, skey = use_tile(("kvk", j))
                kvv_ = slot[:, 0:4096].rearrange("p (k c) -> p k c", c=512)
                for pr in range(4):
                    b = nb()
                    for k in range(KD):
                        S.op("pe", lambda: nc.tensor.matmul(ps[b][:], lhsT=kvv_[:, k, pr * 128:(pr + 1) * 128], rhs=xTb[:, k, :], start=(k == 0), stop=(k == KD - 1)),
                             reads=[skey, ("xTb", k)], writes=[pk(b)], inc=(k == KD - 1))
                    S.op("act", lambda: nc.scalar.copy(out=KT[:, 4 * j + pr, t0:t0 + TS], in_=ps[b][:]), reads=[pk(b)], writes=[("KT", 4 * j + pr)])
                done_tile()
            for j in range(2):
                slot, skey = use_tile(("kvv", j))
                vv = slot[:, 0:4096].rearrange("p (k c) -> p k c", c=512)
                for tt in range(4):
                    b = nb()
                    for k in range(KD):
                        S.op("pe", lambda: nc.tensor.matmul(ps[b][:], lhsT=xTb[:, k, tt * 128:(tt + 1) * 128], rhs=vv[:, k, :],
                                                            start=(k == 0), stop=(k == KD - 1)),
                             reads=[skey, ("xTb", k)], writes=[pk(b)], inc=(k == KD - 1))
                    kt = 4 * sc + tt
                    S.op("dve", lambda: nc.vector.tensor_copy(out=VA[:, kt, 8 * j:8 * j + 8, 0:64], in_=ps[b][:].rearrange("p (h q) -> p h q", q=64)),
                         reads=[pk(b)], writes=[("VA", kt)])
                done_tile()
            bf_ = nb()
            for k in range(KD):
                S.op("pe", lambda: nc.tensor.matmul(ps[bf_][0:16, :], lhsT=wf[:, k, :], rhs=xTb[:, k, :], start=(k == 0), stop=(k == KD - 1)),
                     reads=["wf", ("xTb", k)], writes=[pk(bf_)], inc=(k == KD - 1))
            S.op("dve", lambda: nc.vector.tensor_scalar(out=f_v[:], in0=ps[bf_][0:16, :], scalar1=bf_col[:, 0:1], scalar2=None, op0=ALU.add),
                 reads=[pk(bf_), "bf_col"], writes=["f_v"])
            S.op("act", lambda: nc.scalar.activation(out=f_a[:], in_=f_v[:], func=AF.Abs), reads=["f_v"], writes=["f_a"])
            S.op("act", lambda: nc.scalar.activation(out=f_a[:], in_=f_a[:], func=AF.Exp, scale=-1.0), reads=["f_a"], writes=["f_a"])
            S.op("act", lambda: nc.scalar.activation(out=f_l[:], in_=f_a[:], func=AF.Ln, bias=1.0), reads=["f_a"], writes=["f_l"])
            S.op("dve", lambda: nc.vector.scalar_tensor_tensor(out=f_l[:], in0=f_v[:], scalar=0.0, in1=f_l[:], op0=ALU.min, op1=ALU.subtract),
                 reads=["f_v", "f_l"], writes=["f_l"])
            if first_in_seq:
                S.op("pool", lambda: nc.gpsimd.memset(Fcarry[:], 0.0), writes=["Fcarry"])
            S.op("dve", lambda: nc.vector.tensor_tensor_scan(out=Frow[:], data0=bc(onesf[0:16, 0:1], [16, TS]), data1=f_l[:], initial=Fcarry[:, 0:1],
                                                             op0=ALU.mult, op1=ALU.add),
                 reads=["onesf", "f_l", "Fcarry"], writes=["Frow"])
            S.op("dve", lambda: nc.vector.tensor_copy(out=Fcarry[:], in_=Frow[:, TS - 1:TS]), reads=["Frow"], writes=["Fcarry"])
            bt_ = nb()
            for tt in range(4):
                S.op("pe", lambda: nc.tensor.transpose(out=ps[bt_][:, tt * 16:(tt + 1) * 16], in_=Frow[:, tt * 128:(tt + 1) * 128], identity=identf[0:16, 0:16]),
                     reads=["Frow", "identf"], writes=[pk(bt_)], inc=False)
            S.op("dve", lambda: nc.vector.tensor_scalar(out=fdiag[:], in0=identf[0:16, 0:16], scalar1=Frow[:, 255:256], scalar2=None, op0=ALU.mult),
                 reads=["identf", "Frow"], writes=["fdiag"])
            S.op("pe", lambda: nc.tensor.matmul(ps[bt_][:, 64:80], lhsT=onesf[0:16, :], rhs=fdiag[:], start=True, stop=True),
                 reads=["onesf", "fdiag"], writes=[pk(bt_)])
            S.op("dve", lambda: nc.vector.tensor_copy(out=Fcol[:, 4 * sc:4 * sc + 4, :], in_=ps[bt_][:, 0:64].rearrange("p (t h) -> p t h", h=16)),
                 reads=[pk(bt_)], writes=["Fcol"])
            S.op("dve", lambda: nc.vector.tensor_copy(out=Fq0[:], in_=ps[bt_][:, 64:80]), reads=[pk(bt_)], writes=["Fq0"])
            nkt_all = 4 * sc + 4
            S.op("dve", lambda: nc.vector.tensor_tensor(out=bcol[:, 0:nkt_all, :], in0=bc(Fq0[:].unsqueeze(1), [128, nkt_all, 16]),
                                                        in1=Fcol[:, 0:nkt_all, :], op=ALU.subtract),
                 reads=["Fq0", "Fcol"], writes=["bcol"])
            for j in range(2):
                slot, skey = use_tile(("q", j))
                qvv_ = slot[:, 0:4096].rearrange("p (k c) -> p k c", c=512)
                for pr in range(4):
                    b = nb()
                    for k in range(KD):
                        S.op("pe", lambda: nc.tensor.matmul(ps[b][:], lhsT=qvv_[:, k, pr * 128:(pr + 1) * 128], rhs=xTb[:, k, :], start=(k == 0), stop=(k == KD - 1)),
                             reads=[skey, ("xTb", k)], writes=[pk(b)], inc=(k == KD - 1))
                    S.op("act", lambda: nc.scalar.activation(out=QT[:, 4 * j + pr, :], in_=ps[b][:], func=AF.Copy, scale=0.125),
                         reads=[pk(b)], writes=[("QT", 4 * j + pr)])
                done_tile()
            OT = big
            ring["n"] = 6
            ring["i"] = 0
            jobs = []
            nkt = 4 * sc + 4
            for h in range(AH):
                for kt in range(nkt):
                    jobs.append((h, kt))
            LA = 2
            NPT = 4
            pend = {}
            deferred = []

            def emit_st(i):
                h, kt = jobs[i]
                pr, po = h // 2, (h % 2) * 64
                jd = kt - 4 * sc
                c0 = 128 * jd if jd > 0 else 0
                n = TS - c0
                b = nb()
                S.op("pe", lambda: nc.tensor.matmul(ps[b][:, 0:n], lhsT=KT[po:po + 64, pr, kt * 128:(kt + 1) * 128],
                                                    rhs=QT[po:po + 64, pr, c0:TS], start=True, stop=True),
                     reads=[("KT", pr), ("QT", pr)], writes=[pk(b)])
                pend[i] = b

            def emit_rest(i):
                h, kt = jobs[i]
                ob = 6 + (h % 2)
                okey = pk(ob)
                jd = kt - 4 * sc
                c0 = 128 * jd if jd > 0 else 0
                n = TS - c0
                b = pend.pop(i)
                pt = PT[i % NPT]
                ptk = ("PT", i % NPT)
                oreg = ps[ob][0:65, :]
                S.op("act", lambda: nc.scalar.activation(out=pt[:, c0:TS], in_=ps[b][:, 0:n], func=AF.Exp, bias=bcol[:, kt, h:h + 1]),
                     reads=[pk(b), "bcol"], writes=[ptk])
                if jd >= 0:
                    S.op("pool", lambda: nc.gpsimd.affine_select(out=pt[:, c0:c0 + 128], in_=pt[:, c0:c0 + 128], pattern=[[1, 128]],
                                                                 compare_op=ALU.is_ge, fill=zero_reg, base=0, channel_multiplier=-1),
                         reads=[ptk], writes=[ptk])
                last = (kt == nkt - 1)
                S.op("pe", lambda: nc.tensor.matmul(oreg[:, c0:TS], lhsT=VA[:, kt, h, :], rhs=pt[:, c0:TS], start=(kt == 0), stop=last),
                     reads=[("VA", kt), ptk], writes=[okey], inc=last)
                if last:
                    rr_, Rs_ = rr[h % 2], Rs[h % 2]
                    S.op("dve", lambda: nc.vector.reciprocal(out=rr_[64:65, :], in_=ps[ob][64:65, :]), reads=[okey], writes=[("rr", 0)])

                    def fin(h=h, ob=ob, okey=okey, rr_=rr_, Rs_=Rs_):
                        b2 = nb()
                        S.op("pe", lambda: nc.tensor.matmul(ps[b2][0:64, :], lhsT=onesf[64:65, 0:64], rhs=rr_[64:65, :], start=True, stop=True),
                             reads=["onesf", ("rr", 0)], writes=[pk(b2)])
                        S.op("act", lambda: nc.scalar.copy(out=Rs_[:], in_=ps[b2][0:64, :]), reads=[pk(b2)], writes=[("Rs", 0)])
                        S.op("dve", lambda: nc.vector.tensor_tensor(out=OT[0:64, h, :], in0=ps[ob][0:64, :], in1=Rs_[:], op=ALU.mult),
                             reads=[okey, ("Rs", 0)], writes=[("big", h)])
                    deferred.append([2, fin])

            nj = len(jobs)
            for i in range(nj + LA):
                if i < nj:
                    emit_st(i)
                for dfr in list(deferred):
                    dfr[0] -= 1
                    if dfr[0] <= 0:
                        deferred.remove(dfr)
                        dfr[1]()
                if i >= LA:
                    emit_rest(i - LA)
            for dfr in deferred:
                dfr[1]()
            ring["n"] = 8
            for j in range(4):
                slot, skey = use_tile(("o", j))
                ov = slot[0:64, 0:16 * 256].rearrange("p (h c) -> p h c", c=256)
                for c in range(2):
                    k = 2 * j + c
                    b = nb()
                    for h in range(AH):
                        S.op("pe", lambda: nc.tensor.matmul(ps[b][:], lhsT=ov[:, h, c * 128:(c + 1) * 128], rhs=OT[0:64, h, :],
                                                            start=(h == 0), stop=(h == AH - 1)),
                             reads=[skey, ("big", h)], writes=[pk(b)], inc=(h == AH - 1))
                    ln_accum(k, b, 2)
                done_tile()
            ln_finish(2)

        def xk(tt):
            nm = "lnb" if tt < 2 else "lnsq"
            return [(nm, 4 * (tt % 2) + i) for i in range(4)]
        XK = xk(0) + xk(1) + xk(2) + xk(3)
        xld = big[:, 0:16, :].rearrange("p k t -> p (k t)").bitcast(F32).rearrange("p (t d) -> p t d", d=D)
        BK4 = [[("big", 4 * tt + i) for i in range(4)] for tt in range(4)]

        def load_x(bseq_, sc_):
            S.dma("sp", xld, dr["x"][bseq_, sc_ * TS:(sc_ + 1) * TS, :].rearrange("(t p) d -> p t d", p=128), [], BK4[0] + BK4[1] + BK4[2] + BK4[3], iodom_in)
        S.fence()
        gi = 0
        for bseq in range(NB if stop_after != "setup" else 0):
            for sc in range(NSC):
                tap.idx = gi
                t0 = sc * TS
                first = (sc == 0)
                if gi == 0 or stop_after is not None:
                    load_x(bseq, sc)
                for k in range(KD if stop_after != "xdma" else 0):
                    b = nb()
                    for tt in range(4):
                        S.op("pe", lambda: nc.tensor.transpose(out=ps[b][:, tt * 128:(tt + 1) * 128], in_=xld[:, tt, k * 128:(k + 1) * 128], identity=identf[:]),
                             reads=BK4[tt] + ["identf"], writes=[pk(b)], inc=(tt == 3))
                    S.op("act", lambda: nc.scalar.copy(out=xT[:, k, :], in_=ps[b][:]), reads=[pk(b)], writes=[("xT", k)])
                    S.op("dve", lambda: nc.vector.tensor_copy(out=xTb[:, k, :], in_=ps[b][:]), reads=[pk(b)], writes=[("xTb", k)])
                S.fence()
                if stop_after not in ("xload", "xdma"):
                    ssd_phase(first)
                    tap("dbg_x1", xT[:, 0, :], ("xT", 0), [128, TS])
                if stop_after not in ("xload", "ssd", "xdma"):
                    ffn_phase(0, 1)
                    tap("dbg_x2", xT[:, 0, :], ("xT", 0), [128, TS])
                if stop_after not in ("xload", "ssd", "ffn0", "xdma"):
                    S.fence()
                    attn_phase(sc, first)
                    tap("dbg_x3", xT[:, 0, :], ("xT", 0), [128, TS])
                    ffn_phase(1, 3)
                    nxt = gi + 1
                    if nxt < NB * NSC:
                        load_x(nxt // NSC, nxt % NSC)
                for tt in range(4 if stop_after != "xdma" else 0):
                    for hf in range(2):
                        b = nb()
                        for kq in range(4):
                            k = hf * 4 + kq
                            S.op("pe", lambda: nc.tensor.transpose(out=ps[b][:, kq * 128:(kq + 1) * 128], in_=xT[:, k, tt * 128:(tt + 1) * 128], identity=identf[:]),
                                 reads=[("xT", k), "identf"], writes=[pk(b)], inc=(kq == 3))
                        if hf == 0:
                            S.op("act", lambda: nc.scalar.copy(out=xin[:, tt, hf * 512:(hf + 1) * 512], in_=ps[b][:]), reads=[pk(b)], writes=xk(tt))
                        else:
                            S.op("dve", lambda: nc.vector.tensor_copy(out=xin[:, tt, hf * 512:(hf + 1) * 512], in_=ps[b][:]), reads=[pk(b)], writes=xk(tt))
                S.dma("sp", out_d[bseq, t0:t0 + TS, :].rearrange("(t p) d -> p t d", p=128), xin, XK, [], iodom_out)
                gi += 1
        assert stop_after is not None or wstate["next_use"] == total_tiles, (wstate, total_tiles)
        S.wait_all("sp", [iodom_out, dbgdom])
        build.stats = dict(nins=dict(S.nins), ndma=S.ndma, counts={k: v.count for k, v in S.dom.items()})
    return nc, list(dbg.keys())


_CACHE = {}


def kernel(**inputs):
    n_cores = 8
    x = np.ascontiguousarray(inputs["x"], dtype=np.float32)
    B, SEQ, _ = x.shape
    NB = B // n_cores
    key = (NB, SEQ)
    if key not in _CACHE:
        _CACHE[key] = build(NB, SEQ)[0]
    nc = _CACHE[key]
    in_maps = []
    for c in range(n_cores):
        m = {k: np.ascontiguousarray(v, dtype=np.float32) for k, v in inputs.items() if k != "x"}
        m["x"] = x[c * NB:(c + 1) * NB]
        in_maps.append(m)
    res = run_bass_kernel_spmd(nc, in_maps, core_ids=list(range(n_cores)))
    return np.concatenate([r["out"] for r in res.results], axis=0)
```

```python
import numpy as np
from contextlib import ExitStack
from collections import defaultdict

import concourse.bass as bass
import concourse.mybir as mybir
from concourse.bass_utils import run_bass_kernel_spmd

F32 = mybir.dt.float32
BF16 = mybir.dt.bfloat16
AF = mybir.ActivationFunctionType
ALU = mybir.AluOpType

D = 1024
KD = 8
DI = 2048
NG = 8
NHEAD = 32
DFF = 2816
NF = 22
AH = 16
TS = 512
DEPTH = 2
ALPHA = (2.0 * DEPTH) ** 0.25
LN_EPS = 1e-5
RMS_EPS = 1e-5
SLOT = 4096
NSLOT = 3
NEG = -30000.0


class Dom:
    def __init__(self, nc, es, name, step, epoch):
        self.nc, self.es, self.name, self.step, self.epoch = nc, es, name, step, epoch
        self.sems = []
        self.count = 0

    def sem_for(self, cnt):
        e = (cnt - 1) // self.epoch
        while len(self.sems) <= e:
            self.sems.append(self.es.enter_context(self.nc.semaphore(f"s_{self.name}_{len(self.sems)}")))
        return self.sems[e], ((cnt - 1) % self.epoch + 1) * self.step


class Sched:
    def __init__(self, nc, es):
        self.nc, self.es = nc, es
        self.eng = {"pe": nc.tensor, "act": nc.scalar, "dve": nc.vector, "pool": nc.gpsimd, "sp": nc.sync}
        self.dom = {e: Dom(nc, es, e, 1, 4096) for e in ("pe", "act", "dve", "pool")}
        self.seen = defaultdict(int)
        self.lastw = {}
        self.readers = defaultdict(dict)
        self.ndma = 0
        self.nins = defaultdict(int)

    def new_dma_dom(self, name):
        return Dom(self.nc, self.es, name, 16, 1024)

    def _deps(self, own, reads, writes):
        deps = {}

        def need(dc, same_ok):
            dom, cnt = dc
            if dom is own and same_ok:
                return
            if deps.get(dom, 0) < cnt:
                deps[dom] = cnt

        for k in reads:
            if k in self.lastw:
                need(self.lastw[k], False)
            if isinstance(k, tuple) and k[0] == "ps":
                for dom, cnt in self.readers[k].items():
                    need((dom, cnt), True)
        for k in writes:
            if k in self.lastw:
                need(self.lastw[k], True)
            for dom, cnt in self.readers[k].items():
                need((dom, cnt), True)
        return deps

    def _wait(self, e, deps, own=None):
        for dom, cnt in deps.items():
            if self.seen[(e, dom.name)] >= cnt:
                continue
            if dom is own:
                assert cnt <= own.count, "same-engine wait on a future completion"
            sem, val = dom.sem_for(cnt)
            self.eng[e].wait_ge(sem, val)
            self.nins[e] += 1
            self.seen[(e, dom.name)] = cnt

    def op(self, e, fn, reads=(), writes=(), inc=True):
        own = self.dom[e]
        self._wait(e, self._deps(own, reads, writes), own)
        ins = fn()
        self.nins[e] += 1
        tag = own.count + 1
        if inc:
            own.count += 1
            sem, _ = own.sem_for(own.count)
            ins.then_inc(sem, 1)
        for k in reads:
            if self.readers[k].get(own, 0) < tag:
                self.readers[k][own] = tag
        for k in writes:
            self.lastw[k] = (own, tag)
            self.readers[k] = {}
        return ins

    def dma(self, q, out, in_, reads, writes, dom):
        self._wait(q, self._deps(None, reads, writes))
        ins = self.eng[q].dma_start(out=out, in_=in_)
        self.nins[q] += 1
        self.ndma += 1
        dom.count += 1
        sem, _ = dom.sem_for(dom.count)
        ins.then_inc(sem, 16)
        for k in reads:
            self.readers[k][dom] = dom.count
        for k in writes:
            self.lastw[k] = (dom, dom.count)
            self.readers[k] = {}
        return ins

    def fence(self):
        es_ = ("pe", "act", "dve", "pool")
        for e in es_:
            self._wait(e, {self.dom[f]: self.dom[f].count for f in es_ if f != e and self.dom[f].count > 0})

    def wait_all(self, e, doms):
        for dom in doms:
            if dom.count > 0:
                self._wait(e, {dom: dom.count})


def bc(ap, shape):
    return ap.to_broadcast(list(shape))


def weight_tiles():
    tiles = []
    for g in range(NG):
        tiles.append((("inA", g), 8 * 512, [("ssm_in_w", 0, g * 256, 256, "kpc", 0, 512, 0),
                                             ("ssm_in_w", 0, 2048 + g * 256, 256, "kpc", 0, 512, 256)]))
        tiles.append((("inB", g), 8 * 256, [("ssm_in_w", 0, 4096 + g * 128, 128, "kpc", 0, 256, 0),
                                             ("ssm_in_w", 0, 5120 + g * 128, 128, "kpc", 0, 256, 128)]))
    for j in range(4):
        tiles.append((("out", j), 16 * 256, [("ssm_out_w", 0, j * 256, 256, "kpc", 0, 256, 0)]))

    def ffn(l):
        for j in range(6):
            nfc = 4 if j < 5 else 2
            tiles.append((("g", l, j), 8 * 512, [("ffn_gate_w", l, j * 512, nfc * 128, "kpc", 0, 512, 0)]))
            tiles.append((("u", l, j), 8 * 512, [("ffn_up_w", l, j * 512, nfc * 128, "kpc", 0, 512, 0)]))
        for hf in range(2):
            for fg in range(3):
                nf = 8 if fg < 2 else 6
                tiles.append((("dn", l, hf, fg), nf * 512, [("ffn_down_w", l, hf * 512, 512, "kpc_rows", 0, 512, 0, fg * 8, nf)]))

    ffn(0)
    for j in range(2):
        tiles.append((("kvk", j), 4096, [("kv_w", None, j * 512, 512, "kpc", 0, 512, 0)]))
    for j in range(2):
        tiles.append((("kvv", j), 4096, [("kv_w", None, 1024 + j * 512, 512, "kpc", 0, 512, 0)]))
    for j in range(2):
        tiles.append((("q", j), 4096, [("att_q_w", 0, j * 512, 512, "kpc", 0, 512, 0)]))
    for j in range(4):
        tiles.append((("o", j), 16 * 256, [("att_o_w", 0, j * 256, 256, "hpc", 0, 256, 0)]))
    ffn(1)
    return tiles


IN_SPECS = [
    ("x", None), ("ssm_in_w", [1, 1024, 6176]), ("ssm_conv_w", [1, 4, 4096]), ("ssm_conv_b", [1, 4096]),
    ("ssm_dt_bias", [1, 32]), ("ssm_a_log", [1, 32]), ("ssm_d", [1, 32]), ("ssm_norm_w", [1, 2048]),
    ("ssm_out_w", [1, 2048, 1024]), ("kv_w", [1024, 2064]), ("kv_b_f", [16]), ("att_q_w", [1, 1024, 1024]),
    ("att_o_w", [1, 1024, 1024]), ("ffn_gate_w", [2, 1024, 2816]), ("ffn_up_w", [2, 1024, 2816]),
    ("ffn_down_w", [2, 2816, 1024]), ("ln_mix_g", [2, 1024]), ("ln_mix_b", [2, 1024]), ("ln_ffn_g", [2, 1024]),
    ("ln_ffn_b", [2, 1024]),
]


def build(NB=4, SEQ=2048, debug=False, stop_after=None):
    nc = bass.Bass("TRN2", target_bir_lowering=False)
    NSC = SEQ // TS
    NKT = SEQ // 128
    dr = {}
    for name, shp in IN_SPECS:
        if name == "x":
            shp = [NB, SEQ, D]
        dr[name] = nc.dram_tensor(name, shp, F32, kind="ExternalInput").ap()
    out_d = nc.dram_tensor("out", [NB, SEQ, D], F32, kind="ExternalOutput").ap()
    tiles = weight_tiles()
    NT = len(tiles)
    wscr = nc.dram_tensor("wscr", [NT, 128, SLOT], BF16, kind="Internal").ap()
    dbg = {}

    with ExitStack() as es:
        ec = es.enter_context
        S = Sched(nc, es)

        def sb(name, shape, dt=F32):
            return ec(nc.sbuf_tensor(name, list(shape), dt))

        identf = sb("identf", [128, 128]); identb = sb("identb", [128, 128], BF16)
        onesf = sb("onesf", [128, 128]); trif = sb("trif", [128, 128])
        lnones = sb("lnones", [128, 128], BF16)
        negmask = sb("negmask", [128, 512], BF16)
        prmA = sb("prmA", [128, 128]); prmB = sb("prmB", [128, 128])
        dtb_bc = sb("dtb_bc", [128, 32]); A_bc = sb("A_bc", [128, 32]); D_bc = sb("D_bc", [128, 32])
        bf_col = sb("bf_col", [16, 1])
        wdt = sb("wdt", [128, 8, 32], BF16); wf = sb("wf", [128, 8, 16], BF16)
        wslot = [sb(f"wslot{i}", [128, SLOT], BF16) for i in range(NSLOT)]
        scr16 = sb("scr16", [128, 4096])
        xin = scr16[:].rearrange("p (t d) -> p t d", d=D)
        lnb = scr16[:, 0:2048].bitcast(BF16).rearrange("p (k t) -> p k t", t=TS)
        lnsq = scr16[:, 2048:4096].bitcast(BF16).rearrange("p (k t) -> p k t", t=TS)
        xT = sb("xT", [128, KD, TS]); xTb = sb("xTb", [128, KD, TS], BF16)
        mean_sb = sb("mean_sb", [128, TS]); rstd_sb = sb("rstd_sb", [128, TS])
        big = sb("big", [128, NF, TS], BF16)
        acc = [sb(f"acc{i}", [128, TS]) for i in range(2)]
        lnt = acc
        stateT = sb("stateT", [128, NHEAD * 64]); stbf = sb("stbf", [128, NHEAD * 64], BF16)
        halo = sb("halo", [128, 32, 3])
        KT = sb("KT", [128, 8, SEQ], BF16)
        VA = sb("VA", [128, NKT, AH, 65], BF16)
        Fcarry = sb("Fcarry", [16, 1]); Fcol = sb("Fcol", [128, NKT, AH])
        lneps = sb("lneps", [128, 2])
        ARENA = 7750
        arena = sb("arena", [128, ARENA])
        ar = {"o": 0}

        def carve(shape, dt=F32):
            n = int(np.prod(shape[1:]))
            w = n if dt == F32 else (n + 1) // 2
            o = ar["o"]
            assert o + w <= ARENA, ("arena overflow", o, w)
            ar["o"] = o + w
            v = arena[0:shape[0], o:o + w]
            if dt != F32:
                v = v.bitcast(dt)
            if len(shape) == 3:
                v = v.rearrange("p (a b) -> p a b", b=shape[2])
            return v

        stg1 = carve([128, 128]); stg2 = carve([128, 128])
        ar["o"] = 0
        dt_sb = carve([128, 4, 32]); a_sb = carve([128, 4, 32]); cumcol = carve([128, 4, 32]); expcum = carve([128, 4, 32])
        sp_t = [carve([128, 4, 32]) for _ in range(3)]
        cumcolp = carve([128, 4, 32])
        ubuf = carve([128, 2, TS + 4])
        e4 = [carve([128, 4]) for _ in range(2)]
        ys = carve([128, 256]); ysum = carve([128, 256])
        yg = carve([128, 4, 256]); ss = carve([128, 4]); sd4 = carve([128, 4]); rstd4 = carve([128, 4])
        junk = carve([128, 256]); ygn = carve([128, 4, 256], BF16); sttmp = carve([128, 256])
        zs = [carve([128, 4, 256], BF16), None]
        xbc = [carve([128, 4, TS], BF16), None]

        def chunk_set():
            return dict(xtok=carve([128, 256], BF16), xD=carve([128, 256], BF16), btok=carve([128, 128], BF16),
                        atri=carve([128, 512]), decayT=carve([128, 512], BF16), GT=carve([128, 512], BF16), xw=carve([128, 256], BF16))
        cset = [chunk_set(), None]
        ssd_top = ar["o"]
        sav = (arena, ar["o"])
        arena_main = arena

        def carve16(shape, dt=F32):
            n = int(np.prod(shape[1:]))
            w = n if dt == F32 else (n + 1) // 2
            o = c16["o"]
            assert o + w <= 4096, ("scr16 overflow", o, w)
            c16["o"] = o + w
            v = scr16[0:shape[0], o:o + w]
            if dt != F32:
                v = v.bitcast(dt)
            if len(shape) == 3:
                v = v.rearrange("p (a b) -> p a b", b=shape[2])
            return v
        c16 = {"o": 0}
        zs[1] = carve16([128, 4, 256], BF16)
        xbc[1] = carve16([128, 4, TS], BF16)
        cset[1] = dict(xtok=carve16([128, 256], BF16), xD=carve16([128, 256], BF16), btok=carve16([128, 128], BF16),
                       atri=carve16([128, 512]), decayT=carve16([128, 512], BF16), GT=carve16([128, 512], BF16), xw=carve16([128, 256], BF16))
        ar["o"] = 0
        QT = carve([128, 8, TS], BF16)
        f_v = carve([16, TS]); f_a = carve([16, TS]); f_l = carve([16, TS]); Frow = carve([16, TS])
        fdiag = carve([16, 16]); Fq0 = carve([128, AH]); bcol = carve([128, NKT, AH])
        PT = [carve([128, 512], BF16) for _ in range(4)]
        rr = [carve([128, 512])] * 2; Rs = [carve([64, 512])] * 2
        att_top = ar["o"]
        ps = [ec(nc.psum_tensor(f"ps{i}", [128, 512], F32)) for i in range(8)]

        def psb(i):
            return ps[i][:].bitcast(BF16)

        ring = {"i": 0, "n": 8}

        def nb():
            i = ring["i"] % ring["n"]
            ring["i"] = (i + 1) % ring["n"]
            return i

        def pk(i):
            return ("ps", i)

        P_ = "pool"
        neg_reg = nc.gpsimd.to_reg(NEG)
        zero_reg = nc.gpsimd.to_reg(0.0)
        S.op(P_, lambda: nc.gpsimd.memset(identf[:], 0.0), writes=["identf"])
        S.op(P_, lambda: nc.gpsimd.affine_select(out=identf[:], in_=identf[:], pattern=[[-1, 128]], compare_op=ALU.not_equal,
                                                 fill=1.0, base=0, channel_multiplier=1), reads=["identf"], writes=["identf"])
        S.op(P_, lambda: nc.gpsimd.tensor_copy(out=identb[:], in_=identf[:]), reads=["identf"], writes=["identb"])
        S.op(P_, lambda: nc.gpsimd.memset(onesf[:], 1.0), writes=["onesf"])
        S.op(P_, lambda: nc.gpsimd.memset(lnones[:], 1.0 / D), writes=["lnones"])
        S.op(P_, lambda: nc.gpsimd.memset(trif[:], 1.0), writes=["trif"])
        S.op(P_, lambda: nc.gpsimd.affine_select(out=trif[:], in_=trif[:], pattern=[[1, 128]], compare_op=ALU.is_ge,
                                                 fill=0.0, base=0, channel_multiplier=-1), reads=["trif"], writes=["trif"])
        S.op(P_, lambda: nc.gpsimd.memset(negmask[:], 0.0), writes=["negmask"])
        nm3 = negmask[:].rearrange("p (h l) -> p h l", h=4)
        S.op(P_, lambda: nc.gpsimd.affine_select(out=nm3, in_=nm3, pattern=[[0, 4], [1, 128]], compare_op=ALU.is_ge,
                                                 fill=NEG, base=0, channel_multiplier=-1), reads=["negmask"], writes=["negmask"])
        S.op(P_, lambda: nc.gpsimd.memset(stg2[:], 0.0), writes=["stg2"])

        cdom = S.new_dma_dom("cst")
        S.dma("sp", stg1[:], dr["ssm_conv_w"][0].rearrange("k (c p) -> (k c) p", p=128), [], ["stg1"], cdom)
        rows = [("ssm_conv_b", dr["ssm_conv_b"][0], 32, 0), ("ssm_norm_w", dr["ssm_norm_w"][0], 16, 32),
                ("ln_mix_g", dr["ln_mix_g"].rearrange("l d -> (l d)"), 16, 48), ("ln_mix_b", dr["ln_mix_b"].rearrange("l d -> (l d)"), 16, 64),
                ("ln_ffn_g", dr["ln_ffn_g"].rearrange("l d -> (l d)"), 16, 80), ("ln_ffn_b", dr["ln_ffn_b"].rearrange("l d -> (l d)"), 16, 96)]
        for (_, src, n, r0) in rows:
            S.dma("sp", stg2[r0:r0 + n, :], src.rearrange("(c p) -> c p", p=128), [], ["stg2"], cdom)
        S.dma("sp", dtb_bc[:], dr["ssm_dt_bias"].partition_broadcast(128), [], ["dtb_bc"], cdom)
        S.dma("sp", A_bc[:], dr["ssm_a_log"].partition_broadcast(128), [], ["A_bc"], cdom)
        S.dma("sp", D_bc[:], dr["ssm_d"].partition_broadcast(128), [], ["D_bc"], cdom)
        S.dma("sp", bf_col[:], dr["kv_b_f"].rearrange("(h o) -> h o", o=1), [], ["bf_col"], cdom)
        S.op("act", lambda: nc.scalar.activation(out=A_bc[:], in_=A_bc[:], func=AF.Exp), reads=["A_bc"], writes=["A_bc"])
        S.op("dve", lambda: nc.vector.tensor_scalar_mul(out=A_bc[:], in0=A_bc[:], scalar1=-1.0), reads=["A_bc"], writes=["A_bc"])
        b0 = nb()
        S.op("pe", lambda: nc.tensor.transpose(out=ps[b0][:, 0:128], in_=stg1[:], identity=identf[:]), reads=["stg1", "identf"], writes=[pk(b0)])
        S.op("dve", lambda: nc.vector.tensor_copy(out=prmA[:], in_=ps[b0][:, 0:128]), reads=[pk(b0)], writes=["prmA"])
        b1 = nb()
        S.op("pe", lambda: nc.tensor.transpose(out=ps[b1][:, 0:128], in_=stg2[:], identity=identf[:]), reads=["stg2", "identf"], writes=[pk(b1)])
        S.op("dve", lambda: nc.vector.tensor_copy(out=prmB[:], in_=ps[b1][:, 0:128]), reads=[pk(b1)], writes=["prmB"])
        S.op("dve", lambda: nc.vector.tensor_scalar_mul(out=prmA[:], in0=prmA[:], scalar1=0.5), reads=["prmA"], writes=["prmA"])
        S.op("dve", lambda: nc.vector.tensor_scalar_mul(out=prmB[:, 0:32], in0=prmB[:, 0:32], scalar1=0.5), reads=["prmB"], writes=["prmB"])
        cw = prmA[:].rearrange("p (k c) -> p k c", k=4)
        cb = prmB[:, 0:32]
        normw = prmB[:, 32:48]
        lng = {0: prmB[:, 48:56], 1: prmB[:, 80:88], 2: prmB[:, 56:64], 3: prmB[:, 88:96]}
        lnbias = {0: prmB[:, 64:72], 1: prmB[:, 96:104], 2: prmB[:, 72:80], 3: prmB[:, 104:112]}

        import os
        wcdom = S.new_dma_dom("wcv")
        stf = [scr16[:], xT[:].rearrange("p k t -> p (k t)")]
        stb = [big[:, 0:8, :].rearrange("p k t -> p (k t)"), big[:, 8:16, :].rearrange("p k t -> p (k t)")]
        ktf = KT[:].rearrange("p k t -> p (k t)").bitcast(F32)
        for i in range(ktf.shape[1] // 4096):
            stf.append(ktf[:, i * 4096:(i + 1) * 4096])
        vaf = VA[:].rearrange("p a b c -> p (a b c)")
        for i in range(vaf.shape[1] // 4096):
            stb.append(vaf[:, i * 4096:(i + 1) * 4096])
        NST = min(len(stf), len(stb), 4)
        cvl = [S.new_dma_dom(f"cvl{i}") for i in range(NST)]
        cvs = [S.new_dma_dom(f"cvs{i}") for i in range(NST)]
        cast_eng = ["dve", "act", "pool"]
        for ti, (name, nel, parts) in enumerate(tiles):
            if os.environ.get("SKIP_CONV"):
                break
            sl = ti % NST
            npart = 64 if name[0] == "o" else 128
            for part in parts:
                (src, idx, c0, n, kind, base, cstride, coff) = part[:8]
                w = dr[src] if idx is None else dr[src][idx]
                if kind == "kpc_rows":
                    k0, nk = part[8], part[9]
                    s_ap = w[k0 * 128:(k0 + nk) * 128, c0:c0 + n].rearrange("(k p) c -> p k c", p=128)
                    d_ap = stf[sl][:, base:base + nk * cstride].rearrange("p (k c) -> p k c", c=cstride)[:, :, coff:coff + n]
                elif kind == "kpc":
                    nk = w.shape[0] // 128
                    s_ap = w[:, c0:c0 + n].rearrange("(k p) c -> p k c", p=128)
                    d_ap = stf[sl][:, base:base + nk * cstride].rearrange("p (k c) -> p k c", c=cstride)[:, :, coff:coff + n]
                else:
                    s_ap = w[:, c0:c0 + n].rearrange("(h p) c -> p h c", p=64)
                    d_ap = stf[sl][0:64, base:base + 16 * cstride].rearrange("p (h c) -> p h c", c=cstride)[:, :, coff:coff + n]
                S.dma("act" if (ti % 2) else "sp", d_ap, s_ap, [], [("stf", sl)], cvl[sl])
            ce = cast_eng[ti % 3]
            if ce == "dve":
                S.op("dve", lambda: nc.vector.tensor_copy(out=stb[sl][0:npart, 0:nel], in_=stf[sl][0:npart, 0:nel]), reads=[("stf", sl)], writes=[("stb", sl)])
            elif ce == "act":
                S.op("act", lambda: nc.scalar.copy(out=stb[sl][0:npart, 0:nel], in_=stf[sl][0:npart, 0:nel]), reads=[("stf", sl)], writes=[("stb", sl)])
            else:
                S.op("pool", lambda: nc.gpsimd.tensor_copy(out=stb[sl][0:npart, 0:nel], in_=stf[sl][0:npart, 0:nel]), reads=[("stf", sl)], writes=[("stb", sl)])
            S.dma("act" if (ti % 2) else "sp", wscr[ti][0:npart, 0:nel], stb[sl][0:npart, 0:nel], [("stb", sl)], [("wscr", ti)], cvs[sl])
        S.dma("pool", wdt[:], dr["ssm_in_w"][0][:, 6144:6176].rearrange("(k p) c -> p k c", p=128), [], ["wdt"], wcdom)
        S.dma("pool", wf[:], dr["kv_w"][:, 2048:2064].rearrange("(k p) c -> p k c", p=128), [], ["wf"], wcdom)
        S.wait_all("pool", cvs + cvl)
        S.op("pool", lambda: nc.gpsimd.memset(VA[:], 1.0), reads=[("stb", i) for i in range(NST)], writes=[("VA", kt) for kt in range(NKT)])
        S.wait_all("sp", cvs + cvl)
        S.wait_all("pe", cvs + cvl)
        S.wait_all("act", cvs + cvl)
        S.wait_all("dve", cvs + cvl)
        S.wait_all("pool", cvs + cvl)

        wdoms = [S.new_dma_dom(f"w{i}") for i in range(NSLOT)]
        wstate = {"next_load": 0, "next_use": 0}
        total_tiles = NB * NSC * NT

        def prefetch():
            i = wstate["next_load"]
            if i >= total_tiles:
                return
            wstate["next_load"] += 1
            ti = i % NT
            s = i % NSLOT
            nel = tiles[ti][1]
            npart = 64 if tiles[ti][0][0] == "o" else 128
            S.dma("sp", wslot[s][0:npart, 0:nel], wscr[ti][0:npart, 0:nel], [("wscr", ti)], [("wslot", s)], wdoms[s])

        def use_tile(expect):
            if stop_after is not None:
                while tiles[wstate["next_use"] % NT][0] != expect:
                    wstate["next_use"] += 1
                    prefetch()
            i = wstate["next_use"]
            wstate["next_use"] += 1
            ti = i % NT
            assert tiles[ti][0] == expect, (tiles[ti][0], expect)
            s = i % NSLOT
            return wslot[s], ("wslot", s)

        def done_tile():
            prefetch()

        for _ in range(NSLOT):
            prefetch()

        iodom_in = S.new_dma_dom("xin")
        iodom_out = S.new_dma_dom("xout")
        dbgdom = S.new_dma_dom("dbg")

        def tap(name, ap, key, shape):
            if not debug:
                return
            if name not in dbg:
                dbg[name] = nc.dram_tensor(name, [NB * NSC] + list(shape), ap.dtype, kind="ExternalOutput").ap()
            S.dma("sp", dbg[name][tap.idx], ap, [key], [], dbgdom)
        tap.idx = 0

        def ln_accum(k, bank, ln_idx):
            S.op("dve", lambda: nc.vector.scalar_tensor_tensor(out=xT[:, k, :], in0=xT[:, k, :], scalar=ALPHA, in1=ps[bank][:],
                                                               op0=ALU.mult, op1=ALU.add),
                 reads=[("xT", k), pk(bank)], writes=[("xT", k)])
            S.op("act", lambda: nc.scalar.copy(out=lnb[:, k, :], in_=xT[:, k, :]), reads=[("xT", k)], writes=[("lnb", k)])
            S.op("act", lambda: nc.scalar.activation(out=lnsq[:, k, :], in_=xT[:, k, :], func=AF.Square), reads=[("xT", k)], writes=[("lnsq", k)])

        def ln_finish(ln_idx):
            bm, be = nb(), nb()
            for k in range(KD):
                S.op("pe", lambda: nc.tensor.matmul(ps[bm][:], lhsT=lnones[:], rhs=lnb[:, k, :], start=(k == 0), stop=(k == KD - 1)),
                     reads=[("lnb", k), "lnones"], writes=[pk(bm)], inc=(k == KD - 1))
            for k in range(KD):
                S.op("pe", lambda: nc.tensor.matmul(ps[be][:], lhsT=lnones[:], rhs=lnsq[:, k, :], start=(k == 0), stop=(k == KD - 1)),
                     reads=[("lnsq", k), "lnones"], writes=[pk(be)], inc=(k == KD - 1))
            S.op("act", lambda: nc.scalar.copy(out=mean_sb[:], in_=ps[bm][:]), reads=[pk(bm)], writes=["mean_sb"])
            S.op("dve", lambda: nc.vector.tensor_tensor(out=rstd_sb[:], in0=ps[bm][:], in1=mean_sb[:], op=ALU.mult),
                 reads=[pk(bm), "mean_sb"], writes=["rstd_sb"])
            S.op("dve", lambda: nc.vector.tensor_tensor(out=rstd_sb[:], in0=ps[be][:], in1=rstd_sb[:], op=ALU.subtract),
                 reads=[pk(be), "rstd_sb"], writes=["rstd_sb"])
            S.op("act", lambda: nc.scalar.activation(out=rstd_sb[:], in_=rstd_sb[:], func=AF.Sqrt, bias=lneps[:, 0:1]),
                 reads=["rstd_sb", "lneps"], writes=["rstd_sb"])
            S.op("dve", lambda: nc.vector.reciprocal(out=rstd_sb[:], in_=rstd_sb[:]), reads=["rstd_sb"], writes=["rstd_sb"])
            for k in range(KD):
                t1, t2 = lnt[0], lnt[1]
                k1, k2 = ("acc", 0), ("acc", 1)
                S.op("dve", lambda: nc.vector.tensor_tensor(out=t1[:], in0=xT[:, k, :], in1=mean_sb[:], op=ALU.subtract),
                     reads=[("xT", k), "mean_sb"], writes=[k1])
                S.op("pool", lambda: nc.gpsimd.tensor_tensor(out=t2[:], in0=t1[:], in1=rstd_sb[:], op=ALU.mult),
                     reads=[k1, "rstd_sb"], writes=[k2])
                S.op("act", lambda: nc.scalar.activation(out=xT[:, k, :], in_=t2[:], func=AF.Identity, scale=lng[ln_idx][:, k:k + 1],
                                                         bias=lnbias[ln_idx][:, k:k + 1]),
                     reads=[k2, "prmB"], writes=[("xT", k)])
                S.op("act", lambda: nc.scalar.activation(out=xTb[:, k, :], in_=t2[:], func=AF.Identity, scale=lng[ln_idx][:, k:k + 1],
                                                         bias=lnbias[ln_idx][:, k:k + 1]),
                     reads=[k2, "prmB"], writes=[("xTb", k)])

        S.op("pool", lambda: nc.gpsimd.memset(lneps[:, 0:1], LN_EPS), writes=["lneps"])
        S.op("pool", lambda: nc.gpsimd.memset(lneps[:, 1:2], 4.0 * RMS_EPS), writes=["lneps"])

        sg = [acc[0][:].bitcast(BF16)[:, 0:TS], acc[0][:].bitcast(BF16)[:, TS:2 * TS], acc[1][:].bitcast(BF16)[:, 0:TS], acc[1][:].bitcast(BF16)[:, TS:2 * TS]]

        def ffn_phase(l, ln_idx):
            for j in range(6):
                nfc = 4 if j < 5 else 2
                slot, skey = use_tile(("g", l, j))
                gv = slot[:, 0:4096].rearrange("p (k c) -> p k c", c=512)
                gb_ = []
                for fc in range(nfc):
                    bg = nb()
                    gb_.append(bg)
                    for k in range(KD):
                        S.op("pe", lambda: nc.tensor.matmul(ps[bg][:], lhsT=gv[:, k, fc * 128:(fc + 1) * 128], rhs=xTb[:, k, :], start=(k == 0), stop=(k == KD - 1)),
                             reads=[skey, ("xTb", k)], writes=[pk(bg)], inc=(k == KD - 1))
                    S.op("act", lambda: nc.scalar.activation(out=sg[fc], in_=ps[bg][:], func=AF.Silu), reads=[pk(bg)], writes=[("acc", fc // 2)])
                done_tile()
                slot, skey = use_tile(("u", l, j))
                uv = slot[:, 0:4096].rearrange("p (k c) -> p k c", c=512)
                for fc in range(nfc):
                    f = 4 * j + fc
                    bu = nb()
                    for k in range(KD):
                        S.op("pe", lambda: nc.tensor.matmul(ps[bu][:], lhsT=uv[:, k, fc * 128:(fc + 1) * 128], rhs=xTb[:, k, :], start=(k == 0), stop=(k == KD - 1)),
                             reads=[skey, ("xTb", k)], writes=[pk(bu)], inc=(k == KD - 1))
                    S.op("dve", lambda: nc.vector.tensor_tensor(out=big[:, f, :], in0=sg[fc], in1=ps[bu][:], op=ALU.mult),
                         reads=[("acc", fc // 2), pk(bu)], writes=[("big", f)])
                done_tile()
            for hf in range(2):
                banks = [nb() for _ in range(4)]
                for fg in range(3):
                    nf = 8 if fg < 2 else 6
                    slot, skey = use_tile(("dn", l, hf, fg))
                    dv = slot[:, 0:nf * 512].rearrange("p (f c) -> p f c", c=512)
                    for c in range(4):
                        for fl in range(nf):
                            f = fg * 8 + fl
                            S.op("pe", lambda: nc.tensor.matmul(ps[banks[c]][:], lhsT=dv[:, fl, c * 128:(c + 1) * 128], rhs=big[:, f, :],
                                                                start=(f == 0), stop=(f == NF - 1)),
                                 reads=[skey, ("big", f)], writes=[pk(banks[c])], inc=(fl == nf - 1))
                    done_tile()
                for c in range(4):
                    ln_accum(4 * hf + c, banks[c], ln_idx)
            ln_finish(ln_idx)

        def ssd_phase(first_in_seq):
            bd = nb()
            for tt in range(4):
                for k in range(KD):
                    S.op("pe", lambda: nc.tensor.matmul(ps[bd][:, tt * 32:(tt + 1) * 32], lhsT=xTb[:, k, tt * 128:(tt + 1) * 128], rhs=wdt[:, k, :],
                                                        start=(k == 0), stop=(k == KD - 1)),
                         reads=[("xTb", k), "wdt"], writes=[pk(bd)], inc=(tt == 3 and k == KD - 1))
            pd = ps[bd][:, 0:128].rearrange("p (t h) -> p t h", h=32)
            v_, av_, l_ = sp_t
            S.op("dve", lambda: nc.vector.tensor_tensor(out=v_[:], in0=pd, in1=bc(dtb_bc[:].unsqueeze(1), [128, 4, 32]), op=ALU.add),
                 reads=[pk(bd), "dtb_bc"], writes=["sp_v"])
            S.op("act", lambda: nc.scalar.activation(out=av_[:], in_=v_[:], func=AF.Abs), reads=["sp_v"], writes=["sp_a"])
            S.op("act", lambda: nc.scalar.activation(out=av_[:], in_=av_[:], func=AF.Exp, scale=-1.0), reads=["sp_a"], writes=["sp_a"])
            S.op("act", lambda: nc.scalar.activation(out=l_[:], in_=av_[:], func=AF.Ln, bias=1.0), reads=["sp_a"], writes=["sp_l"])
            S.op("dve", lambda: nc.vector.scalar_tensor_tensor(out=dt_sb[:], in0=v_[:], scalar=0.0, in1=l_[:], op0=ALU.max, op1=ALU.add),
                 reads=["sp_v", "sp_l"], writes=["dt_sb"])
            S.op("act", lambda: nc.scalar.activation(out=l_[:], in_=dt_sb[:], func=AF.Ln), reads=["dt_sb"], writes=["sp_l"])
            S.op("dve", lambda: nc.vector.tensor_tensor(out=a_sb[:], in0=dt_sb[:], in1=bc(A_bc[:].unsqueeze(1), [128, 4, 32]), op=ALU.mult),
                 reads=["dt_sb", "A_bc"], writes=["a_sb"])
            bcu = nb()
            for c in range(4):
                S.op("pe", lambda: nc.tensor.matmul(ps[bcu][:, c * 32:(c + 1) * 32], lhsT=trif[:], rhs=a_sb[:, c, :], start=True, stop=True),
                     reads=["trif", "a_sb"], writes=[pk(bcu)], inc=(c == 3))
            pc = ps[bcu][:, 0:128].rearrange("p (t h) -> p t h", h=32)
            S.op("dve", lambda: nc.vector.tensor_tensor(out=cumcolp[:], in0=pc, in1=l_[:], op=ALU.subtract), reads=[pk(bcu), "sp_l"], writes=["cumcolp"])
            S.op("act", lambda: nc.scalar.activation(out=expcum[:], in_=pc, func=AF.Exp), reads=[pk(bcu)], writes=["expcum"])
            if first_in_seq:
                S.op("pool", lambda: nc.gpsimd.memset(stateT[:], 0.0), writes=[("stateT", g) for g in range(NG)])
                S.op("pool", lambda: nc.gpsimd.memset(stbf[:], 0.0), writes=[("stbf", g) for g in range(NG)])
                S.op("pool", lambda: nc.gpsimd.memset(halo[:], 0.0), writes=[("halo", ci) for ci in range(32)])

            def inproj_pieces(g):
                gb = g % 2
                zs_, xbc_ = zs[gb], xbc[gb]
                st = {}

                def p_open_a():
                    st["slot"], st["skey"] = use_tile(("inA", g))
                    st["Wv"] = st["slot"][:, 0:8 * 512].rearrange("p (k c) -> p k c", c=512)

                def p_z(half):
                    def f():
                        Wv, skey = st["WvA"], st["skeyA"]
                        bz = nb()
                        for t2 in range(2):
                            tt = 2 * half + t2
                            for k in range(KD):
                                S.op("pe", lambda: nc.tensor.matmul(ps[bz][:, t2 * 256:(t2 + 1) * 256], lhsT=xTb[:, k, tt * 128:(tt + 1) * 128],
                                                                    rhs=Wv[:, k, 0:256], start=(k == 0), stop=(k == KD - 1)),
                                     reads=[skey, ("xTb", k)], writes=[pk(bz)], inc=(t2 == 1 and k == KD - 1))
                        S.op("act", lambda: nc.scalar.activation(out=acc[half][:], in_=ps[bz][:], func=AF.Tanh, scale=0.5), reads=[pk(bz)], writes=[("acc", half)])
                        S.op("dve", lambda: nc.vector.scalar_tensor_tensor(out=zs_[:, 2 * half:2 * half + 2, :].rearrange("p t c -> p (t c)"), in0=acc[half][:], scalar=1.0,
                                                                           in1=ps[bz][:], op0=ALU.add, op1=ALU.mult),
                             reads=[("acc", half), pk(bz)], writes=[("zs", gb)])
                        if half == 1:
                            done_tile()
                            done_tile()
                    return f

                def p_x(r0):
                    def f():
                        if r0 == 0:
                            p_open_a()
                            st["WvA"], st["skeyA"] = st["Wv"], st["skey"]
                        if r0 == 2:
                            st["slot"], st["skey"] = use_tile(("inB", g))
                            st["Wv"] = st["slot"][:, 0:8 * 256].rearrange("p (k c) -> p k c", c=256)
                        Wv, skey = st["Wv"], st["skey"]
                        rows = []
                        for r in (r0, r0 + 1):
                            ci = (2 * g + r) if r < 2 else (16 + g if r == 2 else 24 + g)
                            wcol = (256 + r * 128) if r < 2 else (r - 2) * 128
                            bx = nb()
                            for k in range(KD):
                                S.op("pe", lambda: nc.tensor.matmul(ps[bx][:], lhsT=Wv[:, k, wcol:wcol + 128], rhs=xTb[:, k, :],
                                                                    start=(k == 0), stop=(k == KD - 1)),
                                     reads=[skey, ("xTb", k)], writes=[pk(bx)], inc=(k == KD - 1))
                            rows.append((r, ci, bx, r % 2))
                        for (r, ci, bx, ur) in rows:
                            uk = ("ubuf", ur)
                            S.op("pool", lambda: nc.gpsimd.tensor_copy(out=ubuf[:, ur, 0:3], in_=halo[:, ci, :]), reads=[("halo", ci)], writes=[uk])
                            S.op("act", lambda: nc.scalar.copy(out=ubuf[:, ur, 3:TS + 3], in_=ps[bx][:]), reads=[pk(bx)], writes=[uk])
                            S.op("act", lambda: nc.scalar.activation(out=acc[ur][:], in_=ps[bx][:], func=AF.Identity, scale=cw[:, 3, ci:ci + 1], bias=cb[:, ci:ci + 1]),
                                 reads=[pk(bx), "prmA", "prmB"], writes=[("acc", ur)])
                        for kk in range(3):
                            for (r, ci, bx, ur) in rows:
                                S.op("dve", lambda: nc.vector.scalar_tensor_tensor(out=acc[ur][:], in0=ubuf[:, ur, kk:kk + TS], scalar=cw[:, kk, ci:ci + 1], in1=acc[ur][:],
                                                                                   op0=ALU.mult, op1=ALU.add), reads=[("ubuf", ur), ("acc", ur), "prmA"], writes=[("acc", ur)])
                        for (r, ci, bx, ur) in rows:
                            S.op("pool", lambda: nc.gpsimd.tensor_copy(out=halo[:, ci, :], in_=ubuf[:, ur, TS:TS + 3]), reads=[("ubuf", ur)], writes=[("halo", ci)])
                        for (r, ci, bx, ur) in rows:
                            S.op("act", lambda: nc.scalar.activation(out=ubuf[:, ur, 0:TS], in_=acc[ur][:], func=AF.Tanh), reads=[("acc", ur)], writes=[("ubuf", ur)])
                        for (r, ci, bx, ur) in rows:
                            S.op("dve", lambda: nc.vector.scalar_tensor_tensor(out=xbc_[:, r, :], in0=ubuf[:, ur, 0:TS], scalar=1.0, in1=acc[ur][:],
                                                                               op0=ALU.add, op1=ALU.mult), reads=[("ubuf", ur), ("acc", ur)], writes=[("xbc", gb, r)])
                    return f
                return [p_x(0), p_x(2), p_z(0), p_z(1)]

            cst = {}

            def stage_A(g, c):
                gb = g % 2
                xbc_ = xbc[gb]
                cs_ = cset[c % 2]
                ck = ("cs", c % 2)
                hs = slice(4 * g, 4 * g + 4)
                cs = slice(c * 128, (c + 1) * 128)
                bt = nb()
                T1 = psb(bt)
                for j, r in enumerate((0, 1, 2)):
                    S.op("pe", lambda: nc.tensor.transpose(out=T1[:, j * 128:(j + 1) * 128], in_=xbc_[:, r, cs], identity=identb[:]),
                         reads=[("xbc", gb, r), "identb"], writes=[pk(bt)], inc=(j == 2))
                T1x = T1[:, 0:256].rearrange("p (h q) -> p h q", q=64)
                S.op("act", lambda: nc.scalar.copy(out=cs_["xtok"][:], in_=T1[:, 0:256]), reads=[pk(bt)], writes=[(ck, "xtok")])
                S.op("pool", lambda: nc.gpsimd.tensor_tensor(out=cs_["xD"][:].rearrange("p (h q) -> p h q", q=64),
                                                             in0=cs_["xtok"][:].rearrange("p (h q) -> p h q", q=64),
                                                             in1=bc(D_bc[:, hs].unsqueeze(2), [128, 4, 64]), op=ALU.mult),
                     reads=[(ck, "xtok"), "D_bc"], writes=[(ck, "xD")])
                S.op("act", lambda: nc.scalar.copy(out=cs_["btok"][:], in_=T1[:, 256:384]), reads=[pk(bt)], writes=[(ck, "btok")])
                b1_ = nb()
                S.op("pe", lambda: nc.tensor.matmul(ps[b1_][:], lhsT=onesf[:], rhs=cs_["atri"][:], start=True, stop=True),
                     reads=["onesf", (ck, "atri")], writes=[pk(b1_)])
                cst[(g, c)] = dict(b1=b1_)

            def stage_A1(g, c):
                cs_ = cset[c % 2]
                ck = ("cs", c % 2)
                hs = slice(4 * g, 4 * g + 4)
                S.op("pool", lambda: nc.gpsimd.tensor_tensor(out=cs_["atri"][:].rearrange("p (h l) -> p h l", l=128),
                                                             in0=bc(trif[:].unsqueeze(1), [128, 4, 128]),
                                                             in1=bc(a_sb[:, c, hs].unsqueeze(2), [128, 4, 128]), op=ALU.mult),
                     reads=["trif", "a_sb"], writes=[(ck, "atri")])

            def stage_B1(g, c):
                gb = g % 2
                xbc_ = xbc[gb]
                cs_ = cset[c % 2]
                ck = ("cs", c % 2)
                hs = slice(4 * g, 4 * g + 4)
                cs = slice(c * 128, (c + 1) * 128)
                b1_ = cst[(g, c)]["b1"]
                X1 = ps[b1_][:].rearrange("p (h l) -> p h l", l=128)
                seg = cs_["atri"]
                S.op("dve", lambda: nc.vector.tensor_tensor(out=seg[:].rearrange("p (h l) -> p h l", l=128), in0=X1,
                                                            in1=bc(cumcolp[:, c, hs].unsqueeze(2), [128, 4, 128]), op=ALU.subtract),
                     reads=[pk(b1_), "cumcolp"], writes=[(ck, "atri")])
                S.op("act", lambda: nc.scalar.activation(out=e4[c % 2][:], in_=X1[:, :, 127], func=AF.Exp), reads=[pk(b1_)], writes=[("e4", c % 2)])
                seg3 = seg[:].rearrange("p (h l) -> p h l", l=128)
                S.op("pool", lambda: nc.gpsimd.affine_select(out=seg3, in_=seg3, pattern=[[0, 4], [1, 128]], compare_op=ALU.is_ge,
                                                             fill=neg_reg, base=0, channel_multiplier=-1), reads=[(ck, "atri")], writes=[(ck, "atri")])
                S.op("act", lambda: nc.scalar.activation(out=cs_["decayT"][:], in_=seg[:], func=AF.Exp), reads=[(ck, "atri")], writes=[(ck, "decayT")])
                b2_ = nb()
                S.op("pe", lambda: nc.tensor.matmul(ps[b2_][:, 0:128], lhsT=xbc_[:, 2, cs], rhs=xbc_[:, 3, cs], start=True, stop=True),
                     reads=[("xbc", gb, 2), ("xbc", gb, 3)], writes=[pk(b2_)])
                cst[(g, c)]["b2"] = b2_

            def stage_B2(g, c):
                cs_ = cset[c % 2]
                ck = ("cs", c % 2)
                b2_ = cst[(g, c)]["b2"]
                S.op("dve", lambda: nc.vector.tensor_tensor(out=cs_["GT"][:].rearrange("p (h l) -> p h l", l=128),
                                                            in0=cs_["decayT"][:].rearrange("p (h l) -> p h l", l=128),
                                                            in1=bc(ps[b2_][:, 0:128].unsqueeze(1), [128, 4, 128]), op=ALU.mult),
                     reads=[(ck, "decayT"), pk(b2_)], writes=[(ck, "GT")])
                dlast = cs_["decayT"][:].rearrange("p (h l) -> p h l", l=128)[:, :, 127:128]
                S.op("dve", lambda: nc.vector.tensor_tensor(out=cs_["xw"][:].rearrange("p (h q) -> p h q", q=64),
                                                            in0=cs_["xtok"][:].rearrange("p (h q) -> p h q", q=64),
                                                            in1=bc(dlast, [128, 4, 64]), op=ALU.mult),
                     reads=[(ck, "xtok"), (ck, "decayT")], writes=[(ck, "xw")])
                b3_ = nb()
                S.op("pe", lambda: nc.tensor.matmul(ps[b3_][:, 0:256], lhsT=identb[:], rhs=cs_["xD"][:], start=True, stop=False),
                     reads=["identb", (ck, "xD")], writes=[pk(b3_)], inc=False)
                for h in range(4):
                    S.op("pe", lambda: nc.tensor.matmul(ps[b3_][:, h * 64:(h + 1) * 64], lhsT=cs_["GT"][:, h * 128:(h + 1) * 128],
                                                        rhs=cs_["xtok"][:, h * 64:(h + 1) * 64], start=False, stop=True),
                         reads=[(ck, "GT"), (ck, "xtok")], writes=[pk(b3_)], inc=False)
                S.op("pe", lambda: nc.tensor.matmul(ps[b3_][:, 256:512], lhsT=cs_["btok"][:], rhs=cs_["xw"][:], start=True, stop=True),
                     reads=[(ck, "btok"), (ck, "xw")], writes=[pk(b3_)])
                cst[(g, c)]["b3"] = b3_

            def stage_C(g, c):
                gb = g % 2
                xbc_ = xbc[gb]
                hs = slice(4 * g, 4 * g + 4)
                cs = slice(c * 128, (c + 1) * 128)
                b3_ = cst[(g, c)]["b3"]
                st_g = stateT[:, g * 256:(g + 1) * 256]
                stb_g = stbf[:, g * 256:(g + 1) * 256]
                b4_ = nb()
                S.op("pe", lambda: nc.tensor.matmul(ps[b4_][:, 0:256], lhsT=xbc_[:, 3, cs], rhs=stb_g, start=True, stop=True),
                     reads=[("xbc", gb, 3), ("stbf", g)], writes=[pk(b4_)])
                S.op("pool", lambda: nc.gpsimd.tensor_tensor(out=sttmp[:].rearrange("p (h q) -> p h q", q=64),
                                                             in0=st_g.rearrange("p (h q) -> p h q", q=64),
                                                             in1=bc(e4[c % 2][:].unsqueeze(2), [128, 4, 64]), op=ALU.mult),
                     reads=[("stateT", g), ("e4", c % 2)], writes=["sttmp"])
                S.op("dve", lambda: nc.vector.tensor_tensor(out=ys[:].rearrange("p (h q) -> p h q", q=64),
                                                            in0=ps[b4_][:, 0:256].rearrange("p (h q) -> p h q", q=64),
                                                            in1=bc(expcum[:, c, hs].unsqueeze(2), [128, 4, 64]), op=ALU.mult),
                     reads=[pk(b4_), "expcum"], writes=["ys"])
                S.op("dve", lambda: nc.vector.tensor_tensor(out=st_g, in0=sttmp[:], in1=ps[b3_][:, 256:512], op=ALU.add),
                     reads=["sttmp", pk(b3_)], writes=[("stateT", g)])
                S.op("act", lambda: nc.scalar.copy(out=stb_g, in_=st_g), reads=[("stateT", g)], writes=[("stbf", g)])
                S.op("dve", lambda: nc.vector.tensor_tensor(out=ysum[:], in0=ps[b3_][:, 0:256], in1=ys[:], op=ALU.add),
                     reads=[pk(b3_), "ys"], writes=["ysum"])
                S.op("pool", lambda: nc.gpsimd.tensor_tensor(out=yg[:, c, :], in0=ysum[:], in1=zs[gb][:, c, :], op=ALU.mult),
                     reads=["ysum", ("zs", gb)], writes=[("yg", c)])
                S.op("act", lambda: nc.scalar.activation(out=junk[:], in_=yg[:, c, :], func=AF.Square, accum_out=ss[:, c:c + 1]),
                     reads=[("yg", c)], writes=["junk", ("ss", c)])

            def group_end1(g):
                S.op("act", lambda: nc.scalar.activation(out=sd4[:], in_=ss[:], func=AF.Sqrt, scale=1.0 / 256.0, bias=lneps[:, 1:2]),
                     reads=[("ss", c) for c in range(4)] + ["lneps"], writes=["sd4"])
                S.op("dve", lambda: nc.vector.reciprocal(out=rstd4[:], in_=sd4[:]), reads=["sd4"], writes=["rstd4"])
                for c in range(4):
                    S.op("act", lambda: nc.scalar.activation(out=ygn[:, c, :], in_=yg[:, c, :], func=AF.Copy, scale=rstd4[:, c:c + 1]),
                         reads=[("yg", c), "rstd4"], writes=[("ygn", c)])

            def group_end2(g):
                bn_ = nb()
                Tn = psb(bn_)
                for c in range(4):
                    for j in range(2):
                        S.op("pe", lambda: nc.tensor.transpose(out=Tn[:, (j * 4 + c) * 128:(j * 4 + c + 1) * 128], in_=ygn[:, c, j * 128:(j + 1) * 128],
                                                               identity=identb[:]),
                             reads=[("ygn", c), "identb"], writes=[pk(bn_)], inc=(c == 3 and j == 1))
                for j in range(2):
                    kc = 2 * g + j
                    S.op("dve", lambda: nc.vector.tensor_scalar(out=big[:, kc, :], in0=Tn[:, j * 512:(j + 1) * 512], scalar1=normw[:, kc:kc + 1],
                                                                scalar2=None, op0=ALU.mult),
                         reads=[pk(bn_), "prmB"], writes=[("big", kc)])

            for f in inproj_pieces(0):
                f()
            for g in range(NG):
                if g == 0:
                    for e_ in ("pe", "act", "dve", "pool"):
                        S.wait_all(e_, [iodom_out])
                P = inproj_pieces(g + 1) if g + 1 < NG else []
                P = P + [lambda: None] * (4 - len(P))
                order = list(P)
                if g > 0:
                    order.append(lambda: group_end2(g - 1))
                for it in range(-3, 5):
                    for (fn, c) in ((stage_C, it - 1), (stage_B2, it), (stage_B1, it + 1), (stage_A, it + 2), (stage_A1, it + 3)):
                        if 0 <= c <= 3:
                            order.append((lambda fn=fn, c=c: fn(g, c)))
                order.append(lambda: group_end1(g))
                for f in order:
                    f()
            group_end2(NG - 1)
            S.fence()
            for j in range(4):
                slot, skey = use_tile(("out", j))
                ov = slot[:, 0:16 * 256].rearrange("p (k c) -> p k c", c=256)
                for c in range(2):
                    k = 2 * j + c
                    b = nb()
                    for kk in range(16):
                        S.op("pe", lambda: nc.tensor.matmul(ps[b][:], lhsT=ov[:, kk, c * 128:(c + 1) * 128], rhs=big[:, kk, :],
                                                            start=(kk == 0), stop=(kk == 15)),
                             reads=[skey, ("big", kk)], writes=[pk(b)], inc=(kk == 15))
                    ln_accum(k, b, 0)
                done_tile()
            ln_finish(0)

        def attn_phase(sc, first_in_seq):
            t0 = sc * TS
            for j in range(2):
                slot, skey = use_tile(("kvk", j))
                kvv_ = slot[:, 0:4096].rearrange("p (k c) -> p k c", c=512)
                for pr in range(4):
                    b = nb()
                    for k in range(KD):
                        S.op("pe", lambda: nc.tensor.matmul(ps[b][:], lhsT=kvv_[:, k, pr * 128:(pr + 1) * 128], rhs=xTb[:, k, :], start=(k == 0), stop=(k == KD - 1)),
                             reads=[skey, ("xTb", k)], writes=[pk(b)], inc=(k == KD - 1))
                    S.op("act", lambda: nc.scalar.copy(out=KT[:, 4 * j + pr, t0:t0 + TS], in_=ps[b][:]), reads=[pk(b)], writes=[("KT", 4 * j + pr)])
                done_tile()
            for j in range(2):
                slot, skey = use_tile(("kvv", j))
                vv = slot[:, 0:4096].rearrange("p (k c) -> p k c", c=512)
                for tt in range(4):
                    b = nb()
                    for k in range(KD):
                        S.op("pe", lambda: nc.tensor.matmul(ps[b][:], lhsT=xTb[:, k, tt * 128:(tt + 1) * 128], rhs=vv[:, k, :],
                                                            start=(k == 0), stop=(k == KD - 1)),
                             reads=[skey, ("xTb", k)], writes=[pk(b)], inc=(k == KD - 1))
                    kt = 4 * sc + tt
                    S.op("dve", lambda: nc.vector.tensor_copy(out=VA[:, kt, 8 * j:8 * j + 8, 0:64], in_=ps[b][:].rearrange("p (h q) -> p h q", q=64)),
                         reads=[pk(b)], writes=[("VA", kt)])
                done_tile()
            bf_ = nb()
            for k in range(KD):
                S.op("pe", lambda: nc.tensor.matmul(ps[bf_][0:16, :], lhsT=wf[:, k, :], rhs=xTb[:, k, :], start=(k == 0), stop=(k == KD - 1)),
                     reads=["wf", ("xTb", k)], writes=[pk(bf_)], inc=(k == KD - 1))
            S.op("dve", lambda: nc.vector.tensor_scalar(out=f_v[:], in0=ps[bf_][0:16, :], scalar1=bf_col[:, 0:1], scalar2=None, op0=ALU.add),
                 reads=[pk(bf_), "bf_col"], writes=["f_v"])
            S.op("act", lambda: nc.scalar.activation(out=f_a[:], in_=f_v[:], func=AF.Abs), reads=["f_v"], writes=["f_a"])
            S.op("act", lambda: nc.scalar.activation(out=f_a[:], in_=f_a[:], func=AF.Exp, scale=-1.0), reads=["f_a"], writes=["f_a"])
            S.op("act", lambda: nc.scalar.activation(out=f_l[:], in_=f_a[:], func=AF.Ln, bias=1.0), reads=["f_a"], writes=["f_l"])
            S.op("dve", lambda: nc.vector.scalar_tensor_tensor(out=f_l[:], in0=f_v[:], scalar=0.0, in1=f_l[:], op0=ALU.min, op1=ALU.subtract),
                 reads=["f_v", "f_l"], writes=["f_l"])
            if first_in_seq:
                S.op("pool", lambda: nc.gpsimd.memset(Fcarry[:], 0.0), writes=["Fcarry"])
            S.op("dve", lambda: nc.vector.tensor_tensor_scan(out=Frow[:], data0=bc(onesf[0:16, 0:1], [16, TS]), data1=f_l[:], initial=Fcarry[:, 0:1],
                                                             op0=ALU.mult, op1=ALU.add),
                 reads=["onesf", "f_l", "Fcarry"], writes=["Frow"])
            S.op("dve", lambda: nc.vector.tensor_copy(out=Fcarry[:], in_=Frow[:, TS - 1:TS]), reads=["Frow"], writes=["Fcarry"])
            bt_ = nb()
            for tt in range(4):
                S.op("pe", lambda: nc.tensor.transpose(out=ps[bt_][:, tt * 16:(tt + 1) * 16], in_=Frow[:, tt * 128:(tt + 1) * 128], identity=identf[0:16, 0:16]),
                     reads=["Frow", "identf"], writes=[pk(bt_)], inc=False)
            S.op("dve", lambda: nc.vector.tensor_scalar(out=fdiag[:], in0=identf[0:16, 0:16], scalar1=Frow[:, 255:256], scalar2=None, op0=ALU.mult),
                 reads=["identf", "Frow"], writes=["fdiag"])
            S.op("pe", lambda: nc.tensor.matmul(ps[bt_][:, 64:80], lhsT=onesf[0:16, :], rhs=fdiag[:], start=True, stop=True),
                 reads=["onesf", "fdiag"], writes=[pk(bt_)])
            S.op("dve", lambda: nc.vector.tensor_copy(out=Fcol[:, 4 * sc:4 * sc + 4, :], in_=ps[bt_][:, 0:64].rearrange("p (t h) -> p t h", h=16)),
                 reads=[pk(bt_)], writes=["Fcol"])
            S.op("dve", lambda: nc.vector.tensor_copy(out=Fq0[:], in_=ps[bt_][:, 64:80]), reads=[pk(bt_)], writes=["Fq0"])
            nkt_all = 4 * sc + 4
            S.op("dve", lambda: nc.vector.tensor_tensor(out=bcol[:, 0:nkt_all, :], in0=bc(Fq0[:].unsqueeze(1), [128, nkt_all, 16]),
                                                        in1=Fcol[:, 0:nkt_all, :], op=ALU.subtract),
                 reads=["Fq0", "Fcol"], writes=["bcol"])
            for j in range(2):
                slot, skey = use_tile(("q", j))
                qvv_ = slot[:, 0:4096].rearrange("p (k c) -> p k c", c=512)
                for pr in range(4):
                    b = nb()
                    for k in range(KD):
                        S.op("pe", lambda: nc.tensor.matmul(ps[b][:], lhsT=qvv_[:, k, pr * 128:(pr + 1) * 128], rhs=xTb[:, k, :], start=(k == 0), stop=(k == KD - 1)),
                             reads=[skey, ("xTb", k)], writes=[pk(b)], inc=(k == KD - 1))
                    S.op("act", lambda: nc.scalar.activation(out=QT[:, 4 * j + pr, :], in_=ps[b][:], func=AF.Copy, scale=0.125),
                         reads=[pk(b)], writes=[("QT", 4 * j + pr)])
                done_tile()
            OT = big
            ring["n"] = 6
            ring["i"] = 0
            jobs = []
            nkt = 4 * sc + 4
            for h in range(AH):
                for kt in range(nkt):
                    jobs.append((h, kt))
            LA = 2
            NPT = 4
            pend = {}
            deferred = []

            def emit_st(i):
                h, kt = jobs[i]
                pr, po = h // 2, (h % 2) * 64
                jd = kt - 4 * sc
                c0 = 128 * jd if jd > 0 else 0
                n = TS - c0
                b = nb()
                S.op("pe", lambda: nc.tensor.matmul(ps[b][:, 0:n], lhsT=KT[po:po + 64, pr, kt * 128:(kt + 1) * 128],
                                                    rhs=QT[po:po + 64, pr, c0:TS], start=True, stop=True),
                     reads=[("KT", pr), ("QT", pr)], writes=[pk(b)])
                pend[i] = b

            def emit_rest(i):
                h, kt = jobs[i]
                ob = 6 + (h % 2)
                okey = pk(ob)
                jd = kt - 4 * sc
                c0 = 128 * jd if jd > 0 else 0
                n = TS - c0
                b = pend.pop(i)
                pt = PT[i % NPT]
                ptk = ("PT", i % NPT)
                oreg = ps[ob][0:65, :]
                S.op("act", lambda: nc.scalar.activation(out=pt[:, c0:TS], in_=ps[b][:, 0:n], func=AF.Exp, bias=bcol[:, kt, h:h + 1]),
                     reads=[pk(b), "bcol"], writes=[ptk])
                if jd >= 0:
                    S.op("pool", lambda: nc.gpsimd.affine_select(out=pt[:, c0:c0 + 128], in_=pt[:, c0:c0 + 128], pattern=[[1, 128]],
                                                                 compare_op=ALU.is_ge, fill=zero_reg, base=0, channel_multiplier=-1),
                         reads=[ptk], writes=[ptk])
                last = (kt == nkt - 1)
                S.op("pe", lambda: nc.tensor.matmul(oreg[:, c0:TS], lhsT=VA[:, kt, h, :], rhs=pt[:, c0:TS], start=(kt == 0), stop=last),
                     reads=[("VA", kt), ptk], writes=[okey], inc=last)
                if last:
                    rr_, Rs_ = rr[h % 2], Rs[h % 2]
                    S.op("dve", lambda: nc.vector.reciprocal(out=rr_[64:65, :], in_=ps[ob][64:65, :]), reads=[okey], writes=[("rr", 0)])

                    def fin(h=h, ob=ob, okey=okey, rr_=rr_, Rs_=Rs_):
                        b2 = nb()
                        S.op("pe", lambda: nc.tensor.matmul(ps[b2][0:64, :], lhsT=onesf[64:65, 0:64], rhs=rr_[64:65, :], start=True, stop=True),
                             reads=["onesf", ("rr", 0)], writes=[pk(b2)])
                        S.op("act", lambda: nc.scalar.copy(out=Rs_[:], in_=ps[b2][0:64, :]), reads=[pk(b2)], writes=[("Rs", 0)])
                        S.op("dve", lambda: nc.vector.tensor_tensor(out=OT[0:64, h, :], in0=ps[ob][0:64, :], in1=Rs_[:], op=ALU.mult),
                             reads=[okey, ("Rs", 0)], writes=[("big", h)])
                    deferred.append([2, fin])

            nj = len(jobs)
            for i in range(nj + LA):
                if i < nj:
                    emit_st(i)
                for dfr in list(deferred):
                    dfr[0] -= 1
                    if dfr[0] <= 0:
                        deferred.remove(dfr)
                        dfr[1]()
                if i >= LA:
                    emit_rest(i - LA)
            for dfr in deferred:
                dfr[1]()
            ring["n"] = 8
            for j in range(4):
                slot, skey = use_tile(("o", j))
                ov = slot[0:64, 0:16 * 256].rearrange("p (h c) -> p h c", c=256)
                for c in range(2):
                    k = 2 * j + c
                    b = nb()
                    for h in range(AH):
                        S.op("pe", lambda: nc.tensor.matmul(ps[b][:], lhsT=ov[:, h, c * 128:(c + 1) * 128], rhs=OT[0:64, h, :],
                                                            start=(h == 0), stop=(h == AH - 1)),
                             reads=[skey, ("big", h)], writes=[pk(b)], inc=(h == AH - 1))
                    ln_accum(k, b, 2)
                done_tile()
            ln_finish(2)

        def xk(tt):
            nm = "lnb" if tt < 2 else "lnsq"
            return [(nm, 4 * (tt % 2) + i) for i in range(4)]
        XK = xk(0) + xk(1) + xk(2) + xk(3)
        xld = big[:, 0:16, :].rearrange("p k t -> p (k t)").bitcast(F32).rearrange("p (t d) -> p t d", d=D)
        BK4 = [[("big", 4 * tt + i) for i in range(4)] for tt in range(4)]

        def load_x(bseq_, sc_):
            S.dma("sp", xld, dr["x"][bseq_, sc_ * TS:(sc_ + 1) * TS, :].rearrange("(t p) d -> p t d", p=128), [], BK4[0] + BK4[1] + BK4[2] + BK4[3], iodom_in)
        S.fence()
        gi = 0
        for bseq in range(NB if stop_after != "setup" else 0):
            for sc in range(NSC):
                tap.idx = gi
                t0 = sc * TS
                first = (sc == 0)
                if gi == 0 or stop_after is not None:
                    load_x(bseq, sc)
                for k in range(KD if stop_after != "xdma" else 0):
                    b = nb()
                    for tt in range(4):
                        S.op("pe", lambda: nc.tensor.transpose(out=ps[b][:, tt * 128:(tt + 1) * 128], in_=xld[:, tt, k * 128:(k + 1) * 128], identity=identf[:]),
                             reads=BK4[tt] + ["identf"], writes=[pk(b)], inc=(tt == 3))
                    S.op("act", lambda: nc.scalar.copy(out=xT[:, k, :], in_=ps[b][:]), reads=[pk(b)], writes=[("xT", k)])
                    S.op("dve", lambda: nc.vector.tensor_copy(out=xTb[:, k, :], in_=ps[b][:]), reads=[pk(b)], writes=[("xTb", k)])
                S.fence()
                if stop_after not in ("xload", "xdma"):
                    ssd_phase(first)
                    tap("dbg_x1", xT[:, 0, :], ("xT", 0), [128, TS])
                if stop_after not in ("xload", "ssd", "xdma"):
                    ffn_phase(0, 1)
                    tap("dbg_x2", xT[:, 0, :], ("xT", 0), [128, TS])
                if stop_after not in ("xload", "ssd", "ffn0", "xdma"):
                    S.fence()
                    attn_phase(sc, first)
                    tap("dbg_x3", xT[:, 0, :], ("xT", 0), [128, TS])
                    ffn_phase(1, 3)
                    nxt = gi + 1
                    if nxt < NB * NSC:
                        load_x(nxt // NSC, nxt % NSC)
                for tt in range(4 if stop_after != "xdma" else 0):
                    for hf in range(2):
                        b = nb()
                        for kq in range(4):
                            k = hf * 4 + kq
                            S.op("pe", lambda: nc.tensor.transpose(out=ps[b][:, kq * 128:(kq + 1) * 128], in_=xT[:, k, tt * 128:(tt + 1) * 128], identity=identf[:]),
                                 reads=[("xT", k), "identf"], writes=[pk(b)], inc=(kq == 3))
                        if hf == 0:
                            S.op("act", lambda: nc.scalar.copy(out=xin[:, tt, hf * 512:(hf + 1) * 512], in_=ps[b][:]), reads=[pk(b)], writes=xk(tt))
                        else:
                            S.op("dve", lambda: nc.vector.tensor_copy(out=xin[:, tt, hf * 512:(hf + 1) * 512], in_=ps[b][:]), reads=[pk(b)], writes=xk(tt))
                S.dma("sp", out_d[bseq, t0:t0 + TS, :].rearrange("(t p) d -> p t d", p=128), xin, XK, [], iodom_out)
                gi += 1
        assert stop_after is not None or wstate["next_use"] == total_tiles, (wstate, total_tiles)
        S.wait_all("sp", [iodom_out, dbgdom])
        build.stats = dict(nins=dict(S.nins), ndma=S.ndma, counts={k: v.count for k, v in S.dom.items()})
    return nc, list(dbg.keys())


_CACHE = {}


def kernel(**inputs):
    n_cores = 8
    x = np.ascontiguousarray(inputs["x"], dtype=np.float32)
    B, SEQ, _ = x.shape
    NB = B // n_cores
    key = (NB, SEQ)
    if key not in _CACHE:
        _CACHE[key] = build(NB, SEQ)[0]
    nc = _CACHE[key]
    in_maps = []
    for c in range(n_cores):
        m = {k: np.ascontiguousarray(v, dtype=np.float32) for k, v in inputs.items() if k != "x"}
        m["x"] = x[c * NB:(c + 1) * NB]
        in_maps.append(m)
    res = run_bass_kernel_spmd(nc, in_maps, core_ids=list(range(n_cores)))
    return np.concatenate([r["out"] for r in res.results], axis=0)
```

```python
import numpy as np
from contextlib import ExitStack
from collections import defaultdict

import concourse.bass as bass
import concourse.mybir as mybir
from concourse.bass_utils import run_bass_kernel_spmd

F32 = mybir.dt.float32
BF16 = mybir.dt.bfloat16
AF = mybir.ActivationFunctionType
ALU = mybir.AluOpType

D = 1024
KD = 8
DI = 2048
NG = 8
NHEAD = 32
DFF = 2816
NF = 22
AH = 16
TS = 512
DEPTH = 2
ALPHA = (2.0 * DEPTH) ** 0.25
LN_EPS = 1e-5
RMS_EPS = 1e-5
SLOT = 4096
NSLOT = 3
NEG = -30000.0


class Dom:
    def __init__(self, nc, es, name, step, epoch):
        self.nc, self.es, self.name, self.step, self.epoch = nc, es, name, step, epoch
        self.sems = []
        self.count = 0

    def sem_for(self, cnt):
        e = (cnt - 1) // self.epoch
        while len(self.sems) <= e:
            self.sems.append(self.es.enter_context(self.nc.semaphore(f"s_{self.name}_{len(self.sems)}")))
        return self.sems[e], ((cnt - 1) % self.epoch + 1) * self.step


class Sched:
    def __init__(self, nc, es):
        self.nc, self.es = nc, es
        self.eng = {"pe": nc.tensor, "act": nc.scalar, "dve": nc.vector, "pool": nc.gpsimd, "sp": nc.sync}
        self.dom = {e: Dom(nc, es, e, 1, 4096) for e in ("pe", "act", "dve", "pool")}
        self.seen = defaultdict(int)
        self.lastw = {}
        self.readers = defaultdict(dict)
        self.ndma = 0
        self.nins = defaultdict(int)

    def new_dma_dom(self, name):
        return Dom(self.nc, self.es, name, 16, 1024)

    def _deps(self, own, reads, writes):
        deps = {}

        def need(dc, same_ok):
            dom, cnt = dc
            if dom is own and same_ok:
                return
            if deps.get(dom, 0) < cnt:
                deps[dom] = cnt

        for k in reads:
            if k in self.lastw:
                need(self.lastw[k], False)
            if isinstance(k, tuple) and k[0] == "ps":
                for dom, cnt in self.readers[k].items():
                    need((dom, cnt), True)
        for k in writes:
            if k in self.lastw:
                need(self.lastw[k], True)
            for dom, cnt in self.readers[k].items():
                need((dom, cnt), True)
        return deps

    def _wait(self, e, deps, own=None):
        for dom, cnt in deps.items():
            if self.seen[(e, dom.name)] >= cnt:
                continue
            if dom is own:
                assert cnt <= own.count, "same-engine wait on a future completion"
            sem, val = dom.sem_for(cnt)
            self.eng[e].wait_ge(sem, val)
            self.nins[e] += 1
            self.seen[(e, dom.name)] = cnt

    def op(self, e, fn, reads=(), writes=(), inc=True):
        own = self.dom[e]
        self._wait(e, self._deps(own, reads, writes), own)
        ins = fn()
        self.nins[e] += 1
        tag = own.count + 1
        if inc:
            own.count += 1
            sem, _ = own.sem_for(own.count)
            ins.then_inc(sem, 1)
        for k in reads:
            if self.readers[k].get(own, 0) < tag:
                self.readers[k][own] = tag
        for k in writes:
            self.lastw[k] = (own, tag)
            self.readers[k] = {}
        return ins

    def dma(self, q, out, in_, reads, writes, dom):
        self._wait(q, self._deps(None, reads, writes))
        ins = self.eng[q].dma_start(out=out, in_=in_)
        self.nins[q] += 1
        self.ndma += 1
        dom.count += 1
        sem, _ = dom.sem_for(dom.count)
        ins.then_inc(sem, 16)
        for k in reads:
            self.readers[k][dom] = dom.count
        for k in writes:
            self.lastw[k] = (dom, dom.count)
            self.readers[k] = {}
        return ins

    def fence(self):
        es_ = ("pe", "act", "dve", "pool")
        for e in es_:
            self._wait(e, {self.dom[f]: self.dom[f].count for f in es_ if f != e and self.dom[f].count > 0})

    def wait_all(self, e, doms):
        for dom in doms:
            if dom.count > 0:
                self._wait(e, {dom: dom.count})


def bc(ap, shape):
    return ap.to_broadcast(list(shape))


def weight_tiles():
    tiles = []
    for g in range(NG):
        tiles.append((("inA", g), 8 * 512, [("ssm_in_w", 0, g * 256, 256, "kpc", 0, 512, 0),
                                             ("ssm_in_w", 0, 2048 + g * 256, 256, "kpc", 0, 512, 256)]))
        tiles.append((("inB", g), 8 * 256, [("ssm_in_w", 0, 4096 + g * 128, 128, "kpc", 0, 256, 0),
                                             ("ssm_in_w", 0, 5120 + g * 128, 128, "kpc", 0, 256, 128)]))
    for j in range(4):
        tiles.append((("out", j), 16 * 256, [("ssm_out_w", 0, j * 256, 256, "kpc", 0, 256, 0)]))

    def ffn(l):
        for j in range(6):
            nfc = 4 if j < 5 else 2
            tiles.append((("g", l, j), 8 * 512, [("ffn_gate_w", l, j * 512, nfc * 128, "kpc", 0, 512, 0)]))
            tiles.append((("u", l, j), 8 * 512, [("ffn_up_w", l, j * 512, nfc * 128, "kpc", 0, 512, 0)]))
        for hf in range(2):
            for fg in range(3):
                nf = 8 if fg < 2 else 6
                tiles.append((("dn", l, hf, fg), nf * 512, [("ffn_down_w", l, hf * 512, 512, "kpc_rows", 0, 512, 0, fg * 8, nf)]))

    ffn(0)
    for j in range(2):
        tiles.append((("kvk", j), 4096, [("kv_w", None, j * 512, 512, "kpc", 0, 512, 0)]))
    for j in range(2):
        tiles.append((("kvv", j), 4096, [("kv_w", None, 1024 + j * 512, 512, "kpc", 0, 512, 0)]))
    for j in range(2):
        tiles.append((("q", j), 4096, [("att_q_w", 0, j * 512, 512, "kpc", 0, 512, 0)]))
    for j in range(4):
        tiles.append((("o", j), 16 * 256, [("att_o_w", 0, j * 256, 256, "hpc", 0, 256, 0)]))
    ffn(1)
    return tiles


IN_SPECS = [
    ("x", None), ("ssm_in_w", [1, 1024, 6176]), ("ssm_conv_w", [1, 4, 4096]), ("ssm_conv_b", [1, 4096]),
    ("ssm_dt_bias", [1, 32]), ("ssm_a_log", [1, 32]), ("ssm_d", [1, 32]), ("ssm_norm_w", [1, 2048]),
    ("ssm_out_w", [1, 2048, 1024]), ("kv_w", [1024, 2064]), ("kv_b_f", [16]), ("att_q_w", [1, 1024, 1024]),
    ("att_o_w", [1, 1024, 1024]), ("ffn_gate_w", [2, 1024, 2816]), ("ffn_up_w", [2, 1024, 2816]),
    ("ffn_down_w", [2, 2816, 1024]), ("ln_mix_g", [2, 1024]), ("ln_mix_b", [2, 1024]), ("ln_ffn_g", [2, 1024]),
    ("ln_ffn_b", [2, 1024]),
]


def build(NB=4, SEQ=2048, debug=False, stop_after=None):
    nc = bass.Bass("TRN2", target_bir_lowering=False)
    NSC = SEQ // TS
    NKT = SEQ // 128
    dr = {}
    for name, shp in IN_SPECS:
        if name == "x":
            shp = [NB, SEQ, D]
        dr[name] = nc.dram_tensor(name, shp, F32, kind="ExternalInput").ap()
    out_d = nc.dram_tensor("out", [NB, SEQ, D], F32, kind="ExternalOutput").ap()
    tiles = weight_tiles()
    NT = len(tiles)
    wscr = nc.dram_tensor("wscr", [NT, 128, SLOT], BF16, kind="Internal").ap()
    dbg = {}

    with ExitStack() as es:
        ec = es.enter_context
        S = Sched(nc, es)

        def sb(name, shape, dt=F32):
            return ec(nc.sbuf_tensor(name, list(shape), dt))

        identf = sb("identf", [128, 128]); identb = sb("identb", [128, 128], BF16)
        onesf = sb("onesf", [128, 128]); trif = sb("trif", [128, 128])
        lnones = sb("lnones", [128, 128], BF16)
        negmask = sb("negmask", [128, 512], BF16)
        prmA = sb("prmA", [128, 128]); prmB = sb("prmB", [128, 128])
        dtb_bc = sb("dtb_bc", [128, 32]); A_bc = sb("A_bc", [128, 32]); D_bc = sb("D_bc", [128, 32])
        bf_col = sb("bf_col", [16, 1])
        wdt = sb("wdt", [128, 8, 32], BF16); wf = sb("wf", [128, 8, 16], BF16)
        wslot = [sb(f"wslot{i}", [128, SLOT], BF16) for i in range(NSLOT)]
        scr16 = sb("scr16", [128, 4096])
        xin = scr16[:].rearrange("p (t d) -> p t d", d=D)
        lnb = scr16[:, 0:2048].bitcast(BF16).rearrange("p (k t) -> p k t", t=TS)
        lnsq = scr16[:, 2048:4096].bitcast(BF16).rearrange("p (k t) -> p k t", t=TS)
        xT = sb("xT", [128, KD, TS]); xTb = sb("xTb", [128, KD, TS], BF16)
        mean_sb = sb("mean_sb", [128, TS]); rstd_sb = sb("rstd_sb", [128, TS])
        big = sb("big", [128, NF, TS], BF16)
        acc = [sb(f"acc{i}", [128, TS]) for i in range(2)]
        lnt = acc
        stateT = sb("stateT", [128, NHEAD * 64]); stbf = sb("stbf", [128, NHEAD * 64], BF16)
        halo = sb("halo", [128, 32, 3])
        KT = sb("KT", [128, 8, SEQ], BF16)
        VA = sb("VA", [128, NKT, AH, 65], BF16)
        Fcarry = sb("Fcarry", [16, 1]); Fcol = sb("Fcol", [128, NKT, AH])
        lneps = sb("lneps", [128, 2])
        ARENA = 7750
        arena = sb("arena", [128, ARENA])
        ar = {"o": 0}

        def carve(shape, dt=F32):
            n = int(np.prod(shape[1:]))
            w = n if dt == F32 else (n + 1) // 2
            o = ar["o"]
            assert o + w <= ARENA, ("arena overflow", o, w)
            ar["o"] = o + w
            v = arena[0:shape[0], o:o + w]
            if dt != F32:
                v = v.bitcast(dt)
            if len(shape) == 3:
                v = v.rearrange("p (a b) -> p a b", b=shape[2])
            return v

        stg1 = carve([128, 128]); stg2 = carve([128, 128])
        ar["o"] = 0
        dt_sb = carve([128, 4, 32]); a_sb = carve([128, 4, 32]); cumcol = carve([128, 4, 32]); expcum = carve([128, 4, 32])
        sp_t = [carve([128, 4, 32]) for _ in range(3)]
        cumcolp = carve([128, 4, 32])
        ubuf = carve([128, 2, TS + 4])
        e4 = [carve([128, 4]) for _ in range(2)]
        ys = carve([128, 256]); ysum = carve([128, 256])
        yg = carve([128, 4, 256]); ss = carve([128, 4]); sd4 = carve([128, 4]); rstd4 = carve([128, 4])
        junk = carve([128, 256]); ygn = carve([128, 4, 256], BF16); sttmp = carve([128, 256])
        zs = [carve([128, 4, 256], BF16), None]
        xbc = [carve([128, 4, TS], BF16), None]

        def chunk_set():
            return dict(xtok=carve([128, 256], BF16), xD=carve([128, 256], BF16), btok=carve([128, 128], BF16),
                        atri=carve([128, 512]), decayT=carve([128, 512], BF16), GT=carve([128, 512], BF16), xw=carve([128, 256], BF16))
        cset = [chunk_set(), None]
        ssd_top = ar["o"]
        sav = (arena, ar["o"])
        arena_main = arena

        def carve16(shape, dt=F32):
            n = int(np.prod(shape[1:]))
            w = n if dt == F32 else (n + 1) // 2
            o = c16["o"]
            assert o + w <= 4096, ("scr16 overflow", o, w)
            c16["o"] = o + w
            v = scr16[0:shape[0], o:o + w]
            if dt != F32:
                v = v.bitcast(dt)
            if len(shape) == 3:
                v = v.rearrange("p (a b) -> p a b", b=shape[2])
            return v
        c16 = {"o": 0}
        zs[1] = carve16([128, 4, 256], BF16)
        xbc[1] = carve16([128, 4, TS], BF16)
        cset[1] = dict(xtok=carve16([128, 256], BF16), xD=carve16([128, 256], BF16), btok=carve16([128, 128], BF16),
                       atri=carve16([128, 512]), decayT=carve16([128, 512], BF16), GT=carve16([128, 512], BF16), xw=carve16([128, 256], BF16))
        ar["o"] = 0
        QZ = carve([128, AH, TS], BF16)
        fdiag = carve([16, 16]); Fq0 = carve([128, AH]); bcol = carve([128, NKT, AH])
        PT = [carve([128, 512], BF16) for _ in range(4)]
        _pt0 = ar["o"] - 4 * 256
        _sv = ar["o"]
        ar["o"] = _pt0
        f_v = carve([16, TS]); Frow = carve([16, TS])
        ar["o"] = _sv
        f_a = carve([16, TS]); f_l = carve([16, TS])
        rr = [carve([128, 512])] * 2; Rs = [carve([64, 512])] * 2
        att_top = ar["o"]
        ps = [ec(nc.psum_tensor(f"ps{i}", [128, 512], F32)) for i in range(8)]

        def psb(i):
            return ps[i][:].bitcast(BF16)

        ring = {"i": 0, "n": 8}

        def nb():
            i = ring["i"] % ring["n"]
            ring["i"] = (i + 1) % ring["n"]
            return i

        def pk(i):
            return ("ps", i)

        P_ = "pool"
        neg_reg = nc.gpsimd.to_reg(NEG)
        zero_reg = nc.gpsimd.to_reg(0.0)
        S.op(P_, lambda: nc.gpsimd.memset(identf[:], 0.0), writes=["identf"])
        S.op(P_, lambda: nc.gpsimd.affine_select(out=identf[:], in_=identf[:], pattern=[[-1, 128]], compare_op=ALU.not_equal,
                                                 fill=1.0, base=0, channel_multiplier=1), reads=["identf"], writes=["identf"])
        S.op(P_, lambda: nc.gpsimd.tensor_copy(out=identb[:], in_=identf[:]), reads=["identf"], writes=["identb"])
        S.op(P_, lambda: nc.gpsimd.memset(onesf[:], 1.0), writes=["onesf"])
        S.op(P_, lambda: nc.gpsimd.memset(lnones[:], 1.0 / D), writes=["lnones"])
        S.op(P_, lambda: nc.gpsimd.memset(trif[:], 1.0), writes=["trif"])
        S.op(P_, lambda: nc.gpsimd.affine_select(out=trif[:], in_=trif[:], pattern=[[1, 128]], compare_op=ALU.is_ge,
                                                 fill=0.0, base=0, channel_multiplier=-1), reads=["trif"], writes=["trif"])
        S.op(P_, lambda: nc.gpsimd.memset(negmask[:], 0.0), writes=["negmask"])
        nm3 = negmask[:].rearrange("p (h l) -> p h l", h=4)
        S.op(P_, lambda: nc.gpsimd.affine_select(out=nm3, in_=nm3, pattern=[[0, 4], [1, 128]], compare_op=ALU.is_ge,
                                                 fill=NEG, base=0, channel_multiplier=-1), reads=["negmask"], writes=["negmask"])
        S.op(P_, lambda: nc.gpsimd.memset(stg2[:], 0.0), writes=["stg2"])

        cdom = S.new_dma_dom("cst")
        S.dma("sp", stg1[:], dr["ssm_conv_w"][0].rearrange("k (c p) -> (k c) p", p=128), [], ["stg1"], cdom)
        rows = [("ssm_conv_b", dr["ssm_conv_b"][0], 32, 0), ("ssm_norm_w", dr["ssm_norm_w"][0], 16, 32),
                ("ln_mix_g", dr["ln_mix_g"].rearrange("l d -> (l d)"), 16, 48), ("ln_mix_b", dr["ln_mix_b"].rearrange("l d -> (l d)"), 16, 64),
                ("ln_ffn_g", dr["ln_ffn_g"].rearrange("l d -> (l d)"), 16, 80), ("ln_ffn_b", dr["ln_ffn_b"].rearrange("l d -> (l d)"), 16, 96)]
        for (_, src, n, r0) in rows:
            S.dma("sp", stg2[r0:r0 + n, :], src.rearrange("(c p) -> c p", p=128), [], ["stg2"], cdom)
        S.dma("sp", dtb_bc[:], dr["ssm_dt_bias"].partition_broadcast(128), [], ["dtb_bc"], cdom)
        S.dma("sp", A_bc[:], dr["ssm_a_log"].partition_broadcast(128), [], ["A_bc"], cdom)
        S.dma("sp", D_bc[:], dr["ssm_d"].partition_broadcast(128), [], ["D_bc"], cdom)
        S.dma("sp", bf_col[:], dr["kv_b_f"].rearrange("(h o) -> h o", o=1), [], ["bf_col"], cdom)
        S.op("act", lambda: nc.scalar.activation(out=A_bc[:], in_=A_bc[:], func=AF.Exp), reads=["A_bc"], writes=["A_bc"])
        S.op("dve", lambda: nc.vector.tensor_scalar_mul(out=A_bc[:], in0=A_bc[:], scalar1=-1.0), reads=["A_bc"], writes=["A_bc"])
        b0 = nb()
        S.op("pe", lambda: nc.tensor.transpose(out=ps[b0][:, 0:128], in_=stg1[:], identity=identf[:]), reads=["stg1", "identf"], writes=[pk(b0)])
        S.op("dve", lambda: nc.vector.tensor_copy(out=prmA[:], in_=ps[b0][:, 0:128]), reads=[pk(b0)], writes=["prmA"])
        b1 = nb()
        S.op("pe", lambda: nc.tensor.transpose(out=ps[b1][:, 0:128], in_=stg2[:], identity=identf[:]), reads=["stg2", "identf"], writes=[pk(b1)])
        S.op("dve", lambda: nc.vector.tensor_copy(out=prmB[:], in_=ps[b1][:, 0:128]), reads=[pk(b1)], writes=["prmB"])
        S.op("dve", lambda: nc.vector.tensor_scalar_mul(out=prmA[:], in0=prmA[:], scalar1=0.5), reads=["prmA"], writes=["prmA"])
        S.op("dve", lambda: nc.vector.tensor_scalar_mul(out=prmB[:, 0:32], in0=prmB[:, 0:32], scalar1=0.5), reads=["prmB"], writes=["prmB"])
        cw = prmA[:].rearrange("p (k c) -> p k c", k=4)
        cb = prmB[:, 0:32]
        normw = prmB[:, 32:48]
        lng = {0: prmB[:, 48:56], 1: prmB[:, 80:88], 2: prmB[:, 56:64], 3: prmB[:, 88:96]}
        lnbias = {0: prmB[:, 64:72], 1: prmB[:, 96:104], 2: prmB[:, 72:80], 3: prmB[:, 104:112]}

        import os
        wcdom = S.new_dma_dom("wcv")
        stf = [scr16[:], xT[:].rearrange("p k t -> p (k t)")]
        stb = [big[:, 0:8, :].rearrange("p k t -> p (k t)"), big[:, 8:16, :].rearrange("p k t -> p (k t)")]
        ktf = KT[:].rearrange("p k t -> p (k t)").bitcast(F32)
        for i in range(ktf.shape[1] // 4096):
            stf.append(ktf[:, i * 4096:(i + 1) * 4096])
        vaf = VA[:].rearrange("p a b c -> p (a b c)")
        for i in range(vaf.shape[1] // 4096):
            stb.append(vaf[:, i * 4096:(i + 1) * 4096])
        NST = min(len(stf), len(stb), 4)
        cvl = [S.new_dma_dom(f"cvl{i}") for i in range(NST)]
        cvs = [S.new_dma_dom(f"cvs{i}") for i in range(NST)]
        cast_eng = ["dve", "act", "pool"]
        for ti, (name, nel, parts) in enumerate(tiles):
            if os.environ.get("SKIP_CONV"):
                break
            sl = ti % NST
            npart = 64 if name[0] == "o" else 128
            for part in parts:
                (src, idx, c0, n, kind, base, cstride, coff) = part[:8]
                w = dr[src] if idx is None else dr[src][idx]
                if kind == "kpc_rows":
                    k0, nk = part[8], part[9]
                    s_ap = w[k0 * 128:(k0 + nk) * 128, c0:c0 + n].rearrange("(k p) c -> p k c", p=128)
                    d_ap = stf[sl][:, base:base + nk * cstride].rearrange("p (k c) -> p k c", c=cstride)[:, :, coff:coff + n]
                elif kind == "kpc":
                    nk = w.shape[0] // 128
                    s_ap = w[:, c0:c0 + n].rearrange("(k p) c -> p k c", p=128)
                    d_ap = stf[sl][:, base:base + nk * cstride].rearrange("p (k c) -> p k c", c=cstride)[:, :, coff:coff + n]
                else:
                    s_ap = w[:, c0:c0 + n].rearrange("(h p) c -> p h c", p=64)
                    d_ap = stf[sl][0:64, base:base + 16 * cstride].rearrange("p (h c) -> p h c", c=cstride)[:, :, coff:coff + n]
                S.dma("act" if (ti % 2) else "sp", d_ap, s_ap, [], [("stf", sl)], cvl[sl])
            ce = cast_eng[ti % 3]
            if ce == "dve":
                S.op("dve", lambda: nc.vector.tensor_copy(out=stb[sl][0:npart, 0:nel], in_=stf[sl][0:npart, 0:nel]), reads=[("stf", sl)], writes=[("stb", sl)])
            elif ce == "act":
                S.op("act", lambda: nc.scalar.copy(out=stb[sl][0:npart, 0:nel], in_=stf[sl][0:npart, 0:nel]), reads=[("stf", sl)], writes=[("stb", sl)])
            else:
                S.op("pool", lambda: nc.gpsimd.tensor_copy(out=stb[sl][0:npart, 0:nel], in_=stf[sl][0:npart, 0:nel]), reads=[("stf", sl)], writes=[("stb", sl)])
            S.dma("act" if (ti % 2) else "sp", wscr[ti][0:npart, 0:nel], stb[sl][0:npart, 0:nel], [("stb", sl)], [("wscr", ti)], cvs[sl])
        S.dma("pool", wdt[:], dr["ssm_in_w"][0][:, 6144:6176].rearrange("(k p) c -> p k c", p=128), [], ["wdt"], wcdom)
        S.dma("pool", wf[:], dr["kv_w"][:, 2048:2064].rearrange("(k p) c -> p k c", p=128), [], ["wf"], wcdom)
        S.wait_all("pool", cvs + cvl)
        S.op("pool", lambda: nc.gpsimd.memset(VA[:], 1.0), reads=[("stb", i) for i in range(NST)], writes=[("VA", kt) for kt in range(NKT)])
        S.wait_all("sp", cvs + cvl)
        S.wait_all("pe", cvs + cvl)
        S.wait_all("act", cvs + cvl)
        S.wait_all("dve", cvs + cvl)
        S.wait_all("pool", cvs + cvl)

        wdoms = [S.new_dma_dom(f"w{i}") for i in range(NSLOT)]
        wstate = {"next_load": 0, "next_use": 0}
        total_tiles = NB * NSC * NT

        def prefetch():
            i = wstate["next_load"]
            if i >= total_tiles:
                return
            wstate["next_load"] += 1
            ti = i % NT
            s = i % NSLOT
            nel = tiles[ti][1]
            npart = 64 if tiles[ti][0][0] == "o" else 128
            S.dma("sp", wslot[s][0:npart, 0:nel], wscr[ti][0:npart, 0:nel], [("wscr", ti)], [("wslot", s)], wdoms[s])

        def use_tile(expect):
            if stop_after is not None:
                while tiles[wstate["next_use"] % NT][0] != expect:
                    wstate["next_use"] += 1
                    prefetch()
            i = wstate["next_use"]
            wstate["next_use"] += 1
            ti = i % NT
            assert tiles[ti][0] == expect, (tiles[ti][0], expect)
            s = i % NSLOT
            return wslot[s], ("wslot", s)

        def done_tile():
            prefetch()

        for _ in range(NSLOT):
            prefetch()

        iodom_in = S.new_dma_dom("xin")
        iodom_out = S.new_dma_dom("xout")
        dbgdom = S.new_dma_dom("dbg")

        def tap(name, ap, key, shape):
            if not debug:
                return
            if name not in dbg:
                dbg[name] = nc.dram_tensor(name, [NB * NSC] + list(shape), ap.dtype, kind="ExternalOutput").ap()
            S.dma("sp", dbg[name][tap.idx], ap, [key], [], dbgdom)
        tap.idx = 0

        def ln_accum(k, bank, ln_idx):
            S.op("dve", lambda: nc.vector.scalar_tensor_tensor(out=xT[:, k, :], in0=xT[:, k, :], scalar=ALPHA, in1=ps[bank][:],
                                                               op0=ALU.mult, op1=ALU.add),
                 reads=[("xT", k), pk(bank)], writes=[("xT", k)])
            S.op("act", lambda: nc.scalar.copy(out=lnb[:, k, :], in_=xT[:, k, :]), reads=[("xT", k)], writes=[("lnb", k)])
            S.op("act", lambda: nc.scalar.activation(out=lnsq[:, k, :], in_=xT[:, k, :], func=AF.Square), reads=[("xT", k)], writes=[("lnsq", k)])

        def ln_finish(ln_idx):
            bm, be = nb(), nb()
            for k in range(KD):
                S.op("pe", lambda: nc.tensor.matmul(ps[bm][:], lhsT=lnones[:], rhs=lnb[:, k, :], start=(k == 0), stop=(k == KD - 1)),
                     reads=[("lnb", k), "lnones"], writes=[pk(bm)], inc=(k == KD - 1))
            for k in range(KD):
                S.op("pe", lambda: nc.tensor.matmul(ps[be][:], lhsT=lnones[:], rhs=lnsq[:, k, :], start=(k == 0), stop=(k == KD - 1)),
                     reads=[("lnsq", k), "lnones"], writes=[pk(be)], inc=(k == KD - 1))
            S.op("act", lambda: nc.scalar.copy(out=mean_sb[:], in_=ps[bm][:]), reads=[pk(bm)], writes=["mean_sb"])
            S.op("dve", lambda: nc.vector.tensor_tensor(out=rstd_sb[:], in0=ps[bm][:], in1=mean_sb[:], op=ALU.mult),
                 reads=[pk(bm), "mean_sb"], writes=["rstd_sb"])
            S.op("dve", lambda: nc.vector.tensor_tensor(out=rstd_sb[:], in0=ps[be][:], in1=rstd_sb[:], op=ALU.subtract),
                 reads=[pk(be), "rstd_sb"], writes=["rstd_sb"])
            S.op("act", lambda: nc.scalar.activation(out=rstd_sb[:], in_=rstd_sb[:], func=AF.Sqrt, bias=lneps[:, 0:1]),
                 reads=["rstd_sb", "lneps"], writes=["rstd_sb"])
            S.op("dve", lambda: nc.vector.reciprocal(out=rstd_sb[:], in_=rstd_sb[:]), reads=["rstd_sb"], writes=["rstd_sb"])
            for k in range(KD):
                t1, t2 = lnt[0], lnt[1]
                k1, k2 = ("acc", 0), ("acc", 1)
                S.op("dve", lambda: nc.vector.tensor_tensor(out=t1[:], in0=xT[:, k, :], in1=mean_sb[:], op=ALU.subtract),
                     reads=[("xT", k), "mean_sb"], writes=[k1])
                S.op("pool", lambda: nc.gpsimd.tensor_tensor(out=t2[:], in0=t1[:], in1=rstd_sb[:], op=ALU.mult),
                     reads=[k1, "rstd_sb"], writes=[k2])
                S.op("act", lambda: nc.scalar.activation(out=xT[:, k, :], in_=t2[:], func=AF.Identity, scale=lng[ln_idx][:, k:k + 1],
                                                         bias=lnbias[ln_idx][:, k:k + 1]),
                     reads=[k2, "prmB"], writes=[("xT", k)])
                S.op("act", lambda: nc.scalar.activation(out=xTb[:, k, :], in_=t2[:], func=AF.Identity, scale=lng[ln_idx][:, k:k + 1],
                                                         bias=lnbias[ln_idx][:, k:k + 1]),
                     reads=[k2, "prmB"], writes=[("xTb", k)])

        S.op("pool", lambda: nc.gpsimd.memset(lneps[:, 0:1], LN_EPS), writes=["lneps"])
        S.op("pool", lambda: nc.gpsimd.memset(lneps[:, 1:2], 4.0 * RMS_EPS), writes=["lneps"])

        sg = [acc[0][:].bitcast(BF16)[:, 0:TS], acc[0][:].bitcast(BF16)[:, TS:2 * TS], acc[1][:].bitcast(BF16)[:, 0:TS], acc[1][:].bitcast(BF16)[:, TS:2 * TS]]

        def ffn_phase(l, ln_idx):
            for j in range(6):
                nfc = 4 if j < 5 else 2
                slot, skey = use_tile(("g", l, j))
                gv = slot[:, 0:4096].rearrange("p (k c) -> p k c", c=512)
                gb_ = []
                for fc in range(nfc):
                    bg = nb()
                    gb_.append(bg)
                    for k in range(KD):
                        S.op("pe", lambda: nc.tensor.matmul(ps[bg][:], lhsT=gv[:, k, fc * 128:(fc + 1) * 128], rhs=xTb[:, k, :], start=(k == 0), stop=(k == KD - 1)),
                             reads=[skey, ("xTb", k)], writes=[pk(bg)], inc=(k == KD - 1))
                    S.op("act", lambda: nc.scalar.activation(out=sg[fc], in_=ps[bg][:], func=AF.Silu), reads=[pk(bg)], writes=[("acc", fc // 2)])
                done_tile()
                slot, skey = use_tile(("u", l, j))
                uv = slot[:, 0:4096].rearrange("p (k c) -> p k c", c=512)
                for fc in range(nfc):
                    f = 4 * j + fc
                    bu = nb()
                    for k in range(KD):
                        S.op("pe", lambda: nc.tensor.matmul(ps[bu][:], lhsT=uv[:, k, fc * 128:(fc + 1) * 128], rhs=xTb[:, k, :], start=(k == 0), stop=(k == KD - 1)),
                             reads=[skey, ("xTb", k)], writes=[pk(bu)], inc=(k == KD - 1))
                    S.op("dve", lambda: nc.vector.tensor_tensor(out=big[:, f, :], in0=sg[fc], in1=ps[bu][:], op=ALU.mult),
                         reads=[("acc", fc // 2), pk(bu)], writes=[("big", f)])
                done_tile()
            for hf in range(2):
                banks = [nb() for _ in range(4)]
                for fg in range(3):
                    nf = 8 if fg < 2 else 6
                    slot, skey = use_tile(("dn", l, hf, fg))
                    dv = slot[:, 0:nf * 512].rearrange("p (f c) -> p f c", c=512)
                    for c in range(4):
                        for fl in range(nf):
                            f = fg * 8 + fl
                            S.op("pe", lambda: nc.tensor.matmul(ps[banks[c]][:], lhsT=dv[:, fl, c * 128:(c + 1) * 128], rhs=big[:, f, :],
                                                                start=(f == 0), stop=(f == NF - 1)),
                                 reads=[skey, ("big", f)], writes=[pk(banks[c])], inc=(fl == nf - 1))
                    done_tile()
                for c in range(4):
                    ln_accum(4 * hf + c, banks[c], ln_idx)
            ln_finish(ln_idx)

        def ssd_phase(first_in_seq):
            bd = nb()
            for tt in range(4):
                for k in range(KD):
                    S.op("pe", lambda: nc.tensor.matmul(ps[bd][:, tt * 32:(tt + 1) * 32], lhsT=xTb[:, k, tt * 128:(tt + 1) * 128], rhs=wdt[:, k, :],
                                                        start=(k == 0), stop=(k == KD - 1)),
                         reads=[("xTb", k), "wdt"], writes=[pk(bd)], inc=(tt == 3 and k == KD - 1))
            pd = ps[bd][:, 0:128].rearrange("p (t h) -> p t h", h=32)
            v_, av_, l_ = sp_t
            S.op("dve", lambda: nc.vector.tensor_tensor(out=v_[:], in0=pd, in1=bc(dtb_bc[:].unsqueeze(1), [128, 4, 32]), op=ALU.add),
                 reads=[pk(bd), "dtb_bc"], writes=["sp_v"])
            S.op("act", lambda: nc.scalar.activation(out=av_[:], in_=v_[:], func=AF.Abs), reads=["sp_v"], writes=["sp_a"])
            S.op("act", lambda: nc.scalar.activation(out=av_[:], in_=av_[:], func=AF.Exp, scale=-1.0), reads=["sp_a"], writes=["sp_a"])
            S.op("act", lambda: nc.scalar.activation(out=l_[:], in_=av_[:], func=AF.Ln, bias=1.0), reads=["sp_a"], writes=["sp_l"])
            S.op("dve", lambda: nc.vector.scalar_tensor_tensor(out=dt_sb[:], in0=v_[:], scalar=0.0, in1=l_[:], op0=ALU.max, op1=ALU.add),
                 reads=["sp_v", "sp_l"], writes=["dt_sb"])
            S.op("act", lambda: nc.scalar.activation(out=l_[:], in_=dt_sb[:], func=AF.Ln), reads=["dt_sb"], writes=["sp_l"])
            S.op("dve", lambda: nc.vector.tensor_tensor(out=a_sb[:], in0=dt_sb[:], in1=bc(A_bc[:].unsqueeze(1), [128, 4, 32]), op=ALU.mult),
                 reads=["dt_sb", "A_bc"], writes=["a_sb"])
            bcu = nb()
            for c in range(4):
                S.op("pe", lambda: nc.tensor.matmul(ps[bcu][:, c * 32:(c + 1) * 32], lhsT=trif[:], rhs=a_sb[:, c, :], start=True, stop=True),
                     reads=["trif", "a_sb"], writes=[pk(bcu)], inc=(c == 3))
            pc = ps[bcu][:, 0:128].rearrange("p (t h) -> p t h", h=32)
            S.op("dve", lambda: nc.vector.tensor_tensor(out=cumcolp[:], in0=pc, in1=l_[:], op=ALU.subtract), reads=[pk(bcu), "sp_l"], writes=["cumcolp"])
            S.op("act", lambda: nc.scalar.activation(out=expcum[:], in_=pc, func=AF.Exp), reads=[pk(bcu)], writes=["expcum"])
            if first_in_seq:
                S.op("pool", lambda: nc.gpsimd.memset(stateT[:], 0.0), writes=[("stateT", g) for g in range(NG)])
                S.op("pool", lambda: nc.gpsimd.memset(stbf[:], 0.0), writes=[("stbf", g) for g in range(NG)])
                S.op("pool", lambda: nc.gpsimd.memset(halo[:], 0.0), writes=[("halo", ci) for ci in range(32)])

            def inproj_pieces(g):
                gb = g % 2
                zs_, xbc_ = zs[gb], xbc[gb]
                st = {}

                def p_open_a():
                    st["slot"], st["skey"] = use_tile(("inA", g))
                    st["Wv"] = st["slot"][:, 0:8 * 512].rearrange("p (k c) -> p k c", c=512)

                def p_z(half):
                    def f():
                        Wv, skey = st["WvA"], st["skeyA"]
                        bz = nb()
                        for t2 in range(2):
                            tt = 2 * half + t2
                            for k in range(KD):
                                S.op("pe", lambda: nc.tensor.matmul(ps[bz][:, t2 * 256:(t2 + 1) * 256], lhsT=xTb[:, k, tt * 128:(tt + 1) * 128],
                                                                    rhs=Wv[:, k, 0:256], start=(k == 0), stop=(k == KD - 1)),
                                     reads=[skey, ("xTb", k)], writes=[pk(bz)], inc=(t2 == 1 and k == KD - 1))
                        S.op("act", lambda: nc.scalar.activation(out=acc[half][:], in_=ps[bz][:], func=AF.Tanh, scale=0.5), reads=[pk(bz)], writes=[("acc", half)])
                        S.op("dve", lambda: nc.vector.scalar_tensor_tensor(out=zs_[:, 2 * half:2 * half + 2, :].rearrange("p t c -> p (t c)"), in0=acc[half][:], scalar=1.0,
                                                                           in1=ps[bz][:], op0=ALU.add, op1=ALU.mult),
                             reads=[("acc", half), pk(bz)], writes=[("zs", gb)])
                        if half == 1:
                            done_tile()
                            done_tile()
                    return f

                def p_x(r0):
                    def f():
                        if r0 == 0:
                            p_open_a()
                            st["WvA"], st["skeyA"] = st["Wv"], st["skey"]
                        if r0 == 2:
                            st["slot"], st["skey"] = use_tile(("inB", g))
                            st["Wv"] = st["slot"][:, 0:8 * 256].rearrange("p (k c) -> p k c", c=256)
                        Wv, skey = st["Wv"], st["skey"]
                        rows = []
                        for r in (r0, r0 + 1):
                            ci = (2 * g + r) if r < 2 else (16 + g if r == 2 else 24 + g)
                            wcol = (256 + r * 128) if r < 2 else (r - 2) * 128
                            bx = nb()
                            for k in range(KD):
                                S.op("pe", lambda: nc.tensor.matmul(ps[bx][:], lhsT=Wv[:, k, wcol:wcol + 128], rhs=xTb[:, k, :],
                                                                    start=(k == 0), stop=(k == KD - 1)),
                                     reads=[skey, ("xTb", k)], writes=[pk(bx)], inc=(k == KD - 1))
                            rows.append((r, ci, bx, r % 2))
                        for (r, ci, bx, ur) in rows:
                            uk = ("ubuf", ur)
                            S.op("pool", lambda: nc.gpsimd.tensor_copy(out=ubuf[:, ur, 0:3], in_=halo[:, ci, :]), reads=[("halo", ci)], writes=[uk])
                            S.op("act", lambda: nc.scalar.copy(out=ubuf[:, ur, 3:TS + 3], in_=ps[bx][:]), reads=[pk(bx)], writes=[uk])
                            S.op("act", lambda: nc.scalar.activation(out=acc[ur][:], in_=ps[bx][:], func=AF.Identity, scale=cw[:, 3, ci:ci + 1], bias=cb[:, ci:ci + 1]),
                                 reads=[pk(bx), "prmA", "prmB"], writes=[("acc", ur)])
                        for kk in range(3):
                            for (r, ci, bx, ur) in rows:
                                S.op("dve", lambda: nc.vector.scalar_tensor_tensor(out=acc[ur][:], in0=ubuf[:, ur, kk:kk + TS], scalar=cw[:, kk, ci:ci + 1], in1=acc[ur][:],
                                                                                   op0=ALU.mult, op1=ALU.add), reads=[("ubuf", ur), ("acc", ur), "prmA"], writes=[("acc", ur)])
                        for (r, ci, bx, ur) in rows:
                            S.op("pool", lambda: nc.gpsimd.tensor_copy(out=halo[:, ci, :], in_=ubuf[:, ur, TS:TS + 3]), reads=[("ubuf", ur)], writes=[("halo", ci)])
                        for (r, ci, bx, ur) in rows:
                            S.op("act", lambda: nc.scalar.activation(out=ubuf[:, ur, 0:TS], in_=acc[ur][:], func=AF.Tanh), reads=[("acc", ur)], writes=[("ubuf", ur)])
                        for (r, ci, bx, ur) in rows:
                            S.op("dve", lambda: nc.vector.scalar_tensor_tensor(out=xbc_[:, r, :], in0=ubuf[:, ur, 0:TS], scalar=1.0, in1=acc[ur][:],
                                                                               op0=ALU.add, op1=ALU.mult), reads=[("ubuf", ur), ("acc", ur)], writes=[("xbc", gb, r)])
                    return f
                return [p_x(0), p_x(2), p_z(0), p_z(1)]

            cst = {}

            def stage_A(g, c):
                gb = g % 2
                xbc_ = xbc[gb]
                cs_ = cset[c % 2]
                ck = ("cs", c % 2)
                hs = slice(4 * g, 4 * g + 4)
                cs = slice(c * 128, (c + 1) * 128)
                bt = nb()
                T1 = psb(bt)
                for j, r in enumerate((0, 1, 2)):
                    S.op("pe", lambda: nc.tensor.transpose(out=T1[:, j * 128:(j + 1) * 128], in_=xbc_[:, r, cs], identity=identb[:]),
                         reads=[("xbc", gb, r), "identb"], writes=[pk(bt)], inc=(j == 2))
                T1x = T1[:, 0:256].rearrange("p (h q) -> p h q", q=64)
                S.op("act", lambda: nc.scalar.copy(out=cs_["xtok"][:], in_=T1[:, 0:256]), reads=[pk(bt)], writes=[(ck, "xtok")])
                S.op("pool", lambda: nc.gpsimd.tensor_tensor(out=cs_["xD"][:].rearrange("p (h q) -> p h q", q=64),
                                                             in0=cs_["xtok"][:].rearrange("p (h q) -> p h q", q=64),
                                                             in1=bc(D_bc[:, hs].unsqueeze(2), [128, 4, 64]), op=ALU.mult),
                     reads=[(ck, "xtok"), "D_bc"], writes=[(ck, "xD")])
                S.op("act", lambda: nc.scalar.copy(out=cs_["btok"][:], in_=T1[:, 256:384]), reads=[pk(bt)], writes=[(ck, "btok")])
                b1_ = nb()
                S.op("pe", lambda: nc.tensor.matmul(ps[b1_][:], lhsT=onesf[:], rhs=cs_["atri"][:], start=True, stop=True),
                     reads=["onesf", (ck, "atri")], writes=[pk(b1_)])
                cst[(g, c)] = dict(b1=b1_)

            def stage_A1(g, c):
                cs_ = cset[c % 2]
                ck = ("cs", c % 2)
                hs = slice(4 * g, 4 * g + 4)
                S.op("pool", lambda: nc.gpsimd.tensor_tensor(out=cs_["atri"][:].rearrange("p (h l) -> p h l", l=128),
                                                             in0=bc(trif[:].unsqueeze(1), [128, 4, 128]),
                                                             in1=bc(a_sb[:, c, hs].unsqueeze(2), [128, 4, 128]), op=ALU.mult),
                     reads=["trif", "a_sb"], writes=[(ck, "atri")])

            def stage_B1(g, c):
                gb = g % 2
                xbc_ = xbc[gb]
                cs_ = cset[c % 2]
                ck = ("cs", c % 2)
                hs = slice(4 * g, 4 * g + 4)
                cs = slice(c * 128, (c + 1) * 128)
                b1_ = cst[(g, c)]["b1"]
                X1 = ps[b1_][:].rearrange("p (h l) -> p h l", l=128)
                seg = cs_["atri"]
                S.op("dve", lambda: nc.vector.tensor_tensor(out=seg[:].rearrange("p (h l) -> p h l", l=128), in0=X1,
                                                            in1=bc(cumcolp[:, c, hs].unsqueeze(2), [128, 4, 128]), op=ALU.subtract),
                     reads=[pk(b1_), "cumcolp"], writes=[(ck, "atri")])
                S.op("act", lambda: nc.scalar.activation(out=e4[c % 2][:], in_=X1[:, :, 127], func=AF.Exp), reads=[pk(b1_)], writes=[("e4", c % 2)])
                seg3 = seg[:].rearrange("p (h l) -> p h l", l=128)
                S.op("pool", lambda: nc.gpsimd.affine_select(out=seg3, in_=seg3, pattern=[[0, 4], [1, 128]], compare_op=ALU.is_ge,
                                                             fill=neg_reg, base=0, channel_multiplier=-1), reads=[(ck, "atri")], writes=[(ck, "atri")])
                S.op("act", lambda: nc.scalar.activation(out=cs_["decayT"][:], in_=seg[:], func=AF.Exp), reads=[(ck, "atri")], writes=[(ck, "decayT")])
                b2_ = nb()
                S.op("pe", lambda: nc.tensor.matmul(ps[b2_][:, 0:128], lhsT=xbc_[:, 2, cs], rhs=xbc_[:, 3, cs], start=True, stop=True),
                     reads=[("xbc", gb, 2), ("xbc", gb, 3)], writes=[pk(b2_)])
                cst[(g, c)]["b2"] = b2_

            def stage_B2(g, c):
                cs_ = cset[c % 2]
                ck = ("cs", c % 2)
                b2_ = cst[(g, c)]["b2"]
                S.op("dve", lambda: nc.vector.tensor_tensor(out=cs_["GT"][:].rearrange("p (h l) -> p h l", l=128),
                                                            in0=cs_["decayT"][:].rearrange("p (h l) -> p h l", l=128),
                                                            in1=bc(ps[b2_][:, 0:128].unsqueeze(1), [128, 4, 128]), op=ALU.mult),
                     reads=[(ck, "decayT"), pk(b2_)], writes=[(ck, "GT")])
                dlast = cs_["decayT"][:].rearrange("p (h l) -> p h l", l=128)[:, :, 127:128]
                S.op("dve", lambda: nc.vector.tensor_tensor(out=cs_["xw"][:].rearrange("p (h q) -> p h q", q=64),
                                                            in0=cs_["xtok"][:].rearrange("p (h q) -> p h q", q=64),
                                                            in1=bc(dlast, [128, 4, 64]), op=ALU.mult),
                     reads=[(ck, "xtok"), (ck, "decayT")], writes=[(ck, "xw")])
                b3_ = nb()
                S.op("pe", lambda: nc.tensor.matmul(ps[b3_][:, 0:256], lhsT=identb[:], rhs=cs_["xD"][:], start=True, stop=False),
                     reads=["identb", (ck, "xD")], writes=[pk(b3_)], inc=False)
                for h in range(4):
                    S.op("pe", lambda: nc.tensor.matmul(ps[b3_][:, h * 64:(h + 1) * 64], lhsT=cs_["GT"][:, h * 128:(h + 1) * 128],
                                                        rhs=cs_["xtok"][:, h * 64:(h + 1) * 64], start=False, stop=True),
                         reads=[(ck, "GT"), (ck, "xtok")], writes=[pk(b3_)], inc=False)
                S.op("pe", lambda: nc.tensor.matmul(ps[b3_][:, 256:512], lhsT=cs_["btok"][:], rhs=cs_["xw"][:], start=True, stop=True),
                     reads=[(ck, "btok"), (ck, "xw")], writes=[pk(b3_)])
                cst[(g, c)]["b3"] = b3_

            def stage_C(g, c):
                gb = g % 2
                xbc_ = xbc[gb]
                hs = slice(4 * g, 4 * g + 4)
                cs = slice(c * 128, (c + 1) * 128)
                b3_ = cst[(g, c)]["b3"]
                st_g = stateT[:, g * 256:(g + 1) * 256]
                stb_g = stbf[:, g * 256:(g + 1) * 256]
                b4_ = nb()
                S.op("pe", lambda: nc.tensor.matmul(ps[b4_][:, 0:256], lhsT=xbc_[:, 3, cs], rhs=stb_g, start=True, stop=True),
                     reads=[("xbc", gb, 3), ("stbf", g)], writes=[pk(b4_)])
                S.op("pool", lambda: nc.gpsimd.tensor_tensor(out=sttmp[:].rearrange("p (h q) -> p h q", q=64),
                                                             in0=st_g.rearrange("p (h q) -> p h q", q=64),
                                                             in1=bc(e4[c % 2][:].unsqueeze(2), [128, 4, 64]), op=ALU.mult),
                     reads=[("stateT", g), ("e4", c % 2)], writes=["sttmp"])
                S.op("dve", lambda: nc.vector.tensor_tensor(out=ys[:].rearrange("p (h q) -> p h q", q=64),
                                                            in0=ps[b4_][:, 0:256].rearrange("p (h q) -> p h q", q=64),
                                                            in1=bc(expcum[:, c, hs].unsqueeze(2), [128, 4, 64]), op=ALU.mult),
                     reads=[pk(b4_), "expcum"], writes=["ys"])
                S.op("dve", lambda: nc.vector.tensor_tensor(out=st_g, in0=sttmp[:], in1=ps[b3_][:, 256:512], op=ALU.add),
                     reads=["sttmp", pk(b3_)], writes=[("stateT", g)])
                S.op("act", lambda: nc.scalar.copy(out=stb_g, in_=st_g), reads=[("stateT", g)], writes=[("stbf", g)])
                S.op("dve", lambda: nc.vector.tensor_tensor(out=ysum[:], in0=ps[b3_][:, 0:256], in1=ys[:], op=ALU.add),
                     reads=[pk(b3_), "ys"], writes=["ysum"])
                S.op("pool", lambda: nc.gpsimd.tensor_tensor(out=yg[:, c, :], in0=ysum[:], in1=zs[gb][:, c, :], op=ALU.mult),
                     reads=["ysum", ("zs", gb)], writes=[("yg", c)])
                S.op("act", lambda: nc.scalar.activation(out=junk[:], in_=yg[:, c, :], func=AF.Square, accum_out=ss[:, c:c + 1]),
                     reads=[("yg", c)], writes=["junk", ("ss", c)])

            def group_end1(g):
                S.op("act", lambda: nc.scalar.activation(out=sd4[:], in_=ss[:], func=AF.Sqrt, scale=1.0 / 256.0, bias=lneps[:, 1:2]),
                     reads=[("ss", c) for c in range(4)] + ["lneps"], writes=["sd4"])
                S.op("dve", lambda: nc.vector.reciprocal(out=rstd4[:], in_=sd4[:]), reads=["sd4"], writes=["rstd4"])
                for c in range(4):
                    S.op("act", lambda: nc.scalar.activation(out=ygn[:, c, :], in_=yg[:, c, :], func=AF.Copy, scale=rstd4[:, c:c + 1]),
                         reads=[("yg", c), "rstd4"], writes=[("ygn", c)])

            def group_end2(g):
                bn_ = nb()
                Tn = psb(bn_)
                for c in range(4):
                    for j in range(2):
                        S.op("pe", lambda: nc.tensor.transpose(out=Tn[:, (j * 4 + c) * 128:(j * 4 + c + 1) * 128], in_=ygn[:, c, j * 128:(j + 1) * 128],
                                                               identity=identb[:]),
                             reads=[("ygn", c), "identb"], writes=[pk(bn_)], inc=(c == 3 and j == 1))
                for j in range(2):
                    kc = 2 * g + j
                    S.op("dve", lambda: nc.vector.tensor_scalar(out=big[:, kc, :], in0=Tn[:, j * 512:(j + 1) * 512], scalar1=normw[:, kc:kc + 1],
                                                                scalar2=None, op0=ALU.mult),
                         reads=[pk(bn_), "prmB"], writes=[("big", kc)])

            for f in inproj_pieces(0):
                f()
            for g in range(NG):
                if g == 0:
                    for e_ in ("pe", "act", "dve", "pool"):
                        S.wait_all(e_, [iodom_out])
                P = inproj_pieces(g + 1) if g + 1 < NG else []
                P = P + [lambda: None] * (4 - len(P))
                order = list(P)
                if g > 0:
                    order.append(lambda: group_end2(g - 1))
                for it in range(-3, 5):
                    for (fn, c) in ((stage_C, it - 1), (stage_B2, it), (stage_B1, it + 1), (stage_A, it + 2), (stage_A1, it + 3)):
                        if 0 <= c <= 3:
                            order.append((lambda fn=fn, c=c: fn(g, c)))
                order.append(lambda: group_end1(g))
                for f in order:
                    f()
            group_end2(NG - 1)
            S.fence()
            for j in range(4):
                slot, skey = use_tile(("out", j))
                ov = slot[:, 0:16 * 256].rearrange("p (k c) -> p k c", c=256)
                for c in range(2):
                    k = 2 * j + c
                    b = nb()
                    for kk in range(16):
                        S.op("pe", lambda: nc.tensor.matmul(ps[b][:], lhsT=ov[:, kk, c * 128:(c + 1) * 128], rhs=big[:, kk, :],
                                                            start=(kk == 0), stop=(kk == 15)),
                             reads=[skey, ("big", kk)], writes=[pk(b)], inc=(kk == 15))
                    ln_accum(k, b, 0)
                done_tile()
            ln_finish(0)

        FVK = [("PT", 0), ("PT", 1)]
        FRK = [("PT", 2), ("PT", 3)]

        def attn_phase(sc, first_in_seq):
            t0 = sc * TS
            for j in range(2):
                slot, skey = use_tile(("kvk", j))
                kvv_ = slot[:, 0:4096].rearrange("p (k c) -> p k c", c=512)
                for pr in range(4):
                    b = nb()
                    for k in range(KD):
                        S.op("pe", lambda: nc.tensor.matmul(ps[b][:], lhsT=kvv_[:, k, pr * 128:(pr + 1) * 128], rhs=xTb[:, k, :], start=(k == 0), stop=(k == KD - 1)),
                             reads=[skey, ("xTb", k)], writes=[pk(b)], inc=(k == KD - 1))
                    S.op("act", lambda: nc.scalar.copy(out=KT[:, 4 * j + pr, t0:t0 + TS], in_=ps[b][:]), reads=[pk(b)], writes=[("KT", 4 * j + pr)])
                done_tile()
            for j in range(2):
                slot, skey = use_tile(("kvv", j))
                vv = slot[:, 0:4096].rearrange("p (k c) -> p k c", c=512)
                for tt in range(4):
                    b = nb()
                    for k in range(KD):
                        S.op("pe", lambda: nc.tensor.matmul(ps[b][:], lhsT=xTb[:, k, tt * 128:(tt + 1) * 128], rhs=vv[:, k, :],
                                                            start=(k == 0), stop=(k == KD - 1)),
                             reads=[skey, ("xTb", k)], writes=[pk(b)], inc=(k == KD - 1))
                    kt = 4 * sc + tt
                    S.op("dve", lambda: nc.vector.tensor_copy(out=VA[:, kt, 8 * j:8 * j + 8, 0:64], in_=ps[b][:].rearrange("p (h q) -> p h q", q=64)),
                         reads=[pk(b)], writes=[("VA", kt)])
                done_tile()
            bf_ = nb()
            for k in range(KD):
                S.op("pe", lambda: nc.tensor.matmul(ps[bf_][0:16, :], lhsT=wf[:, k, :], rhs=xTb[:, k, :], start=(k == 0), stop=(k == KD - 1)),
                     reads=["wf", ("xTb", k)], writes=[pk(bf_)], inc=(k == KD - 1))
            S.op("dve", lambda: nc.vector.tensor_scalar(out=f_v[:], in0=ps[bf_][0:16, :], scalar1=bf_col[:, 0:1], scalar2=None, op0=ALU.add),
                 reads=[pk(bf_), "bf_col"], writes=[*FVK])
            S.op("act", lambda: nc.scalar.activation(out=f_a[:], in_=f_v[:], func=AF.Abs), reads=[*FVK], writes=["f_a"])
            S.op("act", lambda: nc.scalar.activation(out=f_a[:], in_=f_a[:], func=AF.Exp, scale=-1.0), reads=["f_a"], writes=["f_a"])
            S.op("act", lambda: nc.scalar.activation(out=f_l[:], in_=f_a[:], func=AF.Ln, bias=1.0), reads=["f_a"], writes=["f_l"])
            S.op("dve", lambda: nc.vector.scalar_tensor_tensor(out=f_l[:], in0=f_v[:], scalar=0.0, in1=f_l[:], op0=ALU.min, op1=ALU.subtract),
                 reads=[*FVK, "f_l"], writes=["f_l"])
            if first_in_seq:
                S.op("pool", lambda: nc.gpsimd.memset(Fcarry[:], 0.0), writes=["Fcarry"])
            S.op("dve", lambda: nc.vector.tensor_tensor_scan(out=Frow[:], data0=bc(onesf[0:16, 0:1], [16, TS]), data1=f_l[:], initial=Fcarry[:, 0:1],
                                                             op0=ALU.mult, op1=ALU.add),
                 reads=["onesf", "f_l", "Fcarry"], writes=[*FRK])
            S.op("dve", lambda: nc.vector.tensor_copy(out=Fcarry[:], in_=Frow[:, TS - 1:TS]), reads=[*FRK], writes=["Fcarry"])
            bt_ = nb()
            for tt in range(4):
                S.op("pe", lambda: nc.tensor.transpose(out=ps[bt_][:, tt * 16:(tt + 1) * 16], in_=Frow[:, tt * 128:(tt + 1) * 128], identity=identf[0:16, 0:16]),
                     reads=[*FRK, "identf"], writes=[pk(bt_)], inc=False)
            S.op("dve", lambda: nc.vector.tensor_scalar(out=fdiag[:], in0=identf[0:16, 0:16], scalar1=Frow[:, 255:256], scalar2=None, op0=ALU.mult),
                 reads=["identf", *FRK], writes=["fdiag"])
            S.op("pe", lambda: nc.tensor.matmul(ps[bt_][:, 64:80], lhsT=onesf[0:16, :], rhs=fdiag[:], start=True, stop=True),
                 reads=["onesf", "fdiag"], writes=[pk(bt_)])
            S.op("dve", lambda: nc.vector.tensor_copy(out=Fcol[:, 4 * sc:4 * sc + 4, :], in_=ps[bt_][:, 0:64].rearrange("p (t h) -> p t h", h=16)),
                 reads=[pk(bt_)], writes=["Fcol"])
            S.op("dve", lambda: nc.vector.tensor_copy(out=Fq0[:], in_=ps[bt_][:, 64:80]), reads=[pk(bt_)], writes=["Fq0"])
            nkt_all = 4 * sc + 4
            S.op("dve", lambda: nc.vector.tensor_tensor(out=bcol[:, 0:nkt_all, :], in0=bc(Fq0[:].unsqueeze(1), [128, nkt_all, 16]),
                                                        in1=Fcol[:, 0:nkt_all, :], op=ALU.subtract),
                 reads=["Fq0", "Fcol"], writes=["bcol"])
            S.op("pool", lambda: nc.gpsimd.memset(QZ[:], 0.0), writes=[("QT", p_) for p_ in range(8)])
            for j in range(2):
                slot, skey = use_tile(("q", j))
                qvv_ = slot[:, 0:4096].rearrange("p (k c) -> p k c", c=512)
                for pr in range(4):
                    b = nb()
                    for k in range(KD):
                        S.op("pe", lambda: nc.tensor.matmul(ps[b][:], lhsT=qvv_[:, k, pr * 128:(pr + 1) * 128], rhs=xTb[:, k, :], start=(k == 0), stop=(k == KD - 1)),
                             reads=[skey, ("xTb", k)], writes=[pk(b)], inc=(k == KD - 1))
                    pq = 4 * j + pr
                    S.op("act", lambda: nc.scalar.activation(out=QZ[0:64, 2 * pq, :], in_=ps[b][0:64, :], func=AF.Copy, scale=0.125),
                         reads=[pk(b)], writes=[("QT", pq)])
                    S.op("act", lambda: nc.scalar.activation(out=QZ[64:128, 2 * pq + 1, :], in_=ps[b][64:128, :], func=AF.Copy, scale=0.125),
                         reads=[pk(b)], writes=[("QT", pq)])
                done_tile()
            OT = big
            ring["n"] = 6
            ring["i"] = 0
            jobs = []
            nkt = 4 * sc + 4
            for h in range(AH):
                for kt in range(nkt):
                    jobs.append((h, kt))
            LA = 2
            NPT = 4
            pend = {}
            deferred = []

            def emit_st(i):
                h, kt = jobs[i]
                pr, po = h // 2, (h % 2) * 64
                jd = kt - 4 * sc
                c0 = 128 * jd if jd > 0 else 0
                n = TS - c0
                b = nb()
                S.op("pe", lambda: nc.tensor.matmul(ps[b][:, 0:n], lhsT=KT[:, pr, kt * 128:(kt + 1) * 128],
                                                    rhs=QZ[:, h, c0:TS], start=True, stop=True),
                     reads=[("KT", pr), ("QT", pr)], writes=[pk(b)])
                pend[i] = b

            def emit_rest(i):
                h, kt = jobs[i]
                ob = 6 + (h % 2)
                okey = pk(ob)
                jd = kt - 4 * sc
                c0 = 128 * jd if jd > 0 else 0
                n = TS - c0
                b = pend.pop(i)
                pt = PT[i % NPT]
                ptk = ("PT", i % NPT)
                oreg = ps[ob][0:65, :]
                S.op("act", lambda: nc.scalar.activation(out=pt[:, c0:TS], in_=ps[b][:, 0:n], func=AF.Exp, bias=bcol[:, kt, h:h + 1]),
                     reads=[pk(b), "bcol"], writes=[ptk])
                if jd >= 0:
                    S.op("pool", lambda: nc.gpsimd.affine_select(out=pt[:, c0:c0 + 128], in_=pt[:, c0:c0 + 128], pattern=[[1, 128]],
                                                                 compare_op=ALU.is_ge, fill=zero_reg, base=0, channel_multiplier=-1),
                         reads=[ptk], writes=[ptk])
                last = (kt == nkt - 1)
                S.op("pe", lambda: nc.tensor.matmul(oreg[:, c0:TS], lhsT=VA[:, kt, h, :], rhs=pt[:, c0:TS], start=(kt == 0), stop=last),
                     reads=[("VA", kt), ptk], writes=[okey], inc=last)
                if last:
                    rr_, Rs_ = rr[h % 2], Rs[h % 2]
                    S.op("dve", lambda: nc.vector.reciprocal(out=rr_[64:65, :], in_=ps[ob][64:65, :]), reads=[okey], writes=[("rr", 0)])

                    def fin(h=h, ob=ob, okey=okey, rr_=rr_, Rs_=Rs_):
                        b2 = nb()
                        S.op("pe", lambda: nc.tensor.matmul(ps[b2][0:64, :], lhsT=onesf[64:65, 0:64], rhs=rr_[64:65, :], start=True, stop=True),
                             reads=["onesf", ("rr", 0)], writes=[pk(b2)])
                        S.op("act", lambda: nc.scalar.copy(out=Rs_[:], in_=ps[b2][0:64, :]), reads=[pk(b2)], writes=[("Rs", 0)])
                        S.op("dve", lambda: nc.vector.tensor_tensor(out=OT[0:64, h, :], in0=ps[ob][0:64, :], in1=Rs_[:], op=ALU.mult),
                             reads=[okey, ("Rs", 0)], writes=[("big", h)])
                    deferred.append([2, fin])

            nj = len(jobs)
            for i in range(nj + LA):
                if i < nj:
                    emit_st(i)
                for dfr in list(deferred):
                    dfr[0] -= 1
                    if dfr[0] <= 0:
                        deferred.remove(dfr)
                        dfr[1]()
                if i >= LA:
                    emit_rest(i - LA)
            for dfr in deferred:
                dfr[1]()
            ring["n"] = 8
            for j in range(4):
                slot, skey = use_tile(("o", j))
                ov = slot[0:64, 0:16 * 256].rearrange("p (h c) -> p h c", c=256)
                for c in range(2):
                    k = 2 * j + c
                    b = nb()
                    for h in range(AH):
                        S.op("pe", lambda: nc.tensor.matmul(ps[b][:], lhsT=ov[:, h, c * 128:(c + 1) * 128], rhs=OT[0:64, h, :],
                                                            start=(h == 0), stop=(h == AH - 1)),
                             reads=[skey, ("big", h)], writes=[pk(b)], inc=(h == AH - 1))
                    ln_accum(k, b, 2)
                done_tile()
            ln_finish(2)

        def xk(tt):
            nm = "lnb" if tt < 2 else "lnsq"
            return [(nm, 4 * (tt % 2) + i) for i in range(4)]
        XK = xk(0) + xk(1) + xk(2) + xk(3)
        xld = big[:, 0:16, :].rearrange("p k t -> p (k t)").bitcast(F32).rearrange("p (t d) -> p t d", d=D)
        BK4 = [[("big", 4 * tt + i) for i in range(4)] for tt in range(4)]

        def load_x(bseq_, sc_):
            S.dma("sp", xld, dr["x"][bseq_, sc_ * TS:(sc_ + 1) * TS, :].rearrange("(t p) d -> p t d", p=128), [], BK4[0] + BK4[1] + BK4[2] + BK4[3], iodom_in)
        S.fence()
        gi = 0
        for bseq in range(NB if stop_after != "setup" else 0):
            for sc in range(NSC):
                tap.idx = gi
                t0 = sc * TS
                first = (sc == 0)
                if gi == 0 or stop_after is not None:
                    load_x(bseq, sc)
                for k in range(KD if stop_after != "xdma" else 0):
                    b = nb()
                    for tt in range(4):
                        S.op("pe", lambda: nc.tensor.transpose(out=ps[b][:, tt * 128:(tt + 1) * 128], in_=xld[:, tt, k * 128:(k + 1) * 128], identity=identf[:]),
                             reads=BK4[tt] + ["identf"], writes=[pk(b)], inc=(tt == 3))
                    S.op("act", lambda: nc.scalar.copy(out=xT[:, k, :], in_=ps[b][:]), reads=[pk(b)], writes=[("xT", k)])
                    S.op("dve", lambda: nc.vector.tensor_copy(out=xTb[:, k, :], in_=ps[b][:]), reads=[pk(b)], writes=[("xTb", k)])
                S.fence()
                if stop_after not in ("xload", "xdma"):
                    ssd_phase(first)
                    tap("dbg_x1", xT[:, 0, :], ("xT", 0), [128, TS])
                if stop_after not in ("xload", "ssd", "xdma"):
                    ffn_phase(0, 1)
                    tap("dbg_x2", xT[:, 0, :], ("xT", 0), [128, TS])
                if stop_after not in ("xload", "ssd", "ffn0", "xdma"):
                    S.fence()
                    attn_phase(sc, first)
                    tap("dbg_x3", xT[:, 0, :], ("xT", 0), [128, TS])
                    ffn_phase(1, 3)
                    nxt = gi + 1
                    if nxt < NB * NSC:
                        load_x(nxt // NSC, nxt % NSC)
                for tt in range(4 if stop_after != "xdma" else 0):
                    for hf in range(2):
                        b = nb()
                        for kq in range(4):
                            k = hf * 4 + kq
                            S.op("pe", lambda: nc.tensor.transpose(out=ps[b][:, kq * 128:(kq + 1) * 128], in_=xT[:, k, tt * 128:(tt + 1) * 128], identity=identf[:]),
                                 reads=[("xT", k), "identf"], writes=[pk(b)], inc=(kq == 3))
                        if hf == 0:
                            S.op("act", lambda: nc.scalar.copy(out=xin[:, tt, hf * 512:(hf + 1) * 512], in_=ps[b][:]), reads=[pk(b)], writes=xk(tt))
                        else:
                            S.op("dve", lambda: nc.vector.tensor_copy(out=xin[:, tt, hf * 512:(hf + 1) * 512], in_=ps[b][:]), reads=[pk(b)], writes=xk(tt))
                S.dma("sp", out_d[bseq, t0:t0 + TS, :].rearrange("(t p) d -> p t d", p=128), xin, XK, [], iodom_out)
                gi += 1
        assert stop_after is not None or wstate["next_use"] == total_tiles, (wstate, total_tiles)
        S.wait_all("sp", [iodom_out, dbgdom])
        build.stats = dict(nins=dict(S.nins), ndma=S.ndma, counts={k: v.count for k, v in S.dom.items()})
    return nc, list(dbg.keys())


_CACHE = {}


def kernel(**inputs):
    n_cores = 8
    x = np.ascontiguousarray(inputs["x"], dtype=np.float32)
    B, SEQ, _ = x.shape
    NB = B // n_cores
    key = (NB, SEQ)
    if key not in _CACHE:
        _CACHE[key] = build(NB, SEQ)[0]
    nc = _CACHE[key]
    in_maps = []
    for c in range(n_cores):
        m = {k: np.ascontiguousarray(v, dtype=np.float32) for k, v in inputs.items() if k != "x"}
        m["x"] = x[c * NB:(c + 1) * NB]
        in_maps.append(m)
    res = run_bass_kernel_spmd(nc, in_maps, core_ids=list(range(n_cores)))
    return np.concatenate([r["out"] for r in res.results], axis=0)
```

```python
import numpy as np
from contextlib import ExitStack
from collections import defaultdict

import concourse.bass as bass
import concourse.mybir as mybir
from concourse.bass_utils import run_bass_kernel_spmd

F32 = mybir.dt.float32
BF16 = mybir.dt.bfloat16
AF = mybir.ActivationFunctionType
ALU = mybir.AluOpType

D = 1024
KD = 8
DI = 2048
NG = 8
NHEAD = 32
DFF = 2816
NF = 22
AH = 16
TS = 512
DEPTH = 2
ALPHA = (2.0 * DEPTH) ** 0.25
LN_EPS = 1e-5
RMS_EPS = 1e-5
SLOT = 4096
NSLOT = 3
NEG = -30000.0


class Dom:
    def __init__(self, nc, es, name, step, epoch):
        self.nc, self.es, self.name, self.step, self.epoch = nc, es, name, step, epoch
        self.sems = []
        self.count = 0

    def sem_for(self, cnt):
        e = (cnt - 1) // self.epoch
        while len(self.sems) <= e:
            self.sems.append(self.es.enter_context(self.nc.semaphore(f"s_{self.name}_{len(self.sems)}")))
        return self.sems[e], ((cnt - 1) % self.epoch + 1) * self.step


class Sched:
    def __init__(self, nc, es):
        self.nc, self.es = nc, es
        self.eng = {"pe": nc.tensor, "act": nc.scalar, "dve": nc.vector, "pool": nc.gpsimd, "sp": nc.sync}
        self.dom = {e: Dom(nc, es, e, 1, 4096) for e in ("pe", "act", "dve", "pool")}
        self.seen = defaultdict(int)
        self.lastw = {}
        self.readers = defaultdict(dict)
        self.ndma = 0
        self.nins = defaultdict(int)

    def new_dma_dom(self, name):
        return Dom(self.nc, self.es, name, 16, 1024)

    def _deps(self, own, reads, writes):
        deps = {}

        def need(dc, same_ok):
            dom, cnt = dc
            if dom is own and same_ok:
                return
            if deps.get(dom, 0) < cnt:
                deps[dom] = cnt

        for k in reads:
            if k in self.lastw:
                need(self.lastw[k], False)
            if isinstance(k, tuple) and k[0] == "ps":
                for dom, cnt in self.readers[k].items():
                    need((dom, cnt), True)
        for k in writes:
            if k in self.lastw:
                need(self.lastw[k], True)
            for dom, cnt in self.readers[k].items():
                need((dom, cnt), True)
        return deps

    def _wait(self, e, deps, own=None):
        for dom, cnt in deps.items():
            if self.seen[(e, dom.name)] >= cnt:
                continue
            if dom is own:
                assert cnt <= own.count, "same-engine wait on a future completion"
            sem, val = dom.sem_for(cnt)
            self.eng[e].wait_ge(sem, val)
            self.nins[e] += 1
            self.seen[(e, dom.name)] = cnt

    def op(self, e, fn, reads=(), writes=(), inc=True):
        own = self.dom[e]
        self._wait(e, self._deps(own, reads, writes), own)
        ins = fn()
        self.nins[e] += 1
        tag = own.count + 1
        if inc:
            own.count += 1
            sem, _ = own.sem_for(own.count)
            ins.then_inc(sem, 1)
        for k in reads:
            if self.readers[k].get(own, 0) < tag:
                self.readers[k][own] = tag
        for k in writes:
            self.lastw[k] = (own, tag)
            self.readers[k] = {}
        return ins

    def dma(self, q, out, in_, reads, writes, dom):
        self._wait(q, self._deps(None, reads, writes))
        ins = self.eng[q].dma_start(out=out, in_=in_)
        self.nins[q] += 1
        self.ndma += 1
        dom.count += 1
        sem, _ = dom.sem_for(dom.count)
        ins.then_inc(sem, 16)
        for k in reads:
            self.readers[k][dom] = dom.count
        for k in writes:
            self.lastw[k] = (dom, dom.count)
            self.readers[k] = {}
        return ins

    def fence(self):
        es_ = ("pe", "act", "dve", "pool")
        for e in es_:
            self._wait(e, {self.dom[f]: self.dom[f].count for f in es_ if f != e and self.dom[f].count > 0})

    def wait_all(self, e, doms):
        for dom in doms:
            if dom.count > 0:
                self._wait(e, {dom: dom.count})


def bc(ap, shape):
    return ap.to_broadcast(list(shape))


def weight_tiles():
    tiles = []
    for g in range(NG):
        tiles.append((("inA", g), 8 * 512, [("ssm_in_w", 0, g * 256, 256, "kpc", 0, 512, 0),
                                             ("ssm_in_w", 0, 2048 + g * 256, 256, "kpc", 0, 512, 256)]))
        tiles.append((("inB", g), 8 * 256, [("ssm_in_w", 0, 4096 + g * 128, 128, "kpc", 0, 256, 0),
                                             ("ssm_in_w", 0, 5120 + g * 128, 128, "kpc", 0, 256, 128)]))
    for j in range(4):
        tiles.append((("out", j), 16 * 256, [("ssm_out_w", 0, j * 256, 256, "kpc", 0, 256, 0)]))

    def ffn(l):
        for j in range(6):
            nfc = 4 if j < 5 else 2
            tiles.append((("g", l, j), 8 * 512, [("ffn_gate_w", l, j * 512, nfc * 128, "kpc", 0, 512, 0)]))
            tiles.append((("u", l, j), 8 * 512, [("ffn_up_w", l, j * 512, nfc * 128, "kpc", 0, 512, 0)]))
        for hf in range(2):
            for fg in range(3):
                nf = 8 if fg < 2 else 6
                tiles.append((("dn", l, hf, fg), nf * 512, [("ffn_down_w", l, hf * 512, 512, "kpc_rows", 0, 512, 0, fg * 8, nf)]))

    ffn(0)
    for j in range(2):
        tiles.append((("kvk", j), 4096, [("kv_w", None, j * 512, 512, "kpc", 0, 512, 0)]))
    for j in range(2):
        tiles.append((("kvv", j), 4096, [("kv_w", None, 1024 + j * 512, 512, "kpc", 0, 512, 0)]))
    for j in range(2):
        tiles.append((("q", j), 4096, [("att_q_w", 0, j * 512, 512, "kpc", 0, 512, 0)]))
    for j in range(4):
        tiles.append((("o", j), 16 * 256, [("att_o_w", 0, j * 256, 256, "hpc", 0, 256, 0)]))
    ffn(1)
    return tiles


IN_SPECS = [
    ("x", None), ("ssm_in_w", [1, 1024, 6176]), ("ssm_conv_w", [1, 4, 4096]), ("ssm_conv_b", [1, 4096]),
    ("ssm_dt_bias", [1, 32]), ("ssm_a_log", [1, 32]), ("ssm_d", [1, 32]), ("ssm_norm_w", [1, 2048]),
    ("ssm_out_w", [1, 2048, 1024]), ("kv_w", [1024, 2064]), ("kv_b_f", [16]), ("att_q_w", [1, 1024, 1024]),
    ("att_o_w", [1, 1024, 1024]), ("ffn_gate_w", [2, 1024, 2816]), ("ffn_up_w", [2, 1024, 2816]),
    ("ffn_down_w", [2, 2816, 1024]), ("ln_mix_g", [2, 1024]), ("ln_mix_b", [2, 1024]), ("ln_ffn_g", [2, 1024]),
    ("ln_ffn_b", [2, 1024]),
]


def build(NB=4, SEQ=2048, debug=False, stop_after=None):
    nc = bass.Bass("TRN2", target_bir_lowering=False)
    NSC = SEQ // TS
    NKT = SEQ // 128
    dr = {}
    for name, shp in IN_SPECS:
        if name == "x":
            shp = [NB, SEQ, D]
        dr[name] = nc.dram_tensor(name, shp, F32, kind="ExternalInput").ap()
    out_d = nc.dram_tensor("out", [NB, SEQ, D], F32, kind="ExternalOutput").ap()
    tiles = weight_tiles()
    NT = len(tiles)
    wscr = nc.dram_tensor("wscr", [NT, 128, SLOT], BF16, kind="Internal").ap()
    dbg = {}

    with ExitStack() as es:
        ec = es.enter_context
        S = Sched(nc, es)

        def sb(name, shape, dt=F32):
            return ec(nc.sbuf_tensor(name, list(shape), dt))

        identf = sb("identf", [128, 128]); identb = sb("identb", [128, 128], BF16)
        onesf = sb("onesf", [128, 128]); trif = sb("trif", [128, 128])
        lnones = sb("lnones", [128, 128], BF16)
        negmask = sb("negmask", [128, 512], BF16)
        prmA = sb("prmA", [128, 128]); prmB = sb("prmB", [128, 128])
        dtb_bc = sb("dtb_bc", [128, 32]); A_bc = sb("A_bc", [128, 32]); D_bc = sb("D_bc", [128, 32])
        bf_col = sb("bf_col", [16, 1])
        wdt = sb("wdt", [128, 8, 32], BF16); wf = sb("wf", [128, 8, 16], BF16)
        wslot = [sb(f"wslot{i}", [128, SLOT], BF16) for i in range(NSLOT)]
        scr16 = sb("scr16", [128, 4096])
        xin = scr16[:].rearrange("p (t d) -> p t d", d=D)
        lnb = scr16[:, 0:2048].bitcast(BF16).rearrange("p (k t) -> p k t", t=TS)
        lnsq = scr16[:, 2048:4096].bitcast(BF16).rearrange("p (k t) -> p k t", t=TS)
        xT = sb("xT", [128, KD, TS]); xTb = sb("xTb", [128, KD, TS], BF16)
        mean_sb = sb("mean_sb", [128, TS]); rstd_sb = sb("rstd_sb", [128, TS])
        big = sb("big", [128, NF, TS], BF16)
        acc = [sb(f"acc{i}", [128, TS]) for i in range(2)]
        lnt = acc
        stateT = sb("stateT", [128, NHEAD * 64]); stbf = sb("stbf", [128, NHEAD * 64], BF16)
        halo = sb("halo", [128, 32, 3])
        KT = sb("KT", [128, 8, SEQ], BF16)
        VA = sb("VA", [128, NKT, AH, 65], BF16)
        Fcarry = sb("Fcarry", [16, 1]); Fcol = sb("Fcol", [128, NKT, AH])
        lneps = sb("lneps", [128, 2])
        ARENA = 7750
        arena = sb("arena", [128, ARENA])
        ar = {"o": 0}

        def carve(shape, dt=F32):
            n = int(np.prod(shape[1:]))
            w = n if dt == F32 else (n + 1) // 2
            o = ar["o"]
            assert o + w <= ARENA, ("arena overflow", o, w)
            ar["o"] = o + w
            v = arena[0:shape[0], o:o + w]
            if dt != F32:
                v = v.bitcast(dt)
            if len(shape) == 3:
                v = v.rearrange("p (a b) -> p a b", b=shape[2])
            return v

        stg1 = carve([128, 128]); stg2 = carve([128, 128])
        ar["o"] = 0
        dt_sb = carve([128, 4, 32]); a_sb = carve([128, 4, 32]); cumcol = carve([128, 4, 32]); expcum = carve([128, 4, 32])
        sp_t = [carve([128, 4, 32]) for _ in range(3)]
        cumcolp = carve([128, 4, 32])
        ubuf = carve([128, 2, TS + 4])
        e4 = [carve([128, 4]) for _ in range(2)]
        ys = carve([128, 256]); ysum = carve([128, 256])
        yg = carve([128, 4, 256]); ss = carve([128, 4]); sd4 = carve([128, 4]); rstd4 = carve([128, 4])
        junk = carve([128, 256]); ygn = carve([128, 4, 256], BF16); sttmp = carve([128, 256])
        zs = [carve([128, 4, 256], BF16), None]
        xbc = [carve([128, 4, TS], BF16), None]

        def chunk_set():
            return dict(xtok=carve([128, 256], BF16), xD=carve([128, 256], BF16), btok=carve([128, 128], BF16),
                        atri=carve([128, 512]), decayT=carve([128, 512], BF16), GT=carve([128, 512], BF16), xw=carve([128, 256], BF16))
        cset = [chunk_set(), None]
        ssd_top = ar["o"]
        sav = (arena, ar["o"])
        arena_main = arena

        def carve16(shape, dt=F32):
            n = int(np.prod(shape[1:]))
            w = n if dt == F32 else (n + 1) // 2
            o = c16["o"]
            assert o + w <= 4096, ("scr16 overflow", o, w)
            c16["o"] = o + w
            v = scr16[0:shape[0], o:o + w]
            if dt != F32:
                v = v.bitcast(dt)
            if len(shape) == 3:
                v = v.rearrange("p (a b) -> p a b", b=shape[2])
            return v
        c16 = {"o": 0}
        zs[1] = carve16([128, 4, 256], BF16)
        xbc[1] = carve16([128, 4, TS], BF16)
        cset[1] = dict(xtok=carve16([128, 256], BF16), xD=carve16([128, 256], BF16), btok=carve16([128, 128], BF16),
                       atri=carve16([128, 512]), decayT=carve16([128, 512], BF16), GT=carve16([128, 512], BF16), xw=carve16([128, 256], BF16))
        ar["o"] = 0
        QZ = carve([128, AH, TS], BF16)
        fdiag = carve([16, 16]); Fq0 = carve([128, AH]); bcol = carve([128, NKT, AH])
        PT = [carve([128, 512], BF16) for _ in range(4)]
        f_v = acc[0][0:16, :]; f_a = acc[1][0:16, :]; f_l = mean_sb[0:16, :]; Frow = rstd_sb[0:16, :]
        rr = [carve([128, 512])] * 2; Rs = [carve([64, 512])] * 2
        att_top = ar["o"]
        ps = [ec(nc.psum_tensor(f"ps{i}", [128, 512], F32)) for i in range(8)]

        def psb(i):
            return ps[i][:].bitcast(BF16)

        ring = {"i": 0, "n": 8}

        def nb():
            i = ring["i"] % ring["n"]
            ring["i"] = (i + 1) % ring["n"]
            return i

        def pk(i):
            return ("ps", i)

        P_ = "pool"
        neg_reg = nc.gpsimd.to_reg(NEG)
        zero_reg = nc.gpsimd.to_reg(0.0)
        S.op(P_, lambda: nc.gpsimd.memset(identf[:], 0.0), writes=["identf"])
        S.op(P_, lambda: nc.gpsimd.affine_select(out=identf[:], in_=identf[:], pattern=[[-1, 128]], compare_op=ALU.not_equal,
                                                 fill=1.0, base=0, channel_multiplier=1), reads=["identf"], writes=["identf"])
        S.op(P_, lambda: nc.gpsimd.tensor_copy(out=identb[:], in_=identf[:]), reads=["identf"], writes=["identb"])
        S.op(P_, lambda: nc.gpsimd.memset(onesf[:], 1.0), writes=["onesf"])
        S.op(P_, lambda: nc.gpsimd.memset(lnones[:], 1.0 / D), writes=["lnones"])
        S.op(P_, lambda: nc.gpsimd.memset(trif[:], 1.0), writes=["trif"])
        S.op(P_, lambda: nc.gpsimd.affine_select(out=trif[:], in_=trif[:], pattern=[[1, 128]], compare_op=ALU.is_ge,
                                                 fill=0.0, base=0, channel_multiplier=-1), reads=["trif"], writes=["trif"])
        S.op(P_, lambda: nc.gpsimd.memset(negmask[:], 0.0), writes=["negmask"])
        nm3 = negmask[:].rearrange("p (h l) -> p h l", h=4)
        S.op(P_, lambda: nc.gpsimd.affine_select(out=nm3, in_=nm3, pattern=[[0, 4], [1, 128]], compare_op=ALU.is_ge,
                                                 fill=NEG, base=0, channel_multiplier=-1), reads=["negmask"], writes=["negmask"])
        S.op(P_, lambda: nc.gpsimd.memset(stg2[:], 0.0), writes=["stg2"])

        cdom = S.new_dma_dom("cst")
        S.dma("sp", stg1[:], dr["ssm_conv_w"][0].rearrange("k (c p) -> (k c) p", p=128), [], ["stg1"], cdom)
        rows = [("ssm_conv_b", dr["ssm_conv_b"][0], 32, 0), ("ssm_norm_w", dr["ssm_norm_w"][0], 16, 32),
                ("ln_mix_g", dr["ln_mix_g"].rearrange("l d -> (l d)"), 16, 48), ("ln_mix_b", dr["ln_mix_b"].rearrange("l d -> (l d)"), 16, 64),
                ("ln_ffn_g", dr["ln_ffn_g"].rearrange("l d -> (l d)"), 16, 80), ("ln_ffn_b", dr["ln_ffn_b"].rearrange("l d -> (l d)"), 16, 96)]
        for (_, src, n, r0) in rows:
            S.dma("sp", stg2[r0:r0 + n, :], src.rearrange("(c p) -> c p", p=128), [], ["stg2"], cdom)
        S.dma("sp", dtb_bc[:], dr["ssm_dt_bias"].partition_broadcast(128), [], ["dtb_bc"], cdom)
        S.dma("sp", A_bc[:], dr["ssm_a_log"].partition_broadcast(128), [], ["A_bc"], cdom)
        S.dma("sp", D_bc[:], dr["ssm_d"].partition_broadcast(128), [], ["D_bc"], cdom)
        S.dma("sp", bf_col[:], dr["kv_b_f"].rearrange("(h o) -> h o", o=1), [], ["bf_col"], cdom)
        S.op("act", lambda: nc.scalar.activation(out=A_bc[:], in_=A_bc[:], func=AF.Exp), reads=["A_bc"], writes=["A_bc"])
        S.op("dve", lambda: nc.vector.tensor_scalar_mul(out=A_bc[:], in0=A_bc[:], scalar1=-1.0), reads=["A_bc"], writes=["A_bc"])
        b0 = nb()
        S.op("pe", lambda: nc.tensor.transpose(out=ps[b0][:, 0:128], in_=stg1[:], identity=identf[:]), reads=["stg1", "identf"], writes=[pk(b0)])
        S.op("dve", lambda: nc.vector.tensor_copy(out=prmA[:], in_=ps[b0][:, 0:128]), reads=[pk(b0)], writes=["prmA"])
        b1 = nb()
        S.op("pe", lambda: nc.tensor.transpose(out=ps[b1][:, 0:128], in_=stg2[:], identity=identf[:]), reads=["stg2", "identf"], writes=[pk(b1)])
        S.op("dve", lambda: nc.vector.tensor_copy(out=prmB[:], in_=ps[b1][:, 0:128]), reads=[pk(b1)], writes=["prmB"])
        S.op("dve", lambda: nc.vector.tensor_scalar_mul(out=prmA[:], in0=prmA[:], scalar1=0.5), reads=["prmA"], writes=["prmA"])
        S.op("dve", lambda: nc.vector.tensor_scalar_mul(out=prmB[:, 0:32], in0=prmB[:, 0:32], scalar1=0.5), reads=["prmB"], writes=["prmB"])
        cw = prmA[:].rearrange("p (k c) -> p k c", k=4)
        cb = prmB[:, 0:32]
        normw = prmB[:, 32:48]
        lng = {0: prmB[:, 48:56], 1: prmB[:, 80:88], 2: prmB[:, 56:64], 3: prmB[:, 88:96]}
        lnbias = {0: prmB[:, 64:72], 1: prmB[:, 96:104], 2: prmB[:, 72:80], 3: prmB[:, 104:112]}

        import os
        wcdom = S.new_dma_dom("wcv")
        stf = [scr16[:], xT[:].rearrange("p k t -> p (k t)")]
        stb = [big[:, 0:8, :].rearrange("p k t -> p (k t)"), big[:, 8:16, :].rearrange("p k t -> p (k t)")]
        ktf = KT[:].rearrange("p k t -> p (k t)").bitcast(F32)
        for i in range(ktf.shape[1] // 4096):
            stf.append(ktf[:, i * 4096:(i + 1) * 4096])
        vaf = VA[:].rearrange("p a b c -> p (a b c)")
        for i in range(vaf.shape[1] // 4096):
            stb.append(vaf[:, i * 4096:(i + 1) * 4096])
        NST = min(len(stf), len(stb), 4)
        cvl = [S.new_dma_dom(f"cvl{i}") for i in range(NST)]
        cvs = [S.new_dma_dom(f"cvs{i}") for i in range(NST)]
        cast_eng = ["dve", "act", "pool"]
        for ti, (name, nel, parts) in enumerate(tiles):
            if os.environ.get("SKIP_CONV"):
                break
            sl = ti % NST
            npart = 64 if name[0] == "o" else 128
            for part in parts:
                (src, idx, c0, n, kind, base, cstride, coff) = part[:8]
                w = dr[src] if idx is None else dr[src][idx]
                if kind == "kpc_rows":
                    k0, nk = part[8], part[9]
                    s_ap = w[k0 * 128:(k0 + nk) * 128, c0:c0 + n].rearrange("(k p) c -> p k c", p=128)
                    d_ap = stf[sl][:, base:base + nk * cstride].rearrange("p (k c) -> p k c", c=cstride)[:, :, coff:coff + n]
                elif kind == "kpc":
                    nk = w.shape[0] // 128
                    s_ap = w[:, c0:c0 + n].rearrange("(k p) c -> p k c", p=128)
                    d_ap = stf[sl][:, base:base + nk * cstride].rearrange("p (k c) -> p k c", c=cstride)[:, :, coff:coff + n]
                else:
                    s_ap = w[:, c0:c0 + n].rearrange("(h p) c -> p h c", p=64)
                    d_ap = stf[sl][0:64, base:base + 16 * cstride].rearrange("p (h c) -> p h c", c=cstride)[:, :, coff:coff + n]
                S.dma("act" if (ti % 2) else "sp", d_ap, s_ap, [], [("stf", sl)], cvl[sl])
            ce = cast_eng[ti % 3]
            if ce == "dve":
                S.op("dve", lambda: nc.vector.tensor_copy(out=stb[sl][0:npart, 0:nel], in_=stf[sl][0:npart, 0:nel]), reads=[("stf", sl)], writes=[("stb", sl)])
            elif ce == "act":
                S.op("act", lambda: nc.scalar.copy(out=stb[sl][0:npart, 0:nel], in_=stf[sl][0:npart, 0:nel]), reads=[("stf", sl)], writes=[("stb", sl)])
            else:
                S.op("pool", lambda: nc.gpsimd.tensor_copy(out=stb[sl][0:npart, 0:nel], in_=stf[sl][0:npart, 0:nel]), reads=[("stf", sl)], writes=[("stb", sl)])
            S.dma("act" if (ti % 2) else "sp", wscr[ti][0:npart, 0:nel], stb[sl][0:npart, 0:nel], [("stb", sl)], [("wscr", ti)], cvs[sl])
        S.dma("pool", wdt[:], dr["ssm_in_w"][0][:, 6144:6176].rearrange("(k p) c -> p k c", p=128), [], ["wdt"], wcdom)
        S.dma("pool", wf[:], dr["kv_w"][:, 2048:2064].rearrange("(k p) c -> p k c", p=128), [], ["wf"], wcdom)
        S.wait_all("pool", cvs + cvl)
        S.op("pool", lambda: nc.gpsimd.memset(VA[:], 1.0), reads=[("stb", i) for i in range(NST)], writes=[("VA", kt) for kt in range(NKT)])
        S.wait_all("sp", cvs + cvl)
        S.wait_all("pe", cvs + cvl)
        S.wait_all("act", cvs + cvl)
        S.wait_all("dve", cvs + cvl)
        S.wait_all("pool", cvs + cvl)

        wdoms = [S.new_dma_dom(f"w{i}") for i in range(NSLOT)]
        wstate = {"next_load": 0, "next_use": 0}
        total_tiles = NB * NSC * NT

        def prefetch():
            i = wstate["next_load"]
            if i >= total_tiles:
                return
            wstate["next_load"] += 1
            ti = i % NT
            s = i % NSLOT
            nel = tiles[ti][1]
            npart = 64 if tiles[ti][0][0] == "o" else 128
            S.dma("sp", wslot[s][0:npart, 0:nel], wscr[ti][0:npart, 0:nel], [("wscr", ti)], [("wslot", s)], wdoms[s])

        def use_tile(expect):
            if stop_after is not None:
                while tiles[wstate["next_use"] % NT][0] != expect:
                    wstate["next_use"] += 1
                    prefetch()
            i = wstate["next_use"]
            wstate["next_use"] += 1
            ti = i % NT
            assert tiles[ti][0] == expect, (tiles[ti][0], expect)
            s = i % NSLOT
            return wslot[s], ("wslot", s)

        def done_tile():
            prefetch()

        for _ in range(NSLOT):
            prefetch()

        iodom_in = S.new_dma_dom("xin")
        iodom_out = S.new_dma_dom("xout")
        dbgdom = S.new_dma_dom("dbg")

        def tap(name, ap, key, shape):
            if not debug:
                return
            if name not in dbg:
                dbg[name] = nc.dram_tensor(name, [NB * NSC] + list(shape), ap.dtype, kind="ExternalOutput").ap()
            S.dma("sp", dbg[name][tap.idx], ap, [key], [], dbgdom)
        tap.idx = 0

        def ln_accum(k, bank, ln_idx):
            S.op("dve", lambda: nc.vector.scalar_tensor_tensor(out=xT[:, k, :], in0=xT[:, k, :], scalar=ALPHA, in1=ps[bank][:],
                                                               op0=ALU.mult, op1=ALU.add),
                 reads=[("xT", k), pk(bank)], writes=[("xT", k)])
            S.op("act", lambda: nc.scalar.copy(out=lnb[:, k, :], in_=xT[:, k, :]), reads=[("xT", k)], writes=[("lnb", k)])
            S.op("act", lambda: nc.scalar.activation(out=lnsq[:, k, :], in_=xT[:, k, :], func=AF.Square), reads=[("xT", k)], writes=[("lnsq", k)])

        def ln_finish(ln_idx):
            bm, be = nb(), nb()
            for k in range(KD):
                S.op("pe", lambda: nc.tensor.matmul(ps[bm][:], lhsT=lnones[:], rhs=lnb[:, k, :], start=(k == 0), stop=(k == KD - 1)),
                     reads=[("lnb", k), "lnones"], writes=[pk(bm)], inc=(k == KD - 1))
            for k in range(KD):
                S.op("pe", lambda: nc.tensor.matmul(ps[be][:], lhsT=lnones[:], rhs=lnsq[:, k, :], start=(k == 0), stop=(k == KD - 1)),
                     reads=[("lnsq", k), "lnones"], writes=[pk(be)], inc=(k == KD - 1))
            S.op("act", lambda: nc.scalar.activation(out=rstd_sb[:], in_=ps[bm][:], func=AF.Square), reads=[pk(bm)], writes=["rstd_sb"])
            S.op("act", lambda: nc.scalar.copy(out=mean_sb[:], in_=ps[bm][:]), reads=[pk(bm)], writes=["mean_sb"])
            S.op("dve", lambda: nc.vector.tensor_tensor(out=rstd_sb[:], in0=ps[be][:], in1=rstd_sb[:], op=ALU.subtract),
                 reads=[pk(be), "rstd_sb"], writes=["rstd_sb"])
            S.op("act", lambda: nc.scalar.activation(out=rstd_sb[:], in_=rstd_sb[:], func=AF.Sqrt, bias=lneps[:, 0:1]),
                 reads=["rstd_sb", "lneps"], writes=["rstd_sb"])
            S.op("dve", lambda: nc.vector.reciprocal(out=rstd_sb[:], in_=rstd_sb[:]), reads=["rstd_sb"], writes=["rstd_sb"])
            for k in range(KD):
                t1, t2 = lnt[0], lnt[1]
                k1, k2 = ("acc", 0), ("acc", 1)
                S.op("dve", lambda: nc.vector.tensor_tensor(out=t1[:], in0=xT[:, k, :], in1=mean_sb[:], op=ALU.subtract),
                     reads=[("xT", k), "mean_sb"], writes=[k1])
                S.op("pool", lambda: nc.gpsimd.tensor_tensor(out=t2[:], in0=t1[:], in1=rstd_sb[:], op=ALU.mult),
                     reads=[k1, "rstd_sb"], writes=[k2])
                S.op("act", lambda: nc.scalar.activation(out=xT[:, k, :], in_=t2[:], func=AF.Identity, scale=lng[ln_idx][:, k:k + 1],
                                                         bias=lnbias[ln_idx][:, k:k + 1]),
                     reads=[k2, "prmB"], writes=[("xT", k)])
                S.op("act", lambda: nc.scalar.activation(out=xTb[:, k, :], in_=t2[:], func=AF.Identity, scale=lng[ln_idx][:, k:k + 1],
                                                         bias=lnbias[ln_idx][:, k:k + 1]),
                     reads=[k2, "prmB"], writes=[("xTb", k)])

        S.op("pool", lambda: nc.gpsimd.memset(lneps[:, 0:1], LN_EPS), writes=["lneps"])
        S.op("pool", lambda: nc.gpsimd.memset(lneps[:, 1:2], 4.0 * RMS_EPS), writes=["lneps"])

        sg = [acc[0][:].bitcast(BF16)[:, 0:TS], acc[0][:].bitcast(BF16)[:, TS:2 * TS], acc[1][:].bitcast(BF16)[:, 0:TS], acc[1][:].bitcast(BF16)[:, TS:2 * TS]]

        def ffn_phase(l, ln_idx):
            for j in range(6):
                nfc = 4 if j < 5 else 2
                slot, skey = use_tile(("g", l, j))
                gv = slot[:, 0:4096].rearrange("p (k c) -> p k c", c=512)
                gb_ = []
                for fc in range(nfc):
                    bg = nb()
                    gb_.append(bg)
                    for k in range(KD):
                        S.op("pe", lambda: nc.tensor.matmul(ps[bg][:], lhsT=gv[:, k, fc * 128:(fc + 1) * 128], rhs=xTb[:, k, :], start=(k == 0), stop=(k == KD - 1)),
                             reads=[skey, ("xTb", k)], writes=[pk(bg)], inc=(k == KD - 1))
                    S.op("act", lambda: nc.scalar.activation(out=sg[fc], in_=ps[bg][:], func=AF.Silu), reads=[pk(bg)], writes=[("acc", fc // 2)])
                done_tile()
                slot, skey = use_tile(("u", l, j))
                uv = slot[:, 0:4096].rearrange("p (k c) -> p k c", c=512)
                for fc in range(nfc):
                    f = 4 * j + fc
                    bu = nb()
                    for k in range(KD):
                        S.op("pe", lambda: nc.tensor.matmul(ps[bu][:], lhsT=uv[:, k, fc * 128:(fc + 1) * 128], rhs=xTb[:, k, :], start=(k == 0), stop=(k == KD - 1)),
                             reads=[skey, ("xTb", k)], writes=[pk(bu)], inc=(k == KD - 1))
                    S.op("dve", lambda: nc.vector.tensor_tensor(out=big[:, f, :], in0=sg[fc], in1=ps[bu][:], op=ALU.mult),
                         reads=[("acc", fc // 2), pk(bu)], writes=[("big", f)])
                done_tile()
            for hf in range(2):
                banks = [nb() for _ in range(4)]
                for fg in range(3):
                    nf = 8 if fg < 2 else 6
                    slot, skey = use_tile(("dn", l, hf, fg))
                    dv = slot[:, 0:nf * 512].rearrange("p (f c) -> p f c", c=512)
                    for c in range(4):
                        for fl in range(nf):
                            f = fg * 8 + fl
                            S.op("pe", lambda: nc.tensor.matmul(ps[banks[c]][:], lhsT=dv[:, fl, c * 128:(c + 1) * 128], rhs=big[:, f, :],
                                                                start=(f == 0), stop=(f == NF - 1)),
                                 reads=[skey, ("big", f)], writes=[pk(banks[c])], inc=(fl == nf - 1))
                    done_tile()
                for c in range(4):
                    ln_accum(4 * hf + c, banks[c], ln_idx)
            ln_finish(ln_idx)

        def ssd_phase(first_in_seq):
            bd = nb()
            for tt in range(4):
                for k in range(KD):
                    S.op("pe", lambda: nc.tensor.matmul(ps[bd][:, tt * 32:(tt + 1) * 32], lhsT=xTb[:, k, tt * 128:(tt + 1) * 128], rhs=wdt[:, k, :],
                                                        start=(k == 0), stop=(k == KD - 1)),
                         reads=[("xTb", k), "wdt"], writes=[pk(bd)], inc=(tt == 3 and k == KD - 1))
            pd = ps[bd][:, 0:128].rearrange("p (t h) -> p t h", h=32)
            v_, av_, l_ = sp_t
            S.op("dve", lambda: nc.vector.tensor_tensor(out=v_[:], in0=pd, in1=bc(dtb_bc[:].unsqueeze(1), [128, 4, 32]), op=ALU.add),
                 reads=[pk(bd), "dtb_bc"], writes=["sp_v"])
            S.op("act", lambda: nc.scalar.activation(out=av_[:], in_=v_[:], func=AF.Abs), reads=["sp_v"], writes=["sp_a"])
            S.op("act", lambda: nc.scalar.activation(out=av_[:], in_=av_[:], func=AF.Exp, scale=-1.0), reads=["sp_a"], writes=["sp_a"])
            S.op("act", lambda: nc.scalar.activation(out=l_[:], in_=av_[:], func=AF.Ln, bias=1.0), reads=["sp_a"], writes=["sp_l"])
            S.op("dve", lambda: nc.vector.scalar_tensor_tensor(out=dt_sb[:], in0=v_[:], scalar=0.0, in1=l_[:], op0=ALU.max, op1=ALU.add),
                 reads=["sp_v", "sp_l"], writes=["dt_sb"])
            S.op("act", lambda: nc.scalar.activation(out=l_[:], in_=dt_sb[:], func=AF.Ln), reads=["dt_sb"], writes=["sp_l"])
            S.op("dve", lambda: nc.vector.tensor_tensor(out=a_sb[:], in0=dt_sb[:], in1=bc(A_bc[:].unsqueeze(1), [128, 4, 32]), op=ALU.mult),
                 reads=["dt_sb", "A_bc"], writes=["a_sb"])
            bcu = nb()
            for c in range(4):
                S.op("pe", lambda: nc.tensor.matmul(ps[bcu][:, c * 32:(c + 1) * 32], lhsT=trif[:], rhs=a_sb[:, c, :], start=True, stop=True),
                     reads=["trif", "a_sb"], writes=[pk(bcu)], inc=(c == 3))
            pc = ps[bcu][:, 0:128].rearrange("p (t h) -> p t h", h=32)
            S.op("dve", lambda: nc.vector.tensor_tensor(out=cumcolp[:], in0=pc, in1=l_[:], op=ALU.subtract), reads=[pk(bcu), "sp_l"], writes=["cumcolp"])
            S.op("act", lambda: nc.scalar.activation(out=expcum[:], in_=pc, func=AF.Exp), reads=[pk(bcu)], writes=["expcum"])
            if first_in_seq:
                S.op("pool", lambda: nc.gpsimd.memset(stateT[:], 0.0), writes=[("stateT", g) for g in range(NG)])
                S.op("pool", lambda: nc.gpsimd.memset(stbf[:], 0.0), writes=[("stbf", g) for g in range(NG)])
                S.op("pool", lambda: nc.gpsimd.memset(halo[:], 0.0), writes=[("halo", ci) for ci in range(32)])

            def inproj_pieces(g):
                gb = g % 2
                zs_, xbc_ = zs[gb], xbc[gb]
                st = {}

                def p_open_a():
                    st["slot"], st["skey"] = use_tile(("inA", g))
                    st["Wv"] = st["slot"][:, 0:8 * 512].rearrange("p (k c) -> p k c", c=512)

                def p_z(half):
                    def f():
                        Wv, skey = st["WvA"], st["skeyA"]
                        bz = nb()
                        for t2 in range(2):
                            tt = 2 * half + t2
                            for k in range(KD):
                                S.op("pe", lambda: nc.tensor.matmul(ps[bz][:, t2 * 256:(t2 + 1) * 256], lhsT=xTb[:, k, tt * 128:(tt + 1) * 128],
                                                                    rhs=Wv[:, k, 0:256], start=(k == 0), stop=(k == KD - 1)),
                                     reads=[skey, ("xTb", k)], writes=[pk(bz)], inc=(t2 == 1 and k == KD - 1))
                        S.op("act", lambda: nc.scalar.activation(out=acc[half][:], in_=ps[bz][:], func=AF.Tanh, scale=0.5), reads=[pk(bz)], writes=[("acc", half)])
                        S.op("dve", lambda: nc.vector.scalar_tensor_tensor(out=zs_[:, 2 * half:2 * half + 2, :].rearrange("p t c -> p (t c)"), in0=acc[half][:], scalar=1.0,
                                                                           in1=ps[bz][:], op0=ALU.add, op1=ALU.mult),
                             reads=[("acc", half), pk(bz)], writes=[("zs", gb)])
                        if half == 1:
                            done_tile()
                            done_tile()
                    return f

                def p_x(r0):
                    def f():
                        if r0 == 0:
                            p_open_a()
                            st["WvA"], st["skeyA"] = st["Wv"], st["skey"]
                        if r0 == 2:
                            st["slot"], st["skey"] = use_tile(("inB", g))
                            st["Wv"] = st["slot"][:, 0:8 * 256].rearrange("p (k c) -> p k c", c=256)
                        Wv, skey = st["Wv"], st["skey"]
                        rows = []
                        for r in (r0, r0 + 1):
                            ci = (2 * g + r) if r < 2 else (16 + g if r == 2 else 24 + g)
                            wcol = (256 + r * 128) if r < 2 else (r - 2) * 128
                            bx = nb()
                            for k in range(KD):
                                S.op("pe", lambda: nc.tensor.matmul(ps[bx][:], lhsT=Wv[:, k, wcol:wcol + 128], rhs=xTb[:, k, :],
                                                                    start=(k == 0), stop=(k == KD - 1)),
                                     reads=[skey, ("xTb", k)], writes=[pk(bx)], inc=(k == KD - 1))
                            rows.append((r, ci, bx, r % 2))
                        for (r, ci, bx, ur) in rows:
                            uk = ("ubuf", ur)
                            S.op("pool", lambda: nc.gpsimd.tensor_copy(out=ubuf[:, ur, 0:3], in_=halo[:, ci, :]), reads=[("halo", ci)], writes=[uk])
                            S.op("act", lambda: nc.scalar.copy(out=ubuf[:, ur, 3:TS + 3], in_=ps[bx][:]), reads=[pk(bx)], writes=[uk])
                            S.op("act", lambda: nc.scalar.activation(out=acc[ur][:], in_=ps[bx][:], func=AF.Identity, scale=cw[:, 3, ci:ci + 1], bias=cb[:, ci:ci + 1]),
                                 reads=[pk(bx), "prmA", "prmB"], writes=[("acc", ur)])
                        for kk in range(3):
                            for (r, ci, bx, ur) in rows:
                                S.op("dve", lambda: nc.vector.scalar_tensor_tensor(out=acc[ur][:], in0=ubuf[:, ur, kk:kk + TS], scalar=cw[:, kk, ci:ci + 1], in1=acc[ur][:],
                                                                                   op0=ALU.mult, op1=ALU.add), reads=[("ubuf", ur), ("acc", ur), "prmA"], writes=[("acc", ur)])
                        for (r, ci, bx, ur) in rows:
                            S.op("pool", lambda: nc.gpsimd.tensor_copy(out=halo[:, ci, :], in_=ubuf[:, ur, TS:TS + 3]), reads=[("ubuf", ur)], writes=[("halo", ci)])
                        for (r, ci, bx, ur) in rows:
                            S.op("act", lambda: nc.scalar.activation(out=ubuf[:, ur, 0:TS], in_=acc[ur][:], func=AF.Tanh), reads=[("acc", ur)], writes=[("ubuf", ur)])
                        for (r, ci, bx, ur) in rows:
                            S.op("dve", lambda: nc.vector.scalar_tensor_tensor(out=xbc_[:, r, :], in0=ubuf[:, ur, 0:TS], scalar=1.0, in1=acc[ur][:],
                                                                               op0=ALU.add, op1=ALU.mult), reads=[("ubuf", ur), ("acc", ur)], writes=[("xbc", gb, r)])
                    return f
                return [p_x(0), p_x(2), p_z(0), p_z(1)]

            cst = {}

            def stage_A(g, c):
                gb = g % 2
                xbc_ = xbc[gb]
                cs_ = cset[c % 2]
                ck = ("cs", c % 2)
                hs = slice(4 * g, 4 * g + 4)
                cs = slice(c * 128, (c + 1) * 128)
                bt = nb()
                T1 = psb(bt)
                for j, r in enumerate((0, 1, 2)):
                    S.op("pe", lambda: nc.tensor.transpose(out=T1[:, j * 128:(j + 1) * 128], in_=xbc_[:, r, cs], identity=identb[:]),
                         reads=[("xbc", gb, r), "identb"], writes=[pk(bt)], inc=(j == 2))
                T1x = T1[:, 0:256].rearrange("p (h q) -> p h q", q=64)
                S.op("act", lambda: nc.scalar.copy(out=cs_["xtok"][:], in_=T1[:, 0:256]), reads=[pk(bt)], writes=[(ck, "xtok")])
                S.op("pool", lambda: nc.gpsimd.tensor_tensor(out=cs_["xD"][:].rearrange("p (h q) -> p h q", q=64),
                                                             in0=cs_["xtok"][:].rearrange("p (h q) -> p h q", q=64),
                                                             in1=bc(D_bc[:, hs].unsqueeze(2), [128, 4, 64]), op=ALU.mult),
                     reads=[(ck, "xtok"), "D_bc"], writes=[(ck, "xD")])
                S.op("act", lambda: nc.scalar.copy(out=cs_["btok"][:], in_=T1[:, 256:384]), reads=[pk(bt)], writes=[(ck, "btok")])
                b1_ = nb()
                S.op("pe", lambda: nc.tensor.matmul(ps[b1_][:], lhsT=onesf[:], rhs=cs_["atri"][:], start=True, stop=True),
                     reads=["onesf", (ck, "atri")], writes=[pk(b1_)])
                cst[(g, c)] = dict(b1=b1_)

            def stage_A1(g, c):
                cs_ = cset[c % 2]
                ck = ("cs", c % 2)
                hs = slice(4 * g, 4 * g + 4)
                S.op("pool", lambda: nc.gpsimd.tensor_tensor(out=cs_["atri"][:].rearrange("p (h l) -> p h l", l=128),
                                                             in0=bc(trif[:].unsqueeze(1), [128, 4, 128]),
                                                             in1=bc(a_sb[:, c, hs].unsqueeze(2), [128, 4, 128]), op=ALU.mult),
                     reads=["trif", "a_sb"], writes=[(ck, "atri")])

            def stage_B1(g, c):
                gb = g % 2
                xbc_ = xbc[gb]
                cs_ = cset[c % 2]
                ck = ("cs", c % 2)
                hs = slice(4 * g, 4 * g + 4)
                cs = slice(c * 128, (c + 1) * 128)
                b1_ = cst[(g, c)]["b1"]
                X1 = ps[b1_][:].rearrange("p (h l) -> p h l", l=128)
                seg = cs_["atri"]
                S.op("dve", lambda: nc.vector.tensor_tensor(out=seg[:].rearrange("p (h l) -> p h l", l=128), in0=X1,
                                                            in1=bc(cumcolp[:, c, hs].unsqueeze(2), [128, 4, 128]), op=ALU.subtract),
                     reads=[pk(b1_), "cumcolp"], writes=[(ck, "atri")])
                S.op("act", lambda: nc.scalar.activation(out=e4[c % 2][:], in_=X1[:, :, 127], func=AF.Exp), reads=[pk(b1_)], writes=[("e4", c % 2)])
                seg3 = seg[:].rearrange("p (h l) -> p h l", l=128)
                S.op("pool", lambda: nc.gpsimd.affine_select(out=seg3, in_=seg3, pattern=[[0, 4], [1, 128]], compare_op=ALU.is_ge,
                                                             fill=neg_reg, base=0, channel_multiplier=-1), reads=[(ck, "atri")], writes=[(ck, "atri")])
                S.op("act", lambda: nc.scalar.activation(out=cs_["decayT"][:], in_=seg[:], func=AF.Exp), reads=[(ck, "atri")], writes=[(ck, "decayT")])
                b2_ = nb()
                S.op("pe", lambda: nc.tensor.matmul(ps[b2_][:, 0:128], lhsT=xbc_[:, 2, cs], rhs=xbc_[:, 3, cs], start=True, stop=True),
                     reads=[("xbc", gb, 2), ("xbc", gb, 3)], writes=[pk(b2_)])
                cst[(g, c)]["b2"] = b2_

            def stage_B2(g, c):
                cs_ = cset[c % 2]
                ck = ("cs", c % 2)
                b2_ = cst[(g, c)]["b2"]
                S.op("dve", lambda: nc.vector.tensor_tensor(out=cs_["GT"][:].rearrange("p (h l) -> p h l", l=128),
                                                            in0=cs_["decayT"][:].rearrange("p (h l) -> p h l", l=128),
                                                            in1=bc(ps[b2_][:, 0:128].unsqueeze(1), [128, 4, 128]), op=ALU.mult),
                     reads=[(ck, "decayT"), pk(b2_)], writes=[(ck, "GT")])
                dlast = cs_["decayT"][:].rearrange("p (h l) -> p h l", l=128)[:, :, 127:128]
                S.op("dve", lambda: nc.vector.tensor_tensor(out=cs_["xw"][:].rearrange("p (h q) -> p h q", q=64),
                                                            in0=cs_["xtok"][:].rearrange("p (h q) -> p h q", q=64),
                                                            in1=bc(dlast, [128, 4, 64]), op=ALU.mult),
                     reads=[(ck, "xtok"), (ck, "decayT")], writes=[(ck, "xw")])
                b3_ = nb()
                S.op("pe", lambda: nc.tensor.matmul(ps[b3_][:, 0:256], lhsT=identb[:], rhs=cs_["xD"][:], start=True, stop=False),
                     reads=["identb", (ck, "xD")], writes=[pk(b3_)], inc=False)
                for h in range(4):
                    S.op("pe", lambda: nc.tensor.matmul(ps[b3_][:, h * 64:(h + 1) * 64], lhsT=cs_["GT"][:, h * 128:(h + 1) * 128],
                                                        rhs=cs_["xtok"][:, h * 64:(h + 1) * 64], start=False, stop=True),
                         reads=[(ck, "GT"), (ck, "xtok")], writes=[pk(b3_)], inc=False)
                S.op("pe", lambda: nc.tensor.matmul(ps[b3_][:, 256:512], lhsT=cs_["btok"][:], rhs=cs_["xw"][:], start=True, stop=True),
                     reads=[(ck, "btok"), (ck, "xw")], writes=[pk(b3_)])
                cst[(g, c)]["b3"] = b3_

            def stage_C(g, c):
                gb = g % 2
                xbc_ = xbc[gb]
                hs = slice(4 * g, 4 * g + 4)
                cs = slice(c * 128, (c + 1) * 128)
                b3_ = cst[(g, c)]["b3"]
                st_g = stateT[:, g * 256:(g + 1) * 256]
                stb_g = stbf[:, g * 256:(g + 1) * 256]
                b4_ = nb()
                S.op("pe", lambda: nc.tensor.matmul(ps[b4_][:, 0:256], lhsT=xbc_[:, 3, cs], rhs=stb_g, start=True, stop=True),
                     reads=[("xbc", gb, 3), ("stbf", g)], writes=[pk(b4_)])
                S.op("pool", lambda: nc.gpsimd.tensor_tensor(out=sttmp[:].rearrange("p (h q) -> p h q", q=64),
                                                             in0=st_g.rearrange("p (h q) -> p h q", q=64),
                                                             in1=bc(e4[c % 2][:].unsqueeze(2), [128, 4, 64]), op=ALU.mult),
                     reads=[("stateT", g), ("e4", c % 2)], writes=["sttmp"])
                S.op("dve", lambda: nc.vector.tensor_tensor(out=ys[:].rearrange("p (h q) -> p h q", q=64),
                                                            in0=ps[b4_][:, 0:256].rearrange("p (h q) -> p h q", q=64),
                                                            in1=bc(expcum[:, c, hs].unsqueeze(2), [128, 4, 64]), op=ALU.mult),
                     reads=[pk(b4_), "expcum"], writes=["ys"])
                S.op("dve", lambda: nc.vector.tensor_tensor(out=st_g, in0=sttmp[:], in1=ps[b3_][:, 256:512], op=ALU.add),
                     reads=["sttmp", pk(b3_)], writes=[("stateT", g)])
                S.op("act", lambda: nc.scalar.copy(out=stb_g, in_=st_g), reads=[("stateT", g)], writes=[("stbf", g)])
                S.op("dve", lambda: nc.vector.tensor_tensor(out=ysum[:], in0=ps[b3_][:, 0:256], in1=ys[:], op=ALU.add),
                     reads=[pk(b3_), "ys"], writes=["ysum"])
                S.op("pool", lambda: nc.gpsimd.tensor_tensor(out=yg[:, c, :], in0=ysum[:], in1=zs[gb][:, c, :], op=ALU.mult),
                     reads=["ysum", ("zs", gb)], writes=[("yg", c)])
                S.op("act", lambda: nc.scalar.activation(out=junk[:], in_=yg[:, c, :], func=AF.Square, accum_out=ss[:, c:c + 1]),
                     reads=[("yg", c)], writes=["junk", ("ss", c)])

            def group_end1(g):
                S.op("act", lambda: nc.scalar.activation(out=sd4[:], in_=ss[:], func=AF.Sqrt, scale=1.0 / 256.0, bias=lneps[:, 1:2]),
                     reads=[("ss", c) for c in range(4)] + ["lneps"], writes=["sd4"])
                S.op("dve", lambda: nc.vector.reciprocal(out=rstd4[:], in_=sd4[:]), reads=["sd4"], writes=["rstd4"])
                for c in range(4):
                    S.op("act", lambda: nc.scalar.activation(out=ygn[:, c, :], in_=yg[:, c, :], func=AF.Copy, scale=rstd4[:, c:c + 1]),
                         reads=[("yg", c), "rstd4"], writes=[("ygn", c)])

            def group_end2(g):
                bn_ = nb()
                Tn = psb(bn_)
                for c in range(4):
                    for j in range(2):
                        S.op("pe", lambda: nc.tensor.transpose(out=Tn[:, (j * 4 + c) * 128:(j * 4 + c + 1) * 128], in_=ygn[:, c, j * 128:(j + 1) * 128],
                                                               identity=identb[:]),
                             reads=[("ygn", c), "identb"], writes=[pk(bn_)], inc=(c == 3 and j == 1))
                for j in range(2):
                    kc = 2 * g + j
                    S.op("dve", lambda: nc.vector.tensor_scalar(out=big[:, kc, :], in0=Tn[:, j * 512:(j + 1) * 512], scalar1=normw[:, kc:kc + 1],
                                                                scalar2=None, op0=ALU.mult),
                         reads=[pk(bn_), "prmB"], writes=[("big", kc)])

            for f in inproj_pieces(0):
                f()
            for g in range(NG):
                if g == 0:
                    for e_ in ("pe", "act", "dve", "pool"):
                        S.wait_all(e_, [iodom_out])
                P = inproj_pieces(g + 1) if g + 1 < NG else []
                P = P + [lambda: None] * (4 - len(P))
                order = list(P)
                if g > 0:
                    order.append(lambda: group_end2(g - 1))
                for it in range(-3, 5):
                    for (fn, c) in ((stage_C, it - 1), (stage_B2, it), (stage_B1, it + 1), (stage_A, it + 2), (stage_A1, it + 3)):
                        if 0 <= c <= 3:
                            order.append((lambda fn=fn, c=c: fn(g, c)))
                order.append(lambda: group_end1(g))
                for f in order:
                    f()
            group_end2(NG - 1)
            S.fence()
            for j in range(4):
                slot, skey = use_tile(("out", j))
                ov = slot[:, 0:16 * 256].rearrange("p (k c) -> p k c", c=256)
                for c in range(2):
                    k = 2 * j + c
                    b = nb()
                    for kk in range(16):
                        S.op("pe", lambda: nc.tensor.matmul(ps[b][:], lhsT=ov[:, kk, c * 128:(c + 1) * 128], rhs=big[:, kk, :],
                                                            start=(kk == 0), stop=(kk == 15)),
                             reads=[skey, ("big", kk)], writes=[pk(b)], inc=(kk == 15))
                    ln_accum(k, b, 0)
                done_tile()
            ln_finish(0)

        FVK = [("acc", 0)]
        FRK = ["rstd_sb"]

        def attn_phase(sc, first_in_seq):
            t0 = sc * TS
            for j in range(2):
                slot, skey = use_tile(("kvk", j))
                kvv_ = slot[:, 0:4096].rearrange("p (k c) -> p k c", c=512)
                for pr in range(4):
                    b = nb()
                    for k in range(KD):
                        S.op("pe", lambda: nc.tensor.matmul(ps[b][:], lhsT=kvv_[:, k, pr * 128:(pr + 1) * 128], rhs=xTb[:, k, :], start=(k == 0), stop=(k == KD - 1)),
                             reads=[skey, ("xTb", k)], writes=[pk(b)], inc=(k == KD - 1))
                    S.op("act", lambda: nc.scalar.copy(out=KT[:, 4 * j + pr, t0:t0 + TS], in_=ps[b][:]), reads=[pk(b)], writes=[("KT", 4 * j + pr)])
                done_tile()
            for j in range(2):
                slot, skey = use_tile(("kvv", j))
                vv = slot[:, 0:4096].rearrange("p (k c) -> p k c", c=512)
                for tt in range(4):
                    b = nb()
                    for k in range(KD):
                        S.op("pe", lambda: nc.tensor.matmul(ps[b][:], lhsT=xTb[:, k, tt * 128:(tt + 1) * 128], rhs=vv[:, k, :],
                                                            start=(k == 0), stop=(k == KD - 1)),
                             reads=[skey, ("xTb", k)], writes=[pk(b)], inc=(k == KD - 1))
                    kt = 4 * sc + tt
                    S.op("dve", lambda: nc.vector.tensor_copy(out=VA[:, kt, 8 * j:8 * j + 8, 0:64], in_=ps[b][:].rearrange("p (h q) -> p h q", q=64)),
                         reads=[pk(b)], writes=[("VA", kt)])
                done_tile()
            bf_ = nb()
            for k in range(KD):
                S.op("pe", lambda: nc.tensor.matmul(ps[bf_][0:16, :], lhsT=wf[:, k, :], rhs=xTb[:, k, :], start=(k == 0), stop=(k == KD - 1)),
                     reads=["wf", ("xTb", k)], writes=[pk(bf_)], inc=(k == KD - 1))
            S.op("dve", lambda: nc.vector.tensor_scalar(out=f_v[:], in0=ps[bf_][0:16, :], scalar1=bf_col[:, 0:1], scalar2=None, op0=ALU.add),
                 reads=[pk(bf_), "bf_col"], writes=[*FVK])
            S.op("act", lambda: nc.scalar.activation(out=f_a[:], in_=f_v[:], func=AF.Abs), reads=[*FVK], writes=[("acc", 1)])
            S.op("act", lambda: nc.scalar.activation(out=f_a[:], in_=f_a[:], func=AF.Exp, scale=-1.0), reads=[("acc", 1)], writes=[("acc", 1)])
            S.op("act", lambda: nc.scalar.activation(out=f_l[:], in_=f_a[:], func=AF.Ln, bias=1.0), reads=[("acc", 1)], writes=["mean_sb"])
            S.op("dve", lambda: nc.vector.scalar_tensor_tensor(out=f_l[:], in0=f_v[:], scalar=0.0, in1=f_l[:], op0=ALU.min, op1=ALU.subtract),
                 reads=[*FVK, "mean_sb"], writes=["mean_sb"])
            if first_in_seq:
                S.op("pool", lambda: nc.gpsimd.memset(Fcarry[:], 0.0), writes=["Fcarry"])
            S.op("dve", lambda: nc.vector.tensor_tensor_scan(out=Frow[:], data0=bc(onesf[0:16, 0:1], [16, TS]), data1=f_l[:], initial=Fcarry[:, 0:1],
                                                             op0=ALU.mult, op1=ALU.add),
                 reads=["onesf", "mean_sb", "Fcarry"], writes=[*FRK])
            S.op("dve", lambda: nc.vector.tensor_copy(out=Fcarry[:], in_=Frow[:, TS - 1:TS]), reads=[*FRK], writes=["Fcarry"])
            bt_ = nb()
            for tt in range(4):
                S.op("pe", lambda: nc.tensor.transpose(out=ps[bt_][:, tt * 16:(tt + 1) * 16], in_=Frow[:, tt * 128:(tt + 1) * 128], identity=identf[0:16, 0:16]),
                     reads=[*FRK, "identf"], writes=[pk(bt_)], inc=False)
            S.op("dve", lambda: nc.vector.tensor_scalar(out=fdiag[:], in0=identf[0:16, 0:16], scalar1=Frow[:, 255:256], scalar2=None, op0=ALU.mult),
                 reads=["identf", *FRK], writes=["fdiag"])
            S.op("pe", lambda: nc.tensor.matmul(ps[bt_][:, 64:80], lhsT=onesf[0:16, :], rhs=fdiag[:], start=True, stop=True),
                 reads=["onesf", "fdiag"], writes=[pk(bt_)])
            S.op("dve", lambda: nc.vector.tensor_copy(out=Fcol[:, 4 * sc:4 * sc + 4, :], in_=ps[bt_][:, 0:64].rearrange("p (t h) -> p t h", h=16)),
                 reads=[pk(bt_)], writes=["Fcol"])
            S.op("dve", lambda: nc.vector.tensor_copy(out=Fq0[:], in_=ps[bt_][:, 64:80]), reads=[pk(bt_)], writes=["Fq0"])
            nkt_all = 4 * sc + 4
            S.op("dve", lambda: nc.vector.tensor_tensor(out=bcol[:, 0:nkt_all, :], in0=bc(Fq0[:].unsqueeze(1), [128, nkt_all, 16]),
                                                        in1=Fcol[:, 0:nkt_all, :], op=ALU.subtract),
                 reads=["Fq0", "Fcol"], writes=["bcol"])
            S.op("pool", lambda: nc.gpsimd.memset(QZ[:], 0.0), writes=[("QT", p_) for p_ in range(8)])
            for j in range(2):
                slot, skey = use_tile(("q", j))
                qvv_ = slot[:, 0:4096].rearrange("p (k c) -> p k c", c=512)
                for pr in range(4):
                    b = nb()
                    for k in range(KD):
                        S.op("pe", lambda: nc.tensor.matmul(ps[b][:], lhsT=qvv_[:, k, pr * 128:(pr + 1) * 128], rhs=xTb[:, k, :], start=(k == 0), stop=(k == KD - 1)),
                             reads=[skey, ("xTb", k)], writes=[pk(b)], inc=(k == KD - 1))
                    pq = 4 * j + pr
                    S.op("act", lambda: nc.scalar.activation(out=QZ[0:64, 2 * pq, :], in_=ps[b][0:64, :], func=AF.Copy, scale=0.125),
                         reads=[pk(b)], writes=[("QT", pq)])
                    S.op("act", lambda: nc.scalar.activation(out=QZ[64:128, 2 * pq + 1, :], in_=ps[b][64:128, :], func=AF.Copy, scale=0.125),
                         reads=[pk(b)], writes=[("QT", pq)])
                done_tile()
            OT = big
            ring["n"] = 6
            ring["i"] = 0
            jobs = []
            nkt = 4 * sc + 4
            for h in range(AH):
                for kt in range(nkt):
                    jobs.append((h, kt))
            LA = 2
            NPT = 4
            pend = {}
            deferred = []

            def emit_st(i):
                h, kt = jobs[i]
                pr, po = h // 2, (h % 2) * 64
                jd = kt - 4 * sc
                c0 = 128 * jd if jd > 0 else 0
                n = TS - c0
                b = nb()
                S.op("pe", lambda: nc.tensor.matmul(ps[b][:, 0:n], lhsT=KT[:, pr, kt * 128:(kt + 1) * 128],
                                                    rhs=QZ[:, h, c0:TS], start=True, stop=True),
                     reads=[("KT", pr), ("QT", pr)], writes=[pk(b)])
                pend[i] = b

            def emit_rest(i):
                h, kt = jobs[i]
                ob = 6 + (h % 2)
                okey = pk(ob)
                jd = kt - 4 * sc
                c0 = 128 * jd if jd > 0 else 0
                n = TS - c0
                b = pend.pop(i)
                pt = PT[i % NPT]
                ptk = ("PT", i % NPT)
                oreg = ps[ob][0:65, :]
                S.op("act", lambda: nc.scalar.activation(out=pt[:, c0:TS], in_=ps[b][:, 0:n], func=AF.Exp, bias=bcol[:, kt, h:h + 1]),
                     reads=[pk(b), "bcol"], writes=[ptk])
                if jd >= 0:
                    S.op("pool", lambda: nc.gpsimd.affine_select(out=pt[:, c0:c0 + 128], in_=pt[:, c0:c0 + 128], pattern=[[1, 128]],
                                                                 compare_op=ALU.is_ge, fill=zero_reg, base=0, channel_multiplier=-1),
                         reads=[ptk], writes=[ptk])
                last = (kt == nkt - 1)
                S.op("pe", lambda: nc.tensor.matmul(oreg[:, c0:TS], lhsT=VA[:, kt, h, :], rhs=pt[:, c0:TS], start=(kt == 0), stop=last),
                     reads=[("VA", kt), ptk], writes=[okey], inc=last)
                if last:
                    rr_, Rs_ = rr[h % 2], Rs[h % 2]
                    S.op("dve", lambda: nc.vector.reciprocal(out=rr_[64:65, :], in_=ps[ob][64:65, :]), reads=[okey], writes=[("rr", 0)])

                    def fin(h=h, ob=ob, okey=okey, rr_=rr_, Rs_=Rs_):
                        b2 = nb()
                        S.op("pe", lambda: nc.tensor.matmul(ps[b2][0:64, :], lhsT=onesf[64:65, 0:64], rhs=rr_[64:65, :], start=True, stop=True),
                             reads=["onesf", ("rr", 0)], writes=[pk(b2)])
                        S.op("act", lambda: nc.scalar.copy(out=Rs_[:], in_=ps[b2][0:64, :]), reads=[pk(b2)], writes=[("Rs", 0)])
                        S.op("dve", lambda: nc.vector.tensor_tensor(out=OT[0:64, h, :], in0=ps[ob][0:64, :], in1=Rs_[:], op=ALU.mult),
                             reads=[okey, ("Rs", 0)], writes=[("big", h)])
                    deferred.append([2, fin])

            nj = len(jobs)
            for i in range(nj + LA):
                if i < nj:
                    emit_st(i)
                for dfr in list(deferred):
                    dfr[0] -= 1
                    if dfr[0] <= 0:
                        deferred.remove(dfr)
                        dfr[1]()
                if i >= LA:
                    emit_rest(i - LA)
            for dfr in deferred:
                dfr[1]()
            ring["n"] = 8
            for j in range(4):
                slot, skey = use_tile(("o", j))
                ov = slot[0:64, 0:16 * 256].rearrange("p (h c) -> p h c", c=256)
                for c in range(2):
                    k = 2 * j + c
                    b = nb()
                    for h in range(AH):
                        S.op("pe", lambda: nc.tensor.matmul(ps[b][:], lhsT=ov[:, h, c * 128:(c + 1) * 128], rhs=OT[0:64, h, :],
                                                            start=(h == 0), stop=(h == AH - 1)),
                             reads=[skey, ("big", h)], writes=[pk(b)], inc=(h == AH - 1))
                    ln_accum(k, b, 2)
                done_tile()
            ln_finish(2)

        def xk(tt):
            nm = "lnb" if tt < 2 else "lnsq"
            return [(nm, 4 * (tt % 2) + i) for i in range(4)]
        XK = xk(0) + xk(1) + xk(2) + xk(3)
        xld = big[:, 0:16, :].rearrange("p k t -> p (k t)").bitcast(F32).rearrange("p (t d) -> p t d", d=D)
        BK4 = [[("big", 4 * tt + i) for i in range(4)] for tt in range(4)]

        def load_x(bseq_, sc_):
            S.dma("sp", xld, dr["x"][bseq_, sc_ * TS:(sc_ + 1) * TS, :].rearrange("(t p) d -> p t d", p=128), [], BK4[0] + BK4[1] + BK4[2] + BK4[3], iodom_in)
        S.fence()
        gi = 0
        for bseq in range(NB if stop_after != "setup" else 0):
            for sc in range(NSC):
                tap.idx = gi
                t0 = sc * TS
                first = (sc == 0)
                if gi == 0 or stop_after is not None:
                    load_x(bseq, sc)
                for k in range(KD if stop_after != "xdma" else 0):
                    b = nb()
                    for tt in range(4):
                        S.op("pe", lambda: nc.tensor.transpose(out=ps[b][:, tt * 128:(tt + 1) * 128], in_=xld[:, tt, k * 128:(k + 1) * 128], identity=identf[:]),
                             reads=BK4[tt] + ["identf"], writes=[pk(b)], inc=(tt == 3))
                    S.op("act", lambda: nc.scalar.copy(out=xT[:, k, :], in_=ps[b][:]), reads=[pk(b)], writes=[("xT", k)])
                    S.op("dve", lambda: nc.vector.tensor_copy(out=xTb[:, k, :], in_=ps[b][:]), reads=[pk(b)], writes=[("xTb", k)])
                S.fence()
                if stop_after not in ("xload", "xdma"):
                    ssd_phase(first)
                    tap("dbg_x1", xT[:, 0, :], ("xT", 0), [128, TS])
                if stop_after not in ("xload", "ssd", "xdma"):
                    ffn_phase(0, 1)
                    tap("dbg_x2", xT[:, 0, :], ("xT", 0), [128, TS])
                if stop_after not in ("xload", "ssd", "ffn0", "xdma"):
                    S.fence()
                    attn_phase(sc, first)
                    tap("dbg_x3", xT[:, 0, :], ("xT", 0), [128, TS])
                    ffn_phase(1, 3)
                    nxt = gi + 1
                    if nxt < NB * NSC:
                        load_x(nxt // NSC, nxt % NSC)
                for tt in range(4 if stop_after != "xdma" else 0):
                    for hf in range(2):
                        b = nb()
                        for kq in range(4):
                            k = hf * 4 + kq
                            S.op("pe", lambda: nc.tensor.transpose(out=ps[b][:, kq * 128:(kq + 1) * 128], in_=xT[:, k, tt * 128:(tt + 1) * 128], identity=identf[:]),
                                 reads=[("xT", k), "identf"], writes=[pk(b)], inc=(kq == 3))
                        if hf == 0:
                            S.op("act", lambda: nc.scalar.copy(out=xin[:, tt, hf * 512:(hf + 1) * 512], in_=ps[b][:]), reads=[pk(b)], writes=xk(tt))
                        else:
                            S.op("dve", lambda: nc.vector.tensor_copy(out=xin[:, tt, hf * 512:(hf + 1) * 512], in_=ps[b][:]), reads=[pk(b)], writes=xk(tt))
                S.dma("sp", out_d[bseq, t0:t0 + TS, :].rearrange("(t p) d -> p t d", p=128), xin, XK, [], iodom_out)
                gi += 1
        assert stop_after is not None or wstate["next_use"] == total_tiles, (wstate, total_tiles)
        S.wait_all("sp", [iodom_out, dbgdom])
        build.stats = dict(nins=dict(S.nins), ndma=S.ndma, counts={k: v.count for k, v in S.dom.items()})
    return nc, list(dbg.keys())


_CACHE = {}


def kernel(**inputs):
    n_cores = 8
    x = np.ascontiguousarray(inputs["x"], dtype=np.float32)
    B, SEQ, _ = x.shape
    NB = B // n_cores
    key = (NB, SEQ)
    if key not in _CACHE:
        _CACHE[key] = build(NB, SEQ)[0]
    nc = _CACHE[key]
    in_maps = []
    for c in range(n_cores):
        m = {k: np.ascontiguousarray(v, dtype=np.float32) for k, v in inputs.items() if k != "x"}
        m["x"] = x[c * NB:(c + 1) * NB]
        in_maps.append(m)
    res = run_bass_kernel_spmd(nc, in_maps, core_ids=list(range(n_cores)))
    return np.concatenate([r["out"] for r in res.results], axis=0)
```
